# Optimizing a Trainium2 kernel written in Bass

```python
import math
import jax, jax.numpy as jnp
from jax import lax
import numpy as np


D_MODEL = 2048
BATCH = 4
SEQ = 4096
DEPTH = 4

CHUNK = 64
Q_BLOCK = 128
D_FF = 5504
EPS = 1e-6

HG_HEADS = 6
HG_DK = 128
HG_DV = 128
HG_QK = HG_HEADS * HG_DK
HG_WIDTH = HG_HEADS * HG_DV
DF_HEADS = 4
DF_DQK = 64
DF_DV = 2 * DF_DQK
DF_QK = DF_HEADS * 2 * DF_DQK
DF_WIDTH = DF_HEADS * DF_DV
ML_HEADS = 6
ML_Q_RANK = 512
ML_KV_RANK = 256
ML_NOPE = 128
ML_ROPE = 64
ML_DV = 128
ML_WIDTH = ML_HEADS * ML_DV
ROPE_BASE = 10000.0
REL_BUCKETS = 32
REL_MAX_DIST = 128

D_MIX = HG_WIDTH + DF_WIDTH + ML_WIDTH

kernel_name = 'hybrid_hgrn2_diffattn_mla_macaron'


def _in_sizes():
    return (HG_QK, HG_QK, HG_WIDTH, HG_WIDTH,
            DF_QK, DF_QK, DF_WIDTH,
            ML_Q_RANK, ML_KV_RANK, ML_ROPE)


def _rms(x, g):
    xf = x.astype(jnp.float32)
    y = xf * lax.rsqrt(jnp.mean(xf * xf, axis=-1, keepdims=True) + EPS)
    return (y * g.astype(jnp.float32)).astype(x.dtype)


def _swiglu(h, w_gate, w_up, w_down):
    return (jax.nn.silu(h @ w_gate) * (h @ w_up)) @ w_down


def _chunk_mask(q_pos, k_pos):
    return (k_pos[None, :] // CHUNK) <= (q_pos[:, None] // CHUNK)


def _query_blocks(a):
    b, h, s, d = a.shape
    return a.reshape(b, h, s // Q_BLOCK, Q_BLOCK, d).transpose(2, 0, 1, 3, 4)


def _merge_blocks(o):
    n, b, h, qb, d = o.shape
    return o.transpose(1, 2, 0, 3, 4).reshape(b, h, n * qb, d)


def _t5_bucket(rel):
    half = REL_BUCKETS // 2
    max_exact = half // 2
    ret = (rel > 0).astype(jnp.int32) * half
    n = jnp.abs(rel)
    large = max_exact + (jnp.log(jnp.maximum(n, 1).astype(jnp.float32) / max_exact)
                         / math.log(REL_MAX_DIST / max_exact) * (half - max_exact)).astype(jnp.int32)
    large = jnp.minimum(large, half - 1)
    return ret + jnp.where(n < max_exact, n, large)


def _rope(x, pos):
    r = x.shape[-1]
    freqs = ROPE_BASE ** (-jnp.arange(0, r, 2, dtype=jnp.float32) / r)
    ang = pos.astype(jnp.float32)[:, None] * freqs[None, :]
    cos = jnp.cos(ang)[:, None, :]
    sin = jnp.sin(ang)[:, None, :]
    xf = x.astype(jnp.float32)
    x1, x2 = xf[..., : r // 2], xf[..., r // 2:]
    return jnp.concatenate([x1 * cos - x2 * sin, x2 * cos + x1 * sin], axis=-1).astype(x.dtype)


def _hgrn2(q, f_logit, i, g, lb, o_gain):
    b, s, _ = q.shape
    nc = s // CHUNK

    def heads_chunks(a, d):
        return a.reshape(b, nc, CHUNK, HG_HEADS, d).transpose(1, 0, 3, 2, 4).astype(jnp.float32)

    lbh = lb.astype(jnp.float32).reshape(HG_HEADS, 1, HG_DK)
    f = lbh + (1.0 - lbh) * jax.nn.sigmoid(heads_chunks(f_logit, HG_DK))
    log_f = jnp.log(f)
    k = 1.0 - f
    qc = heads_chunks(q, HG_DK) * (HG_DK ** -0.5)
    vc = heads_chunks(i, HG_DV)
    causal = jnp.tril(jnp.ones((CHUNK, CHUNK), dtype=bool))

    def step(state, inp):
        qt, kt, vt, lf = inp
        cum = jnp.cumsum(lf, axis=-2)
        o_inter = jnp.einsum('bhtk,bhkv->bhtv', qt * jnp.exp(cum), state)
        rel = cum[..., :, None, :] - cum[..., None, :, :]
        decay = jnp.exp(jnp.where(causal[:, :, None], rel, -jnp.inf))
        scores = jnp.einsum('bhtk,bhsk,bhtsk->bhts', qt, kt, decay)
        o = o_inter + jnp.einsum('bhts,bhsv->bhtv', scores, vt)
        last = cum[..., -1:, :]
        k_dec = kt * jnp.exp(last - cum)
        state = jnp.exp(last[..., 0, :])[..., None] * state + jnp.einsum('bhsk,bhsv->bhkv', k_dec, vt)
        return state, o

    s0 = jnp.zeros((b, HG_HEADS, HG_DK, HG_DV), jnp.float32)
    _, o = lax.scan(step, s0, (qc, k, vc, log_f))
    o = o.transpose(1, 0, 3, 2, 4).reshape(b, s, HG_HEADS, HG_DV)
    gh = g.reshape(b, s, HG_HEADS, HG_DV).astype(jnp.float32)
    o = _rms(o, o_gain) * jax.nn.silu(gh)
    return o.reshape(b, s, HG_WIDTH).astype(q.dtype)


def _diff_attn(q, k, v, q_gain, k_gain, lam, lambda_init, subln, rel_bias):
    b, s, _ = q.shape
    qh = _rms(q.reshape(b, s, DF_HEADS, 2, DF_DQK), q_gain).transpose(3, 0, 2, 1, 4)
    kh = _rms(k.reshape(b, s, DF_HEADS, 2, DF_DQK), k_gain).transpose(3, 0, 2, 1, 4)
    vh = v.reshape(b, s, DF_HEADS, DF_DV).transpose(0, 2, 1, 3)
    k1, k2 = kh[0], kh[1]
    k_pos = jnp.arange(s)
    scale = DF_DQK ** -0.5
    table = rel_bias.astype(jnp.float32)

    def blk(args):
        idx, a1, a2 = args
        q_pos = idx * Q_BLOCK + jnp.arange(Q_BLOCK)
        mask = _chunk_mask(q_pos, k_pos)
        bias = table[_t5_bucket(k_pos[None, :] - q_pos[:, None])].transpose(2, 0, 1)

        def probs(a, kk):
            logits = jnp.einsum('bhqd,bhkd->bhqk', a, kk).astype(jnp.float32) * scale + bias
            return jax.nn.softmax(jnp.where(mask, logits, -jnp.inf), axis=-1)

        w = probs(a1, k1) - lam * probs(a2, k2)
        return jnp.einsum('bhqk,bhkd->bhqd', w.astype(vh.dtype), vh)

    nqb = s // Q_BLOCK
    o = _merge_blocks(lax.map(blk, (jnp.arange(nqb), _query_blocks(qh[0]), _query_blocks(qh[1]))))
    o = _rms(o, subln) * (1.0 - lambda_init)
    return o.transpose(0, 2, 1, 3).reshape(b, s, DF_WIDTH).astype(q.dtype)


def _mla(c_q, c_kv, k_rope, q_lora_norm, w_uq, kv_lora_norm, w_ukv, q_gain, k_gain):
    b, s, _ = c_q.shape
    pos = jnp.arange(s)
    q = (_rms(c_q, q_lora_norm) @ w_uq).reshape(b, s, ML_HEADS, ML_NOPE + ML_ROPE)
    kv = (_rms(c_kv, kv_lora_norm) @ w_ukv).reshape(b, s, ML_HEADS, ML_NOPE + ML_DV)
    k_nope, v = kv[..., :ML_NOPE], kv[..., ML_NOPE:]
    k = jnp.concatenate([k_nope, jnp.broadcast_to(k_rope[:, :, None, :], (b, s, ML_HEADS, ML_ROPE))], axis=-1)
    q = _rms(q, q_gain)
    k = _rms(k, k_gain)
    q = jnp.concatenate([q[..., :ML_NOPE], _rope(q[..., ML_NOPE:], pos)], axis=-1).transpose(0, 2, 1, 3)
    k = jnp.concatenate([k[..., :ML_NOPE], _rope(k[..., ML_NOPE:], pos)], axis=-1).transpose(0, 2, 1, 3)
    v = v.transpose(0, 2, 1, 3)
    scale = (ML_NOPE + ML_ROPE) ** -0.5

    def blk(args):
        idx, qb = args
        q_pos = idx * Q_BLOCK + jnp.arange(Q_BLOCK)
        mask = _chunk_mask(q_pos, pos)
        logits = jnp.einsum('bhqd,bhkd->bhqk', qb, k).astype(jnp.float32) * scale
        p = jax.nn.softmax(jnp.where(mask, logits, -jnp.inf), axis=-1)
        return jnp.einsum('bhqk,bhkd->bhqd', p.astype(v.dtype), v)

    o = _merge_blocks(lax.map(blk, (jnp.arange(s // Q_BLOCK), _query_blocks(q))))
    return o.transpose(0, 2, 1, 3).reshape(b, s, ML_WIDTH).astype(c_q.dtype)


def setup_inputs(seed: int = 0) -> dict:
    key = jax.random.key(seed)
    ks = iter(jax.random.split(key, 40))
    L, D, F = DEPTH, D_MODEL, D_FF
    p_in = sum(_in_sizes())

    def w(shape, fan_in):
        return jax.random.normal(next(ks), shape, jnp.float32) * fan_in ** -0.5

    def gain(shape):
        return 1.0 + 0.02 * jax.random.normal(next(ks), shape, jnp.float32)

    def small(shape, sc):
        return sc * jax.random.normal(next(ks), shape, jnp.float32)

    return {
        'x': jax.random.normal(next(ks), (BATCH, SEQ, D), jnp.float32),
        'ffn_a_norm': gain((L, D)),
        'ffn_a_w_gate': w((L, D, F), D),
        'ffn_a_w_up': w((L, D, F), D),
        'ffn_a_w_down': w((L, F, D), F),
        'mix_norm': gain((L, D)),
        'w_in': w((L, D, p_in), D),
        'w_out': w((L, D_MIX, D), D_MIX),
        'hgrn_lb_logits': small((L, HG_QK), 1.0),
        'hgrn_out_norm': gain((L, HG_DV)),
        'diff_q_norm': gain((L, DF_DQK)),
        'diff_k_norm': gain((L, DF_DQK)),
        'diff_lambda_q1': small((L, DF_DQK), 0.1),
        'diff_lambda_k1': small((L, DF_DQK), 0.1),
        'diff_lambda_q2': small((L, DF_DQK), 0.1),
        'diff_lambda_k2': small((L, DF_DQK), 0.1),
        'diff_subln': gain((L, DF_DV)),
        'rel_bias': small((REL_BUCKETS, DF_HEADS), 0.5),
        'mla_q_lora_norm': gain((L, ML_Q_RANK)),
        'mla_w_uq': w((L, ML_Q_RANK, ML_HEADS * (ML_NOPE + ML_ROPE)), ML_Q_RANK),
        'mla_kv_lora_norm': gain((L, ML_KV_RANK)),
        'mla_w_ukv': w((L, ML_KV_RANK, ML_HEADS * (ML_NOPE + ML_DV)), ML_KV_RANK),
        'mla_q_norm': gain((L, ML_NOPE + ML_ROPE)),
        'mla_k_norm': gain((L, ML_NOPE + ML_ROPE)),
        'ffn_b_norm': gain((L, D)),
        'ffn_b_w_gate': w((L, D, F), D),
        'ffn_b_w_up': w((L, D, F), D),
        'ffn_b_w_down': w((L, F, D), F),
    }


def reference(x, ffn_a_norm, ffn_a_w_gate, ffn_a_w_up, ffn_a_w_down, mix_norm, w_in, w_out,
              hgrn_lb_logits, hgrn_out_norm, diff_q_norm, diff_k_norm, diff_lambda_q1, diff_lambda_k1,
              diff_lambda_q2, diff_lambda_k2, diff_subln, rel_bias, mla_q_lora_norm, mla_w_uq,
              mla_kv_lora_norm, mla_w_ukv, mla_q_norm, mla_k_norm, ffn_b_norm, ffn_b_w_gate,
              ffn_b_w_up, ffn_b_w_down):
    lb_all = jnp.cumsum(jax.nn.softmax(hgrn_lb_logits.astype(jnp.float32), axis=0), axis=0)
    lb_all = lb_all - lb_all[0:1]
    sizes = _in_sizes()
    offs = [0]
    for sz in sizes:
        offs.append(offs[-1] + sz)

    for l in range(DEPTH):
        x = x + 0.5 * _swiglu(_rms(x, ffn_a_norm[l]), ffn_a_w_gate[l], ffn_a_w_up[l], ffn_a_w_down[l])
        proj = _rms(x, mix_norm[l]) @ w_in[l]
        p = [proj[..., offs[j]:offs[j + 1]] for j in range(len(sizes))]
        o_hg = _hgrn2(p[0], p[1], p[2], p[3], lb_all[l], hgrn_out_norm[l])
        lambda_init = 0.8 - 0.6 * math.exp(-0.3 * l)
        lam = (jnp.exp(jnp.sum(diff_lambda_q1[l].astype(jnp.float32) * diff_lambda_k1[l].astype(jnp.float32)))
               - jnp.exp(jnp.sum(diff_lambda_q2[l].astype(jnp.float32) * diff_lambda_k2[l].astype(jnp.float32)))
               + lambda_init)
        o_df = _diff_attn(p[4], p[5], p[6], diff_q_norm[l], diff_k_norm[l], lam, lambda_init,
                          diff_subln[l], rel_bias)
        o_ml = _mla(p[7], p[8], p[9], mla_q_lora_norm[l], mla_w_uq[l], mla_kv_lora_norm[l],
                    mla_w_ukv[l], mla_q_norm[l], mla_k_norm[l])
        x = x + (jnp.concatenate([o_hg, o_df, o_ml], axis=-1) @ w_out[l]).astype(x.dtype)
        x = x + 0.5 * _swiglu(_rms(x, ffn_b_norm[l]), ffn_b_w_gate[l], ffn_b_w_up[l], ffn_b_w_down[l])
    return x
```

```python
import math
from contextlib import ExitStack

import numpy as np
import ml_dtypes

import concourse.bass as bass
import concourse.mybir as mybir
from concourse.bass_utils import run_bass_kernel_spmd

F32 = mybir.dt.float32
BF16 = mybir.dt.bfloat16
AF = mybir.ActivationFunctionType
ALU = mybir.AluOpType
AX = mybir.AxisListType

D = 2048
DFF = 5504
NF = DFF // 128
KD = D // 128
EPS = 1e-6
NEG = -60.0


class Buf:
    __slots__ = ("name", "w", "r")

    def __init__(self, name=""):
        self.name = name
        self.w = None
        self.r = []


class T:
    __slots__ = ("t", "b")

    def __init__(self, t, name=""):
        self.t = t
        self.b = Buf(name)


class Sched:
    def __init__(self, nc, stack):
        self.nc = nc
        self.stack = stack
        self.engs = {"pe": nc.tensor, "act": nc.scalar, "dve": nc.vector, "pool": nc.gpsimd, "sp": nc.sync}
        self.sem = {}
        self.cnt = {}
        for e in self.engs:
            self.sem[e] = stack.enter_context(nc.semaphore("s_" + e))
            self.cnt[e] = 0
        self.waited = {}
        self.dsems = {}
        self.dcnt = {}
        self.nsem = 0

    def dsem(self, name):
        if name not in self.dsems:
            s = self.stack.enter_context(self.nc.semaphore("d_" + name))
            self.dsems[name] = s
            self.dcnt[name] = 0
        return name

    def _wait(self, e, deps):
        best = {}
        for (k, v) in deps:
            if k == e and e == "pe":
                continue
            if k not in best or best[k] < v:
                best[k] = v
        for k, v in best.items():
            if self.waited.get((e, k), 0) >= v:
                continue
            s = self.sem[k] if k in self.sem else self.dsems[k]
            self.engs[e].wait_ge(s, v)
            self.waited[(e, k)] = v

    def op(self, e, fn, rd=(), wr=(), dsem=None):
        self.nops = getattr(self, "nops", 0) + 1
        if self.nops > getattr(self, "max_ops", 1 << 60):
            return None
        deps = []
        rd = [b.b if isinstance(b, T) else b for b in rd]
        wr = [b.b if isinstance(b, T) else b for b in wr]
        wr = wr + [b for b in rd if b.name.startswith(("bank", "psb"))]
        rd = [b for b in rd if not b.name.startswith(("bank", "psb"))]
        for b in rd:
            b = b.b if isinstance(b, T) else b
            if b.w is not None:
                deps.append(b.w)
        for b in wr:
            b = b.b if isinstance(b, T) else b
            if b.w is not None:
                deps.append(b.w)
            deps.extend(b.r)
        self._wait(e, deps)
        ins = fn(self.engs[e])
        if dsem is not None:
            self.dsem(dsem)
            self.dcnt[dsem] += 16
            ins.then_inc(self.dsems[dsem], 16)
            tok = (dsem, self.dcnt[dsem])
        else:
            self.cnt[e] += 1
            ins.then_inc(self.sem[e], 1)
            tok = (e, self.cnt[e])
        for b in rd:
            b = b.b if isinstance(b, T) else b
            b.r.append(tok)
            if len(b.r) > 64:
                m = {}
                for (k, v) in b.r:
                    if k not in m or m[k] < v:
                        m[k] = v
                b.r = list(m.items())
        for b in wr:
            b = b.b if isinstance(b, T) else b
            b.w = tok
            b.r = []
        return tok

    def cc(self, fn):
        name = self.dsem("ccsem")
        ins = fn(self.engs["pool"])
        self.dcnt[name] += 1
        ins.then_inc(self.dsems[name], 1)

    def barrier(self, engines=None):
        toks = [(e, c) for e, c in self.cnt.items() if c > 0]
        toks += [(n, c) for n, c in self.dcnt.items() if c > 0]
        for e in (engines or self.engs):
            self._wait_all(e, toks)

    def _wait_all(self, e, toks):
        for (k, v) in toks:
            if k == e:
                continue
            if self.waited.get((e, k), 0) >= v:
                continue
            s = self.sem[k] if k in self.sem else self.dsems[k]
            self.engs[e].wait_ge(s, v)
            self.waited[(e, k)] = v


class Ring:
    def __init__(self, items):
        self.items = items
        self.i = 0

    def next(self):
        x = self.items[self.i % len(self.items)]
        self.i += 1
        return x


class Ctx:
    def __init__(self, nc, st):
        self.nc = nc
        self.sfx = ""
        self.S = Sched(nc, st)
        self.psf = [st.enter_context(nc.psum_tensor("psf%d" % i, [128, 512], F32)) for i in range(7)]
        self.psb = st.enter_context(nc.psum_tensor("psb", [128, 1024], BF16))
        self.bankT = [T(self.psf[i], "bank%d" % i) for i in range(7)]
        self.psbT = [T(None, "psb") for i in range(8)]
        for x in self.psbT:
            x.b = self.psbT[0].b

    def banks(self, ids):
        return [self.bankT[i] for i in ids]


def _groups(n, g):
    out = []
    i = 0
    while i < n:
        out.append((i, min(g, n - i)))
        i += g
    return out


def emit_rstd(S, out_ap, out_T, in_ap, in_T, scale, tmp_ap, tmp_T):
    S.op("act", lambda e: e.activation(out=tmp_ap, in_=in_ap, func=AF.Ln, scale=scale, bias=EPS), rd=[in_T], wr=[tmp_T])
    S.op("act", lambda e: e.activation(out=out_ap, in_=tmp_ap, func=AF.Exp, scale=-0.5), rd=[tmp_T], wr=[out_T])


def norm_tile(C, st_bufs, x_d, tok0, TT, banks):
    S = C.S
    NS = TT // 512
    xin, sq, hT, rstd, lnt, gcol, ones = (st_bufs[k] for k in ("xin", "sq", "hT", "rstd", "lnt", "gcol", "ones"))
    bk = [banks.next() for _ in range(NS)]
    for k in range(KD):
        xi = xin.next()
        si = sq.next()
        S.op("sp", lambda e: e.dma_start(out=xi.t[:, :], in_=x_d[k * 128:(k + 1) * 128, tok0:tok0 + TT]), wr=[xi], dsem=xi.b.name)
        S.op("act", lambda e: e.activation(out=si.t[:, :], in_=xi.t[:, :], func=AF.Square), rd=[xi], wr=[si])
        for s in range(NS):
            S.op("pe", lambda e: e.matmul(bk[s].t[:, :], ones.t[:, :], si.t[:, s * 512:(s + 1) * 512], start=(k == 0), stop=(k == KD - 1)),
                 rd=[ones, si], wr=[bk[s]])
    for s in range(NS):
        emit_rstd(S, rstd.t[:, s * 512:(s + 1) * 512], rstd, bk[s].t[:, :], bk[s], 1.0 / D, lnt.t[:, s * 512:(s + 1) * 512], lnt)
    for k in range(KD):
        xi = xin.next()
        S.op("sp", lambda e: e.dma_start(out=xi.t[:, :], in_=x_d[k * 128:(k + 1) * 128, tok0:tok0 + TT]), wr=[xi], dsem=xi.b.name)
        S.op("dve", lambda e: e.scalar_tensor_tensor(out=hT.t[:, k, :], in0=xi.t[:, :], scalar=gcol.t[:, k:k + 1], in1=rstd.t[:, :],
                                                     op0=ALU.mult, op1=ALU.mult), rd=[xi, gcol, rstd], wr=[hT])


def norm_bufs(C, st, g_d, TT, pfx):
    nc, S = C.nc, C.S
    sb = lambda n, sh, dt: T(st.enter_context(nc.sbuf_tensor(pfx + n + C.sfx, sh, dt)), pfx + n)
    B = {}
    B["xin"] = Ring([sb("xin%d" % i, [128, TT], F32) for i in range(2)])
    B["sq"] = Ring([sb("sq%d" % i, [128, TT], BF16) for i in range(2)])
    B["hT"] = sb("hT", [128, KD, TT], BF16)
    B["rstd"] = sb("rstd", [128, TT], F32)
    B["lnt"] = sb("lnt", [128, TT], F32)
    B["gcol"] = sb("gcol", [128, KD], F32)
    B["ones"] = sb("ones", [128, 128], BF16)
    S.op("sp", lambda e: e.dma_start(out=B["gcol"].t[:, :], in_=g_d.rearrange("(k p) -> p k", p=128), allow_slow_non_contiguous=True),
         wr=[B["gcol"]], dsem=pfx + "gcol")
    S.op("dve", lambda e: e.memset(B["ones"].t[:, :], 1.0), wr=[B["ones"]])
    return B


def phase_ffn(C, x_d, xo_d, g_d, wg_d, wu_d, wd_d, NTOK, TT, pfx):
    nc, S = C.nc, C.S
    NS = TT // 512
    S.barrier()
    with ExitStack() as st:
        sb = lambda n, sh, dt: T(st.enter_context(nc.sbuf_tensor(pfx + n + C.sfx, sh, dt)), pfx + n)
        NB = norm_bufs(C, st, g_d, TT, pfx)
        hT = NB["hT"]
        GW = 256
        wg = Ring([sb("wg%d" % i, [128, KD, GW], BF16) for i in range(2)])
        wu = Ring([sb("wu%d" % i, [128, KD, GW], BF16) for i in range(2)])
        actT = [T(None, pfx + "act%d" % f) for f in range(NF)]
        actT_t = st.enter_context(nc.sbuf_tensor(pfx + "actT" + C.sfx, [128, NF, TT], BF16))
        sg = Ring([sb("sg%d" % i, [128, 512], BF16) for i in range(2)])
        DGC = 4 // NS
        wd = Ring([sb("wd%d" % i, [128, 4, DGC * 128], BF16) for i in range(2)])
        xres = Ring([sb("xres%d" % i, [128, 512], F32) for i in range(2)])
        yo = Ring([sb("yo%d" % i, [128, 512], F32) for i in range(2)])
        banks = Ring(C.banks([0, 1, 2, 3, 4, 5]))
        dbanks = Ring(C.banks([0, 1, 2, 3, 4, 5, 6]))
        for tt in range(NTOK // TT):
            tok0 = tt * TT
            norm_tile(C, NB, x_d, tok0, TT, banks)
            for (f0, nf) in _groups(NF, GW // 128):
                g_s = wg.next()
                u_s = wu.next()
                S.op("pool", lambda e: e.dma_start(out=g_s.t[:, :, 0:nf * 128],
                                                   in_=wg_d[:, f0 * 128:(f0 + nf) * 128].rearrange("(k p) f -> p k f", p=128)),
                     wr=[g_s], dsem=g_s.b.name)
                S.op("pool", lambda e: e.dma_start(out=u_s.t[:, :, 0:nf * 128],
                                                   in_=wu_d[:, f0 * 128:(f0 + nf) * 128].rearrange("(k p) f -> p k f", p=128)),
                     wr=[u_s], dsem=u_s.b.name)
                for fi in range(nf):
                    f = f0 + fi
                    for s in range(NS):
                        bg = banks.next()
                        bu = banks.next()
                        for k in range(KD):
                            S.op("pe", lambda e: e.matmul(bg.t[:, :], g_s.t[:, k, fi * 128:(fi + 1) * 128], hT.t[:, k, s * 512:(s + 1) * 512],
                                                          start=(k == 0), stop=(k == KD - 1)), rd=[g_s, hT], wr=[bg])
                        for k in range(KD):
                            S.op("pe", lambda e: e.matmul(bu.t[:, :], u_s.t[:, k, fi * 128:(fi + 1) * 128], hT.t[:, k, s * 512:(s + 1) * 512],
                                                          start=(k == 0), stop=(k == KD - 1)), rd=[u_s, hT], wr=[bu])
                        sgi = sg.next()
                        S.op("act", lambda e: e.activation(out=sgi.t[:, :], in_=bg.t[:, :], func=AF.Silu), rd=[bg], wr=[sgi])
                        S.op("dve", lambda e: e.tensor_tensor(out=actT_t[:, f, s * 512:(s + 1) * 512], in0=bu.t[:, :], in1=sgi.t[:, :], op=ALU.mult),
                             rd=[bu, sgi], wr=[actT[f]])
            for dg in range(KD // DGC):
                db = [[dbanks.next() for s in range(NS)] for dd in range(DGC)]
                for (f0, nf) in _groups(NF, 4):
                    w_s = wd.next()
                    S.op("pool", lambda e: e.dma_start(out=w_s.t[:, 0:nf, :],
                                                       in_=wd_d[f0 * 128:(f0 + nf) * 128, dg * DGC * 128:(dg + 1) * DGC * 128].rearrange("(j p) c -> p j c", p=128)),
                         wr=[w_s], dsem=w_s.b.name)
                    for fi in range(nf):
                        f = f0 + fi
                        for dd in range(DGC):
                            for s in range(NS):
                                S.op("pe", lambda e: e.matmul(db[dd][s].t[:, :], w_s.t[:, fi, dd * 128:(dd + 1) * 128],
                                                              actT_t[:, f, s * 512:(s + 1) * 512], start=(f == 0), stop=(f == NF - 1)),
                                     rd=[w_s, actT[f]], wr=[db[dd][s]])
                for dd in range(DGC):
                    d = dg * DGC + dd
                    for s in range(NS):
                        xr = xres.next()
                        y = yo.next()
                        c0 = tok0 + s * 512
                        S.op("sp", lambda e: e.dma_start(out=xr.t[:, :], in_=x_d[d * 128:(d + 1) * 128, c0:c0 + 512]), wr=[xr], dsem=xr.b.name)
                        S.op("dve", lambda e: e.scalar_tensor_tensor(out=y.t[:, :], in0=db[dd][s].t[:, :], scalar=0.5, in1=xr.t[:, :],
                                                                     op0=ALU.mult, op1=ALU.add), rd=[db[dd][s], xr], wr=[y])
                        S.op("sp", lambda e: e.dma_start(out=xo_d[d * 128:(d + 1) * 128, c0:c0 + 512], in_=y.t[:, :]), rd=[y], dsem=y.b.name)
        S.barrier()


def phase_norm(C, x_d, g_d, xn_d, NTOK, TT, pfx):
    nc, S = C.nc, C.S
    S.barrier()
    with ExitStack() as st:
        NB = norm_bufs(C, st, g_d, TT, pfx)
        banks = Ring(C.banks([0, 1, 2, 3]))
        for tt in range(NTOK // TT):
            norm_tile(C, NB, x_d, tt * TT, TT, banks)
            for j in range(TT // 128):
                S.op("sp", lambda e: e.dma_start(out=xn_d[tt * (TT // 128) + j, :, :, :], in_=NB["hT"].t[:, :, j * 128:(j + 1) * 128]),
                     rd=[NB["hT"]], dsem=pfx + "xnout")
        S.barrier()


def phase_wout(C, x_d, xo_d, o_d, wo_d, NTOK, pfx, sel_d=None):
    nc, S = C.nc, C.S
    S.barrier()
    with ExitStack() as st:
        sb = lambda n, sh, dt: T(st.enter_context(nc.sbuf_tensor(pfx + n + C.sfx, sh, dt)), pfx + n)
        wo = sb("wo", [128, KD, D], BF16)
        for k in range(KD):
            S.op("pool", lambda e: e.dma_start(out=wo.t[:, k, :], in_=wo_d[k * 128:(k + 1) * 128, :]), wr=[wo], dsem=pfx + "wo")
        ot = Ring([sb("ot%d" % i, [128, KD, 512], BF16) for i in range(2)])
        xres = Ring([sb("xres%d" % i, [128, 512], F32) for i in range(2)])
        yo = Ring([sb("yo%d" % i, [128, 512], F32) for i in range(2)])
        if sel_d is not None:
            selt = load_bcast(C, st, pfx + "sel", sel_d, 16)
            oa, ob = sb("oa", [128, KD, 512], BF16), sb("ob", [128, KD, 512], BF16)
            otmp = sb("otmp", [128, KD, 512], F32)
        banks = Ring(C.banks([0, 1, 2, 3]))
        for s in range(NTOK // 512):
            c0 = s * 512
            o_s = ot.next()
            if sel_d is None:
                S.op("sp", lambda e: e.dma_start(out=o_s.t[:, :, :], in_=o_d[:, c0:c0 + 512].rearrange("(k p) t -> p k t", p=128)),
                     wr=[o_s], dsem=o_s.b.name)
            else:
                S.op("sp", lambda e: e.dma_start(out=oa.t[:, :, :], in_=o_d[:, c0:c0 + 512].rearrange("(k p) t -> p k t", p=128)),
                     wr=[oa], dsem=oa.b.name)
                S.op("sp", lambda e: e.dma_start(out=ob.t[:, :, :], in_=o_d[:, NTOK + c0:NTOK + c0 + 512].rearrange("(k p) t -> p k t", p=128)),
                     wr=[ob], dsem=ob.b.name)
                S.op("dve", lambda e: e.tensor_scalar(out=otmp.t[:, :, :], in0=oa.t[:, :, :], scalar1=selt.t[:, 0:1], scalar2=None, op0=ALU.mult),
                     rd=[oa, selt], wr=[otmp])
                S.op("dve", lambda e: e.scalar_tensor_tensor(out=o_s.t[:, :, :], in0=ob.t[:, :, :], scalar=selt.t[:, 1:2], in1=otmp.t[:, :, :],
                                                             op0=ALU.mult, op1=ALU.add), rd=[ob, selt, otmp], wr=[o_s])
            for d in range(KD):
                bk = banks.next()
                for k in range(KD):
                    S.op("pe", lambda e: e.matmul(bk.t[:, :], wo.t[:, k, d * 128:(d + 1) * 128], o_s.t[:, k, :], start=(k == 0), stop=(k == KD - 1)),
                         rd=[wo, o_s], wr=[bk])
                xr = xres.next()
                y = yo.next()
                S.op("sp", lambda e: e.dma_start(out=xr.t[:, :], in_=x_d[d * 128:(d + 1) * 128, c0:c0 + 512]), wr=[xr], dsem=xr.b.name)
                S.op("dve", lambda e: e.tensor_tensor(out=y.t[:, :], in0=bk.t[:, :], in1=xr.t[:, :], op=ALU.add), rd=[bk, xr], wr=[y])
                S.op("sp", lambda e: e.dma_start(out=xo_d[d * 128:(d + 1) * 128, c0:c0 + 512], in_=y.t[:, :]), rd=[y], dsem=y.b.name)
        S.barrier()


def load_bcast(C, st, name, vec_ap, n):
    t = T(st.enter_context(C.nc.sbuf_tensor(name + C.sfx, [128, n], F32)), name)
    C.S.op("sp", lambda e: e.dma_start(out=t.t[:, :], in_=vec_ap.partition_broadcast(128)), wr=[t], dsem=name)
    return t


def load_const(C, st, name, ap, shape, dt):
    t = T(st.enter_context(C.nc.sbuf_tensor(name + C.sfx, shape, dt)), name)
    eng = "sp" if dt == F32 else "pool"
    C.S.op(eng, lambda e: e.dma_start(out=t.t[:, :], in_=ap), wr=[t], dsem=name)
    return t


def psb_region(C, i):
    return C.psb[:, i * 128:(i + 1) * 128], C.psbT[i]


def phase_hg(C, xn_d, w_d, lbz_d, cm_d, og_d, o_d, consts, NT, pfx, NH=3):
    nc, S = C.nc, C.S
    S.barrier()
    with ExitStack() as st:
        sb = lambda n, sh, dt: T(st.enter_context(nc.sbuf_tensor(pfx + n + C.sfx, sh, dt)), pfx + n)
        w = sb("w", [128, KD, NH * 512], BF16)
        for h in range(NH):
            S.op("pool", lambda e: e.dma_start(out=w.t[:, :, h * 512:(h + 1) * 512],
                                               in_=w_d[:, h * 512:(h + 1) * 512].rearrange("(k p) c -> p k c", p=128)), wr=[w], dsem=pfx + "w")
        cf = load_const(C, st, pfx + "cf", consts, [128, 512], F32)
        cb = sb("cb", [128, 512], BF16)
        S.op("dve", lambda e: e.tensor_copy(out=cb.t[:, :], in_=cf.t[:, :]), rd=[cf], wr=[cb])
        ident = T(cb.t[:, 0:128]); ident.b = cb.b
        u2 = T(cf.t[:, 128:256]); u2.b = cf.b
        cmat = T(cb.t[:, 256:384]); cmat.b = cb.b
        sel = T(cb.t[:, 384:390]); sel.b = cb.b
        ogain = load_bcast(C, st, pfx + "ogain", og_d, 128)
        NC_ = NH * 128
        lbz = sb("lbz", [128, 4, NC_], F32)
        for j in range(4):
            S.op("sp", lambda e: e.dma_start(out=lbz.t[:, j, :], in_=lbz_d[j, :].partition_broadcast(128)), wr=[lbz], dsem=pfx + "lbz")
        cm = load_bcast(C, st, pfx + "cm", cm_d, 16)
        lb = sb("lb", [128, NC_], F32)
        oml = sb("oml", [128, NC_], F32)
        den = sb("den", [128, NC_], F32)
        S.op("act", lambda e: e.activation(out=lbz.t[:, :, :], in_=lbz.t[:, :, :], func=AF.Exp), rd=[lbz], wr=[lbz])
        S.op("dve", lambda e: e.tensor_tensor(out=den.t[:, :], in0=lbz.t[:, 0, :], in1=lbz.t[:, 1, :], op=ALU.add), rd=[lbz], wr=[den])
        S.op("dve", lambda e: e.tensor_tensor(out=den.t[:, :], in0=den.t[:, :], in1=lbz.t[:, 2, :], op=ALU.add), rd=[lbz, den], wr=[den])
        S.op("dve", lambda e: e.tensor_tensor(out=den.t[:, :], in0=den.t[:, :], in1=lbz.t[:, 3, :], op=ALU.add), rd=[lbz, den], wr=[den])
        S.op("dve", lambda e: e.reciprocal(out=den.t[:, :], in_=den.t[:, :]), rd=[den], wr=[den])
        S.op("dve", lambda e: e.tensor_scalar(out=lb.t[:, :], in0=lbz.t[:, 0, :], scalar1=cm.t[:, 0:1], scalar2=None, op0=ALU.mult), rd=[lbz, cm], wr=[lb])
        for j in range(1, 4):
            S.op("dve", lambda e: e.scalar_tensor_tensor(out=lb.t[:, :], in0=lbz.t[:, j, :], scalar=cm.t[:, j:j + 1], in1=lb.t[:, :],
                                                         op0=ALU.mult, op1=ALU.add), rd=[lbz, cm, lb], wr=[lb])
        S.op("dve", lambda e: e.tensor_tensor(out=lb.t[:, :], in0=lb.t[:, :], in1=den.t[:, :], op=ALU.mult), rd=[lb, den], wr=[lb])
        S.op("dve", lambda e: e.tensor_scalar(out=oml.t[:, :], in0=lb.t[:, :], scalar1=-1.0, scalar2=1.0, op0=ALU.mult, op1=ALU.add), rd=[lb], wr=[oml])

        St = [sb("S%d" % h, [128, 128], F32) for h in range(NH)]
        for h in range(NH):
            S.op("dve", lambda e: e.memset(St[h].t[:, :], 0.0), wr=[St[h]])
        xt = Ring([sb("xt%d" % i, [128, KD, 128], BF16) for i in range(2)])
        R = lambda n, sh, dt, k=2: Ring([sb("%s%d" % (n, i), sh, dt) for i in range(k)])
        ef, eg, ff, lf, kk, E, Ei = (R(n, [128, 128], F32) for n in ("ef", "eg", "ff", "lf", "kk", "E", "Ei"))
        esc = R("esc", [128, 6], F32)
        qh, kh, vv, kT, sm, Sp0, Sp1, yb = (R(n, [128, 128], BF16) for n in ("qh", "kh", "vv", "kT", "sm", "Sp0", "Sp1", "yb"))
        A = R("A", [128, 128], F32)
        lfh, lfl = R("lfh", [128, 128], BF16), R("lfl", [128, 128], BF16)
        ss, lnv, rs = (R(n, [128, 1], F32) for n in ("ss", "lnv", "rs"))
        junk = R("junk", [128, 128], F32)
        qA = [sb("qA%d" % i, [128, 128], BF16) for i in range(2)]
        qB = [sb("qB%d" % i, [128, 128], BF16) for i in range(2)]
        for i in range(2):
            S.op("dve", lambda e: e.memset(qA[i].t[:, :], 0.0), wr=[qA[i]])
            S.op("dve", lambda e: e.memset(qB[i].t[:, :], 0.0), wr=[qB[i]])
        qAr, qBr = Ring(qA), Ring(qB)
        kA = [sb("kA%d" % i, [128, 128], BF16) for i in range(2)]
        kB = [sb("kB%d" % i, [128, 128], BF16) for i in range(2)]
        for i in range(2):
            S.op("dve", lambda e: e.memset(kA[i].t[:, :], 0.0), wr=[kA[i]])
            S.op("dve", lambda e: e.memset(kB[i].t[:, :], 0.0), wr=[kB[i]])
        kAr, kBr = Ring(kA), Ring(kB)
        ost = R("ost", [128, NH, 128], BF16)
        pjb = Ring(C.banks([0, 1, 2]))
        wkb = Ring(C.banks([3, 4, 5, 6]))
        pbi = [0]

        def psb_next():
            i = pbi[0] % 8
            pbi[0] += 1
            return psb_region(C, i)

        for t in range(NT):
            x_s = xt.next()
            S.op("sp", lambda e: e.dma_start(out=x_s.t[:, :, :], in_=xn_d[t, :, :, :]), wr=[x_s], dsem=x_s.b.name)
            o_s = ost.next()
            for h in range(NH):
                pj = pjb.next()
                for k in range(KD):
                    S.op("pe", lambda e: e.matmul(pj.t[:, :], x_s.t[:, k, :], w.t[:, k, h * 512:(h + 1) * 512], start=(k == 0), stop=(k == KD - 1)),
                         rd=[x_s, w], wr=[pj])
                q_ap, fl_ap, vi_ap, gt_ap = (pj.t[:, i * 128:(i + 1) * 128] for i in range(4))
                hs = slice(h * 128, (h + 1) * 128)
                ef_, eg_, ff_, lf_, kk_, E_, Ei_ = (r.next() for r in (ef, eg, ff, lf, kk, E, Ei))
                S.op("act", lambda e: e.activation(out=ef_.t[:, :], in_=fl_ap, func=AF.Exp, scale=-1.0), rd=[pj], wr=[ef_])
                S.op("act", lambda e: e.activation(out=eg_.t[:, :], in_=gt_ap, func=AF.Exp, scale=-1.0), rd=[pj], wr=[eg_])
                S.op("dve", lambda e: e.tensor_scalar(out=ef_.t[:, :], in0=ef_.t[:, :], scalar1=1.0, scalar2=None, op0=ALU.add), rd=[ef_], wr=[ef_])
                S.op("dve", lambda e: e.reciprocal(out=ef_.t[:, :], in_=ef_.t[:, :]), rd=[ef_], wr=[ef_])
                S.op("dve", lambda e: e.tensor_tensor(out=ff_.t[:, :], in0=ef_.t[:, :], in1=oml.t[:, hs], op=ALU.mult), rd=[ef_, oml], wr=[ff_])
                S.op("dve", lambda e: e.tensor_tensor(out=ff_.t[:, :], in0=ff_.t[:, :], in1=lb.t[:, hs], op=ALU.add), rd=[ff_, lb], wr=[ff_])
                S.op("act", lambda e: e.activation(out=lf_.t[:, :], in_=ff_.t[:, :], func=AF.Ln), rd=[ff_], wr=[lf_])
                S.op("dve", lambda e: e.tensor_scalar(out=kk_.t[:, :], in0=ff_.t[:, :], scalar1=-1.0, scalar2=1.0, op0=ALU.mult, op1=ALU.add), rd=[ff_], wr=[kk_])
                wk = wkb.next()
                lh_, ll_ = lfh.next(), lfl.next()
                S.op("dve", lambda e: e.tensor_copy(out=lh_.t[:, :], in_=lf_.t[:, :]), rd=[lf_], wr=[lh_])
                S.op("dve", lambda e: e.tensor_tensor(out=ll_.t[:, :], in0=lf_.t[:, :], in1=lh_.t[:, :], op=ALU.subtract), rd=[lf_, lh_], wr=[ll_])
                S.op("pe", lambda e: e.matmul(wk.t[:, 0:128], cmat.t[:, :], lh_.t[:, :], start=True, stop=False), rd=[cmat, lh_], wr=[wk])
                S.op("pe", lambda e: e.matmul(wk.t[:, 0:128], cmat.t[:, :], ll_.t[:, :], start=False, stop=True), rd=[cmat, ll_], wr=[wk])
                S.op("pe", lambda e: e.matmul(wk.t[:, 128:134], lh_.t[:, :], sel.t[:, :], start=True, stop=False), rd=[sel, lh_], wr=[wk])
                S.op("pe", lambda e: e.matmul(wk.t[:, 128:134], ll_.t[:, :], sel.t[:, :], start=False, stop=True), rd=[sel, ll_], wr=[wk])
                esc_ = esc.next()
                S.op("act", lambda e: e.activation(out=E_.t[:, :], in_=wk.t[:, 0:128], func=AF.Exp), rd=[wk], wr=[E_])
                S.op("act", lambda e: e.activation(out=Ei_.t[:, :], in_=wk.t[:, 0:128], func=AF.Exp, scale=-1.0), rd=[wk], wr=[Ei_])
                S.op("act", lambda e: e.activation(out=esc_.t[:, :], in_=wk.t[:, 128:134], func=AF.Exp), rd=[wk], wr=[esc_])
                qh_, kh_, vv_, kT_, sm_, Sp0_, Sp1_, yb_ = (r.next() for r in (qh, kh, vv, kT, sm, Sp0, Sp1, yb))
                S.op("dve", lambda e: e.scalar_tensor_tensor(out=qh_.t[:, :], in0=q_ap, scalar=128.0 ** -0.5, in1=E_.t[:, :], op0=ALU.mult, op1=ALU.mult),
                     rd=[pj, E_], wr=[qh_])
                S.op("dve", lambda e: e.tensor_tensor(out=kh_.t[:, :], in0=kk_.t[:, :], in1=Ei_.t[:, :], op=ALU.mult), rd=[kk_, Ei_], wr=[kh_])
                S.op("dve", lambda e: e.tensor_copy(out=vv_.t[:, :], in_=vi_ap), rd=[pj], wr=[vv_])
                tq_ap, tq = psb_next()
                tk_ap, tk = psb_next()
                S.op("pe", lambda e: e.transpose(tq_ap, qh_.t[:, :], ident.t[:, :]), rd=[qh_, ident], wr=[tq])
                S.op("pe", lambda e: e.transpose(tk_ap, kh_.t[:, :], ident.t[:, :]), rd=[kh_, ident], wr=[tk])
                qA_, qB_ = qAr.next(), qBr.next()
                S.op("act", lambda e: e.copy(out=qA_.t[:, 0:64], in_=tq_ap[:, 0:64]), rd=[tq], wr=[qA_])
                S.op("dve", lambda e: e.tensor_copy(out=qB_.t[:, 64:128], in_=tq_ap[:, 64:128]), rd=[tq], wr=[qB_])
                S.op("act", lambda e: e.copy(out=kT_.t[:, :], in_=tk_ap), rd=[tk], wr=[kT_])
                wk2 = wkb.next()
                S.op("pe", lambda e: e.matmul(wk2.t[:, 0:128], kT_.t[:, :], qA_.t[:, :], start=True, stop=False), rd=[kT_, qA_], wr=[wk2])
                S.op("pe", lambda e: e.matmul(wk2.t[:, 0:128], kT_.t[:, :], qB_.t[:, :], start=False, stop=True), rd=[kT_, qB_], wr=[wk2])
                S.op("dve", lambda e: e.tensor_tensor(out=sm_.t[:, :], in0=wk2.t[:, 0:128], in1=u2.t[:, :], op=ALU.mult), rd=[wk2, u2], wr=[sm_])
                kA_, kB_ = kAr.next(), kBr.next()
                S.op("act", lambda e: e.copy(out=kA_.t[0:64, :], in_=kh_.t[0:64, :]), rd=[kh_], wr=[kA_])
                S.op("act", lambda e: e.copy(out=kB_.t[64:128, :], in_=kh_.t[64:128, :]), rd=[kh_], wr=[kB_])
                S.op("pe", lambda e: e.matmul(wk2.t[:, 128:256], kA_.t[:, :], vv_.t[:, :], start=True, stop=True), rd=[kA_, vv_], wr=[wk2])
                S.op("pe", lambda e: e.matmul(wk2.t[:, 256:384], kB_.t[:, :], vv_.t[:, :], start=True, stop=True), rd=[kB_, vv_], wr=[wk2])
                Sh = St[h]
                for c, Sp_ in ((0, Sp0_), (1, Sp1_)):
                    A_ = A.next()
                    S.op("dve", lambda e: e.tensor_scalar(out=Sp_.t[:, :], in0=Sh.t[:, :], scalar1=esc_.t[:, 3 * c:3 * c + 1], scalar2=None, op0=ALU.mult),
                         rd=[Sh, esc_], wr=[Sp_])
                    S.op("dve", lambda e: e.tensor_scalar(out=A_.t[:, :], in0=Sh.t[:, :], scalar1=esc_.t[:, 3 * c + 2:3 * c + 3], scalar2=None, op0=ALU.mult),
                         rd=[Sh, esc_], wr=[A_])
                    S.op("dve", lambda e: e.scalar_tensor_tensor(out=Sh.t[:, :], in0=wk2.t[:, 128 * (c + 1):128 * (c + 2)],
                                                                 scalar=esc_.t[:, 3 * c + 1:3 * c + 2], in1=A_.t[:, :], op0=ALU.mult, op1=ALU.add),
                         rd=[wk2, esc_, A_], wr=[Sh])
                S.op("pe", lambda e: e.matmul(wk2.t[:, 384:512], sm_.t[:, :], vv_.t[:, :], start=True, stop=False), rd=[sm_, vv_], wr=[wk2])
                S.op("pe", lambda e: e.matmul(wk2.t[:, 384:512], qA_.t[:, :], Sp0_.t[:, :], start=False, stop=False), rd=[qA_, Sp0_], wr=[wk2])
                S.op("pe", lambda e: e.matmul(wk2.t[:, 384:512], qB_.t[:, :], Sp1_.t[:, :], start=False, stop=True), rd=[qB_, Sp1_], wr=[wk2])
                o_ap = wk2.t[:, 384:512]
                ss_, lnv_, rs_, junk_ = ss.next(), lnv.next(), rs.next(), junk.next()
                S.op("dve", lambda e: e.memset(ss_.t[:, :], 0.0), wr=[ss_])
                S.op("act", lambda e: e.activation(out=junk_.t[:, :], in_=o_ap, func=AF.Square, accum_out=ss_.t[:, 0:1]), rd=[wk2, ss_], wr=[junk_, ss_])
                emit_rstd(S, rs_.t[:, :], rs_, ss_.t[:, :], ss_, 1.0 / 128, lnv_.t[:, :], lnv_)
                S.op("dve", lambda e: e.tensor_scalar(out=eg_.t[:, :], in0=eg_.t[:, :], scalar1=1.0, scalar2=None, op0=ALU.add), rd=[eg_], wr=[eg_])
                S.op("dve", lambda e: e.reciprocal(out=eg_.t[:, :], in_=eg_.t[:, :]), rd=[eg_], wr=[eg_])
                S.op("dve", lambda e: e.tensor_tensor(out=eg_.t[:, :], in0=eg_.t[:, :], in1=gt_ap, op=ALU.mult), rd=[eg_, pj], wr=[eg_])
                S.op("dve", lambda e: e.tensor_tensor(out=eg_.t[:, :], in0=eg_.t[:, :], in1=ogain.t[:, :], op=ALU.mult), rd=[eg_, ogain], wr=[eg_])
                S.op("dve", lambda e: e.scalar_tensor_tensor(out=yb_.t[:, :], in0=o_ap, scalar=rs_.t[:, 0:1], in1=eg_.t[:, :], op0=ALU.mult, op1=ALU.mult),
                     rd=[wk2, rs_, eg_], wr=[yb_])
                ty_ap, ty = psb_next()
                S.op("pe", lambda e: e.transpose(ty_ap, yb_.t[:, :], ident.t[:, :]), rd=[yb_, ident], wr=[ty])
                S.op("act", lambda e: e.copy(out=o_s.t[:, h, :], in_=ty_ap), rd=[ty], wr=[o_s])
            S.op("sp", lambda e: e.dma_start(out=o_d[0:NH * 128, t * 128:(t + 1) * 128].rearrange("(h p) t -> p h t", p=128), in_=o_s.t[:, :, :]),
                 rd=[o_s], dsem=o_s.b.name)
        S.barrier()


def attn_tile(C, t, qk_parts, v_fn, acc, acc_ap, scale, far_bias, near, negshift, P, tmpf, qkb):
    S = C.S
    np_ = len(qk_parts)
    for g0 in range(0, t + 1, 4):
        grp = list(range(g0, min(g0 + 4, t + 1)))
        bank = qkb.next()
        for j, kt in enumerate(grp):
            for pi, (K_fn, q_ap, q_T) in enumerate(qk_parts):
                k_ap, k_T = K_fn(kt)
                S.op("pe", lambda e: e.matmul(bank.t[:, j * 128:(j + 1) * 128], k_ap, q_ap, start=(pi == 0), stop=(pi == np_ - 1)),
                     rd=[k_T, q_T], wr=[bank])
        P_ = P.next()
        nfar = len([kt for kt in grp if (kt - t) not in near])
        if nfar:
            S.op("act", lambda e: e.activation(out=P_.t[:, 0:nfar * 128], in_=bank.t[:, 0:nfar * 128], func=AF.Exp, scale=scale, bias=far_bias.t[:, :]),
                 rd=[bank, far_bias], wr=[P_])
        for j, kt in enumerate(grp):
            if (kt - t) in near:
                b = near[kt - t]
                tm = tmpf.next()
                S.op("dve", lambda e: e.scalar_tensor_tensor(out=tm.t[:, :], in0=bank.t[:, j * 128:(j + 1) * 128], scalar=scale, in1=b.t[:, :],
                                                             op0=ALU.mult, op1=ALU.add), rd=[bank, b], wr=[tm])
                S.op("act", lambda e: e.activation(out=P_.t[:, j * 128:(j + 1) * 128], in_=tm.t[:, :], func=AF.Exp, bias=negshift.t[:, :]),
                     rd=[tm, negshift], wr=[P_])
        for j, kt in enumerate(grp):
            v_ap, v_T = v_fn(kt)
            S.op("pe", lambda e: e.matmul(acc_ap, P_.t[:, j * 128:(j + 1) * 128], v_ap, start=(kt == 0), stop=(kt == t)), rd=[P_, v_T], wr=[acc])


def sub_T(parent, ap):
    x = T(ap)
    x.b = parent.b
    return x


VW = 132


def phase_df(C, xn_d, w_d, vecs_d, bias_d, o_d, consts, mask_d, NT, pfx, NH=2, row0=384):
    nc, S = C.nc, C.S
    SHIFT = 8.0
    S.barrier()
    with ExitStack() as st:
        sb = lambda n, sh, dt: T(st.enter_context(nc.sbuf_tensor(pfx + n + C.sfx, sh, dt)), pfx + n)
        R = lambda n, sh, dt, k=2: Ring([sb("%s%d" % (n, i), sh, dt) for i in range(k)])
        w = sb("w", [128, KD, NH * 384], BF16)
        for h in range(NH):
            S.op("pool", lambda e: e.dma_start(out=w.t[:, :, h * 384:(h + 1) * 384],
                                               in_=w_d[:, h * 384:(h + 1) * 384].rearrange("(k p) c -> p k c", p=128)), wr=[w], dsem=pfx + "w")
        cf = load_const(C, st, pfx + "cf", consts, [128, 512], F32)
        cb = sb("cb", [128, 512], BF16)
        S.op("dve", lambda e: e.tensor_copy(out=cb.t[:, :], in_=cf.t[:, :]), rd=[cf], wr=[cb])
        ident = sub_T(cb, cb.t[:, 0:128])
        mask = load_const(C, st, pfx + "mask", mask_d, [128, 128], F32)
        vecs = sb("vecs", [128, 8, 128], F32)
        for j in range(8):
            S.op("sp", lambda e: e.dma_start(out=vecs.t[:, j, :], in_=vecs_d[j, :].partition_broadcast(128)), wr=[vecs], dsem=pfx + "vecs")
        qg, kg = vecs.t[:, 0, 0:64], vecs.t[:, 1, 0:64]
        junk64 = sb("junk64", [128, 64], F32)
        prod = sb("prod", [128, 64], F32)
        lsum = sb("lsum", [128, 2], F32)
        S.op("dve", lambda e: e.memset(lsum.t[:, :], 0.0), wr=[lsum])
        for i in range(2):
            S.op("dve", lambda e: e.tensor_tensor(out=prod.t[:, :], in0=vecs.t[:, 2 + 2 * i, 0:64], in1=vecs.t[:, 3 + 2 * i, 0:64], op=ALU.mult), rd=[vecs], wr=[prod])
            S.op("act", lambda e: e.activation(out=junk64.t[:, :], in_=prod.t[:, :], func=AF.Identity, accum_out=lsum.t[:, i:i + 1]), rd=[prod, lsum], wr=[junk64, lsum])
        S.op("act", lambda e: e.activation(out=lsum.t[:, :], in_=lsum.t[:, :], func=AF.Exp), rd=[lsum], wr=[lsum])
        lam = sb("lam", [128, 1], F32)
        S.op("dve", lambda e: e.tensor_tensor(out=lam.t[:, :], in0=lsum.t[:, 0:1], in1=lsum.t[:, 1:2], op=ALU.subtract), rd=[lsum], wr=[lam])
        S.op("dve", lambda e: e.tensor_tensor(out=lam.t[:, :], in0=lam.t[:, :], in1=vecs.t[:, 7, 0:1], op=ALU.add), rd=[lam, vecs], wr=[lam])
        sgain = sb("sgain", [128, 128], F32)
        S.op("dve", lambda e: e.tensor_scalar(out=sgain.t[:, :], in0=vecs.t[:, 6, :], scalar1=vecs.t[:, 7, 1:2], scalar2=None, op0=ALU.mult), rd=[vecs], wr=[sgain])
        negshift = sb("negshift", [128, 1], F32)
        S.op("dve", lambda e: e.memset(negshift.t[:, :], -SHIFT), wr=[negshift])
        farb = [sb("farb%d" % h, [128, 1], F32) for h in range(NH)]
        for h in range(NH):
            S.op("dve", lambda e: e.tensor_scalar(out=farb[h].t[:, :], in0=vecs.t[:, 7, 2 + h:3 + h], scalar1=-SHIFT, scalar2=None, op0=ALU.add), rd=[vecs], wr=[farb[h]])
        bt = [[sb("bt%d_%d" % (h, r), [128, 128], F32) for r in range(2)] for h in range(NH)]
        for h in range(NH):
            for r in range(2):
                S.op("sp", lambda e: e.dma_start(out=bt[h][r].t[:, :], in_=bias_d[h, r, :, :]), wr=[bt[h][r]], dsem=pfx + "bt%d_%d" % (h, r))
            S.op("dve", lambda e: e.tensor_tensor(out=bt[h][0].t[:, :], in0=bt[h][0].t[:, :], in1=mask.t[:, :], op=ALU.add), rd=[bt[h][0], mask], wr=[bt[h][0]])
        KT = [[sb("KT%d_%d" % (h, m), [128, NT * 128], BF16) for m in range(2)] for h in range(NH)]
        for h in range(NH):
            for m in range(2):
                S.op("dve", lambda e: e.memset(KT[h][m].t[:, :], 0.0), wr=[KT[h][m]])
        Va = sb("Va", [128, NT * NH * VW], BF16)
        S.op("dve", lambda e: e.memset(Va.t[:, :], 1.0), wr=[Va])
        vofs = lambda t_, h_: (t_ * NH + h_) * VW
        xt = R("xt", [128, KD, 128], BF16)
        ss4, ln4, rs4 = (R(n, [128, 4], F32, 3) for n in ("ss4", "ln4", "rs4"))
        junk = R("junk", [128, 128], F32)
        qk = R("qk", [128, 256], BF16, 3)
        qT = [R("qT%d" % h, [128, 128], BF16) for h in range(NH)]
        P = R("P", [128, 512], BF16, 3)
        tmpf = R("tmpf", [128, 128], F32, 3)
        rr = R("rr", [128, 4], F32, 3)
        t2, of = R("t2", [128, 128], F32), R("of", [128, 128], F32)
        ss1, ln1, rs1 = (R(n, [128, 1], F32) for n in ("ss1", "ln1", "rs1"))
        yb = R("yb", [128, 128], BF16)
        ost = R("ost", [128, NH, 128], BF16)
        pjb = C.banks([0, 1])
        qkb = Ring(C.banks([2, 3, 4]))
        accb = Ring(C.banks([5, 6]))
        pbi = [0]

        def psb_next():
            i = pbi[0] % 8
            pbi[0] += 1
            return psb_region(C, i)

        for t in range(NT):
            x_s = xt.next()
            S.op("sp", lambda e: e.dma_start(out=x_s.t[:, :, :], in_=xn_d[t, :, :, :]), wr=[x_s], dsem=x_s.b.name)
            o_s = ost.next()
            qTs = []
            for h in range(NH):
                pj = pjb[h]
                for k in range(KD):
                    S.op("pe", lambda e: e.matmul(pj.t[:, 0:384], x_s.t[:, k, :], w.t[:, k, h * 384:(h + 1) * 384], start=(k == 0), stop=(k == KD - 1)),
                         rd=[x_s, w], wr=[pj])
                ss_, ln_, rs_ = ss4.next(), ln4.next(), rs4.next()
                S.op("dve", lambda e: e.memset(ss_.t[:, :], 0.0), wr=[ss_])
                for j in range(4):
                    jk = junk.next()
                    S.op("act", lambda e: e.activation(out=jk.t[:, 0:64], in_=pj.t[:, j * 64:(j + 1) * 64], func=AF.Square, accum_out=ss_.t[:, j:j + 1]),
                         rd=[pj, ss_], wr=[jk, ss_])
                emit_rstd(S, rs_.t[:, :], rs_, ss_.t[:, :], ss_, 1.0 / 64, ln_.t[:, :], ln_)
                qk_ = qk.next()
                for j in range(4):
                    g_ap = qg if j < 2 else kg
                    S.op("dve", lambda e: e.scalar_tensor_tensor(out=qk_.t[:, j * 64:(j + 1) * 64], in0=pj.t[:, j * 64:(j + 1) * 64], scalar=rs_.t[:, j:j + 1],
                                                                 in1=g_ap, op0=ALU.mult, op1=ALU.mult), rd=[pj, rs_, vecs], wr=[qk_])
                S.op("dve", lambda e: e.tensor_copy(out=Va.t[:, vofs(t, h):vofs(t, h) + 128], in_=pj.t[:, 256:384]), rd=[pj], wr=[Va])
                tq_ap, tq = psb_next()
                tk_ap, tk = psb_next()
                S.op("pe", lambda e: e.transpose(tq_ap, qk_.t[:, 0:128], ident.t), rd=[qk_, ident], wr=[tq])
                S.op("pe", lambda e: e.transpose(tk_ap, qk_.t[:, 128:256], ident.t), rd=[qk_, ident], wr=[tk])
                qT_ = qT[h].next()
                S.op("act", lambda e: e.copy(out=qT_.t[:, :], in_=tq_ap), rd=[tq], wr=[qT_])
                S.op("act", lambda e: e.copy(out=KT[h][0].t[0:64, t * 128:(t + 1) * 128], in_=tk_ap[0:64, :]), rd=[tk], wr=[KT[h][0]])
                S.op("act", lambda e: e.copy(out=KT[h][1].t[64:128, t * 128:(t + 1) * 128], in_=tk_ap[64:128, :]), rd=[tk], wr=[KT[h][1]])
                qTs.append(qT_)
            for h in range(NH):
                acc = accb.next()
                for m in range(2):
                    attn_tile(C, t, [(lambda kt, h=h, m=m: (KT[h][m].t[:, kt * 128:(kt + 1) * 128], KT[h][m]), qTs[h].t[:, :], qTs[h])],
                              lambda kt, h=h: (Va.t[:, vofs(kt, h):vofs(kt, h) + 129], Va), acc, acc.t[:, m * 256:m * 256 + 129], 0.125, farb[h],
                              {0: bt[h][0], -1: bt[h][1]}, negshift, P, tmpf, qkb)
                r_ = rr.next()
                S.op("dve", lambda e: e.reciprocal(out=r_.t[:, 0:1], in_=acc.t[:, 128:129]), rd=[acc], wr=[r_])
                S.op("dve", lambda e: e.reciprocal(out=r_.t[:, 1:2], in_=acc.t[:, 384:385]), rd=[acc], wr=[r_])
                S.op("dve", lambda e: e.tensor_tensor(out=r_.t[:, 2:3], in0=r_.t[:, 1:2], in1=lam.t[:, :], op=ALU.mult), rd=[r_, lam], wr=[r_])
                t2_, of_ = t2.next(), of.next()
                S.op("dve", lambda e: e.tensor_scalar(out=t2_.t[:, :], in0=acc.t[:, 256:384], scalar1=r_.t[:, 2:3], scalar2=None, op0=ALU.mult), rd=[acc, r_], wr=[t2_])
                S.op("dve", lambda e: e.scalar_tensor_tensor(out=of_.t[:, :], in0=acc.t[:, 0:128], scalar=r_.t[:, 0:1], in1=t2_.t[:, :],
                                                             op0=ALU.mult, op1=ALU.subtract), rd=[acc, r_, t2_], wr=[of_])
                s1, l1, r1, jk = ss1.next(), ln1.next(), rs1.next(), junk.next()
                S.op("dve", lambda e: e.memset(s1.t[:, :], 0.0), wr=[s1])
                S.op("act", lambda e: e.activation(out=jk.t[:, :], in_=of_.t[:, :], func=AF.Square, accum_out=s1.t[:, 0:1]), rd=[of_, s1], wr=[jk, s1])
                emit_rstd(S, r1.t[:, :], r1, s1.t[:, :], s1, 1.0 / 128, l1.t[:, :], l1)
                yb_ = yb.next()
                S.op("dve", lambda e: e.scalar_tensor_tensor(out=yb_.t[:, :], in0=of_.t[:, :], scalar=r1.t[:, 0:1], in1=sgain.t[:, :],
                                                             op0=ALU.mult, op1=ALU.mult), rd=[of_, r1, sgain], wr=[yb_])
                ty_ap, ty = psb_next()
                S.op("pe", lambda e: e.transpose(ty_ap, yb_.t[:, :], ident.t), rd=[yb_, ident], wr=[ty])
                S.op("act", lambda e: e.copy(out=o_s.t[:, h, :], in_=ty_ap), rd=[ty], wr=[o_s])
            S.op("sp", lambda e: e.dma_start(out=o_d[row0:row0 + NH * 128, t * 128:(t + 1) * 128].rearrange("(h p) t -> p h t", p=128), in_=o_s.t[:, :, :]),
                 rd=[o_s], dsem=o_s.b.name)
        S.barrier()


def phase_ml(C, xn_d, w_d, wuq_d, wukv_d, vecs_d, cs_d, o_d, consts, mask_d, NT, pfx, NH=3, row0=640):
    nc, S = C.nc, C.S
    SHIFT = 14.0
    SCALE = 192.0 ** -0.5
    S.barrier()
    with ExitStack() as st:
        sb = lambda n, sh, dt: T(st.enter_context(nc.sbuf_tensor(pfx + n + C.sfx, sh, dt)), pfx + n)
        R = lambda n, sh, dt, k=2: Ring([sb("%s%d" % (n, i), sh, dt) for i in range(k)])
        w = sb("w", [128, KD, 832], BF16)
        for (c0, c1) in ((0, 512), (512, 832)):
            S.op("pool", lambda e: e.dma_start(out=w.t[:, :, c0:c1], in_=w_d[:, c0:c1].rearrange("(k p) c -> p k c", p=128)), wr=[w], dsem=pfx + "w")
        wuq = sb("wuq", [128, 4, NH * 192], BF16)
        S.op("pool", lambda e: e.dma_start(out=wuq.t[:, :, :], in_=wuq_d.rearrange("(k p) c -> p k c", p=128)), wr=[wuq], dsem=pfx + "wuq")
        wukv = sb("wukv", [128, 2, NH * 256], BF16)
        S.op("pool", lambda e: e.dma_start(out=wukv.t[:, :, :], in_=wukv_d.rearrange("(k p) c -> p k c", p=128)), wr=[wukv], dsem=pfx + "wukv")
        cf = load_const(C, st, pfx + "cf", consts, [128, 512], F32)
        cb = sb("cb", [128, 512], BF16)
        S.op("dve", lambda e: e.tensor_copy(out=cb.t[:, :], in_=cf.t[:, :]), rd=[cf], wr=[cb])
        ident = sub_T(cb, cb.t[:, 0:128])
        mask = load_const(C, st, pfx + "mask", mask_d, [128, 128], F32)
        vecs = sb("vecs", [128, 4, 512], F32)
        for j in range(4):
            S.op("sp", lambda e: e.dma_start(out=vecs.t[:, j, :], in_=vecs_d[j, :].partition_broadcast(128)), wr=[vecs], dsem=pfx + "vecs")
        cs = load_const(C, st, pfx + "cs", cs_d, [128, 2 * NT * 32], F32)
        negshift = sb("negshift", [128, 1], F32)
        S.op("dve", lambda e: e.memset(negshift.t[:, :], -SHIFT), wr=[negshift])
        KTa = [sb("KTa%d" % h, [128, NT * 128], BF16) for h in range(NH)]
        KTb = [sb("KTb%d" % h, [128, NT * 128], BF16) for h in range(NH)]
        Va = sb("Va", [128, NT * NH * VW], BF16)
        S.op("dve", lambda e: e.memset(Va.t[:, :], 1.0), wr=[Va])
        vofs = lambda t_, h_: (t_ * NH + h_) * VW
        xt = R("xt", [128, KD, 128], BF16)
        ss2, ln2, rs2 = (R(n, [128, 4], F32) for n in ("ss2", "ln2", "rs2"))
        junk = R("junk", [128, 512], F32)
        cqn = R("cqn", [128, 512], BF16)
        ckvn = R("ckvn", [128, 256], BF16)
        kr = R("kr", [128, 64], F32)
        cqT = R("cqT", [128, 4, 128], BF16)
        ckvT = R("ckvT", [128, 2, 128], BF16)
        ssh, lnh, rsh = (R(n, [128, 4], F32, 3) for n in ("ssh", "lnh", "rsh"))
        qn_b = R("qnb", [128, 128], BF16, 3)
        kn_b = R("knb", [128, 128], BF16, 3)
        qr_f = R("qrf", [128, 64], F32, 3)
        kr_f = R("krf", [128, 64], F32, 3)
        ra, rb = R("ra", [128, 32], F32, 3), R("rb", [128, 32], F32, 3)
        qrp = [sb("qrp%d" % i, [128, 128], BF16) for i in range(2)]
        krp = [sb("krp%d" % i, [128, 128], BF16) for i in range(2)]
        for i in range(2):
            S.op("dve", lambda e: e.memset(qrp[i].t[:, :], 0.0), wr=[qrp[i]])
            S.op("dve", lambda e: e.memset(krp[i].t[:, :], 0.0), wr=[krp[i]])
        qrpr, krpr = Ring(qrp), Ring(krp)
        qTa = [R("qTa%d" % h, [128, 128], BF16) for h in range(NH)]
        qTb = [R("qTb%d" % h, [128, 128], BF16) for h in range(NH)]
        P = R("P", [128, 512], BF16, 3)
        tmpf = R("tmpf", [128, 128], F32, 3)
        rr = R("rr", [128, 1], F32, 3)
        yb = R("yb", [128, 128], BF16)
        ost = R("ost", [128, NH, 128], BF16)
        b0, b1, b2, b3, b4 = C.banks([0, 1, 2, 3, 4])
        qreg = [(b2, 0), (b2, 192), (b3, 0)]
        kvreg = [(b4, 0), (b4, 256), (b3, 192)]
        qkb = Ring(C.banks([0, 1, 2, 3]))
        accb = Ring(C.banks([5, 6]))
        pbi = [0]

        def psb_next():
            i = pbi[0] % 8
            pbi[0] += 1
            return psb_region(C, i)

        def transpose_to(src_ap, src_T, dst_ap, dst_T, eng="act"):
            tp_ap, tp = psb_next()
            S.op("pe", lambda e: e.transpose(tp_ap, src_ap, ident.t), rd=[src_T, ident], wr=[tp])
            if eng == "act":
                S.op("act", lambda e: e.copy(out=dst_ap, in_=tp_ap), rd=[tp], wr=[dst_T])
            else:
                S.op("dve", lambda e: e.tensor_copy(out=dst_ap, in_=tp_ap), rd=[tp], wr=[dst_T])

        def rope(src_T, src, dst_T, dst, t):
            cos = cs.t[:, t * 32:(t + 1) * 32]
            sin = cs.t[:, NT * 32 + t * 32:NT * 32 + (t + 1) * 32]
            x1, x2 = src[:, 0:32], src[:, 32:64]
            a, b = ra.next(), rb.next()
            S.op("dve", lambda e: e.tensor_tensor(out=a.t[:, :], in0=x1, in1=cos, op=ALU.mult), rd=[src_T, cs], wr=[a])
            S.op("dve", lambda e: e.tensor_tensor(out=b.t[:, :], in0=x2, in1=sin, op=ALU.mult), rd=[src_T, cs], wr=[b])
            S.op("dve", lambda e: e.tensor_tensor(out=dst[:, 0:32], in0=a.t[:, :], in1=b.t[:, :], op=ALU.subtract), rd=[a, b], wr=[dst_T])
            a, b = ra.next(), rb.next()
            S.op("dve", lambda e: e.tensor_tensor(out=a.t[:, :], in0=x2, in1=cos, op=ALU.mult), rd=[src_T, cs], wr=[a])
            S.op("dve", lambda e: e.tensor_tensor(out=b.t[:, :], in0=x1, in1=sin, op=ALU.mult), rd=[src_T, cs], wr=[b])
            S.op("dve", lambda e: e.tensor_tensor(out=dst[:, 32:64], in0=a.t[:, :], in1=b.t[:, :], op=ALU.add), rd=[a, b], wr=[dst_T])

        for t in range(NT):
            x_s = xt.next()
            S.op("sp", lambda e: e.dma_start(out=x_s.t[:, :, :], in_=xn_d[t, :, :, :]), wr=[x_s], dsem=x_s.b.name)
            o_s = ost.next()
            for k in range(KD):
                S.op("pe", lambda e: e.matmul(b0.t[:, :], x_s.t[:, k, :], w.t[:, k, 0:512], start=(k == 0), stop=(k == KD - 1)), rd=[x_s, w], wr=[b0])
            for k in range(KD):
                S.op("pe", lambda e: e.matmul(b1.t[:, 0:320], x_s.t[:, k, :], w.t[:, k, 512:832], start=(k == 0), stop=(k == KD - 1)), rd=[x_s, w], wr=[b1])
            ss_, ln_, rs_ = ss2.next(), ln2.next(), rs2.next()
            S.op("dve", lambda e: e.memset(ss_.t[:, :], 0.0), wr=[ss_])
            jk = junk.next()
            S.op("act", lambda e: e.activation(out=jk.t[:, :], in_=b0.t[:, :], func=AF.Square, accum_out=ss_.t[:, 0:1]), rd=[b0, ss_], wr=[jk, ss_])
            jk = junk.next()
            S.op("act", lambda e: e.activation(out=jk.t[:, 0:256], in_=b1.t[:, 0:256], func=AF.Square, accum_out=ss_.t[:, 1:2]), rd=[b1, ss_], wr=[jk, ss_])
            kr_ = kr.next()
            S.op("act", lambda e: e.copy(out=kr_.t[:, :], in_=b1.t[:, 256:320]), rd=[b1], wr=[kr_])
            jk = junk.next()
            S.op("act", lambda e: e.activation(out=jk.t[:, 0:64], in_=kr_.t[:, :], func=AF.Square, accum_out=ss_.t[:, 2:3]), rd=[kr_, ss_], wr=[jk, ss_])
            S.op("dve", lambda e: e.tensor_scalar(out=ss_.t[:, 0:1], in0=ss_.t[:, 0:1], scalar1=0.5, scalar2=None, op0=ALU.mult), rd=[ss_], wr=[ss_])
            emit_rstd(S, rs_.t[:, 0:2], rs_, ss_.t[:, 0:2], ss_, 1.0 / 256, ln_.t[:, 0:2], ln_)
            cqn_, ckvn_ = cqn.next(), ckvn.next()
            S.op("dve", lambda e: e.scalar_tensor_tensor(out=cqn_.t[:, :], in0=b0.t[:, :], scalar=rs_.t[:, 0:1], in1=vecs.t[:, 0, :], op0=ALU.mult, op1=ALU.mult),
                 rd=[b0, rs_, vecs], wr=[cqn_])
            S.op("dve", lambda e: e.scalar_tensor_tensor(out=ckvn_.t[:, :], in0=b1.t[:, 0:256], scalar=rs_.t[:, 1:2], in1=vecs.t[:, 1, 0:256], op0=ALU.mult, op1=ALU.mult),
                 rd=[b1, rs_, vecs], wr=[ckvn_])
            cqT_, ckvT_ = cqT.next(), ckvT.next()
            for r in range(4):
                transpose_to(cqn_.t[:, r * 128:(r + 1) * 128], cqn_, cqT_.t[:, r, :], cqT_, "act" if r % 2 == 0 else "dve")
            for r in range(2):
                transpose_to(ckvn_.t[:, r * 128:(r + 1) * 128], ckvn_, ckvT_.t[:, r, :], ckvT_, "act" if r % 2 == 0 else "dve")
            for h in range(NH):
                qb, qo = qreg[h]
                for r in range(4):
                    S.op("pe", lambda e: e.matmul(qb.t[:, qo:qo + 192], cqT_.t[:, r, :], wuq.t[:, r, h * 192:(h + 1) * 192], start=(r == 0), stop=(r == 3)),
                         rd=[cqT_, wuq], wr=[qb])
                kb, ko = kvreg[h]
                for r in range(2):
                    S.op("pe", lambda e: e.matmul(kb.t[:, ko:ko + 256], ckvT_.t[:, r, :], wukv.t[:, r, h * 256:(h + 1) * 256], start=(r == 0), stop=(r == 1)),
                         rd=[ckvT_, wukv], wr=[kb])
            qTs = []
            for h in range(NH):
                qb, qo = qreg[h]
                kb, ko = kvreg[h]
                sh_, lh_, rh_ = ssh.next(), lnh.next(), rsh.next()
                S.op("dve", lambda e: e.memset(sh_.t[:, :], 0.0), wr=[sh_])
                jk = junk.next()
                S.op("act", lambda e: e.activation(out=jk.t[:, 0:192], in_=qb.t[:, qo:qo + 192], func=AF.Square, accum_out=sh_.t[:, 0:1]), rd=[qb, sh_], wr=[jk, sh_])
                jk = junk.next()
                S.op("act", lambda e: e.activation(out=jk.t[:, 0:128], in_=kb.t[:, ko:ko + 128], func=AF.Square, accum_out=sh_.t[:, 1:2]), rd=[kb, sh_], wr=[jk, sh_])
                S.op("dve", lambda e: e.tensor_tensor(out=sh_.t[:, 1:2], in0=sh_.t[:, 1:2], in1=ss_.t[:, 2:3], op=ALU.add), rd=[sh_, ss_], wr=[sh_])
                emit_rstd(S, rh_.t[:, 0:2], rh_, sh_.t[:, 0:2], sh_, 1.0 / 192, lh_.t[:, 0:2], lh_)
                qn_, kn_, qr_, kf_ = qn_b.next(), kn_b.next(), qr_f.next(), kr_f.next()
                S.op("dve", lambda e: e.scalar_tensor_tensor(out=qn_.t[:, :], in0=qb.t[:, qo:qo + 128], scalar=rh_.t[:, 0:1], in1=vecs.t[:, 2, 0:128], op0=ALU.mult, op1=ALU.mult),
                     rd=[qb, rh_, vecs], wr=[qn_])
                S.op("dve", lambda e: e.scalar_tensor_tensor(out=qr_.t[:, :], in0=qb.t[:, qo + 128:qo + 192], scalar=rh_.t[:, 0:1], in1=vecs.t[:, 2, 128:192], op0=ALU.mult, op1=ALU.mult),
                     rd=[qb, rh_, vecs], wr=[qr_])
                S.op("dve", lambda e: e.scalar_tensor_tensor(out=kn_.t[:, :], in0=kb.t[:, ko:ko + 128], scalar=rh_.t[:, 1:2], in1=vecs.t[:, 3, 0:128], op0=ALU.mult, op1=ALU.mult),
                     rd=[kb, rh_, vecs], wr=[kn_])
                S.op("dve", lambda e: e.scalar_tensor_tensor(out=kf_.t[:, :], in0=kr_.t[:, :], scalar=rh_.t[:, 1:2], in1=vecs.t[:, 3, 128:192], op0=ALU.mult, op1=ALU.mult),
                     rd=[kr_, rh_, vecs], wr=[kf_])
                S.op("act", lambda e: e.copy(out=Va.t[:, vofs(t, h):vofs(t, h) + 128], in_=kb.t[:, ko + 128:ko + 256]), rd=[kb], wr=[Va])
                qp_, kp_ = qrpr.next(), krpr.next()
                rope(qr_, qr_.t, qp_, qp_.t, t)
                rope(kf_, kf_.t, kp_, kp_.t, t)
                qa_, qb_ = qTa[h].next(), qTb[h].next()
                transpose_to(qn_.t[:, :], qn_, qa_.t[:, :], qa_, "act")
                transpose_to(qp_.t[:, :], qp_, qb_.t[:, :], qb_, "dve")
                transpose_to(kn_.t[:, :], kn_, KTa[h].t[:, t * 128:(t + 1) * 128], KTa[h], "act")
                transpose_to(kp_.t[:, :], kp_, KTb[h].t[:, t * 128:(t + 1) * 128], KTb[h], "dve")
                qTs.append((qa_, qb_))
            for h in range(NH):
                acc = accb.next()
                qa_, qb_ = qTs[h]
                attn_tile(C, t, [(lambda kt, h=h: (KTa[h].t[:, kt * 128:(kt + 1) * 128], KTa[h]), qa_.t[:, :], qa_),
                                 (lambda kt, h=h: (KTb[h].t[:, kt * 128:(kt + 1) * 128], KTb[h]), qb_.t[:, :], qb_)],
                          lambda kt, h=h: (Va.t[:, vofs(kt, h):vofs(kt, h) + 129], Va), acc, acc.t[:, 0:129], SCALE, negshift,
                          {0: mask}, negshift, P, tmpf, qkb)
                r_ = rr.next()
                S.op("dve", lambda e: e.reciprocal(out=r_.t[:, 0:1], in_=acc.t[:, 128:129]), rd=[acc], wr=[r_])
                yb_ = yb.next()
                S.op("dve", lambda e: e.tensor_scalar(out=yb_.t[:, :], in0=acc.t[:, 0:128], scalar1=r_.t[:, 0:1], scalar2=None, op0=ALU.mult), rd=[acc, r_], wr=[yb_])
                transpose_to(yb_.t[:, :], yb_, o_s.t[:, h, :], o_s, "act")
            S.op("sp", lambda e: e.dma_start(out=o_d[row0:row0 + NH * 128, t * 128:(t + 1) * 128].rearrange("(h p) t -> p h t", p=128), in_=o_s.t[:, :, :]),
                 rd=[o_s], dsem=o_s.b.name)
        S.barrier()


NTOK = 2048
SEQ = 4096
NT_SEQ = SEQ // 128
TT_FFN = 512


def _ffn_inputs(nc, sfx):
    g = nc.dram_tensor("g" + sfx, [D], F32, kind="ExternalInput").ap()
    wg = nc.dram_tensor("wg" + sfx, [D, DFF], F32, kind="ExternalInput").ap()
    wu = nc.dram_tensor("wu" + sfx, [D, DFF], F32, kind="ExternalInput").ap()
    wd = nc.dram_tensor("wd" + sfx, [DFF, D], F32, kind="ExternalInput").ap()
    return g, wg, wu, wd


def build_PA():
    nc = bass.Bass("TRN2", target_bir_lowering=False)
    x = nc.dram_tensor("x", [D, NTOK], F32, kind="ExternalInput").ap()
    g, wg, wu, wd = _ffn_inputs(nc, "")
    gm = nc.dram_tensor("gm", [D], F32, kind="ExternalInput").ap()
    xo = nc.dram_tensor("xo", [D, NTOK], F32, kind="ExternalOutput").ap()
    xn = nc.dram_tensor("xn", [NTOK // 128, 128, KD, 128], BF16, kind="ExternalOutput").ap()
    with ExitStack() as st:
        C = Ctx(nc, st)
        phase_ffn(C, x, xo, g, wg, wu, wd, NTOK, TT_FFN, "A")
        phase_norm(C, xo, gm, xn, NTOK, 512, "N")
        C.S.barrier()
    return nc


def build_PM():
    nc = bass.Bass("TRN2", target_bir_lowering=False)
    dt = lambda n, sh, d=F32: nc.dram_tensor(n, sh, d, kind="ExternalInput").ap()
    xn = dt("xn", [NT_SEQ, 128, KD, 128], BF16)
    whg, wdf, wml = dt("whg", [D, 1536]), dt("wdf", [D, 768]), dt("wml", [D, 832])
    lbz, cm, og = dt("lbz", [4, 384]), dt("cm", [16]), dt("og", [128])
    dfv, dfb = dt("dfv", [8, 128]), dt("dfb", [2, 2, 128, 128])
    wuq, wukv, mlv = dt("wuq", [512, 576]), dt("wukv", [256, 768]), dt("mlv", [4, 512])
    cs = dt("cs", [128, 2 * NT_SEQ * 32])
    call, cmask = dt("c_all", [128, 512]), dt("c_mask", [128, 128])
    o = nc.dram_tensor("o", [1024, SEQ], BF16, kind="ExternalOutput").ap()
    with ExitStack() as st:
        C = Ctx(nc, st)
        phase_hg(C, xn, whg, lbz, cm, og, o, call, NT_SEQ, "H", 3)
        phase_df(C, xn, wdf, dfv, dfb, o, call, cmask, NT_SEQ, "F", 2, row0=384)
        phase_ml(C, xn, wml, wuq, wukv, mlv, cs, o, call, cmask, NT_SEQ, "M", 3, row0=640)
        C.S.barrier()
    return nc


def build_PB():
    nc = bass.Bass("TRN2", target_bir_lowering=False)
    x = nc.dram_tensor("x", [D, NTOK], F32, kind="ExternalInput").ap()
    o = nc.dram_tensor("o", [D, NTOK], BF16, kind="ExternalInput").ap()
    wo = nc.dram_tensor("wo", [D, D], F32, kind="ExternalInput").ap()
    g, wg, wu, wd = _ffn_inputs(nc, "")
    xmid = nc.dram_tensor("xmid", [D, NTOK], F32, kind="ExternalOutput").ap()
    xo = nc.dram_tensor("xo", [D, NTOK], F32, kind="ExternalOutput").ap()
    with ExitStack() as st:
        C = Ctx(nc, st)
        phase_wout(C, x, xmid, o, wo, NTOK, "O")
        phase_ffn(C, xmid, xo, g, wg, wu, wd, NTOK, TT_FFN, "B")
        C.S.barrier()
    return nc


def _consts():
    idx = np.arange(128)
    same = (idx[:, None] // 64) == (idx[None, :] // 64)
    u2 = (same & (idx[:, None] <= idx[None, :])).astype(np.float32)
    mid = (same & ((idx[:, None] % 64) <= 31)).astype(np.float32)
    cmat = u2 - mid
    sel = np.zeros((128, 128), np.float32)
    for c in range(2):
        inch = (idx // 64) == c
        sel[:, 3 * c + 0] = inch & ((idx % 64) <= 31)
        sel[:, 3 * c + 1] = inch & ((idx % 64) >= 32)
        sel[:, 3 * c + 2] = inch
    c_all = np.ascontiguousarray(np.concatenate([np.eye(128, dtype=np.float32), u2, cmat, sel], axis=1))
    ok = (idx[:, None] // 64) <= (idx[None, :] // 64)
    c_mask = np.where(ok, 0.0, NEG).astype(np.float32)
    pos = np.arange(SEQ, dtype=np.float32)
    freqs = (np.float32(10000.0) ** (-np.arange(0, 64, 2, dtype=np.float32) / np.float32(64))).astype(np.float32)
    ang = pos[:, None] * freqs[None, :]
    cos = np.cos(ang).astype(np.float32).reshape(NT_SEQ, 128, 32).transpose(1, 0, 2).reshape(128, NT_SEQ * 32)
    sin = np.sin(ang).astype(np.float32).reshape(NT_SEQ, 128, 32).transpose(1, 0, 2).reshape(128, NT_SEQ * 32)
    cs = np.ascontiguousarray(np.concatenate([cos, sin], axis=1))
    return c_all, c_mask, cs


def _t5_bucket_idx():
    import jax
    import jax.numpy as jnp
    idx = np.arange(128)
    out = []
    with jax.default_device(jax.devices("cpu")[0]):
        for r in (0, -1):
            rel = jnp.asarray(((idx[:, None] + 128 * r) - idx[None, :]).astype(np.int32))
            half, max_exact = 16, 8
            ret = (rel > 0).astype(jnp.int32) * half
            n = jnp.abs(rel)
            large = max_exact + (jnp.log(jnp.maximum(n, 1).astype(jnp.float32) / max_exact)
                                 / math.log(128 / max_exact) * (half - max_exact)).astype(jnp.int32)
            large = jnp.minimum(large, half - 1)
            out.append(np.asarray(ret + jnp.where(n < max_exact, n, large)))
    return np.stack(out, 0)


_IN_OFF = [0, 768, 1536, 2304, 3072, 3584, 4096, 4608, 5120, 5376, 5440]


PAIRS = [[0, 1], [2, 3], [4, 5], [6, 7]]


def build_fused(L=4):
    nc = bass.Bass("TRN2", target_bir_lowering=False)
    dt = lambda n, sh, d=F32: nc.dram_tensor(n, sh, d, kind="ExternalInput").ap()
    x = dt("x", [D, NTOK])
    ffn = {}
    for ab in ("a", "b"):
        ffn[ab] = (dt("ffn_%s_norm" % ab, [L, D]), dt("ffn_%s_w_gate" % ab, [L, D, DFF]), dt("ffn_%s_w_up" % ab, [L, D, DFF]),
                   dt("ffn_%s_w_down" % ab, [L, DFF, D]))
    gm = dt("mix_norm", [L, D])
    whg, wdf, wml = dt("whg", [L, D, 1536]), dt("wdf", [L, D, 768]), dt("wml", [L, D, 832])
    lbz, cm, og = dt("lbz", [4, 384]), dt("cm", [L, 16]), dt("og", [L, 128])
    dfv, dfb = dt("dfv", [L, 8, 128]), dt("dfb", [2, 2, 128, 128])
    wuq, wukv, mlv = dt("wuq", [L, 512, 576]), dt("wukv", [L, 256, 768]), dt("mlv", [L, 4, 512])
    cs = dt("cs", [128, 2 * NT_SEQ * 32])
    call, cmask = dt("c_all", [128, 512]), dt("c_mask", [128, 128])
    wo = dt("wo", [L, D, D])
    sel = dt("sel", [16])
    xo = nc.dram_tensor("xo", [D, NTOK], F32, kind="ExternalOutput").ap()
    internal = lambda n, sh, d: nc.dram_tensor(n, sh, d, kind="Internal").ap()
    local = lambda n, sh, d: nc.dram_tensor(n, sh, d, addr_space="Local", kind="Internal").ap()
    with ExitStack() as st:
        C = Ctx(nc, st)
        S = C.S
        xcur = x
        for l in range(L):
            C.sfx = "_%d" % l
            xa = internal("xa%d" % l, [D, NTOK], F32)
            xb = internal("xb%d" % l, [D, NTOK], F32)
            xc = xo if l == L - 1 else internal("xc%d" % l, [D, NTOK], F32)
            xns = internal("xns%d" % l, [NTOK // 128, 128, KD, 128], BF16)
            xnf = local("xnf%d" % l, [NT_SEQ, 128, KD, 128], BF16)
            osd = internal("osd%d" % l, [1024, SEQ], BF16)
            ofl = local("ofl%d" % l, [2048, SEQ], BF16)
            g, wg, wu, wd = ffn["a"]
            phase_ffn(C, xcur, xa, g[l], wg[l], wu[l], wd[l], NTOK, TT_FFN, "A")
            phase_norm(C, xa, gm[l], xns, NTOK, 512, "N")
            xns2 = xns.rearrange("n p k t -> (n p) (k t)")
            xnf2 = xnf.rearrange("n p k t -> (n p) (k t)")
            S.barrier()
            for j in range(8):
                S.cc(lambda e: e.collective_compute("AllGather", ALU.bypass, replica_groups=PAIRS,
                                                    ins=[xns2[j * 256:(j + 1) * 256, :]], outs=[xnf2[j * 512:(j + 1) * 512, :]]))
            S.barrier()
            xv = XnView(xnf2)
            phase_hg(C, xv, whg[l], lbz, cm[l], og[l], osd, call, NT_SEQ, "H", 3)
            phase_df(C, xv, wdf[l], dfv[l], dfb, osd, call, cmask, NT_SEQ, "F", 2, row0=384)
            phase_ml(C, xv, wml[l], wuq[l], wukv[l], mlv[l], cs, osd, call, cmask, NT_SEQ, "M", 3, row0=640)
            S.barrier()
            for j in range(8):
                S.cc(lambda e: e.collective_compute("AllGather", ALU.bypass, replica_groups=PAIRS,
                                                    ins=[osd[j * 128:(j + 1) * 128, :]], outs=[ofl[j * 256:(j + 1) * 256, :]]))
            S.barrier()
            phase_wout(C, xa, xb, ofl, wo[l], NTOK, "O", sel_d=sel)
            g, wg, wu, wd = ffn["b"]
            phase_ffn(C, xb, xc, g[l], wg[l], wu[l], wd[l], NTOK, TT_FFN, "B")
            xcur = xc
        S.barrier()
    return nc


class XnView:
    def __init__(self, g2):
        self.g2 = g2

    def __getitem__(self, key):
        t = key[0]
        rank, lt = t // 16, t % 16
        r0 = (lt // 2) * 512 + rank * 256 + (lt % 2) * 128
        return self.g2[r0:r0 + 128, :].rearrange("p (k t) -> p k t", t=128)


def _gathered_row_perm():
    perm = []
    for q in range(16):
        j, r = q // 2, q % 2
        if j < 3:
            b = 3 * r + j
        elif j < 5:
            b = 6 + 2 * r + (j - 3)
        else:
            b = 10 + 3 * r + (j - 5)
        perm += list(range(b * 128, (b + 1) * 128))
    return np.asarray(perm)


def kernel(**inputs):
    f = lambda k: np.ascontiguousarray(np.asarray(inputs[k], dtype=np.float32))
    x = f("x")
    L = 4
    c_all, c_mask, cs = _consts()
    bk = _t5_bucket_idx()
    rel_bias = f("rel_bias")
    w_in = f("w_in")
    ar = np.arange(128)
    shared = {k: f(k) for k in ("ffn_a_norm", "ffn_a_w_gate", "ffn_a_w_up", "ffn_a_w_down", "mix_norm",
                                "ffn_b_norm", "ffn_b_w_gate", "ffn_b_w_up", "ffn_b_w_down")}
    shared["wo"] = np.ascontiguousarray(f("w_out")[:, _gathered_row_perm(), :])
    shared["og"] = f("hgrn_out_norm")
    shared["cs"], shared["c_all"], shared["c_mask"] = cs, c_all, c_mask
    cm = np.zeros((L, 16), np.float32)
    dfv = np.zeros((L, 8, 128), np.float32)
    mlv = np.zeros((L, 4, 512), np.float32)
    for l in range(L):
        linit = 0.8 - 0.6 * math.exp(-0.3 * l)
        cm[l, 1:l + 1] = 1.0
        dfv[l, 0, :64] = f("diff_q_norm")[l]
        dfv[l, 1, :64] = f("diff_k_norm")[l]
        dfv[l, 2, :64] = f("diff_lambda_q1")[l]
        dfv[l, 3, :64] = f("diff_lambda_k1")[l]
        dfv[l, 4, :64] = f("diff_lambda_q2")[l]
        dfv[l, 5, :64] = f("diff_lambda_k2")[l]
        dfv[l, 6, :] = f("diff_subln")[l]
        dfv[l, 7, 0] = linit
        dfv[l, 7, 1] = 1.0 - linit
        mlv[l, 0, :] = f("mla_q_lora_norm")[l]
        mlv[l, 1, :256] = f("mla_kv_lora_norm")[l]
        mlv[l, 2, :192] = f("mla_q_norm")[l]
        mlv[l, 3, :192] = f("mla_k_norm")[l]
    shared["cm"], shared["mlv"] = cm, mlv
    per_g = []
    for g in range(2):
        hg_cols = np.concatenate([_IN_OFF[j] + (3 * g + h) * 128 + ar for h in range(3) for j in range(4)])
        df_cols = np.concatenate([_IN_OFF[4 + j] + (2 * g + h) * 128 + ar for h in range(2) for j in range(3)])
        dg = dfv.copy()
        dg[:, 7, 2:4] = rel_bias[15, 2 * g:2 * g + 2]
        sel = np.zeros(16, np.float32)
        sel[g] = 1.0
        per_g.append({
            "whg": np.ascontiguousarray(w_in[:, :, hg_cols]), "wdf": np.ascontiguousarray(w_in[:, :, df_cols]),
            "wml": np.ascontiguousarray(w_in[:, :, 4608:5440]),
            "lbz": np.ascontiguousarray(f("hgrn_lb_logits")[:, g * 384:(g + 1) * 384]),
            "dfv": dg, "dfb": np.ascontiguousarray(rel_bias[bk][..., 2 * g:2 * g + 2].transpose(3, 0, 1, 2)),
            "wuq": np.ascontiguousarray(f("mla_w_uq")[:, :, g * 576:(g + 1) * 576]),
            "wukv": np.ascontiguousarray(f("mla_w_ukv")[:, :, g * 768:(g + 1) * 768]),
            "sel": sel,
        })
    ims = []
    for c in range(8):
        m = dict(shared)
        m.update(per_g[c % 2])
        m["x"] = np.ascontiguousarray(x[c // 2, (c % 2) * NTOK:(c % 2 + 1) * NTOK, :].T)
        ims.append(m)
    nc = build_fused(L)
    res = run_bass_kernel_spmd(nc, ims, core_ids=list(range(8))).results
    out = np.empty((4, SEQ, D), np.float32)
    for c in range(8):
        out[c // 2, (c % 2) * NTOK:(c % 2 + 1) * NTOK, :] = np.asarray(res[c]["xo"]).T
    return out
```

```python
import math
from contextlib import ExitStack

import numpy as np
import ml_dtypes

import concourse.bass as bass
import concourse.mybir as mybir
from concourse.bass_utils import run_bass_kernel_spmd

F32 = mybir.dt.float32
BF16 = mybir.dt.bfloat16
AF = mybir.ActivationFunctionType
ALU = mybir.AluOpType
AX = mybir.AxisListType

D = 2048
DFF = 5504
NF = DFF // 128
KD = D // 128
EPS = 1e-6
NEG = -60.0


class Buf:
    __slots__ = ("name", "w", "r")

    def __init__(self, name=""):
        self.name = name
        self.w = None
        self.r = []


class T:
    __slots__ = ("t", "b")

    def __init__(self, t, name=""):
        self.t = t
        self.b = Buf(name)


class Sched:
    def __init__(self, nc, stack):
        self.nc = nc
        self.stack = stack
        self.engs = {"pe": nc.tensor, "act": nc.scalar, "dve": nc.vector, "pool": nc.gpsimd, "sp": nc.sync}
        self.sem = {}
        self.cnt = {}
        for e in self.engs:
            self.sem[e] = stack.enter_context(nc.semaphore("s_" + e))
            self.cnt[e] = 0
        self.waited = {}
        self.dsems = {}
        self.dcnt = {}
        self.nsem = 0

    def dsem(self, name):
        if name not in self.dsems:
            s = self.stack.enter_context(self.nc.semaphore("d_" + name))
            self.dsems[name] = s
            self.dcnt[name] = 0
        return name

    def _wait(self, e, deps):
        best = {}
        for (k, v) in deps:
            if k == e and e == "pe":
                continue
            if k not in best or best[k] < v:
                best[k] = v
        for k, v in best.items():
            if self.waited.get((e, k), 0) >= v:
                continue
            s = self.sem[k] if k in self.sem else self.dsems[k]
            self.engs[e].wait_ge(s, v)
            self.waited[(e, k)] = v

    def op(self, e, fn, rd=(), wr=(), dsem=None):
        self.nops = getattr(self, "nops", 0) + 1
        if self.nops > getattr(self, "max_ops", 1 << 60):
            return None
        deps = []
        rd = [b.b if isinstance(b, T) else b for b in rd]
        wr = [b.b if isinstance(b, T) else b for b in wr]
        wr = wr + [b for b in rd if b.name.startswith(("bank", "psb"))]
        rd = [b for b in rd if not b.name.startswith(("bank", "psb"))]
        for b in rd:
            b = b.b if isinstance(b, T) else b
            if b.w is not None:
                deps.append(b.w)
        for b in wr:
            b = b.b if isinstance(b, T) else b
            if b.w is not None:
                deps.append(b.w)
            deps.extend(b.r)
        self._wait(e, deps)
        ins = fn(self.engs[e])
        if dsem is not None:
            self.dsem(dsem)
            self.dcnt[dsem] += 16
            ins.then_inc(self.dsems[dsem], 16)
            tok = (dsem, self.dcnt[dsem])
        else:
            self.cnt[e] += 1
            ins.then_inc(self.sem[e], 1)
            tok = (e, self.cnt[e])
        for b in rd:
            b = b.b if isinstance(b, T) else b
            b.r.append(tok)
            if len(b.r) > 64:
                m = {}
                for (k, v) in b.r:
                    if k not in m or m[k] < v:
                        m[k] = v
                b.r = list(m.items())
        for b in wr:
            b = b.b if isinstance(b, T) else b
            b.w = tok
            b.r = []
        return tok

    def cc(self, fn):
        name = self.dsem("ccsem")
        ins = fn(self.engs["pool"])
        self.dcnt[name] += 1
        ins.then_inc(self.dsems[name], 1)

    def barrier(self, engines=None):
        toks = [(e, c) for e, c in self.cnt.items() if c > 0]
        toks += [(n, c) for n, c in self.dcnt.items() if c > 0]
        for e in (engines or self.engs):
            self._wait_all(e, toks)

    def _wait_all(self, e, toks):
        for (k, v) in toks:
            if k == e:
                continue
            if self.waited.get((e, k), 0) >= v:
                continue
            s = self.sem[k] if k in self.sem else self.dsems[k]
            self.engs[e].wait_ge(s, v)
            self.waited[(e, k)] = v


class Ring:
    def __init__(self, items):
        self.items = items
        self.i = 0

    def next(self):
        x = self.items[self.i % len(self.items)]
        self.i += 1
        return x


class Ctx:
    def __init__(self, nc, st):
        self.nc = nc
        self.sfx = ""
        self.S = Sched(nc, st)
        self.psf = [st.enter_context(nc.psum_tensor("psf%d" % i, [128, 512], F32)) for i in range(7)]
        self.psb = st.enter_context(nc.psum_tensor("psb", [128, 1024], BF16))
        self.bankT = [T(self.psf[i], "bank%d" % i) for i in range(7)]
        self.psbT = [T(None, "psb") for i in range(8)]
        for x in self.psbT:
            x.b = self.psbT[0].b

    def banks(self, ids):
        return [self.bankT[i] for i in ids]


def _groups(n, g):
    out = []
    i = 0
    while i < n:
        out.append((i, min(g, n - i)))
        i += g
    return out


def emit_rstd(S, out_ap, out_T, in_ap, in_T, scale, tmp_ap, tmp_T):
    S.op("act", lambda e: e.activation(out=tmp_ap, in_=in_ap, func=AF.Ln, scale=scale, bias=EPS), rd=[in_T], wr=[tmp_T])
    S.op("act", lambda e: e.activation(out=out_ap, in_=tmp_ap, func=AF.Exp, scale=-0.5), rd=[tmp_T], wr=[out_T])


def norm_tile(C, st_bufs, x_d, tok0, TT, banks):
    S = C.S
    NS = TT // 512
    xin, sq, hT, rstd, lnt, gcol, ones = (st_bufs[k] for k in ("xin", "sq", "hT", "rstd", "lnt", "gcol", "ones"))
    bk = [banks.next() for _ in range(NS)]
    for k in range(KD):
        xi = xin.next()
        si = sq.next()
        S.op("sp", lambda e: e.dma_start(out=xi.t[:, :], in_=x_d[k * 128:(k + 1) * 128, tok0:tok0 + TT]), wr=[xi], dsem=xi.b.name)
        S.op("act", lambda e: e.activation(out=si.t[:, :], in_=xi.t[:, :], func=AF.Square), rd=[xi], wr=[si])
        for s in range(NS):
            S.op("pe", lambda e: e.matmul(bk[s].t[:, :], ones.t[:, :], si.t[:, s * 512:(s + 1) * 512], start=(k == 0), stop=(k == KD - 1)),
                 rd=[ones, si], wr=[bk[s]])
    for s in range(NS):
        emit_rstd(S, rstd.t[:, s * 512:(s + 1) * 512], rstd, bk[s].t[:, :], bk[s], 1.0 / D, rstd.t[:, s * 512:(s + 1) * 512], rstd)
    for k in range(KD):
        xi = xin.next()
        S.op("sp", lambda e: e.dma_start(out=xi.t[:, :], in_=x_d[k * 128:(k + 1) * 128, tok0:tok0 + TT]), wr=[xi], dsem=xi.b.name)
        S.op("dve", lambda e: e.scalar_tensor_tensor(out=hT.t[:, k, :], in0=xi.t[:, :], scalar=gcol.t[:, k:k + 1], in1=rstd.t[:, :],
                                                     op0=ALU.mult, op1=ALU.mult), rd=[xi, gcol, rstd], wr=[hT])


def norm_bufs(C, st, g_d, TT, pfx):
    nc, S = C.nc, C.S
    sb = lambda n, sh, dt: T(st.enter_context(nc.sbuf_tensor(pfx + n + C.sfx, sh, dt)), pfx + n)
    B = {}
    B["xin"] = Ring([sb("xin%d" % i, [128, TT], F32) for i in range(2)])
    B["sq"] = Ring([sb("sq%d" % i, [128, TT], BF16) for i in range(2)])
    B["hT"] = sb("hT", [128, KD, TT], BF16)
    B["rstd"] = sb("rstd", [128, TT], F32)
    B["lnt"] = None
    B["gcol"] = sb("gcol", [128, KD], F32)
    B["ones"] = sb("ones", [128, 128], BF16)
    S.op("sp", lambda e: e.dma_start(out=B["gcol"].t[:, :], in_=g_d.rearrange("(k p) -> p k", p=128), allow_slow_non_contiguous=True),
         wr=[B["gcol"]], dsem=pfx + "gcol")
    S.op("dve", lambda e: e.memset(B["ones"].t[:, :], 1.0), wr=[B["ones"]])
    return B


def phase_ffn(C, x_d, xo_d, g_d, wg_d, wu_d, wd_d, NTOK, TT, pfx):
    nc, S = C.nc, C.S
    NS = TT // 512
    S.barrier()
    with ExitStack() as st:
        sb = lambda n, sh, dt: T(st.enter_context(nc.sbuf_tensor(pfx + n + C.sfx, sh, dt)), pfx + n)
        NB = norm_bufs(C, st, g_d, TT, pfx)
        hT = NB["hT"]
        GW = 256
        wg = Ring([sb("wg%d" % i, [128, KD, GW], BF16) for i in range(2)])
        wu = Ring([sb("wu%d" % i, [128, KD, GW], BF16) for i in range(2)])
        actT = [T(None, pfx + "act%d" % f) for f in range(NF)]
        actT_t = st.enter_context(nc.sbuf_tensor(pfx + "actT" + C.sfx, [128, NF, TT], BF16))
        sg = Ring([sb("sg%d" % i, [128, 512], BF16) for i in range(2)])
        DGC = 4 // NS
        wd = Ring([sb("wd%d" % i, [128, 4, DGC * 128], BF16) for i in range(2)])
        xres = Ring([sb("xres%d" % i, [128, 512], F32) for i in range(2)])
        yo = Ring([sb("yo%d" % i, [128, 512], F32) for i in range(2)])
        banks = Ring(C.banks([0, 1, 2, 3, 4, 5]))
        dbanks = Ring(C.banks([0, 1, 2, 3, 4, 5, 6]))
        for tt in range(NTOK // TT):
            tok0 = tt * TT
            norm_tile(C, NB, x_d, tok0, TT, banks)
            for (f0, nf) in _groups(NF, GW // 128):
                g_s = wg.next()
                u_s = wu.next()
                S.op("pool", lambda e: e.dma_start(out=g_s.t[:, :, 0:nf * 128],
                                                   in_=wg_d[:, f0 * 128:(f0 + nf) * 128].rearrange("(k p) f -> p k f", p=128)),
                     wr=[g_s], dsem=g_s.b.name)
                S.op("pool", lambda e: e.dma_start(out=u_s.t[:, :, 0:nf * 128],
                                                   in_=wu_d[:, f0 * 128:(f0 + nf) * 128].rearrange("(k p) f -> p k f", p=128)),
                     wr=[u_s], dsem=u_s.b.name)
                for fi in range(nf):
                    f = f0 + fi
                    for s in range(NS):
                        bg = banks.next()
                        bu = banks.next()
                        for k in range(KD):
                            S.op("pe", lambda e: e.matmul(bg.t[:, :], g_s.t[:, k, fi * 128:(fi + 1) * 128], hT.t[:, k, s * 512:(s + 1) * 512],
                                                          start=(k == 0), stop=(k == KD - 1)), rd=[g_s, hT], wr=[bg])
                        for k in range(KD):
                            S.op("pe", lambda e: e.matmul(bu.t[:, :], u_s.t[:, k, fi * 128:(fi + 1) * 128], hT.t[:, k, s * 512:(s + 1) * 512],
                                                          start=(k == 0), stop=(k == KD - 1)), rd=[u_s, hT], wr=[bu])
                        sgi = sg.next()
                        S.op("act", lambda e: e.activation(out=sgi.t[:, :], in_=bg.t[:, :], func=AF.Silu), rd=[bg], wr=[sgi])
                        S.op("dve", lambda e: e.tensor_tensor(out=actT_t[:, f, s * 512:(s + 1) * 512], in0=bu.t[:, :], in1=sgi.t[:, :], op=ALU.mult),
                             rd=[bu, sgi], wr=[actT[f]])
            for dg in range(KD // DGC):
                db = [[dbanks.next() for s in range(NS)] for dd in range(DGC)]
                for (f0, nf) in _groups(NF, 4):
                    w_s = wd.next()
                    S.op("pool", lambda e: e.dma_start(out=w_s.t[:, 0:nf, :],
                                                       in_=wd_d[f0 * 128:(f0 + nf) * 128, dg * DGC * 128:(dg + 1) * DGC * 128].rearrange("(j p) c -> p j c", p=128)),
                         wr=[w_s], dsem=w_s.b.name)
                    for fi in range(nf):
                        f = f0 + fi
                        for dd in range(DGC):
                            for s in range(NS):
                                S.op("pe", lambda e: e.matmul(db[dd][s].t[:, :], w_s.t[:, fi, dd * 128:(dd + 1) * 128],
                                                              actT_t[:, f, s * 512:(s + 1) * 512], start=(f == 0), stop=(f == NF - 1)),
                                     rd=[w_s, actT[f]], wr=[db[dd][s]])
                for dd in range(DGC):
                    d = dg * DGC + dd
                    for s in range(NS):
                        xr = xres.next()
                        y = yo.next()
                        c0 = tok0 + s * 512
                        S.op("sp", lambda e: e.dma_start(out=xr.t[:, :], in_=x_d[d * 128:(d + 1) * 128, c0:c0 + 512]), wr=[xr], dsem=xr.b.name)
                        S.op("dve", lambda e: e.scalar_tensor_tensor(out=y.t[:, :], in0=db[dd][s].t[:, :], scalar=0.5, in1=xr.t[:, :],
                                                                     op0=ALU.mult, op1=ALU.add), rd=[db[dd][s], xr], wr=[y])
                        S.op("sp", lambda e: e.dma_start(out=xo_d[d * 128:(d + 1) * 128, c0:c0 + 512], in_=y.t[:, :]), rd=[y], dsem=y.b.name)
        S.barrier()


def phase_norm(C, x_d, g_d, xn_d, NTOK, TT, pfx):
    nc, S = C.nc, C.S
    S.barrier()
    with ExitStack() as st:
        NB = norm_bufs(C, st, g_d, TT, pfx)
        banks = Ring(C.banks([0, 1, 2, 3]))
        for tt in range(NTOK // TT):
            norm_tile(C, NB, x_d, tt * TT, TT, banks)
            for j in range(TT // 128):
                S.op("sp", lambda e: e.dma_start(out=xn_d[tt * (TT // 128) + j, :, :, :], in_=NB["hT"].t[:, :, j * 128:(j + 1) * 128]),
                     rd=[NB["hT"]], dsem=pfx + "xnout")
        S.barrier()


def phase_wout(C, x_d, xo_d, o_d, wo_d, NTOK, pfx, sel_d=None):
    nc, S = C.nc, C.S
    S.barrier()
    with ExitStack() as st:
        sb = lambda n, sh, dt: T(st.enter_context(nc.sbuf_tensor(pfx + n + C.sfx, sh, dt)), pfx + n)
        wo = sb("wo", [128, KD, D], BF16)
        for k in range(KD):
            S.op("pool", lambda e: e.dma_start(out=wo.t[:, k, :], in_=wo_d[k * 128:(k + 1) * 128, :]), wr=[wo], dsem=pfx + "wo")
        ot = Ring([sb("ot%d" % i, [128, KD, 512], BF16) for i in range(2)])
        xres = Ring([sb("xres%d" % i, [128, 512], F32) for i in range(2)])
        yo = Ring([sb("yo%d" % i, [128, 512], F32) for i in range(2)])
        if sel_d is not None:
            selt = load_bcast(C, st, pfx + "sel", sel_d, 16)
            oa, ob = sb("oa", [128, KD, 512], BF16), sb("ob", [128, KD, 512], BF16)
            otmp = sb("otmp", [128, KD, 512], F32)
        banks = Ring(C.banks([0, 1, 2, 3]))
        for s in range(NTOK // 512):
            c0 = s * 512
            o_s = ot.next()
            if sel_d is None:
                S.op("sp", lambda e: e.dma_start(out=o_s.t[:, :, :], in_=o_d[:, c0:c0 + 512].rearrange("(k p) t -> p k t", p=128)),
                     wr=[o_s], dsem=o_s.b.name)
            else:
                S.op("sp", lambda e: e.dma_start(out=oa.t[:, :, :], in_=o_d[:, c0:c0 + 512].rearrange("(k p) t -> p k t", p=128)),
                     wr=[oa], dsem=oa.b.name)
                S.op("sp", lambda e: e.dma_start(out=ob.t[:, :, :], in_=o_d[:, NTOK + c0:NTOK + c0 + 512].rearrange("(k p) t -> p k t", p=128)),
                     wr=[ob], dsem=ob.b.name)
                S.op("dve", lambda e: e.tensor_scalar(out=otmp.t[:, :, :], in0=oa.t[:, :, :], scalar1=selt.t[:, 0:1], scalar2=None, op0=ALU.mult),
                     rd=[oa, selt], wr=[otmp])
                S.op("dve", lambda e: e.scalar_tensor_tensor(out=o_s.t[:, :, :], in0=ob.t[:, :, :], scalar=selt.t[:, 1:2], in1=otmp.t[:, :, :],
                                                             op0=ALU.mult, op1=ALU.add), rd=[ob, selt, otmp], wr=[o_s])
            for d in range(KD):
                bk = banks.next()
                for k in range(KD):
                    S.op("pe", lambda e: e.matmul(bk.t[:, :], wo.t[:, k, d * 128:(d + 1) * 128], o_s.t[:, k, :], start=(k == 0), stop=(k == KD - 1)),
                         rd=[wo, o_s], wr=[bk])
                xr = xres.next()
                y = yo.next()
                S.op("sp", lambda e: e.dma_start(out=xr.t[:, :], in_=x_d[d * 128:(d + 1) * 128, c0:c0 + 512]), wr=[xr], dsem=xr.b.name)
                S.op("dve", lambda e: e.tensor_tensor(out=y.t[:, :], in0=bk.t[:, :], in1=xr.t[:, :], op=ALU.add), rd=[bk, xr], wr=[y])
                S.op("sp", lambda e: e.dma_start(out=xo_d[d * 128:(d + 1) * 128, c0:c0 + 512], in_=y.t[:, :]), rd=[y], dsem=y.b.name)
        S.barrier()


def load_bcast(C, st, name, vec_ap, n):
    t = T(st.enter_context(C.nc.sbuf_tensor(name + C.sfx, [128, n], F32)), name)
    C.S.op("sp", lambda e: e.dma_start(out=t.t[:, :], in_=vec_ap.partition_broadcast(128)), wr=[t], dsem=name)
    return t


def load_const(C, st, name, ap, shape, dt):
    t = T(st.enter_context(C.nc.sbuf_tensor(name + C.sfx, shape, dt)), name)
    eng = "sp" if dt == F32 else "pool"
    C.S.op(eng, lambda e: e.dma_start(out=t.t[:, :], in_=ap), wr=[t], dsem=name)
    return t


def psb_region(C, i):
    return C.psb[:, i * 128:(i + 1) * 128], C.psbT[i]


def phase_hg(C, xn_d, w_d, lbz_d, cm_d, og_d, o_d, consts, NT, pfx, NH=3):
    nc, S = C.nc, C.S
    S.barrier()
    with ExitStack() as st:
        sb = lambda n, sh, dt: T(st.enter_context(nc.sbuf_tensor(pfx + n + C.sfx, sh, dt)), pfx + n)
        w = sb("w", [128, KD, NH * 512], BF16)
        for h in range(NH):
            S.op("pool", lambda e: e.dma_start(out=w.t[:, :, h * 512:(h + 1) * 512],
                                               in_=w_d[:, h * 512:(h + 1) * 512].rearrange("(k p) c -> p k c", p=128)), wr=[w], dsem=pfx + "w")
        cf = load_const(C, st, pfx + "cf", consts, [128, 512], F32)
        cb = sb("cb", [128, 512], BF16)
        S.op("dve", lambda e: e.tensor_copy(out=cb.t[:, :], in_=cf.t[:, :]), rd=[cf], wr=[cb])
        ident = T(cb.t[:, 0:128]); ident.b = cb.b
        u2 = T(cf.t[:, 128:256]); u2.b = cf.b
        cmat = T(cb.t[:, 256:384]); cmat.b = cb.b
        sel = T(cb.t[:, 384:390]); sel.b = cb.b
        ogain = load_bcast(C, st, pfx + "ogain", og_d, 128)
        NC_ = NH * 128
        lbz = sb("lbz", [128, 4, NC_], F32)
        for j in range(4):
            S.op("sp", lambda e: e.dma_start(out=lbz.t[:, j, :], in_=lbz_d[j, :].partition_broadcast(128)), wr=[lbz], dsem=pfx + "lbz")
        cm = load_bcast(C, st, pfx + "cm", cm_d, 16)
        lb = sb("lb", [128, NC_], F32)
        oml = sb("oml", [128, NC_], F32)
        den = sb("den", [128, NC_], F32)
        S.op("act", lambda e: e.activation(out=lbz.t[:, :, :], in_=lbz.t[:, :, :], func=AF.Exp), rd=[lbz], wr=[lbz])
        S.op("dve", lambda e: e.tensor_tensor(out=den.t[:, :], in0=lbz.t[:, 0, :], in1=lbz.t[:, 1, :], op=ALU.add), rd=[lbz], wr=[den])
        S.op("dve", lambda e: e.tensor_tensor(out=den.t[:, :], in0=den.t[:, :], in1=lbz.t[:, 2, :], op=ALU.add), rd=[lbz, den], wr=[den])
        S.op("dve", lambda e: e.tensor_tensor(out=den.t[:, :], in0=den.t[:, :], in1=lbz.t[:, 3, :], op=ALU.add), rd=[lbz, den], wr=[den])
        S.op("dve", lambda e: e.reciprocal(out=den.t[:, :], in_=den.t[:, :]), rd=[den], wr=[den])
        S.op("dve", lambda e: e.tensor_scalar(out=lb.t[:, :], in0=lbz.t[:, 0, :], scalar1=cm.t[:, 0:1], scalar2=None, op0=ALU.mult), rd=[lbz, cm], wr=[lb])
        for j in range(1, 4):
            S.op("dve", lambda e: e.scalar_tensor_tensor(out=lb.t[:, :], in0=lbz.t[:, j, :], scalar=cm.t[:, j:j + 1], in1=lb.t[:, :],
                                                         op0=ALU.mult, op1=ALU.add), rd=[lbz, cm, lb], wr=[lb])
        S.op("dve", lambda e: e.tensor_tensor(out=lb.t[:, :], in0=lb.t[:, :], in1=den.t[:, :], op=ALU.mult), rd=[lb, den], wr=[lb])
        S.op("dve", lambda e: e.tensor_scalar(out=oml.t[:, :], in0=lb.t[:, :], scalar1=-1.0, scalar2=1.0, op0=ALU.mult, op1=ALU.add), rd=[lb], wr=[oml])

        St = [sb("S%d" % h, [128, 128], F32) for h in range(NH)]
        for h in range(NH):
            S.op("dve", lambda e: e.memset(St[h].t[:, :], 0.0), wr=[St[h]])
        xt = Ring([sb("xt%d" % i, [128, KD, 128], BF16) for i in range(2)])
        R = lambda n, sh, dt, k=2: Ring([sb("%s%d" % (n, i), sh, dt) for i in range(k)])
        ef, eg, ff, lf, kk, E, Ei = (R(n, [128, 128], F32) for n in ("ef", "eg", "ff", "lf", "kk", "E", "Ei"))
        esc = R("esc", [128, 6], F32)
        qh, kh, vv, kT, sm, Sp0, Sp1, yb = (R(n, [128, 128], BF16) for n in ("qh", "kh", "vv", "kT", "sm", "Sp0", "Sp1", "yb"))
        A = R("A", [128, 128], F32)
        lfh, lfl = R("lfh", [128, 128], BF16), R("lfl", [128, 128], BF16)
        ss, lnv, rs = (R(n, [128, 1], F32) for n in ("ss", "lnv", "rs"))
        junk = R("junk", [128, 128], F32)
        qA = [sb("qA%d" % i, [128, 128], BF16) for i in range(2)]
        qB = [sb("qB%d" % i, [128, 128], BF16) for i in range(2)]
        for i in range(2):
            S.op("dve", lambda e: e.memset(qA[i].t[:, :], 0.0), wr=[qA[i]])
            S.op("dve", lambda e: e.memset(qB[i].t[:, :], 0.0), wr=[qB[i]])
        qAr, qBr = Ring(qA), Ring(qB)
        kA = [sb("kA%d" % i, [128, 128], BF16) for i in range(2)]
        kB = [sb("kB%d" % i, [128, 128], BF16) for i in range(2)]
        for i in range(2):
            S.op("dve", lambda e: e.memset(kA[i].t[:, :], 0.0), wr=[kA[i]])
            S.op("dve", lambda e: e.memset(kB[i].t[:, :], 0.0), wr=[kB[i]])
        kAr, kBr = Ring(kA), Ring(kB)
        ost = R("ost", [128, NH, 128], BF16)
        pjb = Ring(C.banks([0, 1, 2]))
        wkb = Ring(C.banks([3, 4, 5, 6]))
        pbi = [0]

        def psb_next():
            i = pbi[0] % 8
            pbi[0] += 1
            return psb_region(C, i)

        for t in range(NT):
            x_s = xt.next()
            S.op("sp", lambda e: e.dma_start(out=x_s.t[:, :, :], in_=xn_d[t, :, :, :]), wr=[x_s], dsem=x_s.b.name)
            o_s = ost.next()
            for h in range(NH):
                pj = pjb.next()
                for k in range(KD):
                    S.op("pe", lambda e: e.matmul(pj.t[:, :], x_s.t[:, k, :], w.t[:, k, h * 512:(h + 1) * 512], start=(k == 0), stop=(k == KD - 1)),
                         rd=[x_s, w], wr=[pj])
                q_ap, fl_ap, vi_ap, gt_ap = (pj.t[:, i * 128:(i + 1) * 128] for i in range(4))
                hs = slice(h * 128, (h + 1) * 128)
                ef_, eg_, ff_, lf_, kk_, E_, Ei_ = (r.next() for r in (ef, eg, ff, lf, kk, E, Ei))
                S.op("act", lambda e: e.activation(out=ef_.t[:, :], in_=fl_ap, func=AF.Exp, scale=-1.0), rd=[pj], wr=[ef_])
                S.op("act", lambda e: e.activation(out=eg_.t[:, :], in_=gt_ap, func=AF.Exp, scale=-1.0), rd=[pj], wr=[eg_])
                S.op("dve", lambda e: e.tensor_scalar(out=ef_.t[:, :], in0=ef_.t[:, :], scalar1=1.0, scalar2=None, op0=ALU.add), rd=[ef_], wr=[ef_])
                S.op("dve", lambda e: e.reciprocal(out=ef_.t[:, :], in_=ef_.t[:, :]), rd=[ef_], wr=[ef_])
                S.op("dve", lambda e: e.tensor_tensor(out=ff_.t[:, :], in0=ef_.t[:, :], in1=oml.t[:, hs], op=ALU.mult), rd=[ef_, oml], wr=[ff_])
                S.op("dve", lambda e: e.tensor_tensor(out=ff_.t[:, :], in0=ff_.t[:, :], in1=lb.t[:, hs], op=ALU.add), rd=[ff_, lb], wr=[ff_])
                S.op("act", lambda e: e.activation(out=lf_.t[:, :], in_=ff_.t[:, :], func=AF.Ln), rd=[ff_], wr=[lf_])
                S.op("dve", lambda e: e.tensor_scalar(out=kk_.t[:, :], in0=ff_.t[:, :], scalar1=-1.0, scalar2=1.0, op0=ALU.mult, op1=ALU.add), rd=[ff_], wr=[kk_])
                wk = wkb.next()
                lh_, ll_ = lfh.next(), lfl.next()
                S.op("dve", lambda e: e.tensor_copy(out=lh_.t[:, :], in_=lf_.t[:, :]), rd=[lf_], wr=[lh_])
                S.op("dve", lambda e: e.tensor_tensor(out=ll_.t[:, :], in0=lf_.t[:, :], in1=lh_.t[:, :], op=ALU.subtract), rd=[lf_, lh_], wr=[ll_])
                S.op("pe", lambda e: e.matmul(wk.t[:, 0:128], cmat.t[:, :], lh_.t[:, :], start=True, stop=False), rd=[cmat, lh_], wr=[wk])
                S.op("pe", lambda e: e.matmul(wk.t[:, 0:128], cmat.t[:, :], ll_.t[:, :], start=False, stop=True), rd=[cmat, ll_], wr=[wk])
                S.op("pe", lambda e: e.matmul(wk.t[:, 128:134], lh_.t[:, :], sel.t[:, :], start=True, stop=False), rd=[sel, lh_], wr=[wk])
                S.op("pe", lambda e: e.matmul(wk.t[:, 128:134], ll_.t[:, :], sel.t[:, :], start=False, stop=True), rd=[sel, ll_], wr=[wk])
                esc_ = esc.next()
                S.op("act", lambda e: e.activation(out=E_.t[:, :], in_=wk.t[:, 0:128], func=AF.Exp), rd=[wk], wr=[E_])
                S.op("act", lambda e: e.activation(out=Ei_.t[:, :], in_=wk.t[:, 0:128], func=AF.Exp, scale=-1.0), rd=[wk], wr=[Ei_])
                S.op("act", lambda e: e.activation(out=esc_.t[:, :], in_=wk.t[:, 128:134], func=AF.Exp), rd=[wk], wr=[esc_])
                qh_, kh_, vv_, kT_, sm_, Sp0_, Sp1_, yb_ = (r.next() for r in (qh, kh, vv, kT, sm, Sp0, Sp1, yb))
                S.op("dve", lambda e: e.scalar_tensor_tensor(out=qh_.t[:, :], in0=q_ap, scalar=128.0 ** -0.5, in1=E_.t[:, :], op0=ALU.mult, op1=ALU.mult),
                     rd=[pj, E_], wr=[qh_])
                S.op("dve", lambda e: e.tensor_tensor(out=kh_.t[:, :], in0=kk_.t[:, :], in1=Ei_.t[:, :], op=ALU.mult), rd=[kk_, Ei_], wr=[kh_])
                S.op("dve", lambda e: e.tensor_copy(out=vv_.t[:, :], in_=vi_ap), rd=[pj], wr=[vv_])
                tq_ap, tq = psb_next()
                tk_ap, tk = psb_next()
                S.op("pe", lambda e: e.transpose(tq_ap, qh_.t[:, :], ident.t[:, :]), rd=[qh_, ident], wr=[tq])
                S.op("pe", lambda e: e.transpose(tk_ap, kh_.t[:, :], ident.t[:, :]), rd=[kh_, ident], wr=[tk])
                qA_, qB_ = qAr.next(), qBr.next()
                S.op("act", lambda e: e.copy(out=qA_.t[:, 0:64], in_=tq_ap[:, 0:64]), rd=[tq], wr=[qA_])
                S.op("dve", lambda e: e.tensor_copy(out=qB_.t[:, 64:128], in_=tq_ap[:, 64:128]), rd=[tq], wr=[qB_])
                S.op("act", lambda e: e.copy(out=kT_.t[:, :], in_=tk_ap), rd=[tk], wr=[kT_])
                wk2 = wkb.next()
                S.op("pe", lambda e: e.matmul(wk2.t[:, 0:128], kT_.t[:, :], qA_.t[:, :], start=True, stop=False), rd=[kT_, qA_], wr=[wk2])
                S.op("pe", lambda e: e.matmul(wk2.t[:, 0:128], kT_.t[:, :], qB_.t[:, :], start=False, stop=True), rd=[kT_, qB_], wr=[wk2])
                S.op("dve", lambda e: e.tensor_tensor(out=sm_.t[:, :], in0=wk2.t[:, 0:128], in1=u2.t[:, :], op=ALU.mult), rd=[wk2, u2], wr=[sm_])
                kA_, kB_ = kAr.next(), kBr.next()
                S.op("act", lambda e: e.copy(out=kA_.t[0:64, :], in_=kh_.t[0:64, :]), rd=[kh_], wr=[kA_])
                S.op("act", lambda e: e.copy(out=kB_.t[64:128, :], in_=kh_.t[64:128, :]), rd=[kh_], wr=[kB_])
                S.op("pe", lambda e: e.matmul(wk2.t[:, 128:256], kA_.t[:, :], vv_.t[:, :], start=True, stop=True), rd=[kA_, vv_], wr=[wk2])
                S.op("pe", lambda e: e.matmul(wk2.t[:, 256:384], kB_.t[:, :], vv_.t[:, :], start=True, stop=True), rd=[kB_, vv_], wr=[wk2])
                Sh = St[h]
                for c, Sp_ in ((0, Sp0_), (1, Sp1_)):
                    A_ = A.next()
                    S.op("dve", lambda e: e.tensor_scalar(out=Sp_.t[:, :], in0=Sh.t[:, :], scalar1=esc_.t[:, 3 * c:3 * c + 1], scalar2=None, op0=ALU.mult),
                         rd=[Sh, esc_], wr=[Sp_])
                    S.op("dve", lambda e: e.tensor_scalar(out=A_.t[:, :], in0=Sh.t[:, :], scalar1=esc_.t[:, 3 * c + 2:3 * c + 3], scalar2=None, op0=ALU.mult),
                         rd=[Sh, esc_], wr=[A_])
                    S.op("dve", lambda e: e.scalar_tensor_tensor(out=Sh.t[:, :], in0=wk2.t[:, 128 * (c + 1):128 * (c + 2)],
                                                                 scalar=esc_.t[:, 3 * c + 1:3 * c + 2], in1=A_.t[:, :], op0=ALU.mult, op1=ALU.add),
                         rd=[wk2, esc_, A_], wr=[Sh])
                S.op("pe", lambda e: e.matmul(wk2.t[:, 384:512], sm_.t[:, :], vv_.t[:, :], start=True, stop=False), rd=[sm_, vv_], wr=[wk2])
                S.op("pe", lambda e: e.matmul(wk2.t[:, 384:512], qA_.t[:, :], Sp0_.t[:, :], start=False, stop=False), rd=[qA_, Sp0_], wr=[wk2])
                S.op("pe", lambda e: e.matmul(wk2.t[:, 384:512], qB_.t[:, :], Sp1_.t[:, :], start=False, stop=True), rd=[qB_, Sp1_], wr=[wk2])
                o_ap = wk2.t[:, 384:512]
                ss_, lnv_, rs_, junk_ = ss.next(), lnv.next(), rs.next(), junk.next()
                S.op("dve", lambda e: e.memset(ss_.t[:, :], 0.0), wr=[ss_])
                S.op("act", lambda e: e.activation(out=junk_.t[:, :], in_=o_ap, func=AF.Square, accum_out=ss_.t[:, 0:1]), rd=[wk2, ss_], wr=[junk_, ss_])
                emit_rstd(S, rs_.t[:, :], rs_, ss_.t[:, :], ss_, 1.0 / 128, lnv_.t[:, :], lnv_)
                S.op("dve", lambda e: e.tensor_scalar(out=eg_.t[:, :], in0=eg_.t[:, :], scalar1=1.0, scalar2=None, op0=ALU.add), rd=[eg_], wr=[eg_])
                S.op("dve", lambda e: e.reciprocal(out=eg_.t[:, :], in_=eg_.t[:, :]), rd=[eg_], wr=[eg_])
                S.op("dve", lambda e: e.tensor_tensor(out=eg_.t[:, :], in0=eg_.t[:, :], in1=gt_ap, op=ALU.mult), rd=[eg_, pj], wr=[eg_])
                S.op("dve", lambda e: e.tensor_tensor(out=eg_.t[:, :], in0=eg_.t[:, :], in1=ogain.t[:, :], op=ALU.mult), rd=[eg_, ogain], wr=[eg_])
                S.op("dve", lambda e: e.scalar_tensor_tensor(out=yb_.t[:, :], in0=o_ap, scalar=rs_.t[:, 0:1], in1=eg_.t[:, :], op0=ALU.mult, op1=ALU.mult),
                     rd=[wk2, rs_, eg_], wr=[yb_])
                ty_ap, ty = psb_next()
                S.op("pe", lambda e: e.transpose(ty_ap, yb_.t[:, :], ident.t[:, :]), rd=[yb_, ident], wr=[ty])
                S.op("act", lambda e: e.copy(out=o_s.t[:, h, :], in_=ty_ap), rd=[ty], wr=[o_s])
            S.op("sp", lambda e: e.dma_start(out=o_d[0:NH * 128, t * 128:(t + 1) * 128].rearrange("(h p) t -> p h t", p=128), in_=o_s.t[:, :, :]),
                 rd=[o_s], dsem=o_s.b.name)
        S.barrier()


def attn_tile(C, t, qk_parts, v_fn, acc, acc_ap, scale, far_bias, near, negshift, P, tmpf, qkb):
    S = C.S
    np_ = len(qk_parts)
    for g0 in range(0, t + 1, 4):
        grp = list(range(g0, min(g0 + 4, t + 1)))
        bank = qkb.next()
        for j, kt in enumerate(grp):
            for pi, (K_fn, q_ap, q_T) in enumerate(qk_parts):
                k_ap, k_T = K_fn(kt)
                S.op("pe", lambda e: e.matmul(bank.t[:, j * 128:(j + 1) * 128], k_ap, q_ap, start=(pi == 0), stop=(pi == np_ - 1)),
                     rd=[k_T, q_T], wr=[bank])
        P_ = P.next()
        nfar = len([kt for kt in grp if (kt - t) not in near])
        if nfar:
            S.op("act", lambda e: e.activation(out=P_.t[:, 0:nfar * 128], in_=bank.t[:, 0:nfar * 128], func=AF.Exp, scale=scale, bias=far_bias.t[:, :]),
                 rd=[bank, far_bias], wr=[P_])
        for j, kt in enumerate(grp):
            if (kt - t) in near:
                b = near[kt - t]
                tm = tmpf.next()
                S.op("dve", lambda e: e.scalar_tensor_tensor(out=tm.t[:, :], in0=bank.t[:, j * 128:(j + 1) * 128], scalar=scale, in1=b.t[:, :],
                                                             op0=ALU.mult, op1=ALU.add), rd=[bank, b], wr=[tm])
                S.op("act", lambda e: e.activation(out=P_.t[:, j * 128:(j + 1) * 128], in_=tm.t[:, :], func=AF.Exp, bias=negshift.t[:, :]),
                     rd=[tm, negshift], wr=[P_])
        for j, kt in enumerate(grp):
            v_ap, v_T = v_fn(kt)
            S.op("pe", lambda e: e.matmul(acc_ap, P_.t[:, j * 128:(j + 1) * 128], v_ap, start=(kt == 0), stop=(kt == t)), rd=[P_, v_T], wr=[acc])


def sub_T(parent, ap):
    x = T(ap)
    x.b = parent.b
    return x


VW = 132


def phase_df(C, xn_d, w_d, vecs_d, bias_d, o_d, consts, mask_d, NT, pfx, NH=2, row0=384):
    nc, S = C.nc, C.S
    SHIFT = 8.0
    S.barrier()
    with ExitStack() as st:
        sb = lambda n, sh, dt: T(st.enter_context(nc.sbuf_tensor(pfx + n + C.sfx, sh, dt)), pfx + n)
        R = lambda n, sh, dt, k=2: Ring([sb("%s%d" % (n, i), sh, dt) for i in range(k)])
        w = sb("w", [128, KD, NH * 384], BF16)
        for h in range(NH):
            S.op("pool", lambda e: e.dma_start(out=w.t[:, :, h * 384:(h + 1) * 384],
                                               in_=w_d[:, h * 384:(h + 1) * 384].rearrange("(k p) c -> p k c", p=128)), wr=[w], dsem=pfx + "w")
        cf = load_const(C, st, pfx + "cf", consts, [128, 512], F32)
        cb = sb("cb", [128, 512], BF16)
        S.op("dve", lambda e: e.tensor_copy(out=cb.t[:, :], in_=cf.t[:, :]), rd=[cf], wr=[cb])
        ident = sub_T(cb, cb.t[:, 0:128])
        mask = load_const(C, st, pfx + "mask", mask_d, [128, 128], F32)
        vecs = sb("vecs", [128, 8, 128], F32)
        for j in range(8):
            S.op("sp", lambda e: e.dma_start(out=vecs.t[:, j, :], in_=vecs_d[j, :].partition_broadcast(128)), wr=[vecs], dsem=pfx + "vecs")
        qg, kg = vecs.t[:, 0, 0:64], vecs.t[:, 1, 0:64]
        junk64 = sb("junk64", [128, 64], F32)
        prod = sb("prod", [128, 64], F32)
        lsum = sb("lsum", [128, 2], F32)
        S.op("dve", lambda e: e.memset(lsum.t[:, :], 0.0), wr=[lsum])
        for i in range(2):
            S.op("dve", lambda e: e.tensor_tensor(out=prod.t[:, :], in0=vecs.t[:, 2 + 2 * i, 0:64], in1=vecs.t[:, 3 + 2 * i, 0:64], op=ALU.mult), rd=[vecs], wr=[prod])
            S.op("act", lambda e: e.activation(out=junk64.t[:, :], in_=prod.t[:, :], func=AF.Identity, accum_out=lsum.t[:, i:i + 1]), rd=[prod, lsum], wr=[junk64, lsum])
        S.op("act", lambda e: e.activation(out=lsum.t[:, :], in_=lsum.t[:, :], func=AF.Exp), rd=[lsum], wr=[lsum])
        lam = sb("lam", [128, 1], F32)
        S.op("dve", lambda e: e.tensor_tensor(out=lam.t[:, :], in0=lsum.t[:, 0:1], in1=lsum.t[:, 1:2], op=ALU.subtract), rd=[lsum], wr=[lam])
        S.op("dve", lambda e: e.tensor_tensor(out=lam.t[:, :], in0=lam.t[:, :], in1=vecs.t[:, 7, 0:1], op=ALU.add), rd=[lam, vecs], wr=[lam])
        sgain = sb("sgain", [128, 128], F32)
        S.op("dve", lambda e: e.tensor_scalar(out=sgain.t[:, :], in0=vecs.t[:, 6, :], scalar1=vecs.t[:, 7, 1:2], scalar2=None, op0=ALU.mult), rd=[vecs], wr=[sgain])
        negshift = sb("negshift", [128, 1], F32)
        S.op("dve", lambda e: e.memset(negshift.t[:, :], -SHIFT), wr=[negshift])
        farb = [sb("farb%d" % h, [128, 1], F32) for h in range(NH)]
        for h in range(NH):
            S.op("dve", lambda e: e.tensor_scalar(out=farb[h].t[:, :], in0=vecs.t[:, 7, 2 + h:3 + h], scalar1=-SHIFT, scalar2=None, op0=ALU.add), rd=[vecs], wr=[farb[h]])
        bt = [[sb("bt%d_%d" % (h, r), [128, 128], F32) for r in range(2)] for h in range(NH)]
        for h in range(NH):
            for r in range(2):
                S.op("sp", lambda e: e.dma_start(out=bt[h][r].t[:, :], in_=bias_d[h, r, :, :]), wr=[bt[h][r]], dsem=pfx + "bt%d_%d" % (h, r))
            S.op("dve", lambda e: e.tensor_tensor(out=bt[h][0].t[:, :], in0=bt[h][0].t[:, :], in1=mask.t[:, :], op=ALU.add), rd=[bt[h][0], mask], wr=[bt[h][0]])
        KT = [[sb("KT%d_%d" % (h, m), [128, NT * 128], BF16) for m in range(2)] for h in range(NH)]
        for h in range(NH):
            for m in range(2):
                S.op("dve", lambda e: e.memset(KT[h][m].t[:, :], 0.0), wr=[KT[h][m]])
        Va = sb("Va", [128, NT * NH * VW], BF16)
        S.op("dve", lambda e: e.memset(Va.t[:, :], 1.0), wr=[Va])
        vofs = lambda t_, h_: (t_ * NH + h_) * VW
        xt = R("xt", [128, KD, 128], BF16)
        ss4, ln4, rs4 = (R(n, [128, 4], F32, 3) for n in ("ss4", "ln4", "rs4"))
        junk = R("junk", [128, 128], F32)
        qk = R("qk", [128, 256], BF16, 3)
        qT = [R("qT%d" % h, [128, 128], BF16) for h in range(NH)]
        P = R("P", [128, 512], BF16, 3)
        tmpf = R("tmpf", [128, 128], F32, 3)
        rr = R("rr", [128, 4], F32, 3)
        t2, of = R("t2", [128, 128], F32), R("of", [128, 128], F32)
        ss1, ln1, rs1 = (R(n, [128, 1], F32) for n in ("ss1", "ln1", "rs1"))
        yb = R("yb", [128, 128], BF16)
        ost = R("ost", [128, NH, 128], BF16)
        pjb = C.banks([0, 1])
        qkb = Ring(C.banks([2, 3, 4]))
        accb = Ring(C.banks([5, 6]))
        pbi = [0]

        def psb_next():
            i = pbi[0] % 8
            pbi[0] += 1
            return psb_region(C, i)

        for t in range(NT):
            x_s = xt.next()
            S.op("sp", lambda e: e.dma_start(out=x_s.t[:, :, :], in_=xn_d[t, :, :, :]), wr=[x_s], dsem=x_s.b.name)
            o_s = ost.next()
            qTs = []
            for h in range(NH):
                pj = pjb[h]
                for k in range(KD):
                    S.op("pe", lambda e: e.matmul(pj.t[:, 0:384], x_s.t[:, k, :], w.t[:, k, h * 384:(h + 1) * 384], start=(k == 0), stop=(k == KD - 1)),
                         rd=[x_s, w], wr=[pj])
                ss_, ln_, rs_ = ss4.next(), ln4.next(), rs4.next()
                S.op("dve", lambda e: e.memset(ss_.t[:, :], 0.0), wr=[ss_])
                for j in range(4):
                    jk = junk.next()
                    S.op("act", lambda e: e.activation(out=jk.t[:, 0:64], in_=pj.t[:, j * 64:(j + 1) * 64], func=AF.Square, accum_out=ss_.t[:, j:j + 1]),
                         rd=[pj, ss_], wr=[jk, ss_])
                emit_rstd(S, rs_.t[:, :], rs_, ss_.t[:, :], ss_, 1.0 / 64, ln_.t[:, :], ln_)
                qk_ = qk.next()
                for j in range(4):
                    g_ap = qg if j < 2 else kg
                    S.op("dve", lambda e: e.scalar_tensor_tensor(out=qk_.t[:, j * 64:(j + 1) * 64], in0=pj.t[:, j * 64:(j + 1) * 64], scalar=rs_.t[:, j:j + 1],
                                                                 in1=g_ap, op0=ALU.mult, op1=ALU.mult), rd=[pj, rs_, vecs], wr=[qk_])
                S.op("dve", lambda e: e.tensor_copy(out=Va.t[:, vofs(t, h):vofs(t, h) + 128], in_=pj.t[:, 256:384]), rd=[pj], wr=[Va])
                tq_ap, tq = psb_next()
                tk_ap, tk = psb_next()
                S.op("pe", lambda e: e.transpose(tq_ap, qk_.t[:, 0:128], ident.t), rd=[qk_, ident], wr=[tq])
                S.op("pe", lambda e: e.transpose(tk_ap, qk_.t[:, 128:256], ident.t), rd=[qk_, ident], wr=[tk])
                qT_ = qT[h].next()
                S.op("act", lambda e: e.copy(out=qT_.t[:, :], in_=tq_ap), rd=[tq], wr=[qT_])
                S.op("act", lambda e: e.copy(out=KT[h][0].t[0:64, t * 128:(t + 1) * 128], in_=tk_ap[0:64, :]), rd=[tk], wr=[KT[h][0]])
                S.op("act", lambda e: e.copy(out=KT[h][1].t[64:128, t * 128:(t + 1) * 128], in_=tk_ap[64:128, :]), rd=[tk], wr=[KT[h][1]])
                qTs.append(qT_)
            for h in range(NH):
                acc = accb.next()
                for m in range(2):
                    attn_tile(C, t, [(lambda kt, h=h, m=m: (KT[h][m].t[:, kt * 128:(kt + 1) * 128], KT[h][m]), qTs[h].t[:, :], qTs[h])],
                              lambda kt, h=h: (Va.t[:, vofs(kt, h):vofs(kt, h) + 129], Va), acc, acc.t[:, m * 256:m * 256 + 129], 0.125, farb[h],
                              {0: bt[h][0], -1: bt[h][1]}, negshift, P, tmpf, qkb)
                r_ = rr.next()
                S.op("dve", lambda e: e.reciprocal(out=r_.t[:, 0:1], in_=acc.t[:, 128:129]), rd=[acc], wr=[r_])
                S.op("dve", lambda e: e.reciprocal(out=r_.t[:, 1:2], in_=acc.t[:, 384:385]), rd=[acc], wr=[r_])
                S.op("dve", lambda e: e.tensor_tensor(out=r_.t[:, 2:3], in0=r_.t[:, 1:2], in1=lam.t[:, :], op=ALU.mult), rd=[r_, lam], wr=[r_])
                t2_, of_ = t2.next(), of.next()
                S.op("dve", lambda e: e.tensor_scalar(out=t2_.t[:, :], in0=acc.t[:, 256:384], scalar1=r_.t[:, 2:3], scalar2=None, op0=ALU.mult), rd=[acc, r_], wr=[t2_])
                S.op("dve", lambda e: e.scalar_tensor_tensor(out=of_.t[:, :], in0=acc.t[:, 0:128], scalar=r_.t[:, 0:1], in1=t2_.t[:, :],
                                                             op0=ALU.mult, op1=ALU.subtract), rd=[acc, r_, t2_], wr=[of_])
                s1, l1, r1, jk = ss1.next(), ln1.next(), rs1.next(), junk.next()
                S.op("dve", lambda e: e.memset(s1.t[:, :], 0.0), wr=[s1])
                S.op("act", lambda e: e.activation(out=jk.t[:, :], in_=of_.t[:, :], func=AF.Square, accum_out=s1.t[:, 0:1]), rd=[of_, s1], wr=[jk, s1])
                emit_rstd(S, r1.t[:, :], r1, s1.t[:, :], s1, 1.0 / 128, l1.t[:, :], l1)
                yb_ = yb.next()
                S.op("dve", lambda e: e.scalar_tensor_tensor(out=yb_.t[:, :], in0=of_.t[:, :], scalar=r1.t[:, 0:1], in1=sgain.t[:, :],
                                                             op0=ALU.mult, op1=ALU.mult), rd=[of_, r1, sgain], wr=[yb_])
                ty_ap, ty = psb_next()
                S.op("pe", lambda e: e.transpose(ty_ap, yb_.t[:, :], ident.t), rd=[yb_, ident], wr=[ty])
                S.op("act", lambda e: e.copy(out=o_s.t[:, h, :], in_=ty_ap), rd=[ty], wr=[o_s])
            S.op("sp", lambda e: e.dma_start(out=o_d[row0:row0 + NH * 128, t * 128:(t + 1) * 128].rearrange("(h p) t -> p h t", p=128), in_=o_s.t[:, :, :]),
                 rd=[o_s], dsem=o_s.b.name)
        S.barrier()


def phase_ml(C, xn_d, w_d, wuq_d, wukv_d, vecs_d, cs_d, o_d, consts, mask_d, NT, pfx, NH=3, row0=640):
    nc, S = C.nc, C.S
    SHIFT = 14.0
    SCALE = 192.0 ** -0.5
    S.barrier()
    with ExitStack() as st:
        sb = lambda n, sh, dt: T(st.enter_context(nc.sbuf_tensor(pfx + n + C.sfx, sh, dt)), pfx + n)
        R = lambda n, sh, dt, k=2: Ring([sb("%s%d" % (n, i), sh, dt) for i in range(k)])
        w = sb("w", [128, KD, 832], BF16)
        for (c0, c1) in ((0, 512), (512, 832)):
            S.op("pool", lambda e: e.dma_start(out=w.t[:, :, c0:c1], in_=w_d[:, c0:c1].rearrange("(k p) c -> p k c", p=128)), wr=[w], dsem=pfx + "w")
        wuq = sb("wuq", [128, 4, NH * 192], BF16)
        S.op("pool", lambda e: e.dma_start(out=wuq.t[:, :, :], in_=wuq_d.rearrange("(k p) c -> p k c", p=128)), wr=[wuq], dsem=pfx + "wuq")
        wukv = sb("wukv", [128, 2, NH * 256], BF16)
        S.op("pool", lambda e: e.dma_start(out=wukv.t[:, :, :], in_=wukv_d.rearrange("(k p) c -> p k c", p=128)), wr=[wukv], dsem=pfx + "wukv")
        cf = load_const(C, st, pfx + "cf", consts, [128, 512], F32)
        cb = sb("cb", [128, 512], BF16)
        S.op("dve", lambda e: e.tensor_copy(out=cb.t[:, :], in_=cf.t[:, :]), rd=[cf], wr=[cb])
        ident = sub_T(cb, cb.t[:, 0:128])
        mask = load_const(C, st, pfx + "mask", mask_d, [128, 128], F32)
        vecs = sb("vecs", [128, 4, 512], F32)
        for j in range(4):
            S.op("sp", lambda e: e.dma_start(out=vecs.t[:, j, :], in_=vecs_d[j, :].partition_broadcast(128)), wr=[vecs], dsem=pfx + "vecs")
        cs = load_const(C, st, pfx + "cs", cs_d, [128, 2 * NT * 32], F32)
        negshift = sb("negshift", [128, 1], F32)
        S.op("dve", lambda e: e.memset(negshift.t[:, :], -SHIFT), wr=[negshift])
        KTa = [sb("KTa%d" % h, [128, NT * 128], BF16) for h in range(NH)]
        KTb = [sb("KTb%d" % h, [128, NT * 128], BF16) for h in range(NH)]
        Va = sb("Va", [128, NT * NH * VW], BF16)
        S.op("dve", lambda e: e.memset(Va.t[:, :], 1.0), wr=[Va])
        vofs = lambda t_, h_: (t_ * NH + h_) * VW
        xt = R("xt", [128, KD, 128], BF16)
        ss2, ln2, rs2 = (R(n, [128, 4], F32) for n in ("ss2", "ln2", "rs2"))
        junk = R("junk", [128, 512], F32)
        cqn = R("cqn", [128, 512], BF16)
        ckvn = R("ckvn", [128, 256], BF16)
        kr = R("kr", [128, 64], F32)
        cqT = R("cqT", [128, 4, 128], BF16)
        ckvT = R("ckvT", [128, 2, 128], BF16)
        ssh, lnh, rsh = (R(n, [128, 4], F32, 3) for n in ("ssh", "lnh", "rsh"))
        qn_b = R("qnb", [128, 128], BF16, 3)
        kn_b = R("knb", [128, 128], BF16, 3)
        qr_f = R("qrf", [128, 64], F32, 3)
        kr_f = R("krf", [128, 64], F32, 3)
        ra, rb = R("ra", [128, 32], F32, 3), R("rb", [128, 32], F32, 3)
        qrp = [sb("qrp%d" % i, [128, 128], BF16) for i in range(2)]
        krp = [sb("krp%d" % i, [128, 128], BF16) for i in range(2)]
        for i in range(2):
            S.op("dve", lambda e: e.memset(qrp[i].t[:, :], 0.0), wr=[qrp[i]])
            S.op("dve", lambda e: e.memset(krp[i].t[:, :], 0.0), wr=[krp[i]])
        qrpr, krpr = Ring(qrp), Ring(krp)
        qTa = [R("qTa%d" % h, [128, 128], BF16) for h in range(NH)]
        qTb = [R("qTb%d" % h, [128, 128], BF16) for h in range(NH)]
        P = R("P", [128, 512], BF16, 3)
        tmpf = R("tmpf", [128, 128], F32, 3)
        rr = R("rr", [128, 1], F32, 3)
        yb = R("yb", [128, 128], BF16)
        ost = R("ost", [128, NH, 128], BF16)
        b0, b1, b2, b3, b4 = C.banks([0, 1, 2, 3, 4])
        qreg = [(b2, 0), (b2, 192), (b3, 0)]
        kvreg = [(b4, 0), (b4, 256), (b3, 192)]
        qkb = Ring(C.banks([0, 1, 2, 3]))
        accb = Ring(C.banks([5, 6]))
        pbi = [0]

        def psb_next():
            i = pbi[0] % 8
            pbi[0] += 1
            return psb_region(C, i)

        def transpose_to(src_ap, src_T, dst_ap, dst_T, eng="act"):
            tp_ap, tp = psb_next()
            S.op("pe", lambda e: e.transpose(tp_ap, src_ap, ident.t), rd=[src_T, ident], wr=[tp])
            if eng == "act":
                S.op("act", lambda e: e.copy(out=dst_ap, in_=tp_ap), rd=[tp], wr=[dst_T])
            else:
                S.op("dve", lambda e: e.tensor_copy(out=dst_ap, in_=tp_ap), rd=[tp], wr=[dst_T])

        def rope(src_T, src, dst_T, dst, t):
            cos = cs.t[:, t * 32:(t + 1) * 32]
            sin = cs.t[:, NT * 32 + t * 32:NT * 32 + (t + 1) * 32]
            x1, x2 = src[:, 0:32], src[:, 32:64]
            a, b = ra.next(), rb.next()
            S.op("dve", lambda e: e.tensor_tensor(out=a.t[:, :], in0=x1, in1=cos, op=ALU.mult), rd=[src_T, cs], wr=[a])
            S.op("dve", lambda e: e.tensor_tensor(out=b.t[:, :], in0=x2, in1=sin, op=ALU.mult), rd=[src_T, cs], wr=[b])
            S.op("dve", lambda e: e.tensor_tensor(out=dst[:, 0:32], in0=a.t[:, :], in1=b.t[:, :], op=ALU.subtract), rd=[a, b], wr=[dst_T])
            a, b = ra.next(), rb.next()
            S.op("dve", lambda e: e.tensor_tensor(out=a.t[:, :], in0=x2, in1=cos, op=ALU.mult), rd=[src_T, cs], wr=[a])
            S.op("dve", lambda e: e.tensor_tensor(out=b.t[:, :], in0=x1, in1=sin, op=ALU.mult), rd=[src_T, cs], wr=[b])
            S.op("dve", lambda e: e.tensor_tensor(out=dst[:, 32:64], in0=a.t[:, :], in1=b.t[:, :], op=ALU.add), rd=[a, b], wr=[dst_T])

        for t in range(NT):
            x_s = xt.next()
            S.op("sp", lambda e: e.dma_start(out=x_s.t[:, :, :], in_=xn_d[t, :, :, :]), wr=[x_s], dsem=x_s.b.name)
            o_s = ost.next()
            for k in range(KD):
                S.op("pe", lambda e: e.matmul(b0.t[:, :], x_s.t[:, k, :], w.t[:, k, 0:512], start=(k == 0), stop=(k == KD - 1)), rd=[x_s, w], wr=[b0])
            for k in range(KD):
                S.op("pe", lambda e: e.matmul(b1.t[:, 0:320], x_s.t[:, k, :], w.t[:, k, 512:832], start=(k == 0), stop=(k == KD - 1)), rd=[x_s, w], wr=[b1])
            ss_, ln_, rs_ = ss2.next(), ln2.next(), rs2.next()
            S.op("dve", lambda e: e.memset(ss_.t[:, :], 0.0), wr=[ss_])
            jk = junk.next()
            S.op("act", lambda e: e.activation(out=jk.t[:, :], in_=b0.t[:, :], func=AF.Square, accum_out=ss_.t[:, 0:1]), rd=[b0, ss_], wr=[jk, ss_])
            jk = junk.next()
            S.op("act", lambda e: e.activation(out=jk.t[:, 0:256], in_=b1.t[:, 0:256], func=AF.Square, accum_out=ss_.t[:, 1:2]), rd=[b1, ss_], wr=[jk, ss_])
            kr_ = kr.next()
            S.op("act", lambda e: e.copy(out=kr_.t[:, :], in_=b1.t[:, 256:320]), rd=[b1], wr=[kr_])
            jk = junk.next()
            S.op("act", lambda e: e.activation(out=jk.t[:, 0:64], in_=kr_.t[:, :], func=AF.Square, accum_out=ss_.t[:, 2:3]), rd=[kr_, ss_], wr=[jk, ss_])
            S.op("dve", lambda e: e.tensor_scalar(out=ss_.t[:, 0:1], in0=ss_.t[:, 0:1], scalar1=0.5, scalar2=None, op0=ALU.mult), rd=[ss_], wr=[ss_])
            emit_rstd(S, rs_.t[:, 0:2], rs_, ss_.t[:, 0:2], ss_, 1.0 / 256, ln_.t[:, 0:2], ln_)
            cqn_, ckvn_ = cqn.next(), ckvn.next()
            S.op("dve", lambda e: e.scalar_tensor_tensor(out=cqn_.t[:, :], in0=b0.t[:, :], scalar=rs_.t[:, 0:1], in1=vecs.t[:, 0, :], op0=ALU.mult, op1=ALU.mult),
                 rd=[b0, rs_, vecs], wr=[cqn_])
            S.op("dve", lambda e: e.scalar_tensor_tensor(out=ckvn_.t[:, :], in0=b1.t[:, 0:256], scalar=rs_.t[:, 1:2], in1=vecs.t[:, 1, 0:256], op0=ALU.mult, op1=ALU.mult),
                 rd=[b1, rs_, vecs], wr=[ckvn_])
            cqT_, ckvT_ = cqT.next(), ckvT.next()
            for r in range(4):
                transpose_to(cqn_.t[:, r * 128:(r + 1) * 128], cqn_, cqT_.t[:, r, :], cqT_, "act" if r % 2 == 0 else "dve")
            for r in range(2):
                transpose_to(ckvn_.t[:, r * 128:(r + 1) * 128], ckvn_, ckvT_.t[:, r, :], ckvT_, "act" if r % 2 == 0 else "dve")
            for h in range(NH):
                qb, qo = qreg[h]
                for r in range(4):
                    S.op("pe", lambda e: e.matmul(qb.t[:, qo:qo + 192], cqT_.t[:, r, :], wuq.t[:, r, h * 192:(h + 1) * 192], start=(r == 0), stop=(r == 3)),
                         rd=[cqT_, wuq], wr=[qb])
                kb, ko = kvreg[h]
                for r in range(2):
                    S.op("pe", lambda e: e.matmul(kb.t[:, ko:ko + 256], ckvT_.t[:, r, :], wukv.t[:, r, h * 256:(h + 1) * 256], start=(r == 0), stop=(r == 1)),
                         rd=[ckvT_, wukv], wr=[kb])
            qTs = []
            for h in range(NH):
                qb, qo = qreg[h]
                kb, ko = kvreg[h]
                sh_, lh_, rh_ = ssh.next(), lnh.next(), rsh.next()
                S.op("dve", lambda e: e.memset(sh_.t[:, :], 0.0), wr=[sh_])
                jk = junk.next()
                S.op("act", lambda e: e.activation(out=jk.t[:, 0:192], in_=qb.t[:, qo:qo + 192], func=AF.Square, accum_out=sh_.t[:, 0:1]), rd=[qb, sh_], wr=[jk, sh_])
                jk = junk.next()
                S.op("act", lambda e: e.activation(out=jk.t[:, 0:128], in_=kb.t[:, ko:ko + 128], func=AF.Square, accum_out=sh_.t[:, 1:2]), rd=[kb, sh_], wr=[jk, sh_])
                S.op("dve", lambda e: e.tensor_tensor(out=sh_.t[:, 1:2], in0=sh_.t[:, 1:2], in1=ss_.t[:, 2:3], op=ALU.add), rd=[sh_, ss_], wr=[sh_])
                emit_rstd(S, rh_.t[:, 0:2], rh_, sh_.t[:, 0:2], sh_, 1.0 / 192, lh_.t[:, 0:2], lh_)
                qn_, kn_, qr_, kf_ = qn_b.next(), kn_b.next(), qr_f.next(), kr_f.next()
                S.op("dve", lambda e: e.scalar_tensor_tensor(out=qn_.t[:, :], in0=qb.t[:, qo:qo + 128], scalar=rh_.t[:, 0:1], in1=vecs.t[:, 2, 0:128], op0=ALU.mult, op1=ALU.mult),
                     rd=[qb, rh_, vecs], wr=[qn_])
                S.op("dve", lambda e: e.scalar_tensor_tensor(out=qr_.t[:, :], in0=qb.t[:, qo + 128:qo + 192], scalar=rh_.t[:, 0:1], in1=vecs.t[:, 2, 128:192], op0=ALU.mult, op1=ALU.mult),
                     rd=[qb, rh_, vecs], wr=[qr_])
                S.op("dve", lambda e: e.scalar_tensor_tensor(out=kn_.t[:, :], in0=kb.t[:, ko:ko + 128], scalar=rh_.t[:, 1:2], in1=vecs.t[:, 3, 0:128], op0=ALU.mult, op1=ALU.mult),
                     rd=[kb, rh_, vecs], wr=[kn_])
                S.op("dve", lambda e: e.scalar_tensor_tensor(out=kf_.t[:, :], in0=kr_.t[:, :], scalar=rh_.t[:, 1:2], in1=vecs.t[:, 3, 128:192], op0=ALU.mult, op1=ALU.mult),
                     rd=[kr_, rh_, vecs], wr=[kf_])
                S.op("act", lambda e: e.copy(out=Va.t[:, vofs(t, h):vofs(t, h) + 128], in_=kb.t[:, ko + 128:ko + 256]), rd=[kb], wr=[Va])
                qp_, kp_ = qrpr.next(), krpr.next()
                rope(qr_, qr_.t, qp_, qp_.t, t)
                rope(kf_, kf_.t, kp_, kp_.t, t)
                qa_, qb_ = qTa[h].next(), qTb[h].next()
                transpose_to(qn_.t[:, :], qn_, qa_.t[:, :], qa_, "act")
                transpose_to(qp_.t[:, :], qp_, qb_.t[:, :], qb_, "dve")
                transpose_to(kn_.t[:, :], kn_, KTa[h].t[:, t * 128:(t + 1) * 128], KTa[h], "act")
                transpose_to(kp_.t[:, :], kp_, KTb[h].t[:, t * 128:(t + 1) * 128], KTb[h], "dve")
                qTs.append((qa_, qb_))
            for h in range(NH):
                acc = accb.next()
                qa_, qb_ = qTs[h]
                attn_tile(C, t, [(lambda kt, h=h: (KTa[h].t[:, kt * 128:(kt + 1) * 128], KTa[h]), qa_.t[:, :], qa_),
                                 (lambda kt, h=h: (KTb[h].t[:, kt * 128:(kt + 1) * 128], KTb[h]), qb_.t[:, :], qb_)],
                          lambda kt, h=h: (Va.t[:, vofs(kt, h):vofs(kt, h) + 129], Va), acc, acc.t[:, 0:129], SCALE, negshift,
                          {0: mask}, negshift, P, tmpf, qkb)
                r_ = rr.next()
                S.op("dve", lambda e: e.reciprocal(out=r_.t[:, 0:1], in_=acc.t[:, 128:129]), rd=[acc], wr=[r_])
                yb_ = yb.next()
                S.op("dve", lambda e: e.tensor_scalar(out=yb_.t[:, :], in0=acc.t[:, 0:128], scalar1=r_.t[:, 0:1], scalar2=None, op0=ALU.mult), rd=[acc, r_], wr=[yb_])
                transpose_to(yb_.t[:, :], yb_, o_s.t[:, h, :], o_s, "act")
            S.op("sp", lambda e: e.dma_start(out=o_d[row0:row0 + NH * 128, t * 128:(t + 1) * 128].rearrange("(h p) t -> p h t", p=128), in_=o_s.t[:, :, :]),
                 rd=[o_s], dsem=o_s.b.name)
        S.barrier()


NTOK = 2048
SEQ = 4096
NT_SEQ = SEQ // 128
TT_FFN = 1024


def _ffn_inputs(nc, sfx):
    g = nc.dram_tensor("g" + sfx, [D], F32, kind="ExternalInput").ap()
    wg = nc.dram_tensor("wg" + sfx, [D, DFF], F32, kind="ExternalInput").ap()
    wu = nc.dram_tensor("wu" + sfx, [D, DFF], F32, kind="ExternalInput").ap()
    wd = nc.dram_tensor("wd" + sfx, [DFF, D], F32, kind="ExternalInput").ap()
    return g, wg, wu, wd


def build_PA():
    nc = bass.Bass("TRN2", target_bir_lowering=False)
    x = nc.dram_tensor("x", [D, NTOK], F32, kind="ExternalInput").ap()
    g, wg, wu, wd = _ffn_inputs(nc, "")
    gm = nc.dram_tensor("gm", [D], F32, kind="ExternalInput").ap()
    xo = nc.dram_tensor("xo", [D, NTOK], F32, kind="ExternalOutput").ap()
    xn = nc.dram_tensor("xn", [NTOK // 128, 128, KD, 128], BF16, kind="ExternalOutput").ap()
    with ExitStack() as st:
        C = Ctx(nc, st)
        phase_ffn(C, x, xo, g, wg, wu, wd, NTOK, TT_FFN, "A")
        phase_norm(C, xo, gm, xn, NTOK, 512, "N")
        C.S.barrier()
    return nc


def build_PM():
    nc = bass.Bass("TRN2", target_bir_lowering=False)
    dt = lambda n, sh, d=F32: nc.dram_tensor(n, sh, d, kind="ExternalInput").ap()
    xn = dt("xn", [NT_SEQ, 128, KD, 128], BF16)
    whg, wdf, wml = dt("whg", [D, 1536]), dt("wdf", [D, 768]), dt("wml", [D, 832])
    lbz, cm, og = dt("lbz", [4, 384]), dt("cm", [16]), dt("og", [128])
    dfv, dfb = dt("dfv", [8, 128]), dt("dfb", [2, 2, 128, 128])
    wuq, wukv, mlv = dt("wuq", [512, 576]), dt("wukv", [256, 768]), dt("mlv", [4, 512])
    cs = dt("cs", [128, 2 * NT_SEQ * 32])
    call, cmask = dt("c_all", [128, 512]), dt("c_mask", [128, 128])
    o = nc.dram_tensor("o", [1024, SEQ], BF16, kind="ExternalOutput").ap()
    with ExitStack() as st:
        C = Ctx(nc, st)
        phase_hg(C, xn, whg, lbz, cm, og, o, call, NT_SEQ, "H", 3)
        phase_df(C, xn, wdf, dfv, dfb, o, call, cmask, NT_SEQ, "F", 2, row0=384)
        phase_ml(C, xn, wml, wuq, wukv, mlv, cs, o, call, cmask, NT_SEQ, "M", 3, row0=640)
        C.S.barrier()
    return nc


def build_PB():
    nc = bass.Bass("TRN2", target_bir_lowering=False)
    x = nc.dram_tensor("x", [D, NTOK], F32, kind="ExternalInput").ap()
    o = nc.dram_tensor("o", [D, NTOK], BF16, kind="ExternalInput").ap()
    wo = nc.dram_tensor("wo", [D, D], F32, kind="ExternalInput").ap()
    g, wg, wu, wd = _ffn_inputs(nc, "")
    xmid = nc.dram_tensor("xmid", [D, NTOK], F32, kind="ExternalOutput").ap()
    xo = nc.dram_tensor("xo", [D, NTOK], F32, kind="ExternalOutput").ap()
    with ExitStack() as st:
        C = Ctx(nc, st)
        phase_wout(C, x, xmid, o, wo, NTOK, "O")
        phase_ffn(C, xmid, xo, g, wg, wu, wd, NTOK, TT_FFN, "B")
        C.S.barrier()
    return nc


def _consts():
    idx = np.arange(128)
    same = (idx[:, None] // 64) == (idx[None, :] // 64)
    u2 = (same & (idx[:, None] <= idx[None, :])).astype(np.float32)
    mid = (same & ((idx[:, None] % 64) <= 31)).astype(np.float32)
    cmat = u2 - mid
    sel = np.zeros((128, 128), np.float32)
    for c in range(2):
        inch = (idx // 64) == c
        sel[:, 3 * c + 0] = inch & ((idx % 64) <= 31)
        sel[:, 3 * c + 1] = inch & ((idx % 64) >= 32)
        sel[:, 3 * c + 2] = inch
    c_all = np.ascontiguousarray(np.concatenate([np.eye(128, dtype=np.float32), u2, cmat, sel], axis=1))
    ok = (idx[:, None] // 64) <= (idx[None, :] // 64)
    c_mask = np.where(ok, 0.0, NEG).astype(np.float32)
    pos = np.arange(SEQ, dtype=np.float32)
    freqs = (np.float32(10000.0) ** (-np.arange(0, 64, 2, dtype=np.float32) / np.float32(64))).astype(np.float32)
    ang = pos[:, None] * freqs[None, :]
    cos = np.cos(ang).astype(np.float32).reshape(NT_SEQ, 128, 32).transpose(1, 0, 2).reshape(128, NT_SEQ * 32)
    sin = np.sin(ang).astype(np.float32).reshape(NT_SEQ, 128, 32).transpose(1, 0, 2).reshape(128, NT_SEQ * 32)
    cs = np.ascontiguousarray(np.concatenate([cos, sin], axis=1))
    return c_all, c_mask, cs


def _t5_bucket_idx():
    import jax
    import jax.numpy as jnp
    idx = np.arange(128)
    out = []
    with jax.default_device(jax.devices("cpu")[0]):
        for r in (0, -1):
            rel = jnp.asarray(((idx[:, None] + 128 * r) - idx[None, :]).astype(np.int32))
            half, max_exact = 16, 8
            ret = (rel > 0).astype(jnp.int32) * half
            n = jnp.abs(rel)
            large = max_exact + (jnp.log(jnp.maximum(n, 1).astype(jnp.float32) / max_exact)
                                 / math.log(128 / max_exact) * (half - max_exact)).astype(jnp.int32)
            large = jnp.minimum(large, half - 1)
            out.append(np.asarray(ret + jnp.where(n < max_exact, n, large)))
    return np.stack(out, 0)


_IN_OFF = [0, 768, 1536, 2304, 3072, 3584, 4096, 4608, 5120, 5376, 5440]


PAIRS = [[0, 1], [2, 3], [4, 5], [6, 7]]


def build_fused(L=4):
    nc = bass.Bass("TRN2", target_bir_lowering=False)
    dt = lambda n, sh, d=F32: nc.dram_tensor(n, sh, d, kind="ExternalInput").ap()
    x = dt("x", [D, NTOK])
    ffn = {}
    for ab in ("a", "b"):
        ffn[ab] = (dt("ffn_%s_norm" % ab, [L, D]), dt("ffn_%s_w_gate" % ab, [L, D, DFF]), dt("ffn_%s_w_up" % ab, [L, D, DFF]),
                   dt("ffn_%s_w_down" % ab, [L, DFF, D]))
    gm = dt("mix_norm", [L, D])
    whg, wdf, wml = dt("whg", [L, D, 1536]), dt("wdf", [L, D, 768]), dt("wml", [L, D, 832])
    lbz, cm, og = dt("lbz", [4, 384]), dt("cm", [L, 16]), dt("og", [L, 128])
    dfv, dfb = dt("dfv", [L, 8, 128]), dt("dfb", [2, 2, 128, 128])
    wuq, wukv, mlv = dt("wuq", [L, 512, 576]), dt("wukv", [L, 256, 768]), dt("mlv", [L, 4, 512])
    cs = dt("cs", [128, 2 * NT_SEQ * 32])
    call, cmask = dt("c_all", [128, 512]), dt("c_mask", [128, 128])
    wo = dt("wo", [L, D, D])
    sel = dt("sel", [16])
    xo = nc.dram_tensor("xo", [D, NTOK], F32, kind="ExternalOutput").ap()
    internal = lambda n, sh, d: nc.dram_tensor(n, sh, d, kind="Internal").ap()
    local = lambda n, sh, d: nc.dram_tensor(n, sh, d, addr_space="Local", kind="Internal").ap()
    with ExitStack() as st:
        C = Ctx(nc, st)
        S = C.S
        xcur = x
        for l in range(L):
            C.sfx = "_%d" % l
            xa = internal("xa%d" % l, [D, NTOK], F32)
            xb = internal("xb%d" % l, [D, NTOK], F32)
            xc = xo if l == L - 1 else internal("xc%d" % l, [D, NTOK], F32)
            xns = internal("xns%d" % l, [NTOK // 128, 128, KD, 128], BF16)
            xnf = local("xnf%d" % l, [NT_SEQ, 128, KD, 128], BF16)
            osd = internal("osd%d" % l, [1024, SEQ], BF16)
            ofl = local("ofl%d" % l, [2048, SEQ], BF16)
            g, wg, wu, wd = ffn["a"]
            phase_ffn(C, xcur, xa, g[l], wg[l], wu[l], wd[l], NTOK, TT_FFN, "A")
            phase_norm(C, xa, gm[l], xns, NTOK, 512, "N")
            xns2 = xns.rearrange("n p k t -> (n p) (k t)")
            xnf2 = xnf.rearrange("n p k t -> (n p) (k t)")
            S.barrier()
            for j in range(8):
                S.cc(lambda e: e.collective_compute("AllGather", ALU.bypass, replica_groups=PAIRS,
                                                    ins=[xns2[j * 256:(j + 1) * 256, :]], outs=[xnf2[j * 512:(j + 1) * 512, :]]))
            S.barrier()
            xv = XnView(xnf2)
            phase_hg(C, xv, whg[l], lbz, cm[l], og[l], osd, call, NT_SEQ, "H", 3)
            phase_df(C, xv, wdf[l], dfv[l], dfb, osd, call, cmask, NT_SEQ, "F", 2, row0=384)
            phase_ml(C, xv, wml[l], wuq[l], wukv[l], mlv[l], cs, osd, call, cmask, NT_SEQ, "M", 3, row0=640)
            S.barrier()
            for j in range(8):
                S.cc(lambda e: e.collective_compute("AllGather", ALU.bypass, replica_groups=PAIRS,
                                                    ins=[osd[j * 128:(j + 1) * 128, :]], outs=[ofl[j * 256:(j + 1) * 256, :]]))
            S.barrier()
            phase_wout(C, xa, xb, ofl, wo[l], NTOK, "O", sel_d=sel)
            g, wg, wu, wd = ffn["b"]
            phase_ffn(C, xb, xc, g[l], wg[l], wu[l], wd[l], NTOK, TT_FFN, "B")
            xcur = xc
        S.barrier()
    return nc


class XnView:
    def __init__(self, g2):
        self.g2 = g2

    def __getitem__(self, key):
        t = key[0]
        rank, lt = t // 16, t % 16
        r0 = (lt // 2) * 512 + rank * 256 + (lt % 2) * 128
        return self.g2[r0:r0 + 128, :].rearrange("p (k t) -> p k t", t=128)


def _gathered_row_perm():
    perm = []
    for q in range(16):
        j, r = q // 2, q % 2
        if j < 3:
            b = 3 * r + j
        elif j < 5:
            b = 6 + 2 * r + (j - 3)
        else:
            b = 10 + 3 * r + (j - 5)
        perm += list(range(b * 128, (b + 1) * 128))
    return np.asarray(perm)


def kernel(**inputs):
    f = lambda k: np.ascontiguousarray(np.asarray(inputs[k], dtype=np.float32))
    x = f("x")
    L = 4
    c_all, c_mask, cs = _consts()
    bk = _t5_bucket_idx()
    rel_bias = f("rel_bias")
    w_in = f("w_in")
    ar = np.arange(128)
    shared = {k: f(k) for k in ("ffn_a_norm", "ffn_a_w_gate", "ffn_a_w_up", "ffn_a_w_down", "mix_norm",
                                "ffn_b_norm", "ffn_b_w_gate", "ffn_b_w_up", "ffn_b_w_down")}
    shared["wo"] = np.ascontiguousarray(f("w_out")[:, _gathered_row_perm(), :])
    shared["og"] = f("hgrn_out_norm")
    shared["cs"], shared["c_all"], shared["c_mask"] = cs, c_all, c_mask
    cm = np.zeros((L, 16), np.float32)
    dfv = np.zeros((L, 8, 128), np.float32)
    mlv = np.zeros((L, 4, 512), np.float32)
    for l in range(L):
        linit = 0.8 - 0.6 * math.exp(-0.3 * l)
        cm[l, 1:l + 1] = 1.0
        dfv[l, 0, :64] = f("diff_q_norm")[l]
        dfv[l, 1, :64] = f("diff_k_norm")[l]
        dfv[l, 2, :64] = f("diff_lambda_q1")[l]
        dfv[l, 3, :64] = f("diff_lambda_k1")[l]
        dfv[l, 4, :64] = f("diff_lambda_q2")[l]
        dfv[l, 5, :64] = f("diff_lambda_k2")[l]
        dfv[l, 6, :] = f("diff_subln")[l]
        dfv[l, 7, 0] = linit
        dfv[l, 7, 1] = 1.0 - linit
        mlv[l, 0, :] = f("mla_q_lora_norm")[l]
        mlv[l, 1, :256] = f("mla_kv_lora_norm")[l]
        mlv[l, 2, :192] = f("mla_q_norm")[l]
        mlv[l, 3, :192] = f("mla_k_norm")[l]
    shared["cm"], shared["mlv"] = cm, mlv
    per_g = []
    for g in range(2):
        hg_cols = np.concatenate([_IN_OFF[j] + (3 * g + h) * 128 + ar for h in range(3) for j in range(4)])
        df_cols = np.concatenate([_IN_OFF[4 + j] + (2 * g + h) * 128 + ar for h in range(2) for j in range(3)])
        dg = dfv.copy()
        dg[:, 7, 2:4] = rel_bias[15, 2 * g:2 * g + 2]
        sel = np.zeros(16, np.float32)
        sel[g] = 1.0
        per_g.append({
            "whg": np.ascontiguousarray(w_in[:, :, hg_cols]), "wdf": np.ascontiguousarray(w_in[:, :, df_cols]),
            "wml": np.ascontiguousarray(w_in[:, :, 4608:5440]),
            "lbz": np.ascontiguousarray(f("hgrn_lb_logits")[:, g * 384:(g + 1) * 384]),
            "dfv": dg, "dfb": np.ascontiguousarray(rel_bias[bk][..., 2 * g:2 * g + 2].transpose(3, 0, 1, 2)),
            "wuq": np.ascontiguousarray(f("mla_w_uq")[:, :, g * 576:(g + 1) * 576]),
            "wukv": np.ascontiguousarray(f("mla_w_ukv")[:, :, g * 768:(g + 1) * 768]),
            "sel": sel,
        })
    ims = []
    for c in range(8):
        m = dict(shared)
        m.update(per_g[c % 2])
        m["x"] = np.ascontiguousarray(x[c // 2, (c % 2) * NTOK:(c % 2 + 1) * NTOK, :].T)
        ims.append(m)
    nc = build_fused(L)
    res = run_bass_kernel_spmd(nc, ims, core_ids=list(range(8))).results
    out = np.empty((4, SEQ, D), np.float32)
    for c in range(8):
        out[c // 2, (c % 2) * NTOK:(c % 2 + 1) * NTOK, :] = np.asarray(res[c]["xo"]).T
    return out
```

```python
import math
from contextlib import ExitStack

import numpy as np
import ml_dtypes

import concourse.bass as bass
import concourse.mybir as mybir
from concourse.bass_utils import run_bass_kernel_spmd

F32 = mybir.dt.float32
BF16 = mybir.dt.bfloat16
AF = mybir.ActivationFunctionType
ALU = mybir.AluOpType
AX = mybir.AxisListType

D = 2048
DFF = 5504
NF = DFF // 128
KD = D // 128
EPS = 1e-6
NEG = -60.0


class Buf:
    __slots__ = ("name", "w", "r")

    def __init__(self, name=""):
        self.name = name
        self.w = None
        self.r = []


class T:
    __slots__ = ("t", "b")

    def __init__(self, t, name=""):
        self.t = t
        self.b = Buf(name)


class Sched:
    def __init__(self, nc, stack):
        self.nc = nc
        self.stack = stack
        self.engs = {"pe": nc.tensor, "act": nc.scalar, "dve": nc.vector, "pool": nc.gpsimd, "sp": nc.sync}
        self.sem = {}
        self.cnt = {}
        for e in self.engs:
            self.sem[e] = stack.enter_context(nc.semaphore("s_" + e))
            self.cnt[e] = 0
        self.waited = {}
        self.dsems = {}
        self.dcnt = {}
        self.nsem = 0

    def dsem(self, name):
        if name not in self.dsems:
            s = self.stack.enter_context(self.nc.semaphore("d_" + name))
            self.dsems[name] = s
            self.dcnt[name] = 0
        return name

    def _wait(self, e, deps):
        best = {}
        for (k, v) in deps:
            if k == e and e == "pe":
                continue
            if k not in best or best[k] < v:
                best[k] = v
        for k, v in best.items():
            if self.waited.get((e, k), 0) >= v:
                continue
            s = self.sem[k] if k in self.sem else self.dsems[k]
            self.engs[e].wait_ge(s, v)
            self.waited[(e, k)] = v

    def op(self, e, fn, rd=(), wr=(), dsem=None):
        self.nops = getattr(self, "nops", 0) + 1
        if self.nops > getattr(self, "max_ops", 1 << 60):
            return None
        deps = []
        rd = [b.b if isinstance(b, T) else b for b in rd]
        wr = [b.b if isinstance(b, T) else b for b in wr]
        wr = wr + [b for b in rd if b.name.startswith(("bank", "psb"))]
        rd = [b for b in rd if not b.name.startswith(("bank", "psb"))]
        for b in rd:
            b = b.b if isinstance(b, T) else b
            if b.w is not None:
                deps.append(b.w)
        for b in wr:
            b = b.b if isinstance(b, T) else b
            if b.w is not None:
                deps.append(b.w)
            deps.extend(b.r)
        self._wait(e, deps)
        ins = fn(self.engs[e])
        if dsem is not None:
            self.dsem(dsem)
            self.dcnt[dsem] += 16
            ins.then_inc(self.dsems[dsem], 16)
            tok = (dsem, self.dcnt[dsem])
        else:
            self.cnt[e] += 1
            ins.then_inc(self.sem[e], 1)
            tok = (e, self.cnt[e])
        for b in rd:
            b = b.b if isinstance(b, T) else b
            b.r.append(tok)
            if len(b.r) > 64:
                m = {}
                for (k, v) in b.r:
                    if k not in m or m[k] < v:
                        m[k] = v
                b.r = list(m.items())
        for b in wr:
            b = b.b if isinstance(b, T) else b
            b.w = tok
            b.r = []
        return tok

    def cc(self, fn):
        name = self.dsem("ccsem")
        ins = fn(self.engs["pool"])
        self.dcnt[name] += 1
        ins.then_inc(self.dsems[name], 1)

    def barrier(self, engines=None):
        toks = [(e, c) for e, c in self.cnt.items() if c > 0]
        toks += [(n, c) for n, c in self.dcnt.items() if c > 0]
        for e in (engines or self.engs):
            self._wait_all(e, toks)

    def _wait_all(self, e, toks):
        for (k, v) in toks:
            if k == e:
                continue
            if self.waited.get((e, k), 0) >= v:
                continue
            s = self.sem[k] if k in self.sem else self.dsems[k]
            self.engs[e].wait_ge(s, v)
            self.waited[(e, k)] = v


class Ring:
    def __init__(self, items):
        self.items = items
        self.i = 0

    def next(self):
        x = self.items[self.i % len(self.items)]
        self.i += 1
        return x


class Ctx:
    def __init__(self, nc, st):
        self.nc = nc
        self.sfx = ""
        self.S = Sched(nc, st)
        self.psf = [st.enter_context(nc.psum_tensor("psf%d" % i, [128, 512], F32)) for i in range(7)]
        self.psb = st.enter_context(nc.psum_tensor("psb", [128, 1024], BF16))
        self.bankT = [T(self.psf[i], "bank%d" % i) for i in range(7)]
        self.psbT = [T(None, "psb") for i in range(8)]
        for x in self.psbT:
            x.b = self.psbT[0].b

    def banks(self, ids):
        return [self.bankT[i] for i in ids]


def _groups(n, g):
    out = []
    i = 0
    while i < n:
        out.append((i, min(g, n - i)))
        i += g
    return out


def emit_rstd(S, out_ap, out_T, in_ap, in_T, scale, tmp_ap, tmp_T):
    S.op("act", lambda e: e.activation(out=tmp_ap, in_=in_ap, func=AF.Ln, scale=scale, bias=EPS), rd=[in_T], wr=[tmp_T])
    S.op("act", lambda e: e.activation(out=out_ap, in_=tmp_ap, func=AF.Exp, scale=-0.5), rd=[tmp_T], wr=[out_T])


def norm_tile(C, st_bufs, x_d, tok0, TT, banks):
    S = C.S
    NS = TT // 512
    xin, sq, hT, rstd, lnt, gcol, ones = (st_bufs[k] for k in ("xin", "sq", "hT", "rstd", "lnt", "gcol", "ones"))
    bk = [banks.next() for _ in range(NS)]
    for k in range(KD):
        xi = xin.next()
        si = sq.next()
        S.op("sp", lambda e: e.dma_start(out=xi.t[:, :], in_=x_d[k * 128:(k + 1) * 128, tok0:tok0 + TT]), wr=[xi], dsem=xi.b.name)
        S.op("act", lambda e: e.activation(out=si.t[:, :], in_=xi.t[:, :], func=AF.Square), rd=[xi], wr=[si])
        for s in range(NS):
            S.op("pe", lambda e: e.matmul(bk[s].t[:, :], ones.t[:, :], si.t[:, s * 512:(s + 1) * 512], start=(k == 0), stop=(k == KD - 1)),
                 rd=[ones, si], wr=[bk[s]])
    for s in range(NS):
        emit_rstd(S, rstd.t[:, s * 512:(s + 1) * 512], rstd, bk[s].t[:, :], bk[s], 1.0 / D, rstd.t[:, s * 512:(s + 1) * 512], rstd)
    for k in range(KD):
        xi = xin.next()
        S.op("sp", lambda e: e.dma_start(out=xi.t[:, :], in_=x_d[k * 128:(k + 1) * 128, tok0:tok0 + TT]), wr=[xi], dsem=xi.b.name)
        S.op("dve", lambda e: e.scalar_tensor_tensor(out=hT.t[:, k, :], in0=xi.t[:, :], scalar=gcol.t[:, k:k + 1], in1=rstd.t[:, :],
                                                     op0=ALU.mult, op1=ALU.mult), rd=[xi, gcol, rstd], wr=[hT])


def norm_bufs(C, st, g_d, TT, pfx):
    nc, S = C.nc, C.S
    sb = lambda n, sh, dt: T(st.enter_context(nc.sbuf_tensor(pfx + n + C.sfx, sh, dt)), pfx + n)
    B = {}
    B["xin"] = Ring([sb("xin%d" % i, [128, TT], F32) for i in range(2)])
    B["sq"] = Ring([sb("sq%d" % i, [128, TT], BF16) for i in range(2)])
    B["hT"] = sb("hT", [128, KD, TT], BF16)
    B["rstd"] = sb("rstd", [128, TT], F32)
    B["lnt"] = None
    B["gcol"] = sb("gcol", [128, KD], F32)
    B["ones"] = sb("ones", [128, 128], BF16)
    S.op("sp", lambda e: e.dma_start(out=B["gcol"].t[:, :], in_=g_d.rearrange("(k p) -> p k", p=128), allow_slow_non_contiguous=True),
         wr=[B["gcol"]], dsem=pfx + "gcol")
    S.op("dve", lambda e: e.memset(B["ones"].t[:, :], 1.0), wr=[B["ones"]])
    return B


def phase_ffn(C, x_d, xo_d, g_d, wg_d, wu_d, wd_d, NTOK, TT, pfx):
    nc, S = C.nc, C.S
    NS = TT // 512
    S.barrier()
    with ExitStack() as st:
        sb = lambda n, sh, dt: T(st.enter_context(nc.sbuf_tensor(pfx + n + C.sfx, sh, dt)), pfx + n)
        NB = norm_bufs(C, st, g_d, TT, pfx)
        hT = NB["hT"]
        GW = 256
        wg = Ring([sb("wg%d" % i, [128, KD, GW], BF16) for i in range(2)])
        wu = Ring([sb("wu%d" % i, [128, KD, GW], BF16) for i in range(2)])
        actT = [T(None, pfx + "act%d" % f) for f in range(NF)]
        actT_t = st.enter_context(nc.sbuf_tensor(pfx + "actT" + C.sfx, [128, NF, TT], BF16))
        sg = Ring([sb("sg%d" % i, [128, 512], BF16) for i in range(2)])
        DGC = 4 // NS
        wd = Ring([sb("wd%d" % i, [128, 4, DGC * 128], BF16) for i in range(2)])
        xres = Ring([sb("xres%d" % i, [128, 512], F32) for i in range(2)])
        yo = Ring([sb("yo%d" % i, [128, 512], F32) for i in range(2)])
        banks = Ring(C.banks([0, 1, 2, 3, 4, 5]))
        dbanks = Ring(C.banks([0, 1, 2, 3, 4, 5, 6]))
        for tt in range(NTOK // TT):
            tok0 = tt * TT
            norm_tile(C, NB, x_d, tok0, TT, banks)
            for (f0, nf) in _groups(NF, GW // 128):
                g_s = wg.next()
                u_s = wu.next()
                S.op("pool", lambda e: e.dma_start(out=g_s.t[:, :, 0:nf * 128],
                                                   in_=wg_d[:, f0 * 128:(f0 + nf) * 128].rearrange("(k p) f -> p k f", p=128)),
                     wr=[g_s], dsem=g_s.b.name)
                S.op("pool", lambda e: e.dma_start(out=u_s.t[:, :, 0:nf * 128],
                                                   in_=wu_d[:, f0 * 128:(f0 + nf) * 128].rearrange("(k p) f -> p k f", p=128)),
                     wr=[u_s], dsem=u_s.b.name)
                for fi in range(nf):
                    f = f0 + fi
                    for s in range(NS):
                        bg = banks.next()
                        bu = banks.next()
                        for k in range(KD):
                            S.op("pe", lambda e: e.matmul(bg.t[:, :], g_s.t[:, k, fi * 128:(fi + 1) * 128], hT.t[:, k, s * 512:(s + 1) * 512],
                                                          start=(k == 0), stop=(k == KD - 1)), rd=[g_s, hT], wr=[bg])
                        for k in range(KD):
                            S.op("pe", lambda e: e.matmul(bu.t[:, :], u_s.t[:, k, fi * 128:(fi + 1) * 128], hT.t[:, k, s * 512:(s + 1) * 512],
                                                          start=(k == 0), stop=(k == KD - 1)), rd=[u_s, hT], wr=[bu])
                        sgi = sg.next()
                        S.op("act", lambda e: e.activation(out=sgi.t[:, :], in_=bg.t[:, :], func=AF.Silu), rd=[bg], wr=[sgi])
                        S.op("dve", lambda e: e.tensor_tensor(out=actT_t[:, f, s * 512:(s + 1) * 512], in0=bu.t[:, :], in1=sgi.t[:, :], op=ALU.mult),
                             rd=[bu, sgi], wr=[actT[f]])
            for dg in range(KD // DGC):
                db = [[dbanks.next() for s in range(NS)] for dd in range(DGC)]
                for (f0, nf) in _groups(NF, 4):
                    w_s = wd.next()
                    S.op("pool", lambda e: e.dma_start(out=w_s.t[:, 0:nf, :],
                                                       in_=wd_d[f0 * 128:(f0 + nf) * 128, dg * DGC * 128:(dg + 1) * DGC * 128].rearrange("(j p) c -> p j c", p=128)),
                         wr=[w_s], dsem=w_s.b.name)
                    for fi in range(nf):
                        f = f0 + fi
                        for dd in range(DGC):
                            for s in range(NS):
                                S.op("pe", lambda e: e.matmul(db[dd][s].t[:, :], w_s.t[:, fi, dd * 128:(dd + 1) * 128],
                                                              actT_t[:, f, s * 512:(s + 1) * 512], start=(f == 0), stop=(f == NF - 1)),
                                     rd=[w_s, actT[f]], wr=[db[dd][s]])
                for dd in range(DGC):
                    d = dg * DGC + dd
                    for s in range(NS):
                        xr = xres.next()
                        y = yo.next()
                        c0 = tok0 + s * 512
                        S.op("sp", lambda e: e.dma_start(out=xr.t[:, :], in_=x_d[d * 128:(d + 1) * 128, c0:c0 + 512]), wr=[xr], dsem=xr.b.name)
                        S.op("dve", lambda e: e.scalar_tensor_tensor(out=y.t[:, :], in0=db[dd][s].t[:, :], scalar=0.5, in1=xr.t[:, :],
                                                                     op0=ALU.mult, op1=ALU.add), rd=[db[dd][s], xr], wr=[y])
                        S.op("sp", lambda e: e.dma_start(out=xo_d[d * 128:(d + 1) * 128, c0:c0 + 512], in_=y.t[:, :]), rd=[y], dsem=y.b.name)
        S.barrier()


def phase_norm(C, x_d, g_d, xn_d, NTOK, TT, pfx):
    nc, S = C.nc, C.S
    S.barrier()
    with ExitStack() as st:
        NB = norm_bufs(C, st, g_d, TT, pfx)
        banks = Ring(C.banks([0, 1, 2, 3]))
        for tt in range(NTOK // TT):
            norm_tile(C, NB, x_d, tt * TT, TT, banks)
            for j in range(TT // 128):
                S.op("sp", lambda e: e.dma_start(out=xn_d[tt * (TT // 128) + j, :, :, :], in_=NB["hT"].t[:, :, j * 128:(j + 1) * 128]),
                     rd=[NB["hT"]], dsem=pfx + "xnout")
        S.barrier()


def phase_wout(C, x_d, xo_d, o_d, wo_d, NTOK, pfx, sel_d=None):
    nc, S = C.nc, C.S
    S.barrier()
    with ExitStack() as st:
        sb = lambda n, sh, dt: T(st.enter_context(nc.sbuf_tensor(pfx + n + C.sfx, sh, dt)), pfx + n)
        wo = sb("wo", [128, KD, D], BF16)
        for k in range(KD):
            S.op("pool", lambda e: e.dma_start(out=wo.t[:, k, :], in_=wo_d[k * 128:(k + 1) * 128, :]), wr=[wo], dsem=pfx + "wo")
        ot = Ring([sb("ot%d" % i, [128, KD, 512], BF16) for i in range(2)])
        xres = Ring([sb("xres%d" % i, [128, 512], F32) for i in range(2)])
        yo = Ring([sb("yo%d" % i, [128, 512], F32) for i in range(2)])
        if sel_d is not None:
            selt = load_bcast(C, st, pfx + "sel", sel_d, 16)
            oa, ob = sb("oa", [128, KD, 512], BF16), sb("ob", [128, KD, 512], BF16)
            otmp = sb("otmp", [128, KD, 512], F32)
        banks = Ring(C.banks([0, 1, 2, 3]))
        for s in range(NTOK // 512):
            c0 = s * 512
            o_s = ot.next()
            if sel_d is None:
                S.op("sp", lambda e: e.dma_start(out=o_s.t[:, :, :], in_=o_d[:, c0:c0 + 512].rearrange("(k p) t -> p k t", p=128)),
                     wr=[o_s], dsem=o_s.b.name)
            else:
                S.op("sp", lambda e: e.dma_start(out=oa.t[:, :, :], in_=o_d[:, c0:c0 + 512].rearrange("(k p) t -> p k t", p=128)),
                     wr=[oa], dsem=oa.b.name)
                S.op("sp", lambda e: e.dma_start(out=ob.t[:, :, :], in_=o_d[:, NTOK + c0:NTOK + c0 + 512].rearrange("(k p) t -> p k t", p=128)),
                     wr=[ob], dsem=ob.b.name)
                S.op("dve", lambda e: e.tensor_scalar(out=otmp.t[:, :, :], in0=oa.t[:, :, :], scalar1=selt.t[:, 0:1], scalar2=None, op0=ALU.mult),
                     rd=[oa, selt], wr=[otmp])
                S.op("dve", lambda e: e.scalar_tensor_tensor(out=o_s.t[:, :, :], in0=ob.t[:, :, :], scalar=selt.t[:, 1:2], in1=otmp.t[:, :, :],
                                                             op0=ALU.mult, op1=ALU.add), rd=[ob, selt, otmp], wr=[o_s])
            for d in range(KD):
                bk = banks.next()
                for k in range(KD):
                    S.op("pe", lambda e: e.matmul(bk.t[:, :], wo.t[:, k, d * 128:(d + 1) * 128], o_s.t[:, k, :], start=(k == 0), stop=(k == KD - 1)),
                         rd=[wo, o_s], wr=[bk])
                xr = xres.next()
                y = yo.next()
                S.op("sp", lambda e: e.dma_start(out=xr.t[:, :], in_=x_d[d * 128:(d + 1) * 128, c0:c0 + 512]), wr=[xr], dsem=xr.b.name)
                S.op("dve", lambda e: e.tensor_tensor(out=y.t[:, :], in0=bk.t[:, :], in1=xr.t[:, :], op=ALU.add), rd=[bk, xr], wr=[y])
                S.op("sp", lambda e: e.dma_start(out=xo_d[d * 128:(d + 1) * 128, c0:c0 + 512], in_=y.t[:, :]), rd=[y], dsem=y.b.name)
        S.barrier()


def load_bcast(C, st, name, vec_ap, n):
    t = T(st.enter_context(C.nc.sbuf_tensor(name + C.sfx, [128, n], F32)), name)
    C.S.op("sp", lambda e: e.dma_start(out=t.t[:, :], in_=vec_ap.partition_broadcast(128)), wr=[t], dsem=name)
    return t


def load_const(C, st, name, ap, shape, dt):
    t = T(st.enter_context(C.nc.sbuf_tensor(name + C.sfx, shape, dt)), name)
    eng = "sp" if dt == F32 else "pool"
    C.S.op(eng, lambda e: e.dma_start(out=t.t[:, :], in_=ap), wr=[t], dsem=name)
    return t


def psb_region(C, i):
    return C.psb[:, i * 128:(i + 1) * 128], C.psbT[i]


def phase_hg(C, xn_d, w_d, lbz_d, cm_d, og_d, o_d, consts, NT, pfx, NH=3):
    nc, S = C.nc, C.S
    S.barrier()
    with ExitStack() as st:
        sb = lambda n, sh, dt: T(st.enter_context(nc.sbuf_tensor(pfx + n + C.sfx, sh, dt)), pfx + n)
        w = sb("w", [128, KD, NH * 512], BF16)
        for h in range(NH):
            S.op("pool", lambda e: e.dma_start(out=w.t[:, :, h * 512:(h + 1) * 512],
                                               in_=w_d[:, h * 512:(h + 1) * 512].rearrange("(k p) c -> p k c", p=128)), wr=[w], dsem=pfx + "w")
        cf = load_const(C, st, pfx + "cf", consts, [128, 512], F32)
        cb = sb("cb", [128, 512], BF16)
        S.op("dve", lambda e: e.tensor_copy(out=cb.t[:, :], in_=cf.t[:, :]), rd=[cf], wr=[cb])
        ident = T(cb.t[:, 0:128]); ident.b = cb.b
        u2 = T(cf.t[:, 128:256]); u2.b = cf.b
        cmat = T(cb.t[:, 256:384]); cmat.b = cb.b
        sel = T(cb.t[:, 384:390]); sel.b = cb.b
        ogain = load_bcast(C, st, pfx + "ogain", og_d, 128)
        NC_ = NH * 128
        lbz = sb("lbz", [128, 4, NC_], F32)
        for j in range(4):
            S.op("sp", lambda e: e.dma_start(out=lbz.t[:, j, :], in_=lbz_d[j, :].partition_broadcast(128)), wr=[lbz], dsem=pfx + "lbz")
        cm = load_bcast(C, st, pfx + "cm", cm_d, 16)
        lb = sb("lb", [128, NC_], F32)
        oml = sb("oml", [128, NC_], F32)
        den = sb("den", [128, NC_], F32)
        S.op("act", lambda e: e.activation(out=lbz.t[:, :, :], in_=lbz.t[:, :, :], func=AF.Exp), rd=[lbz], wr=[lbz])
        S.op("dve", lambda e: e.tensor_tensor(out=den.t[:, :], in0=lbz.t[:, 0, :], in1=lbz.t[:, 1, :], op=ALU.add), rd=[lbz], wr=[den])
        S.op("dve", lambda e: e.tensor_tensor(out=den.t[:, :], in0=den.t[:, :], in1=lbz.t[:, 2, :], op=ALU.add), rd=[lbz, den], wr=[den])
        S.op("dve", lambda e: e.tensor_tensor(out=den.t[:, :], in0=den.t[:, :], in1=lbz.t[:, 3, :], op=ALU.add), rd=[lbz, den], wr=[den])
        S.op("dve", lambda e: e.reciprocal(out=den.t[:, :], in_=den.t[:, :]), rd=[den], wr=[den])
        S.op("dve", lambda e: e.tensor_scalar(out=lb.t[:, :], in0=lbz.t[:, 0, :], scalar1=cm.t[:, 0:1], scalar2=None, op0=ALU.mult), rd=[lbz, cm], wr=[lb])
        for j in range(1, 4):
            S.op("dve", lambda e: e.scalar_tensor_tensor(out=lb.t[:, :], in0=lbz.t[:, j, :], scalar=cm.t[:, j:j + 1], in1=lb.t[:, :],
                                                         op0=ALU.mult, op1=ALU.add), rd=[lbz, cm, lb], wr=[lb])
        S.op("dve", lambda e: e.tensor_tensor(out=lb.t[:, :], in0=lb.t[:, :], in1=den.t[:, :], op=ALU.mult), rd=[lb, den], wr=[lb])
        S.op("dve", lambda e: e.tensor_scalar(out=oml.t[:, :], in0=lb.t[:, :], scalar1=-1.0, scalar2=1.0, op0=ALU.mult, op1=ALU.add), rd=[lb], wr=[oml])

        St = [sb("S%d" % h, [128, 128], F32) for h in range(NH)]
        for h in range(NH):
            S.op("dve", lambda e: e.memset(St[h].t[:, :], 0.0), wr=[St[h]])
        xt = Ring([sb("xt%d" % i, [128, KD, 128], BF16) for i in range(2)])
        R = lambda n, sh, dt, k=2: Ring([sb("%s%d" % (n, i), sh, dt) for i in range(k)])
        ef, eg, ff, lf, kk, E, Ei = (R(n, [128, 128], F32) for n in ("ef", "eg", "ff", "lf", "kk", "E", "Ei"))
        esc = R("esc", [128, 6], F32)
        qh, kh, vv, kT, sm, Sp0, Sp1, yb = (R(n, [128, 128], BF16) for n in ("qh", "kh", "vv", "kT", "sm", "Sp0", "Sp1", "yb"))
        A = R("A", [128, 128], F32)
        lfh, lfl = R("lfh", [128, 128], BF16), R("lfl", [128, 128], BF16)
        ss, lnv, rs = (R(n, [128, 1], F32) for n in ("ss", "lnv", "rs"))
        junk = R("junk", [128, 128], F32)
        qA = [sb("qA%d" % i, [128, 128], BF16) for i in range(2)]
        qB = [sb("qB%d" % i, [128, 128], BF16) for i in range(2)]
        for i in range(2):
            S.op("dve", lambda e: e.memset(qA[i].t[:, :], 0.0), wr=[qA[i]])
            S.op("dve", lambda e: e.memset(qB[i].t[:, :], 0.0), wr=[qB[i]])
        qAr, qBr = Ring(qA), Ring(qB)
        kA = [sb("kA%d" % i, [128, 128], BF16) for i in range(2)]
        kB = [sb("kB%d" % i, [128, 128], BF16) for i in range(2)]
        for i in range(2):
            S.op("dve", lambda e: e.memset(kA[i].t[:, :], 0.0), wr=[kA[i]])
            S.op("dve", lambda e: e.memset(kB[i].t[:, :], 0.0), wr=[kB[i]])
        kAr, kBr = Ring(kA), Ring(kB)
        ost = R("ost", [128, NH, 128], BF16)
        pjb = Ring(C.banks([0, 1, 2]))
        wkb = Ring(C.banks([3, 4, 5, 6]))
        pbi = [0]

        def psb_next():
            i = pbi[0] % 8
            pbi[0] += 1
            return psb_region(C, i)

        for t in range(NT):
            x_s = xt.next()
            S.op("sp", lambda e: e.dma_start(out=x_s.t[:, :, :], in_=xn_d[t, :, :, :]), wr=[x_s], dsem=x_s.b.name)
            o_s = ost.next()
            for h in range(NH):
                pj = pjb.next()
                for k in range(KD):
                    S.op("pe", lambda e: e.matmul(pj.t[:, :], x_s.t[:, k, :], w.t[:, k, h * 512:(h + 1) * 512], start=(k == 0), stop=(k == KD - 1)),
                         rd=[x_s, w], wr=[pj])
                q_ap, fl_ap, vi_ap, gt_ap = (pj.t[:, i * 128:(i + 1) * 128] for i in range(4))
                hs = slice(h * 128, (h + 1) * 128)
                ef_, eg_, ff_, lf_, kk_, E_, Ei_ = (r.next() for r in (ef, eg, ff, lf, kk, E, Ei))
                S.op("act", lambda e: e.activation(out=ef_.t[:, :], in_=fl_ap, func=AF.Exp, scale=-1.0), rd=[pj], wr=[ef_])
                S.op("act", lambda e: e.activation(out=eg_.t[:, :], in_=gt_ap, func=AF.Exp, scale=-1.0), rd=[pj], wr=[eg_])
                S.op("dve", lambda e: e.tensor_scalar(out=ef_.t[:, :], in0=ef_.t[:, :], scalar1=1.0, scalar2=None, op0=ALU.add), rd=[ef_], wr=[ef_])
                S.op("dve", lambda e: e.reciprocal(out=ef_.t[:, :], in_=ef_.t[:, :]), rd=[ef_], wr=[ef_])
                S.op("dve", lambda e: e.tensor_tensor(out=ff_.t[:, :], in0=ef_.t[:, :], in1=oml.t[:, hs], op=ALU.mult), rd=[ef_, oml], wr=[ff_])
                S.op("dve", lambda e: e.tensor_tensor(out=ff_.t[:, :], in0=ff_.t[:, :], in1=lb.t[:, hs], op=ALU.add), rd=[ff_, lb], wr=[ff_])
                S.op("act", lambda e: e.activation(out=lf_.t[:, :], in_=ff_.t[:, :], func=AF.Ln), rd=[ff_], wr=[lf_])
                S.op("dve", lambda e: e.tensor_scalar(out=kk_.t[:, :], in0=ff_.t[:, :], scalar1=-1.0, scalar2=1.0, op0=ALU.mult, op1=ALU.add), rd=[ff_], wr=[kk_])
                wk = wkb.next()
                lh_, ll_ = lfh.next(), lfl.next()
                S.op("dve", lambda e: e.tensor_copy(out=lh_.t[:, :], in_=lf_.t[:, :]), rd=[lf_], wr=[lh_])
                S.op("dve", lambda e: e.tensor_tensor(out=ll_.t[:, :], in0=lf_.t[:, :], in1=lh_.t[:, :], op=ALU.subtract), rd=[lf_, lh_], wr=[ll_])
                S.op("pe", lambda e: e.matmul(wk.t[:, 0:128], cmat.t[:, :], lh_.t[:, :], start=True, stop=False), rd=[cmat, lh_], wr=[wk])
                S.op("pe", lambda e: e.matmul(wk.t[:, 0:128], cmat.t[:, :], ll_.t[:, :], start=False, stop=True), rd=[cmat, ll_], wr=[wk])
                S.op("pe", lambda e: e.matmul(wk.t[:, 128:134], lh_.t[:, :], sel.t[:, :], start=True, stop=False), rd=[sel, lh_], wr=[wk])
                S.op("pe", lambda e: e.matmul(wk.t[:, 128:134], ll_.t[:, :], sel.t[:, :], start=False, stop=True), rd=[sel, ll_], wr=[wk])
                esc_ = esc.next()
                S.op("act", lambda e: e.activation(out=E_.t[:, :], in_=wk.t[:, 0:128], func=AF.Exp), rd=[wk], wr=[E_])
                S.op("act", lambda e: e.activation(out=Ei_.t[:, :], in_=wk.t[:, 0:128], func=AF.Exp, scale=-1.0), rd=[wk], wr=[Ei_])
                S.op("act", lambda e: e.activation(out=esc_.t[:, :], in_=wk.t[:, 128:134], func=AF.Exp), rd=[wk], wr=[esc_])
                qh_, kh_, vv_, kT_, sm_, Sp0_, Sp1_, yb_ = (r.next() for r in (qh, kh, vv, kT, sm, Sp0, Sp1, yb))
                S.op("dve", lambda e: e.scalar_tensor_tensor(out=qh_.t[:, :], in0=q_ap, scalar=128.0 ** -0.5, in1=E_.t[:, :], op0=ALU.mult, op1=ALU.mult),
                     rd=[pj, E_], wr=[qh_])
                S.op("dve", lambda e: e.tensor_tensor(out=kh_.t[:, :], in0=kk_.t[:, :], in1=Ei_.t[:, :], op=ALU.mult), rd=[kk_, Ei_], wr=[kh_])
                S.op("dve", lambda e: e.tensor_copy(out=vv_.t[:, :], in_=vi_ap), rd=[pj], wr=[vv_])
                tq_ap, tq = psb_next()
                tk_ap, tk = psb_next()
                S.op("pe", lambda e: e.transpose(tq_ap, qh_.t[:, :], ident.t[:, :]), rd=[qh_, ident], wr=[tq])
                S.op("pe", lambda e: e.transpose(tk_ap, kh_.t[:, :], ident.t[:, :]), rd=[kh_, ident], wr=[tk])
                qA_, qB_ = qAr.next(), qBr.next()
                S.op("act", lambda e: e.copy(out=qA_.t[:, 0:64], in_=tq_ap[:, 0:64]), rd=[tq], wr=[qA_])
                S.op("dve", lambda e: e.tensor_copy(out=qB_.t[:, 64:128], in_=tq_ap[:, 64:128]), rd=[tq], wr=[qB_])
                S.op("act", lambda e: e.copy(out=kT_.t[:, :], in_=tk_ap), rd=[tk], wr=[kT_])
                wk2 = wkb.next()
                S.op("pe", lambda e: e.matmul(wk2.t[:, 0:128], kT_.t[:, :], qA_.t[:, :], start=True, stop=False), rd=[kT_, qA_], wr=[wk2])
                S.op("pe", lambda e: e.matmul(wk2.t[:, 0:128], kT_.t[:, :], qB_.t[:, :], start=False, stop=True), rd=[kT_, qB_], wr=[wk2])
                S.op("dve", lambda e: e.tensor_tensor(out=sm_.t[:, :], in0=wk2.t[:, 0:128], in1=u2.t[:, :], op=ALU.mult), rd=[wk2, u2], wr=[sm_])
                kA_, kB_ = kAr.next(), kBr.next()
                S.op("act", lambda e: e.copy(out=kA_.t[0:64, :], in_=kh_.t[0:64, :]), rd=[kh_], wr=[kA_])
                S.op("act", lambda e: e.copy(out=kB_.t[64:128, :], in_=kh_.t[64:128, :]), rd=[kh_], wr=[kB_])
                S.op("pe", lambda e: e.matmul(wk2.t[:, 128:256], kA_.t[:, :], vv_.t[:, :], start=True, stop=True), rd=[kA_, vv_], wr=[wk2])
                S.op("pe", lambda e: e.matmul(wk2.t[:, 256:384], kB_.t[:, :], vv_.t[:, :], start=True, stop=True), rd=[kB_, vv_], wr=[wk2])
                Sh = St[h]
                for c, Sp_ in ((0, Sp0_), (1, Sp1_)):
                    A_ = A.next()
                    S.op("dve", lambda e: e.tensor_scalar(out=Sp_.t[:, :], in0=Sh.t[:, :], scalar1=esc_.t[:, 3 * c:3 * c + 1], scalar2=None, op0=ALU.mult),
                         rd=[Sh, esc_], wr=[Sp_])
                    S.op("dve", lambda e: e.tensor_scalar(out=A_.t[:, :], in0=Sh.t[:, :], scalar1=esc_.t[:, 3 * c + 2:3 * c + 3], scalar2=None, op0=ALU.mult),
                         rd=[Sh, esc_], wr=[A_])
                    S.op("dve", lambda e: e.scalar_tensor_tensor(out=Sh.t[:, :], in0=wk2.t[:, 128 * (c + 1):128 * (c + 2)],
                                                                 scalar=esc_.t[:, 3 * c + 1:3 * c + 2], in1=A_.t[:, :], op0=ALU.mult, op1=ALU.add),
                         rd=[wk2, esc_, A_], wr=[Sh])
                S.op("pe", lambda e: e.matmul(wk2.t[:, 384:512], sm_.t[:, :], vv_.t[:, :], start=True, stop=False), rd=[sm_, vv_], wr=[wk2])
                S.op("pe", lambda e: e.matmul(wk2.t[:, 384:512], qA_.t[:, :], Sp0_.t[:, :], start=False, stop=False), rd=[qA_, Sp0_], wr=[wk2])
                S.op("pe", lambda e: e.matmul(wk2.t[:, 384:512], qB_.t[:, :], Sp1_.t[:, :], start=False, stop=True), rd=[qB_, Sp1_], wr=[wk2])
                o_ap = wk2.t[:, 384:512]
                ss_, lnv_, rs_, junk_ = ss.next(), lnv.next(), rs.next(), junk.next()
                S.op("dve", lambda e: e.memset(ss_.t[:, :], 0.0), wr=[ss_])
                S.op("act", lambda e: e.activation(out=junk_.t[:, :], in_=o_ap, func=AF.Square, accum_out=ss_.t[:, 0:1]), rd=[wk2, ss_], wr=[junk_, ss_])
                emit_rstd(S, rs_.t[:, :], rs_, ss_.t[:, :], ss_, 1.0 / 128, lnv_.t[:, :], lnv_)
                S.op("dve", lambda e: e.tensor_scalar(out=eg_.t[:, :], in0=eg_.t[:, :], scalar1=1.0, scalar2=None, op0=ALU.add), rd=[eg_], wr=[eg_])
                S.op("dve", lambda e: e.reciprocal(out=eg_.t[:, :], in_=eg_.t[:, :]), rd=[eg_], wr=[eg_])
                S.op("dve", lambda e: e.tensor_tensor(out=eg_.t[:, :], in0=eg_.t[:, :], in1=gt_ap, op=ALU.mult), rd=[eg_, pj], wr=[eg_])
                S.op("dve", lambda e: e.tensor_tensor(out=eg_.t[:, :], in0=eg_.t[:, :], in1=ogain.t[:, :], op=ALU.mult), rd=[eg_, ogain], wr=[eg_])
                S.op("dve", lambda e: e.scalar_tensor_tensor(out=yb_.t[:, :], in0=o_ap, scalar=rs_.t[:, 0:1], in1=eg_.t[:, :], op0=ALU.mult, op1=ALU.mult),
                     rd=[wk2, rs_, eg_], wr=[yb_])
                ty_ap, ty = psb_next()
                S.op("pe", lambda e: e.transpose(ty_ap, yb_.t[:, :], ident.t[:, :]), rd=[yb_, ident], wr=[ty])
                S.op("act", lambda e: e.copy(out=o_s.t[:, h, :], in_=ty_ap), rd=[ty], wr=[o_s])
            S.op("sp", lambda e: e.dma_start(out=o_d[0:NH * 128, t * 128:(t + 1) * 128].rearrange("(h p) t -> p h t", p=128), in_=o_s.t[:, :, :]),
                 rd=[o_s], dsem=o_s.b.name)
        S.barrier()


def attn_tile(C, t, qk_parts, v_fn, acc, acc_ap, scale, far_bias, near, negshift, P, tmpf, qkb):
    S = C.S
    np_ = len(qk_parts)
    groups = [list(range(g0, min(g0 + 4, t + 1))) for g0 in range(0, t + 1, 4)]

    def emit_qk(grp):
        bank = qkb.next()
        for j, kt in enumerate(grp):
            for pi, (K_fn, q_ap, q_T) in enumerate(qk_parts):
                k_ap, k_T = K_fn(kt)
                S.op("pe", lambda e: e.matmul(bank.t[:, j * 128:(j + 1) * 128], k_ap, q_ap, start=(pi == 0), stop=(pi == np_ - 1)),
                     rd=[k_T, q_T], wr=[bank])
        return bank

    def emit_exp(grp, bank):
        P_ = P.next()
        nfar = len([kt for kt in grp if (kt - t) not in near])
        if nfar:
            S.op("act", lambda e: e.activation(out=P_.t[:, 0:nfar * 128], in_=bank.t[:, 0:nfar * 128], func=AF.Exp, scale=scale, bias=far_bias.t[:, :]),
                 rd=[bank, far_bias], wr=[P_])
        for j, kt in enumerate(grp):
            if (kt - t) in near:
                b = near[kt - t]
                tm = tmpf.next()
                S.op("dve", lambda e: e.scalar_tensor_tensor(out=tm.t[:, :], in0=bank.t[:, j * 128:(j + 1) * 128], scalar=scale, in1=b.t[:, :],
                                                             op0=ALU.mult, op1=ALU.add), rd=[bank, b], wr=[tm])
                S.op("act", lambda e: e.activation(out=P_.t[:, j * 128:(j + 1) * 128], in_=tm.t[:, :], func=AF.Exp, bias=negshift.t[:, :]),
                     rd=[tm, negshift], wr=[P_])
        return P_

    def emit_pv(grp, P_):
        for j, kt in enumerate(grp):
            v_ap, v_T = v_fn(kt)
            S.op("pe", lambda e: e.matmul(acc_ap, P_.t[:, j * 128:(j + 1) * 128], v_ap, start=(kt == 0), stop=(kt == t)), rd=[P_, v_T], wr=[acc])

    bank = emit_qk(groups[0])
    for gi, grp in enumerate(groups):
        nxt = emit_qk(groups[gi + 1]) if gi + 1 < len(groups) else None
        P_ = emit_exp(grp, bank)
        emit_pv(grp, P_)
        bank = nxt


def sub_T(parent, ap):
    x = T(ap)
    x.b = parent.b
    return x


VW = 132


def phase_df(C, xn_d, w_d, vecs_d, bias_d, o_d, consts, mask_d, NT, pfx, NH=2, row0=384):
    nc, S = C.nc, C.S
    SHIFT = 8.0
    S.barrier()
    with ExitStack() as st:
        sb = lambda n, sh, dt: T(st.enter_context(nc.sbuf_tensor(pfx + n + C.sfx, sh, dt)), pfx + n)
        R = lambda n, sh, dt, k=2: Ring([sb("%s%d" % (n, i), sh, dt) for i in range(k)])
        w = sb("w", [128, KD, NH * 384], BF16)
        for h in range(NH):
            S.op("pool", lambda e: e.dma_start(out=w.t[:, :, h * 384:(h + 1) * 384],
                                               in_=w_d[:, h * 384:(h + 1) * 384].rearrange("(k p) c -> p k c", p=128)), wr=[w], dsem=pfx + "w")
        cf = load_const(C, st, pfx + "cf", consts, [128, 512], F32)
        cb = sb("cb", [128, 512], BF16)
        S.op("dve", lambda e: e.tensor_copy(out=cb.t[:, :], in_=cf.t[:, :]), rd=[cf], wr=[cb])
        ident = sub_T(cb, cb.t[:, 0:128])
        mask = load_const(C, st, pfx + "mask", mask_d, [128, 128], F32)
        vecs = sb("vecs", [128, 8, 128], F32)
        for j in range(8):
            S.op("sp", lambda e: e.dma_start(out=vecs.t[:, j, :], in_=vecs_d[j, :].partition_broadcast(128)), wr=[vecs], dsem=pfx + "vecs")
        qg, kg = vecs.t[:, 0, 0:64], vecs.t[:, 1, 0:64]
        junk64 = sb("junk64", [128, 64], F32)
        prod = sb("prod", [128, 64], F32)
        lsum = sb("lsum", [128, 2], F32)
        S.op("dve", lambda e: e.memset(lsum.t[:, :], 0.0), wr=[lsum])
        for i in range(2):
            S.op("dve", lambda e: e.tensor_tensor(out=prod.t[:, :], in0=vecs.t[:, 2 + 2 * i, 0:64], in1=vecs.t[:, 3 + 2 * i, 0:64], op=ALU.mult), rd=[vecs], wr=[prod])
            S.op("act", lambda e: e.activation(out=junk64.t[:, :], in_=prod.t[:, :], func=AF.Identity, accum_out=lsum.t[:, i:i + 1]), rd=[prod, lsum], wr=[junk64, lsum])
        S.op("act", lambda e: e.activation(out=lsum.t[:, :], in_=lsum.t[:, :], func=AF.Exp), rd=[lsum], wr=[lsum])
        lam = sb("lam", [128, 1], F32)
        S.op("dve", lambda e: e.tensor_tensor(out=lam.t[:, :], in0=lsum.t[:, 0:1], in1=lsum.t[:, 1:2], op=ALU.subtract), rd=[lsum], wr=[lam])
        S.op("dve", lambda e: e.tensor_tensor(out=lam.t[:, :], in0=lam.t[:, :], in1=vecs.t[:, 7, 0:1], op=ALU.add), rd=[lam, vecs], wr=[lam])
        sgain = sb("sgain", [128, 128], F32)
        S.op("dve", lambda e: e.tensor_scalar(out=sgain.t[:, :], in0=vecs.t[:, 6, :], scalar1=vecs.t[:, 7, 1:2], scalar2=None, op0=ALU.mult), rd=[vecs], wr=[sgain])
        negshift = sb("negshift", [128, 1], F32)
        S.op("dve", lambda e: e.memset(negshift.t[:, :], -SHIFT), wr=[negshift])
        farb = [sb("farb%d" % h, [128, 1], F32) for h in range(NH)]
        for h in range(NH):
            S.op("dve", lambda e: e.tensor_scalar(out=farb[h].t[:, :], in0=vecs.t[:, 7, 2 + h:3 + h], scalar1=-SHIFT, scalar2=None, op0=ALU.add), rd=[vecs], wr=[farb[h]])
        bt = [[sb("bt%d_%d" % (h, r), [128, 128], F32) for r in range(2)] for h in range(NH)]
        for h in range(NH):
            for r in range(2):
                S.op("sp", lambda e: e.dma_start(out=bt[h][r].t[:, :], in_=bias_d[h, r, :, :]), wr=[bt[h][r]], dsem=pfx + "bt%d_%d" % (h, r))
            S.op("dve", lambda e: e.tensor_tensor(out=bt[h][0].t[:, :], in0=bt[h][0].t[:, :], in1=mask.t[:, :], op=ALU.add), rd=[bt[h][0], mask], wr=[bt[h][0]])
        KT = [[sb("KT%d_%d" % (h, m), [128, NT * 128], BF16) for m in range(2)] for h in range(NH)]
        for h in range(NH):
            for m in range(2):
                S.op("dve", lambda e: e.memset(KT[h][m].t[:, :], 0.0), wr=[KT[h][m]])
        Va = sb("Va", [128, NT * NH * VW], BF16)
        S.op("dve", lambda e: e.memset(Va.t[:, :], 1.0), wr=[Va])
        vofs = lambda t_, h_: (t_ * NH + h_) * VW
        xt = R("xt", [128, KD, 128], BF16)
        ss4, ln4, rs4 = (R(n, [128, 4], F32, 3) for n in ("ss4", "ln4", "rs4"))
        junk = R("junk", [128, 128], F32)
        qk = R("qk", [128, 256], BF16, 3)
        qT = [R("qT%d" % h, [128, 128], BF16) for h in range(NH)]
        P = R("P", [128, 512], BF16, 3)
        tmpf = R("tmpf", [128, 128], F32, 3)
        rr = R("rr", [128, 4], F32, 3)
        t2, of = R("t2", [128, 128], F32), R("of", [128, 128], F32)
        ss1, ln1, rs1 = (R(n, [128, 1], F32) for n in ("ss1", "ln1", "rs1"))
        yb = R("yb", [128, 128], BF16)
        ost = R("ost", [128, NH, 128], BF16)
        pjb = C.banks([0, 1])
        qkb = Ring(C.banks([2, 3, 4]))
        accb = Ring(C.banks([5, 6]))
        pbi = [0]

        def psb_next():
            i = pbi[0] % 8
            pbi[0] += 1
            return psb_region(C, i)

        for t in range(NT):
            x_s = xt.next()
            S.op("sp", lambda e: e.dma_start(out=x_s.t[:, :, :], in_=xn_d[t, :, :, :]), wr=[x_s], dsem=x_s.b.name)
            o_s = ost.next()
            qTs = []
            for h in range(NH):
                pj = pjb[h]
                for k in range(KD):
                    S.op("pe", lambda e: e.matmul(pj.t[:, 0:384], x_s.t[:, k, :], w.t[:, k, h * 384:(h + 1) * 384], start=(k == 0), stop=(k == KD - 1)),
                         rd=[x_s, w], wr=[pj])
                ss_, ln_, rs_ = ss4.next(), ln4.next(), rs4.next()
                S.op("dve", lambda e: e.memset(ss_.t[:, :], 0.0), wr=[ss_])
                for j in range(4):
                    jk = junk.next()
                    S.op("act", lambda e: e.activation(out=jk.t[:, 0:64], in_=pj.t[:, j * 64:(j + 1) * 64], func=AF.Square, accum_out=ss_.t[:, j:j + 1]),
                         rd=[pj, ss_], wr=[jk, ss_])
                emit_rstd(S, rs_.t[:, :], rs_, ss_.t[:, :], ss_, 1.0 / 64, ln_.t[:, :], ln_)
                qk_ = qk.next()
                for j in range(4):
                    g_ap = qg if j < 2 else kg
                    S.op("dve", lambda e: e.scalar_tensor_tensor(out=qk_.t[:, j * 64:(j + 1) * 64], in0=pj.t[:, j * 64:(j + 1) * 64], scalar=rs_.t[:, j:j + 1],
                                                                 in1=g_ap, op0=ALU.mult, op1=ALU.mult), rd=[pj, rs_, vecs], wr=[qk_])
                S.op("dve", lambda e: e.tensor_copy(out=Va.t[:, vofs(t, h):vofs(t, h) + 128], in_=pj.t[:, 256:384]), rd=[pj], wr=[Va])
                tq_ap, tq = psb_next()
                tk_ap, tk = psb_next()
                S.op("pe", lambda e: e.transpose(tq_ap, qk_.t[:, 0:128], ident.t), rd=[qk_, ident], wr=[tq])
                S.op("pe", lambda e: e.transpose(tk_ap, qk_.t[:, 128:256], ident.t), rd=[qk_, ident], wr=[tk])
                qT_ = qT[h].next()
                S.op("act", lambda e: e.copy(out=qT_.t[:, :], in_=tq_ap), rd=[tq], wr=[qT_])
                S.op("act", lambda e: e.copy(out=KT[h][0].t[0:64, t * 128:(t + 1) * 128], in_=tk_ap[0:64, :]), rd=[tk], wr=[KT[h][0]])
                S.op("act", lambda e: e.copy(out=KT[h][1].t[64:128, t * 128:(t + 1) * 128], in_=tk_ap[64:128, :]), rd=[tk], wr=[KT[h][1]])
                qTs.append(qT_)
            for h in range(NH):
                acc = accb.next()
                for m in range(2):
                    attn_tile(C, t, [(lambda kt, h=h, m=m: (KT[h][m].t[:, kt * 128:(kt + 1) * 128], KT[h][m]), qTs[h].t[:, :], qTs[h])],
                              lambda kt, h=h: (Va.t[:, vofs(kt, h):vofs(kt, h) + 129], Va), acc, acc.t[:, m * 256:m * 256 + 129], 0.125, farb[h],
                              {0: bt[h][0], -1: bt[h][1]}, negshift, P, tmpf, qkb)
                r_ = rr.next()
                S.op("dve", lambda e: e.reciprocal(out=r_.t[:, 0:1], in_=acc.t[:, 128:129]), rd=[acc], wr=[r_])
                S.op("dve", lambda e: e.reciprocal(out=r_.t[:, 1:2], in_=acc.t[:, 384:385]), rd=[acc], wr=[r_])
                S.op("dve", lambda e: e.tensor_tensor(out=r_.t[:, 2:3], in0=r_.t[:, 1:2], in1=lam.t[:, :], op=ALU.mult), rd=[r_, lam], wr=[r_])
                t2_, of_ = t2.next(), of.next()
                S.op("dve", lambda e: e.tensor_scalar(out=t2_.t[:, :], in0=acc.t[:, 256:384], scalar1=r_.t[:, 2:3], scalar2=None, op0=ALU.mult), rd=[acc, r_], wr=[t2_])
                S.op("dve", lambda e: e.scalar_tensor_tensor(out=of_.t[:, :], in0=acc.t[:, 0:128], scalar=r_.t[:, 0:1], in1=t2_.t[:, :],
                                                             op0=ALU.mult, op1=ALU.subtract), rd=[acc, r_, t2_], wr=[of_])
                s1, l1, r1, jk = ss1.next(), ln1.next(), rs1.next(), junk.next()
                S.op("dve", lambda e: e.memset(s1.t[:, :], 0.0), wr=[s1])
                S.op("act", lambda e: e.activation(out=jk.t[:, :], in_=of_.t[:, :], func=AF.Square, accum_out=s1.t[:, 0:1]), rd=[of_, s1], wr=[jk, s1])
                emit_rstd(S, r1.t[:, :], r1, s1.t[:, :], s1, 1.0 / 128, l1.t[:, :], l1)
                yb_ = yb.next()
                S.op("dve", lambda e: e.scalar_tensor_tensor(out=yb_.t[:, :], in0=of_.t[:, :], scalar=r1.t[:, 0:1], in1=sgain.t[:, :],
                                                             op0=ALU.mult, op1=ALU.mult), rd=[of_, r1, sgain], wr=[yb_])
                ty_ap, ty = psb_next()
                S.op("pe", lambda e: e.transpose(ty_ap, yb_.t[:, :], ident.t), rd=[yb_, ident], wr=[ty])
                S.op("act", lambda e: e.copy(out=o_s.t[:, h, :], in_=ty_ap), rd=[ty], wr=[o_s])
            S.op("sp", lambda e: e.dma_start(out=o_d[row0:row0 + NH * 128, t * 128:(t + 1) * 128].rearrange("(h p) t -> p h t", p=128), in_=o_s.t[:, :, :]),
                 rd=[o_s], dsem=o_s.b.name)
        S.barrier()


def phase_ml(C, xn_d, w_d, wuq_d, wukv_d, vecs_d, cs_d, o_d, consts, mask_d, NT, pfx, NH=3, row0=640):
    nc, S = C.nc, C.S
    SHIFT = 14.0
    SCALE = 192.0 ** -0.5
    S.barrier()
    with ExitStack() as st:
        sb = lambda n, sh, dt: T(st.enter_context(nc.sbuf_tensor(pfx + n + C.sfx, sh, dt)), pfx + n)
        R = lambda n, sh, dt, k=2: Ring([sb("%s%d" % (n, i), sh, dt) for i in range(k)])
        w = sb("w", [128, KD, 832], BF16)
        for (c0, c1) in ((0, 512), (512, 832)):
            S.op("pool", lambda e: e.dma_start(out=w.t[:, :, c0:c1], in_=w_d[:, c0:c1].rearrange("(k p) c -> p k c", p=128)), wr=[w], dsem=pfx + "w")
        wuq = sb("wuq", [128, 4, NH * 192], BF16)
        S.op("pool", lambda e: e.dma_start(out=wuq.t[:, :, :], in_=wuq_d.rearrange("(k p) c -> p k c", p=128)), wr=[wuq], dsem=pfx + "wuq")
        wukv = sb("wukv", [128, 2, NH * 256], BF16)
        S.op("pool", lambda e: e.dma_start(out=wukv.t[:, :, :], in_=wukv_d.rearrange("(k p) c -> p k c", p=128)), wr=[wukv], dsem=pfx + "wukv")
        cf = load_const(C, st, pfx + "cf", consts, [128, 512], F32)
        cb = sb("cb", [128, 512], BF16)
        S.op("dve", lambda e: e.tensor_copy(out=cb.t[:, :], in_=cf.t[:, :]), rd=[cf], wr=[cb])
        ident = sub_T(cb, cb.t[:, 0:128])
        mask = load_const(C, st, pfx + "mask", mask_d, [128, 128], F32)
        vecs = sb("vecs", [128, 4, 512], F32)
        for j in range(4):
            S.op("sp", lambda e: e.dma_start(out=vecs.t[:, j, :], in_=vecs_d[j, :].partition_broadcast(128)), wr=[vecs], dsem=pfx + "vecs")
        cs = load_const(C, st, pfx + "cs", cs_d, [128, 2 * NT * 32], F32)
        negshift = sb("negshift", [128, 1], F32)
        S.op("dve", lambda e: e.memset(negshift.t[:, :], -SHIFT), wr=[negshift])
        KTa = [sb("KTa%d" % h, [128, NT * 128], BF16) for h in range(NH)]
        KTb = [sb("KTb%d" % h, [128, NT * 128], BF16) for h in range(NH)]
        Va = sb("Va", [128, NT * NH * VW], BF16)
        S.op("dve", lambda e: e.memset(Va.t[:, :], 1.0), wr=[Va])
        vofs = lambda t_, h_: (t_ * NH + h_) * VW
        xt = R("xt", [128, KD, 128], BF16)
        ss2, ln2, rs2 = (R(n, [128, 4], F32) for n in ("ss2", "ln2", "rs2"))
        junk = R("junk", [128, 512], F32)
        cqn = R("cqn", [128, 512], BF16)
        ckvn = R("ckvn", [128, 256], BF16)
        kr = R("kr", [128, 64], F32)
        cqT = R("cqT", [128, 4, 128], BF16)
        ckvT = R("ckvT", [128, 2, 128], BF16)
        ssh, lnh, rsh = (R(n, [128, 4], F32, 3) for n in ("ssh", "lnh", "rsh"))
        qn_b = R("qnb", [128, 128], BF16, 3)
        kn_b = R("knb", [128, 128], BF16, 3)
        qr_f = R("qrf", [128, 64], F32, 3)
        kr_f = R("krf", [128, 64], F32, 3)
        ra, rb = R("ra", [128, 32], F32, 3), R("rb", [128, 32], F32, 3)
        qrp = [sb("qrp%d" % i, [128, 128], BF16) for i in range(2)]
        krp = [sb("krp%d" % i, [128, 128], BF16) for i in range(2)]
        for i in range(2):
            S.op("dve", lambda e: e.memset(qrp[i].t[:, :], 0.0), wr=[qrp[i]])
            S.op("dve", lambda e: e.memset(krp[i].t[:, :], 0.0), wr=[krp[i]])
        qrpr, krpr = Ring(qrp), Ring(krp)
        qTa = [R("qTa%d" % h, [128, 128], BF16) for h in range(NH)]
        qTb = [R("qTb%d" % h, [128, 128], BF16) for h in range(NH)]
        P = R("P", [128, 512], BF16, 3)
        tmpf = R("tmpf", [128, 128], F32, 3)
        rr = R("rr", [128, 1], F32, 3)
        yb = R("yb", [128, 128], BF16)
        ost = R("ost", [128, NH, 128], BF16)
        b0, b1, b2, b3, b4 = C.banks([0, 1, 2, 3, 4])
        qreg = [(b2, 0), (b2, 192), (b3, 0)]
        kvreg = [(b4, 0), (b4, 256), (b3, 192)]
        qkb = Ring(C.banks([0, 1, 2, 3]))
        accb = Ring(C.banks([5, 6]))
        pbi = [0]

        def psb_next():
            i = pbi[0] % 8
            pbi[0] += 1
            return psb_region(C, i)

        def transpose_to(src_ap, src_T, dst_ap, dst_T, eng="act"):
            tp_ap, tp = psb_next()
            S.op("pe", lambda e: e.transpose(tp_ap, src_ap, ident.t), rd=[src_T, ident], wr=[tp])
            if eng == "act":
                S.op("act", lambda e: e.copy(out=dst_ap, in_=tp_ap), rd=[tp], wr=[dst_T])
            else:
                S.op("dve", lambda e: e.tensor_copy(out=dst_ap, in_=tp_ap), rd=[tp], wr=[dst_T])

        def rope(src_T, src, dst_T, dst, t):
            cos = cs.t[:, t * 32:(t + 1) * 32]
            sin = cs.t[:, NT * 32 + t * 32:NT * 32 + (t + 1) * 32]
            x1, x2 = src[:, 0:32], src[:, 32:64]
            a, b = ra.next(), rb.next()
            S.op("dve", lambda e: e.tensor_tensor(out=a.t[:, :], in0=x1, in1=cos, op=ALU.mult), rd=[src_T, cs], wr=[a])
            S.op("dve", lambda e: e.tensor_tensor(out=b.t[:, :], in0=x2, in1=sin, op=ALU.mult), rd=[src_T, cs], wr=[b])
            S.op("dve", lambda e: e.tensor_tensor(out=dst[:, 0:32], in0=a.t[:, :], in1=b.t[:, :], op=ALU.subtract), rd=[a, b], wr=[dst_T])
            a, b = ra.next(), rb.next()
            S.op("dve", lambda e: e.tensor_tensor(out=a.t[:, :], in0=x2, in1=cos, op=ALU.mult), rd=[src_T, cs], wr=[a])
            S.op("dve", lambda e: e.tensor_tensor(out=b.t[:, :], in0=x1, in1=sin, op=ALU.mult), rd=[src_T, cs], wr=[b])
            S.op("dve", lambda e: e.tensor_tensor(out=dst[:, 32:64], in0=a.t[:, :], in1=b.t[:, :], op=ALU.add), rd=[a, b], wr=[dst_T])

        for t in range(NT):
            x_s = xt.next()
            S.op("sp", lambda e: e.dma_start(out=x_s.t[:, :, :], in_=xn_d[t, :, :, :]), wr=[x_s], dsem=x_s.b.name)
            o_s = ost.next()
            for k in range(KD):
                S.op("pe", lambda e: e.matmul(b0.t[:, :], x_s.t[:, k, :], w.t[:, k, 0:512], start=(k == 0), stop=(k == KD - 1)), rd=[x_s, w], wr=[b0])
            for k in range(KD):
                S.op("pe", lambda e: e.matmul(b1.t[:, 0:320], x_s.t[:, k, :], w.t[:, k, 512:832], start=(k == 0), stop=(k == KD - 1)), rd=[x_s, w], wr=[b1])
            ss_, ln_, rs_ = ss2.next(), ln2.next(), rs2.next()
            S.op("dve", lambda e: e.memset(ss_.t[:, :], 0.0), wr=[ss_])
            jk = junk.next()
            S.op("act", lambda e: e.activation(out=jk.t[:, :], in_=b0.t[:, :], func=AF.Square, accum_out=ss_.t[:, 0:1]), rd=[b0, ss_], wr=[jk, ss_])
            jk = junk.next()
            S.op("act", lambda e: e.activation(out=jk.t[:, 0:256], in_=b1.t[:, 0:256], func=AF.Square, accum_out=ss_.t[:, 1:2]), rd=[b1, ss_], wr=[jk, ss_])
            kr_ = kr.next()
            S.op("act", lambda e: e.copy(out=kr_.t[:, :], in_=b1.t[:, 256:320]), rd=[b1], wr=[kr_])
            jk = junk.next()
            S.op("act", lambda e: e.activation(out=jk.t[:, 0:64], in_=kr_.t[:, :], func=AF.Square, accum_out=ss_.t[:, 2:3]), rd=[kr_, ss_], wr=[jk, ss_])
            S.op("dve", lambda e: e.tensor_scalar(out=ss_.t[:, 0:1], in0=ss_.t[:, 0:1], scalar1=0.5, scalar2=None, op0=ALU.mult), rd=[ss_], wr=[ss_])
            emit_rstd(S, rs_.t[:, 0:2], rs_, ss_.t[:, 0:2], ss_, 1.0 / 256, ln_.t[:, 0:2], ln_)
            cqn_, ckvn_ = cqn.next(), ckvn.next()
            S.op("dve", lambda e: e.scalar_tensor_tensor(out=cqn_.t[:, :], in0=b0.t[:, :], scalar=rs_.t[:, 0:1], in1=vecs.t[:, 0, :], op0=ALU.mult, op1=ALU.mult),
                 rd=[b0, rs_, vecs], wr=[cqn_])
            S.op("dve", lambda e: e.scalar_tensor_tensor(out=ckvn_.t[:, :], in0=b1.t[:, 0:256], scalar=rs_.t[:, 1:2], in1=vecs.t[:, 1, 0:256], op0=ALU.mult, op1=ALU.mult),
                 rd=[b1, rs_, vecs], wr=[ckvn_])
            cqT_, ckvT_ = cqT.next(), ckvT.next()
            for r in range(4):
                transpose_to(cqn_.t[:, r * 128:(r + 1) * 128], cqn_, cqT_.t[:, r, :], cqT_, "act" if r % 2 == 0 else "dve")
            for r in range(2):
                transpose_to(ckvn_.t[:, r * 128:(r + 1) * 128], ckvn_, ckvT_.t[:, r, :], ckvT_, "act" if r % 2 == 0 else "dve")
            for h in range(NH):
                qb, qo = qreg[h]
                for r in range(4):
                    S.op("pe", lambda e: e.matmul(qb.t[:, qo:qo + 192], cqT_.t[:, r, :], wuq.t[:, r, h * 192:(h + 1) * 192], start=(r == 0), stop=(r == 3)),
                         rd=[cqT_, wuq], wr=[qb])
                kb, ko = kvreg[h]
                for r in range(2):
                    S.op("pe", lambda e: e.matmul(kb.t[:, ko:ko + 256], ckvT_.t[:, r, :], wukv.t[:, r, h * 256:(h + 1) * 256], start=(r == 0), stop=(r == 1)),
                         rd=[ckvT_, wukv], wr=[kb])
            qTs = []
            for h in range(NH):
                qb, qo = qreg[h]
                kb, ko = kvreg[h]
                sh_, lh_, rh_ = ssh.next(), lnh.next(), rsh.next()
                S.op("dve", lambda e: e.memset(sh_.t[:, :], 0.0), wr=[sh_])
                jk = junk.next()
                S.op("act", lambda e: e.activation(out=jk.t[:, 0:192], in_=qb.t[:, qo:qo + 192], func=AF.Square, accum_out=sh_.t[:, 0:1]), rd=[qb, sh_], wr=[jk, sh_])
                jk = junk.next()
                S.op("act", lambda e: e.activation(out=jk.t[:, 0:128], in_=kb.t[:, ko:ko + 128], func=AF.Square, accum_out=sh_.t[:, 1:2]), rd=[kb, sh_], wr=[jk, sh_])
                S.op("dve", lambda e: e.tensor_tensor(out=sh_.t[:, 1:2], in0=sh_.t[:, 1:2], in1=ss_.t[:, 2:3], op=ALU.add), rd=[sh_, ss_], wr=[sh_])
                emit_rstd(S, rh_.t[:, 0:2], rh_, sh_.t[:, 0:2], sh_, 1.0 / 192, lh_.t[:, 0:2], lh_)
                qn_, kn_, qr_, kf_ = qn_b.next(), kn_b.next(), qr_f.next(), kr_f.next()
                S.op("dve", lambda e: e.scalar_tensor_tensor(out=qn_.t[:, :], in0=qb.t[:, qo:qo + 128], scalar=rh_.t[:, 0:1], in1=vecs.t[:, 2, 0:128], op0=ALU.mult, op1=ALU.mult),
                     rd=[qb, rh_, vecs], wr=[qn_])
                S.op("dve", lambda e: e.scalar_tensor_tensor(out=qr_.t[:, :], in0=qb.t[:, qo + 128:qo + 192], scalar=rh_.t[:, 0:1], in1=vecs.t[:, 2, 128:192], op0=ALU.mult, op1=ALU.mult),
                     rd=[qb, rh_, vecs], wr=[qr_])
                S.op("dve", lambda e: e.scalar_tensor_tensor(out=kn_.t[:, :], in0=kb.t[:, ko:ko + 128], scalar=rh_.t[:, 1:2], in1=vecs.t[:, 3, 0:128], op0=ALU.mult, op1=ALU.mult),
                     rd=[kb, rh_, vecs], wr=[kn_])
                S.op("dve", lambda e: e.scalar_tensor_tensor(out=kf_.t[:, :], in0=kr_.t[:, :], scalar=rh_.t[:, 1:2], in1=vecs.t[:, 3, 128:192], op0=ALU.mult, op1=ALU.mult),
                     rd=[kr_, rh_, vecs], wr=[kf_])
                S.op("act", lambda e: e.copy(out=Va.t[:, vofs(t, h):vofs(t, h) + 128], in_=kb.t[:, ko + 128:ko + 256]), rd=[kb], wr=[Va])
                qp_, kp_ = qrpr.next(), krpr.next()
                rope(qr_, qr_.t, qp_, qp_.t, t)
                rope(kf_, kf_.t, kp_, kp_.t, t)
                qa_, qb_ = qTa[h].next(), qTb[h].next()
                transpose_to(qn_.t[:, :], qn_, qa_.t[:, :], qa_, "act")
                transpose_to(qp_.t[:, :], qp_, qb_.t[:, :], qb_, "dve")
                transpose_to(kn_.t[:, :], kn_, KTa[h].t[:, t * 128:(t + 1) * 128], KTa[h], "act")
                transpose_to(kp_.t[:, :], kp_, KTb[h].t[:, t * 128:(t + 1) * 128], KTb[h], "dve")
                qTs.append((qa_, qb_))
            for h in range(NH):
                acc = accb.next()
                qa_, qb_ = qTs[h]
                attn_tile(C, t, [(lambda kt, h=h: (KTa[h].t[:, kt * 128:(kt + 1) * 128], KTa[h]), qa_.t[:, :], qa_),
                                 (lambda kt, h=h: (KTb[h].t[:, kt * 128:(kt + 1) * 128], KTb[h]), qb_.t[:, :], qb_)],
                          lambda kt, h=h: (Va.t[:, vofs(kt, h):vofs(kt, h) + 129], Va), acc, acc.t[:, 0:129], SCALE, negshift,
                          {0: mask}, negshift, P, tmpf, qkb)
                r_ = rr.next()
                S.op("dve", lambda e: e.reciprocal(out=r_.t[:, 0:1], in_=acc.t[:, 128:129]), rd=[acc], wr=[r_])
                yb_ = yb.next()
                S.op("dve", lambda e: e.tensor_scalar(out=yb_.t[:, :], in0=acc.t[:, 0:128], scalar1=r_.t[:, 0:1], scalar2=None, op0=ALU.mult), rd=[acc, r_], wr=[yb_])
                transpose_to(yb_.t[:, :], yb_, o_s.t[:, h, :], o_s, "act")
            S.op("sp", lambda e: e.dma_start(out=o_d[row0:row0 + NH * 128, t * 128:(t + 1) * 128].rearrange("(h p) t -> p h t", p=128), in_=o_s.t[:, :, :]),
                 rd=[o_s], dsem=o_s.b.name)
        S.barrier()


NTOK = 2048
SEQ = 4096
NT_SEQ = SEQ // 128
TT_FFN = 1024


def _ffn_inputs(nc, sfx):
    g = nc.dram_tensor("g" + sfx, [D], F32, kind="ExternalInput").ap()
    wg = nc.dram_tensor("wg" + sfx, [D, DFF], F32, kind="ExternalInput").ap()
    wu = nc.dram_tensor("wu" + sfx, [D, DFF], F32, kind="ExternalInput").ap()
    wd = nc.dram_tensor("wd" + sfx, [DFF, D], F32, kind="ExternalInput").ap()
    return g, wg, wu, wd


def build_PA():
    nc = bass.Bass("TRN2", target_bir_lowering=False)
    x = nc.dram_tensor("x", [D, NTOK], F32, kind="ExternalInput").ap()
    g, wg, wu, wd = _ffn_inputs(nc, "")
    gm = nc.dram_tensor("gm", [D], F32, kind="ExternalInput").ap()
    xo = nc.dram_tensor("xo", [D, NTOK], F32, kind="ExternalOutput").ap()
    xn = nc.dram_tensor("xn", [NTOK // 128, 128, KD, 128], BF16, kind="ExternalOutput").ap()
    with ExitStack() as st:
        C = Ctx(nc, st)
        phase_ffn(C, x, xo, g, wg, wu, wd, NTOK, TT_FFN, "A")
        phase_norm(C, xo, gm, xn, NTOK, 512, "N")
        C.S.barrier()
    return nc


def build_PM():
    nc = bass.Bass("TRN2", target_bir_lowering=False)
    dt = lambda n, sh, d=F32: nc.dram_tensor(n, sh, d, kind="ExternalInput").ap()
    xn = dt("xn", [NT_SEQ, 128, KD, 128], BF16)
    whg, wdf, wml = dt("whg", [D, 1536]), dt("wdf", [D, 768]), dt("wml", [D, 832])
    lbz, cm, og = dt("lbz", [4, 384]), dt("cm", [16]), dt("og", [128])
    dfv, dfb = dt("dfv", [8, 128]), dt("dfb", [2, 2, 128, 128])
    wuq, wukv, mlv = dt("wuq", [512, 576]), dt("wukv", [256, 768]), dt("mlv", [4, 512])
    cs = dt("cs", [128, 2 * NT_SEQ * 32])
    call, cmask = dt("c_all", [128, 512]), dt("c_mask", [128, 128])
    o = nc.dram_tensor("o", [1024, SEQ], BF16, kind="ExternalOutput").ap()
    with ExitStack() as st:
        C = Ctx(nc, st)
        phase_hg(C, xn, whg, lbz, cm, og, o, call, NT_SEQ, "H", 3)
        phase_df(C, xn, wdf, dfv, dfb, o, call, cmask, NT_SEQ, "F", 2, row0=384)
        phase_ml(C, xn, wml, wuq, wukv, mlv, cs, o, call, cmask, NT_SEQ, "M", 3, row0=640)
        C.S.barrier()
    return nc


def build_PB():
    nc = bass.Bass("TRN2", target_bir_lowering=False)
    x = nc.dram_tensor("x", [D, NTOK], F32, kind="ExternalInput").ap()
    o = nc.dram_tensor("o", [D, NTOK], BF16, kind="ExternalInput").ap()
    wo = nc.dram_tensor("wo", [D, D], F32, kind="ExternalInput").ap()
    g, wg, wu, wd = _ffn_inputs(nc, "")
    xmid = nc.dram_tensor("xmid", [D, NTOK], F32, kind="ExternalOutput").ap()
    xo = nc.dram_tensor("xo", [D, NTOK], F32, kind="ExternalOutput").ap()
    with ExitStack() as st:
        C = Ctx(nc, st)
        phase_wout(C, x, xmid, o, wo, NTOK, "O")
        phase_ffn(C, xmid, xo, g, wg, wu, wd, NTOK, TT_FFN, "B")
        C.S.barrier()
    return nc


def _consts():
    idx = np.arange(128)
    same = (idx[:, None] // 64) == (idx[None, :] // 64)
    u2 = (same & (idx[:, None] <= idx[None, :])).astype(np.float32)
    mid = (same & ((idx[:, None] % 64) <= 31)).astype(np.float32)
    cmat = u2 - mid
    sel = np.zeros((128, 128), np.float32)
    for c in range(2):
        inch = (idx // 64) == c
        sel[:, 3 * c + 0] = inch & ((idx % 64) <= 31)
        sel[:, 3 * c + 1] = inch & ((idx % 64) >= 32)
        sel[:, 3 * c + 2] = inch
    c_all = np.ascontiguousarray(np.concatenate([np.eye(128, dtype=np.float32), u2, cmat, sel], axis=1))
    ok = (idx[:, None] // 64) <= (idx[None, :] // 64)
    c_mask = np.where(ok, 0.0, NEG).astype(np.float32)
    pos = np.arange(SEQ, dtype=np.float32)
    freqs = (np.float32(10000.0) ** (-np.arange(0, 64, 2, dtype=np.float32) / np.float32(64))).astype(np.float32)
    ang = pos[:, None] * freqs[None, :]
    cos = np.cos(ang).astype(np.float32).reshape(NT_SEQ, 128, 32).transpose(1, 0, 2).reshape(128, NT_SEQ * 32)
    sin = np.sin(ang).astype(np.float32).reshape(NT_SEQ, 128, 32).transpose(1, 0, 2).reshape(128, NT_SEQ * 32)
    cs = np.ascontiguousarray(np.concatenate([cos, sin], axis=1))
    return c_all, c_mask, cs


def _t5_bucket_idx():
    import jax
    import jax.numpy as jnp
    idx = np.arange(128)
    out = []
    with jax.default_device(jax.devices("cpu")[0]):
        for r in (0, -1):
            rel = jnp.asarray(((idx[:, None] + 128 * r) - idx[None, :]).astype(np.int32))
            half, max_exact = 16, 8
            ret = (rel > 0).astype(jnp.int32) * half
            n = jnp.abs(rel)
            large = max_exact + (jnp.log(jnp.maximum(n, 1).astype(jnp.float32) / max_exact)
                                 / math.log(128 / max_exact) * (half - max_exact)).astype(jnp.int32)
            large = jnp.minimum(large, half - 1)
            out.append(np.asarray(ret + jnp.where(n < max_exact, n, large)))
    return np.stack(out, 0)


_IN_OFF = [0, 768, 1536, 2304, 3072, 3584, 4096, 4608, 5120, 5376, 5440]


PAIRS = [[0, 1], [2, 3], [4, 5], [6, 7]]


def build_fused(L=4):
    nc = bass.Bass("TRN2", target_bir_lowering=False)
    dt = lambda n, sh, d=F32: nc.dram_tensor(n, sh, d, kind="ExternalInput").ap()
    x = dt("x", [D, NTOK])
    ffn = {}
    for ab in ("a", "b"):
        ffn[ab] = (dt("ffn_%s_norm" % ab, [L, D]), dt("ffn_%s_w_gate" % ab, [L, D, DFF]), dt("ffn_%s_w_up" % ab, [L, D, DFF]),
                   dt("ffn_%s_w_down" % ab, [L, DFF, D]))
    gm = dt("mix_norm", [L, D])
    whg, wdf, wml = dt("whg", [L, D, 1536]), dt("wdf", [L, D, 768]), dt("wml", [L, D, 832])
    lbz, cm, og = dt("lbz", [4, 384]), dt("cm", [L, 16]), dt("og", [L, 128])
    dfv, dfb = dt("dfv", [L, 8, 128]), dt("dfb", [2, 2, 128, 128])
    wuq, wukv, mlv = dt("wuq", [L, 512, 576]), dt("wukv", [L, 256, 768]), dt("mlv", [L, 4, 512])
    cs = dt("cs", [128, 2 * NT_SEQ * 32])
    call, cmask = dt("c_all", [128, 512]), dt("c_mask", [128, 128])
    wo = dt("wo", [L, D, D])
    sel = dt("sel", [16])
    xo = nc.dram_tensor("xo", [D, NTOK], F32, kind="ExternalOutput").ap()
    internal = lambda n, sh, d: nc.dram_tensor(n, sh, d, kind="Internal").ap()
    local = lambda n, sh, d: nc.dram_tensor(n, sh, d, addr_space="Local", kind="Internal").ap()
    with ExitStack() as st:
        C = Ctx(nc, st)
        S = C.S
        xcur = x
        for l in range(L):
            C.sfx = "_%d" % l
            xa = internal("xa%d" % l, [D, NTOK], F32)
            xb = internal("xb%d" % l, [D, NTOK], F32)
            xc = xo if l == L - 1 else internal("xc%d" % l, [D, NTOK], F32)
            xns = internal("xns%d" % l, [NTOK // 128, 128, KD, 128], BF16)
            xnf = local("xnf%d" % l, [NT_SEQ, 128, KD, 128], BF16)
            osd = internal("osd%d" % l, [1024, SEQ], BF16)
            ofl = local("ofl%d" % l, [2048, SEQ], BF16)
            g, wg, wu, wd = ffn["a"]
            phase_ffn(C, xcur, xa, g[l], wg[l], wu[l], wd[l], NTOK, TT_FFN, "A")
            phase_norm(C, xa, gm[l], xns, NTOK, 512, "N")
            xns2 = xns.rearrange("n p k t -> (n p) (k t)")
            xnf2 = xnf.rearrange("n p k t -> (n p) (k t)")
            S.barrier()
            for j in range(8):
                S.cc(lambda e: e.collective_compute("AllGather", ALU.bypass, replica_groups=PAIRS,
                                                    ins=[xns2[j * 256:(j + 1) * 256, :]], outs=[xnf2[j * 512:(j + 1) * 512, :]]))
            S.barrier()
            xv = XnView(xnf2)
            phase_hg(C, xv, whg[l], lbz, cm[l], og[l], osd, call, NT_SEQ, "H", 3)
            phase_df(C, xv, wdf[l], dfv[l], dfb, osd, call, cmask, NT_SEQ, "F", 2, row0=384)
            phase_ml(C, xv, wml[l], wuq[l], wukv[l], mlv[l], cs, osd, call, cmask, NT_SEQ, "M", 3, row0=640)
            S.barrier()
            for j in range(8):
                S.cc(lambda e: e.collective_compute("AllGather", ALU.bypass, replica_groups=PAIRS,
                                                    ins=[osd[j * 128:(j + 1) * 128, :]], outs=[ofl[j * 256:(j + 1) * 256, :]]))
            S.barrier()
            phase_wout(C, xa, xb, ofl, wo[l], NTOK, "O", sel_d=sel)
            g, wg, wu, wd = ffn["b"]
            phase_ffn(C, xb, xc, g[l], wg[l], wu[l], wd[l], NTOK, TT_FFN, "B")
            xcur = xc
        S.barrier()
    return nc


class XnView:
    def __init__(self, g2):
        self.g2 = g2

    def __getitem__(self, key):
        t = key[0]
        rank, lt = t // 16, t % 16
        r0 = (lt // 2) * 512 + rank * 256 + (lt % 2) * 128
        return self.g2[r0:r0 + 128, :].rearrange("p (k t) -> p k t", t=128)


def _gathered_row_perm():
    perm = []
    for q in range(16):
        j, r = q // 2, q % 2
        if j < 3:
            b = 3 * r + j
        elif j < 5:
            b = 6 + 2 * r + (j - 3)
        else:
            b = 10 + 3 * r + (j - 5)
        perm += list(range(b * 128, (b + 1) * 128))
    return np.asarray(perm)


def kernel(**inputs):
    f = lambda k: np.ascontiguousarray(np.asarray(inputs[k], dtype=np.float32))
    x = f("x")
    L = 4
    c_all, c_mask, cs = _consts()
    bk = _t5_bucket_idx()
    rel_bias = f("rel_bias")
    w_in = f("w_in")
    ar = np.arange(128)
    shared = {k: f(k) for k in ("ffn_a_norm", "ffn_a_w_gate", "ffn_a_w_up", "ffn_a_w_down", "mix_norm",
                                "ffn_b_norm", "ffn_b_w_gate", "ffn_b_w_up", "ffn_b_w_down")}
    shared["wo"] = np.ascontiguousarray(f("w_out")[:, _gathered_row_perm(), :])
    shared["og"] = f("hgrn_out_norm")
    shared["cs"], shared["c_all"], shared["c_mask"] = cs, c_all, c_mask
    cm = np.zeros((L, 16), np.float32)
    dfv = np.zeros((L, 8, 128), np.float32)
    mlv = np.zeros((L, 4, 512), np.float32)
    for l in range(L):
        linit = 0.8 - 0.6 * math.exp(-0.3 * l)
        cm[l, 1:l + 1] = 1.0
        dfv[l, 0, :64] = f("diff_q_norm")[l]
        dfv[l, 1, :64] = f("diff_k_norm")[l]
        dfv[l, 2, :64] = f("diff_lambda_q1")[l]
        dfv[l, 3, :64] = f("diff_lambda_k1")[l]
        dfv[l, 4, :64] = f("diff_lambda_q2")[l]
        dfv[l, 5, :64] = f("diff_lambda_k2")[l]
        dfv[l, 6, :] = f("diff_subln")[l]
        dfv[l, 7, 0] = linit
        dfv[l, 7, 1] = 1.0 - linit
        mlv[l, 0, :] = f("mla_q_lora_norm")[l]
        mlv[l, 1, :256] = f("mla_kv_lora_norm")[l]
        mlv[l, 2, :192] = f("mla_q_norm")[l]
        mlv[l, 3, :192] = f("mla_k_norm")[l]
    shared["cm"], shared["mlv"] = cm, mlv
    per_g = []
    for g in range(2):
        hg_cols = np.concatenate([_IN_OFF[j] + (3 * g + h) * 128 + ar for h in range(3) for j in range(4)])
        df_cols = np.concatenate([_IN_OFF[4 + j] + (2 * g + h) * 128 + ar for h in range(2) for j in range(3)])
        dg = dfv.copy()
        dg[:, 7, 2:4] = rel_bias[15, 2 * g:2 * g + 2]
        sel = np.zeros(16, np.float32)
        sel[g] = 1.0
        per_g.append({
            "whg": np.ascontiguousarray(w_in[:, :, hg_cols]), "wdf": np.ascontiguousarray(w_in[:, :, df_cols]),
            "wml": np.ascontiguousarray(w_in[:, :, 4608:5440]),
            "lbz": np.ascontiguousarray(f("hgrn_lb_logits")[:, g * 384:(g + 1) * 384]),
            "dfv": dg, "dfb": np.ascontiguousarray(rel_bias[bk][..., 2 * g:2 * g + 2].transpose(3, 0, 1, 2)),
            "wuq": np.ascontiguousarray(f("mla_w_uq")[:, :, g * 576:(g + 1) * 576]),
            "wukv": np.ascontiguousarray(f("mla_w_ukv")[:, :, g * 768:(g + 1) * 768]),
            "sel": sel,
        })
    ims = []
    for c in range(8):
        m = dict(shared)
        m.update(per_g[c % 2])
        m["x"] = np.ascontiguousarray(x[c // 2, (c % 2) * NTOK:(c % 2 + 1) * NTOK, :].T)
        ims.append(m)
    nc = build_fused(L)
    res = run_bass_kernel_spmd(nc, ims, core_ids=list(range(8))).results
    out = np.empty((4, SEQ, D), np.float32)
    for c in range(8):
        out[c // 2, (c % 2) * NTOK:(c % 2 + 1) * NTOK, :] = np.asarray(res[c]["xo"]).T
    return out
```

```python
import math
from contextlib import ExitStack

import numpy as np
import ml_dtypes

import concourse.bass as bass
import concourse.mybir as mybir
from concourse.bass_utils import run_bass_kernel_spmd

F32 = mybir.dt.float32
BF16 = mybir.dt.bfloat16
AF = mybir.ActivationFunctionType
ALU = mybir.AluOpType
AX = mybir.AxisListType

D = 2048
DFF = 5504
NF = DFF // 128
KD = D // 128
EPS = 1e-6
NEG = -60.0


class Buf:
    __slots__ = ("name", "w", "r")

    def __init__(self, name=""):
        self.name = name
        self.w = None
        self.r = []


class T:
    __slots__ = ("t", "b")

    def __init__(self, t, name=""):
        self.t = t
        self.b = Buf(name)


class Sched:
    def __init__(self, nc, stack):
        self.nc = nc
        self.stack = stack
        self.engs = {"pe": nc.tensor, "act": nc.scalar, "dve": nc.vector, "pool": nc.gpsimd, "sp": nc.sync}
        self.sem = {}
        self.cnt = {}
        for e in self.engs:
            self.sem[e] = stack.enter_context(nc.semaphore("s_" + e))
            self.cnt[e] = 0
        self.waited = {}
        self.dsems = {}
        self.dcnt = {}
        self.nsem = 0

    def dsem(self, name):
        if name not in self.dsems:
            s = self.stack.enter_context(self.nc.semaphore("d_" + name))
            self.dsems[name] = s
            self.dcnt[name] = 0
        return name

    def _wait(self, e, deps):
        best = {}
        for (k, v) in deps:
            if k == e and e == "pe":
                continue
            if k not in best or best[k] < v:
                best[k] = v
        for k, v in best.items():
            if self.waited.get((e, k), 0) >= v:
                continue
            s = self.sem[k] if k in self.sem else self.dsems[k]
            self.engs[e].wait_ge(s, v)
            self.waited[(e, k)] = v

    def op(self, e, fn, rd=(), wr=(), dsem=None):
        self.nops = getattr(self, "nops", 0) + 1
        if self.nops > getattr(self, "max_ops", 1 << 60):
            return None
        deps = []
        rd = [b.b if isinstance(b, T) else b for b in rd]
        wr = [b.b if isinstance(b, T) else b for b in wr]
        wr = wr + [b for b in rd if b.name.startswith(("bank", "psb"))]
        rd = [b for b in rd if not b.name.startswith(("bank", "psb"))]
        for b in rd:
            b = b.b if isinstance(b, T) else b
            if b.w is not None:
                deps.append(b.w)
        for b in wr:
            b = b.b if isinstance(b, T) else b
            if b.w is not None:
                deps.append(b.w)
            deps.extend(b.r)
        self._wait(e, deps)
        ins = fn(self.engs[e])
        if dsem is not None:
            self.dsem(dsem)
            self.dcnt[dsem] += 16
            ins.then_inc(self.dsems[dsem], 16)
            tok = (dsem, self.dcnt[dsem])
        else:
            self.cnt[e] += 1
            ins.then_inc(self.sem[e], 1)
            tok = (e, self.cnt[e])
        for b in rd:
            b = b.b if isinstance(b, T) else b
            b.r.append(tok)
            if len(b.r) > 64:
                m = {}
                for (k, v) in b.r:
                    if k not in m or m[k] < v:
                        m[k] = v
                b.r = list(m.items())
        for b in wr:
            b = b.b if isinstance(b, T) else b
            b.w = tok
            b.r = []
        return tok

    def cc(self, fn):
        name = self.dsem("ccsem")
        ins = fn(self.engs["pool"])
        self.dcnt[name] += 1
        ins.then_inc(self.dsems[name], 1)

    def barrier(self, engines=None):
        toks = [(e, c) for e, c in self.cnt.items() if c > 0]
        toks += [(n, c) for n, c in self.dcnt.items() if c > 0]
        for e in (engines or self.engs):
            self._wait_all(e, toks)

    def _wait_all(self, e, toks):
        for (k, v) in toks:
            if k == e:
                continue
            if self.waited.get((e, k), 0) >= v:
                continue
            s = self.sem[k] if k in self.sem else self.dsems[k]
            self.engs[e].wait_ge(s, v)
            self.waited[(e, k)] = v


class Ring:
    def __init__(self, items):
        self.items = items
        self.i = 0

    def next(self):
        x = self.items[self.i % len(self.items)]
        self.i += 1
        return x


class Ctx:
    def __init__(self, nc, st):
        self.nc = nc
        self.sfx = ""
        self.S = Sched(nc, st)
        self.psf = [st.enter_context(nc.psum_tensor("psf%d" % i, [128, 512], F32)) for i in range(7)]
        self.psb = st.enter_context(nc.psum_tensor("psb", [128, 1024], BF16))
        self.bankT = [T(self.psf[i], "bank%d" % i) for i in range(7)]
        self.psbT = [T(None, "psb") for i in range(8)]
        for x in self.psbT:
            x.b = self.psbT[0].b

    def banks(self, ids):
        return [self.bankT[i] for i in ids]


def _groups(n, g):
    out = []
    i = 0
    while i < n:
        out.append((i, min(g, n - i)))
        i += g
    return out


def emit_rstd(S, out_ap, out_T, in_ap, in_T, scale, tmp_ap, tmp_T):
    S.op("act", lambda e: e.activation(out=tmp_ap, in_=in_ap, func=AF.Ln, scale=scale, bias=EPS), rd=[in_T], wr=[tmp_T])
    S.op("act", lambda e: e.activation(out=out_ap, in_=tmp_ap, func=AF.Exp, scale=-0.5), rd=[tmp_T], wr=[out_T])


def norm_tile(C, st_bufs, x_d, tok0, TT, banks):
    S = C.S
    NS = TT // 512
    xin, sq, hT, rstd, lnt, gcol, ones = (st_bufs[k] for k in ("xin", "sq", "hT", "rstd", "lnt", "gcol", "ones"))
    bk = [banks.next() for _ in range(NS)]
    for k in range(KD):
        xi = xin.next()
        si = sq.next()
        S.op("sp", lambda e: e.dma_start(out=xi.t[:, :], in_=x_d[k * 128:(k + 1) * 128, tok0:tok0 + TT]), wr=[xi], dsem=xi.b.name)
        S.op("act", lambda e: e.activation(out=si.t[:, :], in_=xi.t[:, :], func=AF.Square), rd=[xi], wr=[si])
        for s in range(NS):
            S.op("pe", lambda e: e.matmul(bk[s].t[:, :], ones.t[:, :], si.t[:, s * 512:(s + 1) * 512], start=(k == 0), stop=(k == KD - 1)),
                 rd=[ones, si], wr=[bk[s]])
    for s in range(NS):
        emit_rstd(S, rstd.t[:, s * 512:(s + 1) * 512], rstd, bk[s].t[:, :], bk[s], 1.0 / D, rstd.t[:, s * 512:(s + 1) * 512], rstd)
    for k in range(KD):
        xi = xin.next()
        S.op("sp", lambda e: e.dma_start(out=xi.t[:, :], in_=x_d[k * 128:(k + 1) * 128, tok0:tok0 + TT]), wr=[xi], dsem=xi.b.name)
        S.op("dve", lambda e: e.scalar_tensor_tensor(out=hT.t[:, k, :], in0=xi.t[:, :], scalar=gcol.t[:, k:k + 1], in1=rstd.t[:, :],
                                                     op0=ALU.mult, op1=ALU.mult), rd=[xi, gcol, rstd], wr=[hT])


def norm_bufs(C, st, g_d, TT, pfx):
    nc, S = C.nc, C.S
    sb = lambda n, sh, dt: T(st.enter_context(nc.sbuf_tensor(pfx + n + C.sfx, sh, dt)), pfx + n)
    B = {}
    B["xin"] = Ring([sb("xin%d" % i, [128, TT], F32) for i in range(2)])
    B["sq"] = Ring([sb("sq%d" % i, [128, TT], BF16) for i in range(2)])
    B["hT"] = sb("hT", [128, KD, TT], BF16)
    B["rstd"] = sb("rstd", [128, TT], F32)
    B["lnt"] = None
    B["gcol"] = sb("gcol", [128, KD], F32)
    B["ones"] = sb("ones", [128, 128], BF16)
    S.op("sp", lambda e: e.dma_start(out=B["gcol"].t[:, :], in_=g_d.rearrange("(k p) -> p k", p=128), allow_slow_non_contiguous=True),
         wr=[B["gcol"]], dsem=pfx + "gcol")
    S.op("dve", lambda e: e.memset(B["ones"].t[:, :], 1.0), wr=[B["ones"]])
    return B


def phase_ffn(C, x_d, xo_d, g_d, wg_d, wu_d, wd_d, NTOK, TT, pfx):
    nc, S = C.nc, C.S
    NS = TT // 512
    S.barrier()
    with ExitStack() as st:
        sb = lambda n, sh, dt: T(st.enter_context(nc.sbuf_tensor(pfx + n + C.sfx, sh, dt)), pfx + n)
        NB = norm_bufs(C, st, g_d, TT, pfx)
        hT = NB["hT"]
        GW = 256
        wg = Ring([sb("wg%d" % i, [128, KD, GW], BF16) for i in range(2)])
        wu = Ring([sb("wu%d" % i, [128, KD, GW], BF16) for i in range(2)])
        actT = [T(None, pfx + "act%d" % f) for f in range(NF)]
        actT_t = st.enter_context(nc.sbuf_tensor(pfx + "actT" + C.sfx, [128, NF, TT], BF16))
        sg = Ring([sb("sg%d" % i, [128, 512], BF16) for i in range(2)])
        DGC = 4 // NS
        wd = Ring([sb("wd%d" % i, [128, 4, DGC * 128], BF16) for i in range(2)])
        xres = Ring([sb("xres%d" % i, [128, 512], F32) for i in range(2)])
        yo = Ring([sb("yo%d" % i, [128, 512], F32) for i in range(2)])
        banks = Ring(C.banks([0, 1, 2, 3, 4, 5]))
        dbanks = Ring(C.banks([0, 1, 2, 3, 4, 5, 6]))
        for tt in range(NTOK // TT):
            tok0 = tt * TT
            norm_tile(C, NB, x_d, tok0, TT, banks)
            for (f0, nf) in _groups(NF, GW // 128):
                g_s = wg.next()
                u_s = wu.next()
                S.op("pool", lambda e: e.dma_start(out=g_s.t[:, :, 0:nf * 128],
                                                   in_=wg_d[:, f0 * 128:(f0 + nf) * 128].rearrange("(k p) f -> p k f", p=128)),
                     wr=[g_s], dsem=g_s.b.name)
                S.op("pool", lambda e: e.dma_start(out=u_s.t[:, :, 0:nf * 128],
                                                   in_=wu_d[:, f0 * 128:(f0 + nf) * 128].rearrange("(k p) f -> p k f", p=128)),
                     wr=[u_s], dsem=u_s.b.name)
                for fi in range(nf):
                    f = f0 + fi
                    for s in range(NS):
                        bg = banks.next()
                        bu = banks.next()
                        for k in range(KD):
                            S.op("pe", lambda e: e.matmul(bg.t[:, :], g_s.t[:, k, fi * 128:(fi + 1) * 128], hT.t[:, k, s * 512:(s + 1) * 512],
                                                          start=(k == 0), stop=(k == KD - 1)), rd=[g_s, hT], wr=[bg])
                        for k in range(KD):
                            S.op("pe", lambda e: e.matmul(bu.t[:, :], u_s.t[:, k, fi * 128:(fi + 1) * 128], hT.t[:, k, s * 512:(s + 1) * 512],
                                                          start=(k == 0), stop=(k == KD - 1)), rd=[u_s, hT], wr=[bu])
                        sgi = sg.next()
                        S.op("act", lambda e: e.activation(out=sgi.t[:, :], in_=bg.t[:, :], func=AF.Silu), rd=[bg], wr=[sgi])
                        S.op("dve", lambda e: e.tensor_tensor(out=actT_t[:, f, s * 512:(s + 1) * 512], in0=bu.t[:, :], in1=sgi.t[:, :], op=ALU.mult),
                             rd=[bu, sgi], wr=[actT[f]])
            for dg in range(KD // DGC):
                db = [[dbanks.next() for s in range(NS)] for dd in range(DGC)]
                for (f0, nf) in _groups(NF, 4):
                    w_s = wd.next()
                    S.op("pool", lambda e: e.dma_start(out=w_s.t[:, 0:nf, :],
                                                       in_=wd_d[f0 * 128:(f0 + nf) * 128, dg * DGC * 128:(dg + 1) * DGC * 128].rearrange("(j p) c -> p j c", p=128)),
                         wr=[w_s], dsem=w_s.b.name)
                    for fi in range(nf):
                        f = f0 + fi
                        for dd in range(DGC):
                            for s in range(NS):
                                S.op("pe", lambda e: e.matmul(db[dd][s].t[:, :], w_s.t[:, fi, dd * 128:(dd + 1) * 128],
                                                              actT_t[:, f, s * 512:(s + 1) * 512], start=(f == 0), stop=(f == NF - 1)),
                                     rd=[w_s, actT[f]], wr=[db[dd][s]])
                for dd in range(DGC):
                    d = dg * DGC + dd
                    for s in range(NS):
                        xr = xres.next()
                        y = yo.next()
                        c0 = tok0 + s * 512
                        S.op("sp", lambda e: e.dma_start(out=xr.t[:, :], in_=x_d[d * 128:(d + 1) * 128, c0:c0 + 512]), wr=[xr], dsem=xr.b.name)
                        S.op("dve", lambda e: e.scalar_tensor_tensor(out=y.t[:, :], in0=db[dd][s].t[:, :], scalar=0.5, in1=xr.t[:, :],
                                                                     op0=ALU.mult, op1=ALU.add), rd=[db[dd][s], xr], wr=[y])
                        S.op("sp", lambda e: e.dma_start(out=xo_d[d * 128:(d + 1) * 128, c0:c0 + 512], in_=y.t[:, :]), rd=[y], dsem=y.b.name)
        S.barrier()


def phase_norm(C, x_d, g_d, xn_d, NTOK, TT, pfx):
    nc, S = C.nc, C.S
    S.barrier()
    with ExitStack() as st:
        NB = norm_bufs(C, st, g_d, TT, pfx)
        banks = Ring(C.banks([0, 1, 2, 3]))
        for tt in range(NTOK // TT):
            norm_tile(C, NB, x_d, tt * TT, TT, banks)
            for j in range(TT // 128):
                S.op("sp", lambda e: e.dma_start(out=xn_d[tt * (TT // 128) + j, :, :, :], in_=NB["hT"].t[:, :, j * 128:(j + 1) * 128]),
                     rd=[NB["hT"]], dsem=pfx + "xnout")
        S.barrier()


def phase_wout(C, x_d, xo_d, o_d, wo_d, NTOK, pfx, sel_d=None):
    nc, S = C.nc, C.S
    S.barrier()
    with ExitStack() as st:
        sb = lambda n, sh, dt: T(st.enter_context(nc.sbuf_tensor(pfx + n + C.sfx, sh, dt)), pfx + n)
        wo = sb("wo", [128, KD, D], BF16)
        for k in range(KD):
            S.op("pool", lambda e: e.dma_start(out=wo.t[:, k, :], in_=wo_d[k * 128:(k + 1) * 128, :]), wr=[wo], dsem=pfx + "wo")
        ot = Ring([sb("ot%d" % i, [128, KD, 512], BF16) for i in range(2)])
        xres = Ring([sb("xres%d" % i, [128, 512], F32) for i in range(2)])
        yo = Ring([sb("yo%d" % i, [128, 512], F32) for i in range(2)])
        if sel_d is not None:
            selt = load_bcast(C, st, pfx + "sel", sel_d, 16)
            oa, ob = sb("oa", [128, KD, 512], BF16), sb("ob", [128, KD, 512], BF16)
            otmp = sb("otmp", [128, KD, 512], F32)
        banks = Ring(C.banks([0, 1, 2, 3]))
        for s in range(NTOK // 512):
            c0 = s * 512
            o_s = ot.next()
            if sel_d is None:
                S.op("sp", lambda e: e.dma_start(out=o_s.t[:, :, :], in_=o_d[:, c0:c0 + 512].rearrange("(k p) t -> p k t", p=128)),
                     wr=[o_s], dsem=o_s.b.name)
            else:
                S.op("sp", lambda e: e.dma_start(out=oa.t[:, :, :], in_=o_d[:, c0:c0 + 512].rearrange("(k p) t -> p k t", p=128)),
                     wr=[oa], dsem=oa.b.name)
                S.op("sp", lambda e: e.dma_start(out=ob.t[:, :, :], in_=o_d[:, NTOK + c0:NTOK + c0 + 512].rearrange("(k p) t -> p k t", p=128)),
                     wr=[ob], dsem=ob.b.name)
                S.op("dve", lambda e: e.tensor_scalar(out=otmp.t[:, :, :], in0=oa.t[:, :, :], scalar1=selt.t[:, 0:1], scalar2=None, op0=ALU.mult),
                     rd=[oa, selt], wr=[otmp])
                S.op("dve", lambda e: e.scalar_tensor_tensor(out=o_s.t[:, :, :], in0=ob.t[:, :, :], scalar=selt.t[:, 1:2], in1=otmp.t[:, :, :],
                                                             op0=ALU.mult, op1=ALU.add), rd=[ob, selt, otmp], wr=[o_s])
            for d in range(KD):
                bk = banks.next()
                for k in range(KD):
                    S.op("pe", lambda e: e.matmul(bk.t[:, :], wo.t[:, k, d * 128:(d + 1) * 128], o_s.t[:, k, :], start=(k == 0), stop=(k == KD - 1)),
                         rd=[wo, o_s], wr=[bk])
                xr = xres.next()
                y = yo.next()
                S.op("sp", lambda e: e.dma_start(out=xr.t[:, :], in_=x_d[d * 128:(d + 1) * 128, c0:c0 + 512]), wr=[xr], dsem=xr.b.name)
                S.op("dve", lambda e: e.tensor_tensor(out=y.t[:, :], in0=bk.t[:, :], in1=xr.t[:, :], op=ALU.add), rd=[bk, xr], wr=[y])
                S.op("sp", lambda e: e.dma_start(out=xo_d[d * 128:(d + 1) * 128, c0:c0 + 512], in_=y.t[:, :]), rd=[y], dsem=y.b.name)
        S.barrier()


def load_bcast(C, st, name, vec_ap, n):
    t = T(st.enter_context(C.nc.sbuf_tensor(name + C.sfx, [128, n], F32)), name)
    C.S.op("sp", lambda e: e.dma_start(out=t.t[:, :], in_=vec_ap.partition_broadcast(128)), wr=[t], dsem=name)
    return t


def load_const(C, st, name, ap, shape, dt):
    t = T(st.enter_context(C.nc.sbuf_tensor(name + C.sfx, shape, dt)), name)
    eng = "sp" if dt == F32 else "pool"
    C.S.op(eng, lambda e: e.dma_start(out=t.t[:, :], in_=ap), wr=[t], dsem=name)
    return t


def psb_region(C, i):
    return C.psb[:, i * 128:(i + 1) * 128], C.psbT[i]


def phase_hg(C, xn_d, w_d, lbz_d, cm_d, og_d, o_d, consts, NT, pfx, NH=3):
    nc, S = C.nc, C.S
    S.barrier()
    with ExitStack() as st:
        sb = lambda n, sh, dt: T(st.enter_context(nc.sbuf_tensor(pfx + n + C.sfx, sh, dt)), pfx + n)
        w = sb("w", [128, KD, NH * 512], BF16)
        for h in range(NH):
            S.op("pool", lambda e: e.dma_start(out=w.t[:, :, h * 512:(h + 1) * 512],
                                               in_=w_d[:, h * 512:(h + 1) * 512].rearrange("(k p) c -> p k c", p=128)), wr=[w], dsem=pfx + "w")
        cf = load_const(C, st, pfx + "cf", consts, [128, 512], F32)
        cb = sb("cb", [128, 512], BF16)
        S.op("dve", lambda e: e.tensor_copy(out=cb.t[:, :], in_=cf.t[:, :]), rd=[cf], wr=[cb])
        ident = T(cb.t[:, 0:128]); ident.b = cb.b
        u2 = T(cf.t[:, 128:256]); u2.b = cf.b
        cmat = T(cb.t[:, 256:384]); cmat.b = cb.b
        sel = T(cb.t[:, 384:390]); sel.b = cb.b
        ogain = load_bcast(C, st, pfx + "ogain", og_d, 128)
        NC_ = NH * 128
        lbz = sb("lbz", [128, 4, NC_], F32)
        for j in range(4):
            S.op("sp", lambda e: e.dma_start(out=lbz.t[:, j, :], in_=lbz_d[j, :].partition_broadcast(128)), wr=[lbz], dsem=pfx + "lbz")
        cm = load_bcast(C, st, pfx + "cm", cm_d, 16)
        lb = sb("lb", [128, NC_], F32)
        oml = sb("oml", [128, NC_], F32)
        den = sb("den", [128, NC_], F32)
        S.op("act", lambda e: e.activation(out=lbz.t[:, :, :], in_=lbz.t[:, :, :], func=AF.Exp), rd=[lbz], wr=[lbz])
        S.op("dve", lambda e: e.tensor_tensor(out=den.t[:, :], in0=lbz.t[:, 0, :], in1=lbz.t[:, 1, :], op=ALU.add), rd=[lbz], wr=[den])
        S.op("dve", lambda e: e.tensor_tensor(out=den.t[:, :], in0=den.t[:, :], in1=lbz.t[:, 2, :], op=ALU.add), rd=[lbz, den], wr=[den])
        S.op("dve", lambda e: e.tensor_tensor(out=den.t[:, :], in0=den.t[:, :], in1=lbz.t[:, 3, :], op=ALU.add), rd=[lbz, den], wr=[den])
        S.op("dve", lambda e: e.reciprocal(out=den.t[:, :], in_=den.t[:, :]), rd=[den], wr=[den])
        S.op("dve", lambda e: e.tensor_scalar(out=lb.t[:, :], in0=lbz.t[:, 0, :], scalar1=cm.t[:, 0:1], scalar2=None, op0=ALU.mult), rd=[lbz, cm], wr=[lb])
        for j in range(1, 4):
            S.op("dve", lambda e: e.scalar_tensor_tensor(out=lb.t[:, :], in0=lbz.t[:, j, :], scalar=cm.t[:, j:j + 1], in1=lb.t[:, :],
                                                         op0=ALU.mult, op1=ALU.add), rd=[lbz, cm, lb], wr=[lb])
        S.op("dve", lambda e: e.tensor_tensor(out=lb.t[:, :], in0=lb.t[:, :], in1=den.t[:, :], op=ALU.mult), rd=[lb, den], wr=[lb])
        S.op("dve", lambda e: e.tensor_scalar(out=oml.t[:, :], in0=lb.t[:, :], scalar1=-1.0, scalar2=1.0, op0=ALU.mult, op1=ALU.add), rd=[lb], wr=[oml])

        St = [sb("S%d" % h, [128, 128], F32) for h in range(NH)]
        for h in range(NH):
            S.op("dve", lambda e: e.memset(St[h].t[:, :], 0.0), wr=[St[h]])
        xt = Ring([sb("xt%d" % i, [128, KD, 128], BF16) for i in range(2)])
        R = lambda n, sh, dt, k=2: Ring([sb("%s%d" % (n, i), sh, dt) for i in range(k)])
        ef, eg, ff, lf, kk, E, Ei = (R(n, [128, 128], F32) for n in ("ef", "eg", "ff", "lf", "kk", "E", "Ei"))
        esc = R("esc", [128, 6], F32)
        qh, kh, vv, kT, sm, Sp0, Sp1, yb = (R(n, [128, 128], BF16) for n in ("qh", "kh", "vv", "kT", "sm", "Sp0", "Sp1", "yb"))
        A = R("A", [128, 128], F32)
        lfh, lfl = R("lfh", [128, 128], BF16), R("lfl", [128, 128], BF16)
        ss, lnv, rs = (R(n, [128, 1], F32) for n in ("ss", "lnv", "rs"))
        junk = R("junk", [128, 128], F32)
        qA = [sb("qA%d" % i, [128, 128], BF16) for i in range(2)]
        qB = [sb("qB%d" % i, [128, 128], BF16) for i in range(2)]
        for i in range(2):
            S.op("dve", lambda e: e.memset(qA[i].t[:, :], 0.0), wr=[qA[i]])
            S.op("dve", lambda e: e.memset(qB[i].t[:, :], 0.0), wr=[qB[i]])
        qAr, qBr = Ring(qA), Ring(qB)
        kA = [sb("kA%d" % i, [128, 128], BF16) for i in range(2)]
        kB = [sb("kB%d" % i, [128, 128], BF16) for i in range(2)]
        for i in range(2):
            S.op("dve", lambda e: e.memset(kA[i].t[:, :], 0.0), wr=[kA[i]])
            S.op("dve", lambda e: e.memset(kB[i].t[:, :], 0.0), wr=[kB[i]])
        kAr, kBr = Ring(kA), Ring(kB)
        ost = R("ost", [128, NH, 128], BF16)
        pjb = Ring(C.banks([0, 1, 2]))
        wkb = Ring(C.banks([3, 4, 5, 6]))
        pbi = [0]

        def psb_next():
            i = pbi[0] % 8
            pbi[0] += 1
            return psb_region(C, i)

        for t in range(NT):
            x_s = xt.next()
            S.op("sp", lambda e: e.dma_start(out=x_s.t[:, :, :], in_=xn_d[t, :, :, :]), wr=[x_s], dsem=x_s.b.name)
            o_s = ost.next()
            for h in range(NH):
                pj = pjb.next()
                for k in range(KD):
                    S.op("pe", lambda e: e.matmul(pj.t[:, :], x_s.t[:, k, :], w.t[:, k, h * 512:(h + 1) * 512], start=(k == 0), stop=(k == KD - 1)),
                         rd=[x_s, w], wr=[pj])
                q_ap, fl_ap, vi_ap, gt_ap = (pj.t[:, i * 128:(i + 1) * 128] for i in range(4))
                hs = slice(h * 128, (h + 1) * 128)
                ef_, eg_, ff_, lf_, kk_, E_, Ei_ = (r.next() for r in (ef, eg, ff, lf, kk, E, Ei))
                S.op("act", lambda e: e.activation(out=ef_.t[:, :], in_=fl_ap, func=AF.Exp, scale=-1.0), rd=[pj], wr=[ef_])
                S.op("act", lambda e: e.activation(out=eg_.t[:, :], in_=gt_ap, func=AF.Exp, scale=-1.0), rd=[pj], wr=[eg_])
                S.op("dve", lambda e: e.tensor_scalar(out=ef_.t[:, :], in0=ef_.t[:, :], scalar1=1.0, scalar2=None, op0=ALU.add), rd=[ef_], wr=[ef_])
                S.op("dve", lambda e: e.reciprocal(out=ef_.t[:, :], in_=ef_.t[:, :]), rd=[ef_], wr=[ef_])
                S.op("dve", lambda e: e.tensor_tensor(out=ff_.t[:, :], in0=ef_.t[:, :], in1=oml.t[:, hs], op=ALU.mult), rd=[ef_, oml], wr=[ff_])
                S.op("dve", lambda e: e.tensor_tensor(out=ff_.t[:, :], in0=ff_.t[:, :], in1=lb.t[:, hs], op=ALU.add), rd=[ff_, lb], wr=[ff_])
                S.op("act", lambda e: e.activation(out=lf_.t[:, :], in_=ff_.t[:, :], func=AF.Ln), rd=[ff_], wr=[lf_])
                S.op("dve", lambda e: e.tensor_scalar(out=kk_.t[:, :], in0=ff_.t[:, :], scalar1=-1.0, scalar2=1.0, op0=ALU.mult, op1=ALU.add), rd=[ff_], wr=[kk_])
                wk = wkb.next()
                lh_, ll_ = lfh.next(), lfl.next()
                S.op("dve", lambda e: e.tensor_copy(out=lh_.t[:, :], in_=lf_.t[:, :]), rd=[lf_], wr=[lh_])
                S.op("dve", lambda e: e.tensor_tensor(out=ll_.t[:, :], in0=lf_.t[:, :], in1=lh_.t[:, :], op=ALU.subtract), rd=[lf_, lh_], wr=[ll_])
                S.op("pe", lambda e: e.matmul(wk.t[:, 0:128], cmat.t[:, :], lh_.t[:, :], start=True, stop=False), rd=[cmat, lh_], wr=[wk])
                S.op("pe", lambda e: e.matmul(wk.t[:, 0:128], cmat.t[:, :], ll_.t[:, :], start=False, stop=True), rd=[cmat, ll_], wr=[wk])
                S.op("pe", lambda e: e.matmul(wk.t[:, 128:134], lh_.t[:, :], sel.t[:, :], start=True, stop=False), rd=[sel, lh_], wr=[wk])
                S.op("pe", lambda e: e.matmul(wk.t[:, 128:134], ll_.t[:, :], sel.t[:, :], start=False, stop=True), rd=[sel, ll_], wr=[wk])
                esc_ = esc.next()
                S.op("act", lambda e: e.activation(out=E_.t[:, :], in_=wk.t[:, 0:128], func=AF.Exp), rd=[wk], wr=[E_])
                S.op("act", lambda e: e.activation(out=Ei_.t[:, :], in_=wk.t[:, 0:128], func=AF.Exp, scale=-1.0), rd=[wk], wr=[Ei_])
                S.op("act", lambda e: e.activation(out=esc_.t[:, :], in_=wk.t[:, 128:134], func=AF.Exp), rd=[wk], wr=[esc_])
                qh_, kh_, vv_, kT_, sm_, Sp0_, Sp1_, yb_ = (r.next() for r in (qh, kh, vv, kT, sm, Sp0, Sp1, yb))
                S.op("dve", lambda e: e.scalar_tensor_tensor(out=qh_.t[:, :], in0=q_ap, scalar=128.0 ** -0.5, in1=E_.t[:, :], op0=ALU.mult, op1=ALU.mult),
                     rd=[pj, E_], wr=[qh_])
                S.op("dve", lambda e: e.tensor_tensor(out=kh_.t[:, :], in0=kk_.t[:, :], in1=Ei_.t[:, :], op=ALU.mult), rd=[kk_, Ei_], wr=[kh_])
                S.op("dve", lambda e: e.tensor_copy(out=vv_.t[:, :], in_=vi_ap), rd=[pj], wr=[vv_])
                tq_ap, tq = psb_next()
                tk_ap, tk = psb_next()
                S.op("pe", lambda e: e.transpose(tq_ap, qh_.t[:, :], ident.t[:, :]), rd=[qh_, ident], wr=[tq])
                S.op("pe", lambda e: e.transpose(tk_ap, kh_.t[:, :], ident.t[:, :]), rd=[kh_, ident], wr=[tk])
                qA_, qB_ = qAr.next(), qBr.next()
                S.op("act", lambda e: e.copy(out=qA_.t[:, 0:64], in_=tq_ap[:, 0:64]), rd=[tq], wr=[qA_])
                S.op("dve", lambda e: e.tensor_copy(out=qB_.t[:, 64:128], in_=tq_ap[:, 64:128]), rd=[tq], wr=[qB_])
                S.op("act", lambda e: e.copy(out=kT_.t[:, :], in_=tk_ap), rd=[tk], wr=[kT_])
                wk2 = wkb.next()
                S.op("pe", lambda e: e.matmul(wk2.t[:, 0:128], kT_.t[:, :], qA_.t[:, :], start=True, stop=False), rd=[kT_, qA_], wr=[wk2])
                S.op("pe", lambda e: e.matmul(wk2.t[:, 0:128], kT_.t[:, :], qB_.t[:, :], start=False, stop=True), rd=[kT_, qB_], wr=[wk2])
                S.op("dve", lambda e: e.tensor_tensor(out=sm_.t[:, :], in0=wk2.t[:, 0:128], in1=u2.t[:, :], op=ALU.mult), rd=[wk2, u2], wr=[sm_])
                kA_, kB_ = kAr.next(), kBr.next()
                S.op("act", lambda e: e.copy(out=kA_.t[0:64, :], in_=kh_.t[0:64, :]), rd=[kh_], wr=[kA_])
                S.op("act", lambda e: e.copy(out=kB_.t[64:128, :], in_=kh_.t[64:128, :]), rd=[kh_], wr=[kB_])
                S.op("pe", lambda e: e.matmul(wk2.t[:, 128:256], kA_.t[:, :], vv_.t[:, :], start=True, stop=True), rd=[kA_, vv_], wr=[wk2])
                S.op("pe", lambda e: e.matmul(wk2.t[:, 256:384], kB_.t[:, :], vv_.t[:, :], start=True, stop=True), rd=[kB_, vv_], wr=[wk2])
                Sh = St[h]
                for c, Sp_ in ((0, Sp0_), (1, Sp1_)):
                    A_ = A.next()
                    S.op("dve", lambda e: e.tensor_scalar(out=Sp_.t[:, :], in0=Sh.t[:, :], scalar1=esc_.t[:, 3 * c:3 * c + 1], scalar2=None, op0=ALU.mult),
                         rd=[Sh, esc_], wr=[Sp_])
                    S.op("dve", lambda e: e.tensor_scalar(out=A_.t[:, :], in0=Sh.t[:, :], scalar1=esc_.t[:, 3 * c + 2:3 * c + 3], scalar2=None, op0=ALU.mult),
                         rd=[Sh, esc_], wr=[A_])
                    S.op("dve", lambda e: e.scalar_tensor_tensor(out=Sh.t[:, :], in0=wk2.t[:, 128 * (c + 1):128 * (c + 2)],
                                                                 scalar=esc_.t[:, 3 * c + 1:3 * c + 2], in1=A_.t[:, :], op0=ALU.mult, op1=ALU.add),
                         rd=[wk2, esc_, A_], wr=[Sh])
                S.op("pe", lambda e: e.matmul(wk2.t[:, 384:512], sm_.t[:, :], vv_.t[:, :], start=True, stop=False), rd=[sm_, vv_], wr=[wk2])
                S.op("pe", lambda e: e.matmul(wk2.t[:, 384:512], qA_.t[:, :], Sp0_.t[:, :], start=False, stop=False), rd=[qA_, Sp0_], wr=[wk2])
                S.op("pe", lambda e: e.matmul(wk2.t[:, 384:512], qB_.t[:, :], Sp1_.t[:, :], start=False, stop=True), rd=[qB_, Sp1_], wr=[wk2])
                o_ap = wk2.t[:, 384:512]
                ss_, lnv_, rs_, junk_ = ss.next(), lnv.next(), rs.next(), junk.next()
                S.op("dve", lambda e: e.memset(ss_.t[:, :], 0.0), wr=[ss_])
                S.op("act", lambda e: e.activation(out=junk_.t[:, :], in_=o_ap, func=AF.Square, accum_out=ss_.t[:, 0:1]), rd=[wk2, ss_], wr=[junk_, ss_])
                emit_rstd(S, rs_.t[:, :], rs_, ss_.t[:, :], ss_, 1.0 / 128, lnv_.t[:, :], lnv_)
                S.op("dve", lambda e: e.tensor_scalar(out=eg_.t[:, :], in0=eg_.t[:, :], scalar1=1.0, scalar2=None, op0=ALU.add), rd=[eg_], wr=[eg_])
                S.op("dve", lambda e: e.reciprocal(out=eg_.t[:, :], in_=eg_.t[:, :]), rd=[eg_], wr=[eg_])
                S.op("dve", lambda e: e.tensor_tensor(out=eg_.t[:, :], in0=eg_.t[:, :], in1=gt_ap, op=ALU.mult), rd=[eg_, pj], wr=[eg_])
                S.op("dve", lambda e: e.tensor_tensor(out=eg_.t[:, :], in0=eg_.t[:, :], in1=ogain.t[:, :], op=ALU.mult), rd=[eg_, ogain], wr=[eg_])
                S.op("dve", lambda e: e.scalar_tensor_tensor(out=yb_.t[:, :], in0=o_ap, scalar=rs_.t[:, 0:1], in1=eg_.t[:, :], op0=ALU.mult, op1=ALU.mult),
                     rd=[wk2, rs_, eg_], wr=[yb_])
                ty_ap, ty = psb_next()
                S.op("pe", lambda e: e.transpose(ty_ap, yb_.t[:, :], ident.t[:, :]), rd=[yb_, ident], wr=[ty])
                S.op("act", lambda e: e.copy(out=o_s.t[:, h, :], in_=ty_ap), rd=[ty], wr=[o_s])
            S.op("sp", lambda e: e.dma_start(out=o_d[0:NH * 128, t * 128:(t + 1) * 128].rearrange("(h p) t -> p h t", p=128), in_=o_s.t[:, :, :]),
                 rd=[o_s], dsem=o_s.b.name)
        S.barrier()


def attn_tile(C, t, qk_parts, v_fn, acc, acc_ap, scale, far_bias, near, negshift, P, tmpf, qkb):
    S = C.S
    np_ = len(qk_parts)
    groups = [list(range(g0, min(g0 + 4, t + 1))) for g0 in range(0, t + 1, 4)]

    def emit_qk(grp):
        bank = qkb.next()
        for j, kt in enumerate(grp):
            for pi, (K_fn, q_ap, q_T) in enumerate(qk_parts):
                k_ap, k_T = K_fn(kt)
                S.op("pe", lambda e: e.matmul(bank.t[:, j * 128:(j + 1) * 128], k_ap, q_ap, start=(pi == 0), stop=(pi == np_ - 1)),
                     rd=[k_T, q_T], wr=[bank])
        return bank

    def emit_exp(grp, bank):
        P_ = P.next()
        nfar = len([kt for kt in grp if (kt - t) not in near])
        if nfar:
            S.op("act", lambda e: e.activation(out=P_.t[:, 0:nfar * 128], in_=bank.t[:, 0:nfar * 128], func=AF.Exp, scale=scale, bias=far_bias.t[:, :]),
                 rd=[bank, far_bias], wr=[P_])
        for j, kt in enumerate(grp):
            if (kt - t) in near:
                b = near[kt - t]
                tm = tmpf.next()
                S.op("dve", lambda e: e.scalar_tensor_tensor(out=tm.t[:, :], in0=bank.t[:, j * 128:(j + 1) * 128], scalar=scale, in1=b.t[:, :],
                                                             op0=ALU.mult, op1=ALU.add), rd=[bank, b], wr=[tm])
                S.op("act", lambda e: e.activation(out=P_.t[:, j * 128:(j + 1) * 128], in_=tm.t[:, :], func=AF.Exp, bias=negshift.t[:, :]),
                     rd=[tm, negshift], wr=[P_])
        return P_

    def emit_pv(grp, P_):
        for j, kt in enumerate(grp):
            v_ap, v_T = v_fn(kt)
            S.op("pe", lambda e: e.matmul(acc_ap, P_.t[:, j * 128:(j + 1) * 128], v_ap, start=(kt == 0), stop=(kt == t)), rd=[P_, v_T], wr=[acc])

    look = len(qkb.items) >= 2
    bank = emit_qk(groups[0])
    yield
    for gi, grp in enumerate(groups):
        nxt = None
        if look and gi + 1 < len(groups):
            nxt = emit_qk(groups[gi + 1])
            yield
        P_ = emit_exp(grp, bank)
        yield
        emit_pv(grp, P_)
        yield
        if not look and gi + 1 < len(groups):
            nxt = emit_qk(groups[gi + 1])
            yield
        bank = nxt


def run_interleaved(gens):
    gens = list(gens)
    while gens:
        for g in list(gens):
            try:
                next(g)
            except StopIteration:
                gens.remove(g)


def sub_T(parent, ap):
    x = T(ap)
    x.b = parent.b
    return x


VW = 132


def phase_df(C, xn_d, w_d, vecs_d, bias_d, o_d, consts, mask_d, NT, pfx, NH=2, row0=384):
    nc, S = C.nc, C.S
    SHIFT = 8.0
    S.barrier()
    with ExitStack() as st:
        sb = lambda n, sh, dt: T(st.enter_context(nc.sbuf_tensor(pfx + n + C.sfx, sh, dt)), pfx + n)
        R = lambda n, sh, dt, k=2: Ring([sb("%s%d" % (n, i), sh, dt) for i in range(k)])
        w = sb("w", [128, KD, NH * 384], BF16)
        for h in range(NH):
            S.op("pool", lambda e: e.dma_start(out=w.t[:, :, h * 384:(h + 1) * 384],
                                               in_=w_d[:, h * 384:(h + 1) * 384].rearrange("(k p) c -> p k c", p=128)), wr=[w], dsem=pfx + "w")
        cf = load_const(C, st, pfx + "cf", consts, [128, 512], F32)
        cb = sb("cb", [128, 512], BF16)
        S.op("dve", lambda e: e.tensor_copy(out=cb.t[:, :], in_=cf.t[:, :]), rd=[cf], wr=[cb])
        ident = sub_T(cb, cb.t[:, 0:128])
        mask = load_const(C, st, pfx + "mask", mask_d, [128, 128], F32)
        vecs = sb("vecs", [128, 8, 128], F32)
        for j in range(8):
            S.op("sp", lambda e: e.dma_start(out=vecs.t[:, j, :], in_=vecs_d[j, :].partition_broadcast(128)), wr=[vecs], dsem=pfx + "vecs")
        qg, kg = vecs.t[:, 0, 0:64], vecs.t[:, 1, 0:64]
        junk64 = sb("junk64", [128, 64], F32)
        prod = sb("prod", [128, 64], F32)
        lsum = sb("lsum", [128, 2], F32)
        S.op("dve", lambda e: e.memset(lsum.t[:, :], 0.0), wr=[lsum])
        for i in range(2):
            S.op("dve", lambda e: e.tensor_tensor(out=prod.t[:, :], in0=vecs.t[:, 2 + 2 * i, 0:64], in1=vecs.t[:, 3 + 2 * i, 0:64], op=ALU.mult), rd=[vecs], wr=[prod])
            S.op("act", lambda e: e.activation(out=junk64.t[:, :], in_=prod.t[:, :], func=AF.Identity, accum_out=lsum.t[:, i:i + 1]), rd=[prod, lsum], wr=[junk64, lsum])
        S.op("act", lambda e: e.activation(out=lsum.t[:, :], in_=lsum.t[:, :], func=AF.Exp), rd=[lsum], wr=[lsum])
        lam = sb("lam", [128, 1], F32)
        S.op("dve", lambda e: e.tensor_tensor(out=lam.t[:, :], in0=lsum.t[:, 0:1], in1=lsum.t[:, 1:2], op=ALU.subtract), rd=[lsum], wr=[lam])
        S.op("dve", lambda e: e.tensor_tensor(out=lam.t[:, :], in0=lam.t[:, :], in1=vecs.t[:, 7, 0:1], op=ALU.add), rd=[lam, vecs], wr=[lam])
        sgain = sb("sgain", [128, 128], F32)
        S.op("dve", lambda e: e.tensor_scalar(out=sgain.t[:, :], in0=vecs.t[:, 6, :], scalar1=vecs.t[:, 7, 1:2], scalar2=None, op0=ALU.mult), rd=[vecs], wr=[sgain])
        negshift = sb("negshift", [128, 1], F32)
        S.op("dve", lambda e: e.memset(negshift.t[:, :], -SHIFT), wr=[negshift])
        farb = [sb("farb%d" % h, [128, 1], F32) for h in range(NH)]
        for h in range(NH):
            S.op("dve", lambda e: e.tensor_scalar(out=farb[h].t[:, :], in0=vecs.t[:, 7, 2 + h:3 + h], scalar1=-SHIFT, scalar2=None, op0=ALU.add), rd=[vecs], wr=[farb[h]])
        bt = [[sb("bt%d_%d" % (h, r), [128, 128], F32) for r in range(2)] for h in range(NH)]
        for h in range(NH):
            for r in range(2):
                S.op("sp", lambda e: e.dma_start(out=bt[h][r].t[:, :], in_=bias_d[h, r, :, :]), wr=[bt[h][r]], dsem=pfx + "bt%d_%d" % (h, r))
            S.op("dve", lambda e: e.tensor_tensor(out=bt[h][0].t[:, :], in0=bt[h][0].t[:, :], in1=mask.t[:, :], op=ALU.add), rd=[bt[h][0], mask], wr=[bt[h][0]])
        KT = [[sb("KT%d_%d" % (h, m), [128, NT * 128], BF16) for m in range(2)] for h in range(NH)]
        for h in range(NH):
            for m in range(2):
                S.op("dve", lambda e: e.memset(KT[h][m].t[:, :], 0.0), wr=[KT[h][m]])
        Va = sb("Va", [128, NT * NH * VW], BF16)
        S.op("dve", lambda e: e.memset(Va.t[:, :], 1.0), wr=[Va])
        vofs = lambda t_, h_: (t_ * NH + h_) * VW
        xt = R("xt", [128, KD, 128], BF16)
        ss4, ln4, rs4 = (R(n, [128, 4], F32, 3) for n in ("ss4", "ln4", "rs4"))
        junk = R("junk", [128, 128], F32)
        qk = R("qk", [128, 256], BF16, 3)
        qT = [R("qT%d" % h, [128, 128], BF16) for h in range(NH)]
        P = R("P", [128, 512], BF16, 3)
        tmpf = R("tmpf", [128, 128], F32, 3)
        rr = R("rr", [128, 4], F32, 3)
        t2, of = R("t2", [128, 128], F32), R("of", [128, 128], F32)
        ss1, ln1, rs1 = (R(n, [128, 1], F32) for n in ("ss1", "ln1", "rs1"))
        yb = R("yb", [128, 128], BF16)
        ost = R("ost", [128, NH, 128], BF16)
        pjb = C.banks([0, 1])
        qkbs = [Ring(C.banks([0, 1])), Ring(C.banks([2, 3]))]
        Ps = [R("Ps%d_" % m, [128, 512], BF16, 2) for m in range(2)]
        tmpfs = [R("tmpfs%d_" % m, [128, 128], F32, 2) for m in range(2)]
        accs = C.banks([5, 6])
        pbi = [0]

        def psb_next():
            i = pbi[0] % 8
            pbi[0] += 1
            return psb_region(C, i)

        for t in range(NT):
            x_s = xt.next()
            S.op("sp", lambda e: e.dma_start(out=x_s.t[:, :, :], in_=xn_d[t, :, :, :]), wr=[x_s], dsem=x_s.b.name)
            o_s = ost.next()
            qTs = []
            for h in range(NH):
                pj = pjb[h]
                for k in range(KD):
                    S.op("pe", lambda e: e.matmul(pj.t[:, 0:384], x_s.t[:, k, :], w.t[:, k, h * 384:(h + 1) * 384], start=(k == 0), stop=(k == KD - 1)),
                         rd=[x_s, w], wr=[pj])
                ss_, ln_, rs_ = ss4.next(), ln4.next(), rs4.next()
                S.op("dve", lambda e: e.memset(ss_.t[:, :], 0.0), wr=[ss_])
                for j in range(4):
                    jk = junk.next()
                    S.op("act", lambda e: e.activation(out=jk.t[:, 0:64], in_=pj.t[:, j * 64:(j + 1) * 64], func=AF.Square, accum_out=ss_.t[:, j:j + 1]),
                         rd=[pj, ss_], wr=[jk, ss_])
                emit_rstd(S, rs_.t[:, :], rs_, ss_.t[:, :], ss_, 1.0 / 64, ln_.t[:, :], ln_)
                qk_ = qk.next()
                for j in range(4):
                    g_ap = qg if j < 2 else kg
                    S.op("dve", lambda e: e.scalar_tensor_tensor(out=qk_.t[:, j * 64:(j + 1) * 64], in0=pj.t[:, j * 64:(j + 1) * 64], scalar=rs_.t[:, j:j + 1],
                                                                 in1=g_ap, op0=ALU.mult, op1=ALU.mult), rd=[pj, rs_, vecs], wr=[qk_])
                S.op("dve", lambda e: e.tensor_copy(out=Va.t[:, vofs(t, h):vofs(t, h) + 128], in_=pj.t[:, 256:384]), rd=[pj], wr=[Va])
                tq_ap, tq = psb_next()
                tk_ap, tk = psb_next()
                S.op("pe", lambda e: e.transpose(tq_ap, qk_.t[:, 0:128], ident.t), rd=[qk_, ident], wr=[tq])
                S.op("pe", lambda e: e.transpose(tk_ap, qk_.t[:, 128:256], ident.t), rd=[qk_, ident], wr=[tk])
                qT_ = qT[h].next()
                S.op("act", lambda e: e.copy(out=qT_.t[:, :], in_=tq_ap), rd=[tq], wr=[qT_])
                S.op("act", lambda e: e.copy(out=KT[h][0].t[0:64, t * 128:(t + 1) * 128], in_=tk_ap[0:64, :]), rd=[tk], wr=[KT[h][0]])
                S.op("act", lambda e: e.copy(out=KT[h][1].t[64:128, t * 128:(t + 1) * 128], in_=tk_ap[64:128, :]), rd=[tk], wr=[KT[h][1]])
                qTs.append(qT_)
            for h in range(NH):
                acc0, acc1 = accs
                run_interleaved([
                    attn_tile(C, t, [(lambda kt, h=h, m=m: (KT[h][m].t[:, kt * 128:(kt + 1) * 128], KT[h][m]), qTs[h].t[:, :], qTs[h])],
                              lambda kt, h=h: (Va.t[:, vofs(kt, h):vofs(kt, h) + 129], Va), accs[m], accs[m].t[:, 0:129], 0.125, farb[h],
                              {0: bt[h][0], -1: bt[h][1]}, negshift, Ps[m], tmpfs[m], qkbs[m])
                    for m in range(2)])
                r_ = rr.next()
                S.op("dve", lambda e: e.reciprocal(out=r_.t[:, 0:1], in_=acc0.t[:, 128:129]), rd=[acc0], wr=[r_])
                S.op("dve", lambda e: e.reciprocal(out=r_.t[:, 1:2], in_=acc1.t[:, 128:129]), rd=[acc1], wr=[r_])
                S.op("dve", lambda e: e.tensor_tensor(out=r_.t[:, 2:3], in0=r_.t[:, 1:2], in1=lam.t[:, :], op=ALU.mult), rd=[r_, lam], wr=[r_])
                t2_, of_ = t2.next(), of.next()
                S.op("dve", lambda e: e.tensor_scalar(out=t2_.t[:, :], in0=acc1.t[:, 0:128], scalar1=r_.t[:, 2:3], scalar2=None, op0=ALU.mult), rd=[acc1, r_], wr=[t2_])
                S.op("dve", lambda e: e.scalar_tensor_tensor(out=of_.t[:, :], in0=acc0.t[:, 0:128], scalar=r_.t[:, 0:1], in1=t2_.t[:, :],
                                                             op0=ALU.mult, op1=ALU.subtract), rd=[acc0, r_, t2_], wr=[of_])
                s1, l1, r1, jk = ss1.next(), ln1.next(), rs1.next(), junk.next()
                S.op("dve", lambda e: e.memset(s1.t[:, :], 0.0), wr=[s1])
                S.op("act", lambda e: e.activation(out=jk.t[:, :], in_=of_.t[:, :], func=AF.Square, accum_out=s1.t[:, 0:1]), rd=[of_, s1], wr=[jk, s1])
                emit_rstd(S, r1.t[:, :], r1, s1.t[:, :], s1, 1.0 / 128, l1.t[:, :], l1)
                yb_ = yb.next()
                S.op("dve", lambda e: e.scalar_tensor_tensor(out=yb_.t[:, :], in0=of_.t[:, :], scalar=r1.t[:, 0:1], in1=sgain.t[:, :],
                                                             op0=ALU.mult, op1=ALU.mult), rd=[of_, r1, sgain], wr=[yb_])
                ty_ap, ty = psb_next()
                S.op("pe", lambda e: e.transpose(ty_ap, yb_.t[:, :], ident.t), rd=[yb_, ident], wr=[ty])
                S.op("act", lambda e: e.copy(out=o_s.t[:, h, :], in_=ty_ap), rd=[ty], wr=[o_s])
            S.op("sp", lambda e: e.dma_start(out=o_d[row0:row0 + NH * 128, t * 128:(t + 1) * 128].rearrange("(h p) t -> p h t", p=128), in_=o_s.t[:, :, :]),
                 rd=[o_s], dsem=o_s.b.name)
        S.barrier()


def phase_ml(C, xn_d, w_d, wuq_d, wukv_d, vecs_d, cs_d, o_d, consts, mask_d, NT, pfx, NH=3, row0=640):
    nc, S = C.nc, C.S
    SHIFT = 14.0
    SCALE = 192.0 ** -0.5
    S.barrier()
    with ExitStack() as st:
        sb = lambda n, sh, dt: T(st.enter_context(nc.sbuf_tensor(pfx + n + C.sfx, sh, dt)), pfx + n)
        R = lambda n, sh, dt, k=2: Ring([sb("%s%d" % (n, i), sh, dt) for i in range(k)])
        w = sb("w", [128, KD, 832], BF16)
        for (c0, c1) in ((0, 512), (512, 832)):
            S.op("pool", lambda e: e.dma_start(out=w.t[:, :, c0:c1], in_=w_d[:, c0:c1].rearrange("(k p) c -> p k c", p=128)), wr=[w], dsem=pfx + "w")
        wuq = sb("wuq", [128, 4, NH * 192], BF16)
        S.op("pool", lambda e: e.dma_start(out=wuq.t[:, :, :], in_=wuq_d.rearrange("(k p) c -> p k c", p=128)), wr=[wuq], dsem=pfx + "wuq")
        wukv = sb("wukv", [128, 2, NH * 256], BF16)
        S.op("pool", lambda e: e.dma_start(out=wukv.t[:, :, :], in_=wukv_d.rearrange("(k p) c -> p k c", p=128)), wr=[wukv], dsem=pfx + "wukv")
        cf = load_const(C, st, pfx + "cf", consts, [128, 512], F32)
        cb = sb("cb", [128, 512], BF16)
        S.op("dve", lambda e: e.tensor_copy(out=cb.t[:, :], in_=cf.t[:, :]), rd=[cf], wr=[cb])
        ident = sub_T(cb, cb.t[:, 0:128])
        mask = load_const(C, st, pfx + "mask", mask_d, [128, 128], F32)
        vecs = sb("vecs", [128, 4, 512], F32)
        for j in range(4):
            S.op("sp", lambda e: e.dma_start(out=vecs.t[:, j, :], in_=vecs_d[j, :].partition_broadcast(128)), wr=[vecs], dsem=pfx + "vecs")
        cs = load_const(C, st, pfx + "cs", cs_d, [128, 2 * NT * 32], F32)
        negshift = sb("negshift", [128, 1], F32)
        S.op("dve", lambda e: e.memset(negshift.t[:, :], -SHIFT), wr=[negshift])
        KTa = [sb("KTa%d" % h, [128, NT * 128], BF16) for h in range(NH)]
        KTb = [sb("KTb%d" % h, [128, NT * 128], BF16) for h in range(NH)]
        Va = sb("Va", [128, NT * NH * VW], BF16)
        S.op("dve", lambda e: e.memset(Va.t[:, :], 1.0), wr=[Va])
        vofs = lambda t_, h_: (t_ * NH + h_) * VW
        xt = R("xt", [128, KD, 128], BF16)
        ss2, ln2, rs2 = (R(n, [128, 4], F32) for n in ("ss2", "ln2", "rs2"))
        junk = R("junk", [128, 512], F32)
        cqn = R("cqn", [128, 512], BF16)
        ckvn = R("ckvn", [128, 256], BF16)
        kr = R("kr", [128, 64], F32)
        cqT = R("cqT", [128, 4, 128], BF16)
        ckvT = R("ckvT", [128, 2, 128], BF16)
        ssh, lnh, rsh = (R(n, [128, 4], F32, 3) for n in ("ssh", "lnh", "rsh"))
        qn_b = R("qnb", [128, 128], BF16, 3)
        kn_b = R("knb", [128, 128], BF16, 3)
        qr_f = R("qrf", [128, 64], F32, 3)
        kr_f = R("krf", [128, 64], F32, 3)
        ra, rb = R("ra", [128, 32], F32, 3), R("rb", [128, 32], F32, 3)
        qrp = [sb("qrp%d" % i, [128, 128], BF16) for i in range(2)]
        krp = [sb("krp%d" % i, [128, 128], BF16) for i in range(2)]
        for i in range(2):
            S.op("dve", lambda e: e.memset(qrp[i].t[:, :], 0.0), wr=[qrp[i]])
            S.op("dve", lambda e: e.memset(krp[i].t[:, :], 0.0), wr=[krp[i]])
        qrpr, krpr = Ring(qrp), Ring(krp)
        qTa = [R("qTa%d" % h, [128, 128], BF16) for h in range(NH)]
        qTb = [R("qTb%d" % h, [128, 128], BF16) for h in range(NH)]
        P = R("P", [128, 512], BF16, 3)
        tmpf = R("tmpf", [128, 128], F32, 3)
        rr = R("rr", [128, 1], F32, 3)
        yb = R("yb", [128, 128], BF16)
        ost = R("ost", [128, NH, 128], BF16)
        b0, b1, b2, b3, b4 = C.banks([0, 1, 2, 3, 4])
        qreg = [(b2, 0), (b2, 192), (b3, 0)]
        kvreg = [(b4, 0), (b4, 256), (b3, 192)]
        qkbs = [Ring(C.banks([h])) for h in range(NH)]
        accs = C.banks([3, 4, 5])
        Ps = [R("Ps%d_" % h, [128, 512], BF16, 2) for h in range(NH)]
        tmpfs = [R("tmpfs%d_" % h, [128, 128], F32, 2) for h in range(NH)]
        accT = C.banks([6])[0]
        pbi = [0]

        def psb_next():
            i = pbi[0] % 8
            pbi[0] += 1
            return psb_region(C, i)

        def transpose_to(src_ap, src_T, dst_ap, dst_T, eng="act"):
            tp_ap, tp = psb_next()
            S.op("pe", lambda e: e.transpose(tp_ap, src_ap, ident.t), rd=[src_T, ident], wr=[tp])
            if eng == "act":
                S.op("act", lambda e: e.copy(out=dst_ap, in_=tp_ap), rd=[tp], wr=[dst_T])
            else:
                S.op("dve", lambda e: e.tensor_copy(out=dst_ap, in_=tp_ap), rd=[tp], wr=[dst_T])

        def rope(src_T, src, dst_T, dst, t):
            cos = cs.t[:, t * 32:(t + 1) * 32]
            sin = cs.t[:, NT * 32 + t * 32:NT * 32 + (t + 1) * 32]
            x1, x2 = src[:, 0:32], src[:, 32:64]
            a, b = ra.next(), rb.next()
            S.op("dve", lambda e: e.tensor_tensor(out=a.t[:, :], in0=x1, in1=cos, op=ALU.mult), rd=[src_T, cs], wr=[a])
            S.op("dve", lambda e: e.tensor_tensor(out=b.t[:, :], in0=x2, in1=sin, op=ALU.mult), rd=[src_T, cs], wr=[b])
            S.op("dve", lambda e: e.tensor_tensor(out=dst[:, 0:32], in0=a.t[:, :], in1=b.t[:, :], op=ALU.subtract), rd=[a, b], wr=[dst_T])
            a, b = ra.next(), rb.next()
            S.op("dve", lambda e: e.tensor_tensor(out=a.t[:, :], in0=x2, in1=cos, op=ALU.mult), rd=[src_T, cs], wr=[a])
            S.op("dve", lambda e: e.tensor_tensor(out=b.t[:, :], in0=x1, in1=sin, op=ALU.mult), rd=[src_T, cs], wr=[b])
            S.op("dve", lambda e: e.tensor_tensor(out=dst[:, 32:64], in0=a.t[:, :], in1=b.t[:, :], op=ALU.add), rd=[a, b], wr=[dst_T])

        for t in range(NT):
            x_s = xt.next()
            S.op("sp", lambda e: e.dma_start(out=x_s.t[:, :, :], in_=xn_d[t, :, :, :]), wr=[x_s], dsem=x_s.b.name)
            o_s = ost.next()
            for k in range(KD):
                S.op("pe", lambda e: e.matmul(b0.t[:, :], x_s.t[:, k, :], w.t[:, k, 0:512], start=(k == 0), stop=(k == KD - 1)), rd=[x_s, w], wr=[b0])
            for k in range(KD):
                S.op("pe", lambda e: e.matmul(b1.t[:, 0:320], x_s.t[:, k, :], w.t[:, k, 512:832], start=(k == 0), stop=(k == KD - 1)), rd=[x_s, w], wr=[b1])
            ss_, ln_, rs_ = ss2.next(), ln2.next(), rs2.next()
            S.op("dve", lambda e: e.memset(ss_.t[:, :], 0.0), wr=[ss_])
            jk = junk.next()
            S.op("act", lambda e: e.activation(out=jk.t[:, :], in_=b0.t[:, :], func=AF.Square, accum_out=ss_.t[:, 0:1]), rd=[b0, ss_], wr=[jk, ss_])
            jk = junk.next()
            S.op("act", lambda e: e.activation(out=jk.t[:, 0:256], in_=b1.t[:, 0:256], func=AF.Square, accum_out=ss_.t[:, 1:2]), rd=[b1, ss_], wr=[jk, ss_])
            kr_ = kr.next()
            S.op("act", lambda e: e.copy(out=kr_.t[:, :], in_=b1.t[:, 256:320]), rd=[b1], wr=[kr_])
            jk = junk.next()
            S.op("act", lambda e: e.activation(out=jk.t[:, 0:64], in_=kr_.t[:, :], func=AF.Square, accum_out=ss_.t[:, 2:3]), rd=[kr_, ss_], wr=[jk, ss_])
            S.op("dve", lambda e: e.tensor_scalar(out=ss_.t[:, 0:1], in0=ss_.t[:, 0:1], scalar1=0.5, scalar2=None, op0=ALU.mult), rd=[ss_], wr=[ss_])
            emit_rstd(S, rs_.t[:, 0:2], rs_, ss_.t[:, 0:2], ss_, 1.0 / 256, ln_.t[:, 0:2], ln_)
            cqn_, ckvn_ = cqn.next(), ckvn.next()
            S.op("dve", lambda e: e.scalar_tensor_tensor(out=cqn_.t[:, :], in0=b0.t[:, :], scalar=rs_.t[:, 0:1], in1=vecs.t[:, 0, :], op0=ALU.mult, op1=ALU.mult),
                 rd=[b0, rs_, vecs], wr=[cqn_])
            S.op("dve", lambda e: e.scalar_tensor_tensor(out=ckvn_.t[:, :], in0=b1.t[:, 0:256], scalar=rs_.t[:, 1:2], in1=vecs.t[:, 1, 0:256], op0=ALU.mult, op1=ALU.mult),
                 rd=[b1, rs_, vecs], wr=[ckvn_])
            cqT_, ckvT_ = cqT.next(), ckvT.next()
            for r in range(4):
                transpose_to(cqn_.t[:, r * 128:(r + 1) * 128], cqn_, cqT_.t[:, r, :], cqT_, "act" if r % 2 == 0 else "dve")
            for r in range(2):
                transpose_to(ckvn_.t[:, r * 128:(r + 1) * 128], ckvn_, ckvT_.t[:, r, :], ckvT_, "act" if r % 2 == 0 else "dve")
            for h in range(NH):
                qb, qo = qreg[h]
                for r in range(4):
                    S.op("pe", lambda e: e.matmul(qb.t[:, qo:qo + 192], cqT_.t[:, r, :], wuq.t[:, r, h * 192:(h + 1) * 192], start=(r == 0), stop=(r == 3)),
                         rd=[cqT_, wuq], wr=[qb])
                kb, ko = kvreg[h]
                for r in range(2):
                    S.op("pe", lambda e: e.matmul(kb.t[:, ko:ko + 256], ckvT_.t[:, r, :], wukv.t[:, r, h * 256:(h + 1) * 256], start=(r == 0), stop=(r == 1)),
                         rd=[ckvT_, wukv], wr=[kb])
            qTs = []
            for h in range(NH):
                qb, qo = qreg[h]
                kb, ko = kvreg[h]
                sh_, lh_, rh_ = ssh.next(), lnh.next(), rsh.next()
                S.op("dve", lambda e: e.memset(sh_.t[:, :], 0.0), wr=[sh_])
                jk = junk.next()
                S.op("act", lambda e: e.activation(out=jk.t[:, 0:192], in_=qb.t[:, qo:qo + 192], func=AF.Square, accum_out=sh_.t[:, 0:1]), rd=[qb, sh_], wr=[jk, sh_])
                jk = junk.next()
                S.op("act", lambda e: e.activation(out=jk.t[:, 0:128], in_=kb.t[:, ko:ko + 128], func=AF.Square, accum_out=sh_.t[:, 1:2]), rd=[kb, sh_], wr=[jk, sh_])
                S.op("dve", lambda e: e.tensor_tensor(out=sh_.t[:, 1:2], in0=sh_.t[:, 1:2], in1=ss_.t[:, 2:3], op=ALU.add), rd=[sh_, ss_], wr=[sh_])
                emit_rstd(S, rh_.t[:, 0:2], rh_, sh_.t[:, 0:2], sh_, 1.0 / 192, lh_.t[:, 0:2], lh_)
                qn_, kn_, qr_, kf_ = qn_b.next(), kn_b.next(), qr_f.next(), kr_f.next()
                S.op("dve", lambda e: e.scalar_tensor_tensor(out=qn_.t[:, :], in0=qb.t[:, qo:qo + 128], scalar=rh_.t[:, 0:1], in1=vecs.t[:, 2, 0:128], op0=ALU.mult, op1=ALU.mult),
                     rd=[qb, rh_, vecs], wr=[qn_])
                S.op("dve", lambda e: e.scalar_tensor_tensor(out=qr_.t[:, :], in0=qb.t[:, qo + 128:qo + 192], scalar=rh_.t[:, 0:1], in1=vecs.t[:, 2, 128:192], op0=ALU.mult, op1=ALU.mult),
                     rd=[qb, rh_, vecs], wr=[qr_])
                S.op("dve", lambda e: e.scalar_tensor_tensor(out=kn_.t[:, :], in0=kb.t[:, ko:ko + 128], scalar=rh_.t[:, 1:2], in1=vecs.t[:, 3, 0:128], op0=ALU.mult, op1=ALU.mult),
                     rd=[kb, rh_, vecs], wr=[kn_])
                S.op("dve", lambda e: e.scalar_tensor_tensor(out=kf_.t[:, :], in0=kr_.t[:, :], scalar=rh_.t[:, 1:2], in1=vecs.t[:, 3, 128:192], op0=ALU.mult, op1=ALU.mult),
                     rd=[kr_, rh_, vecs], wr=[kf_])
                S.op("act", lambda e: e.copy(out=Va.t[:, vofs(t, h):vofs(t, h) + 128], in_=kb.t[:, ko + 128:ko + 256]), rd=[kb], wr=[Va])
                qp_, kp_ = qrpr.next(), krpr.next()
                rope(qr_, qr_.t, qp_, qp_.t, t)
                rope(kf_, kf_.t, kp_, kp_.t, t)
                qa_, qb_ = qTa[h].next(), qTb[h].next()
                transpose_to(qn_.t[:, :], qn_, qa_.t[:, :], qa_, "act")
                transpose_to(qp_.t[:, :], qp_, qb_.t[:, :], qb_, "dve")
                transpose_to(kn_.t[:, :], kn_, KTa[h].t[:, t * 128:(t + 1) * 128], KTa[h], "act")
                transpose_to(kp_.t[:, :], kp_, KTb[h].t[:, t * 128:(t + 1) * 128], KTb[h], "dve")
                qTs.append((qa_, qb_))
            run_interleaved([
                attn_tile(C, t, [(lambda kt, h=h: (KTa[h].t[:, kt * 128:(kt + 1) * 128], KTa[h]), qTs[h][0].t[:, :], qTs[h][0]),
                                 (lambda kt, h=h: (KTb[h].t[:, kt * 128:(kt + 1) * 128], KTb[h]), qTs[h][1].t[:, :], qTs[h][1])],
                          lambda kt, h=h: (Va.t[:, vofs(kt, h):vofs(kt, h) + 129], Va), accs[h], accs[h].t[:, 0:129], SCALE, negshift,
                          {0: mask}, negshift, Ps[h], tmpfs[h], qkbs[h])
                for h in range(NH)])
            for h in range(NH):
                a0 = 0
                acc = accs[h]
                r_ = rr.next()
                S.op("dve", lambda e: e.reciprocal(out=r_.t[:, 0:1], in_=acc.t[:, a0 + 128:a0 + 129]), rd=[acc], wr=[r_])
                yb_ = yb.next()
                S.op("dve", lambda e: e.tensor_scalar(out=yb_.t[:, :], in0=acc.t[:, a0:a0 + 128], scalar1=r_.t[:, 0:1], scalar2=None, op0=ALU.mult), rd=[acc, r_], wr=[yb_])
                transpose_to(yb_.t[:, :], yb_, o_s.t[:, h, :], o_s, "act")
            S.op("sp", lambda e: e.dma_start(out=o_d[row0:row0 + NH * 128, t * 128:(t + 1) * 128].rearrange("(h p) t -> p h t", p=128), in_=o_s.t[:, :, :]),
                 rd=[o_s], dsem=o_s.b.name)
        S.barrier()


NTOK = 2048
SEQ = 4096
NT_SEQ = SEQ // 128
TT_FFN = 1024


def _ffn_inputs(nc, sfx):
    g = nc.dram_tensor("g" + sfx, [D], F32, kind="ExternalInput").ap()
    wg = nc.dram_tensor("wg" + sfx, [D, DFF], F32, kind="ExternalInput").ap()
    wu = nc.dram_tensor("wu" + sfx, [D, DFF], F32, kind="ExternalInput").ap()
    wd = nc.dram_tensor("wd" + sfx, [DFF, D], F32, kind="ExternalInput").ap()
    return g, wg, wu, wd


def build_PA():
    nc = bass.Bass("TRN2", target_bir_lowering=False)
    x = nc.dram_tensor("x", [D, NTOK], F32, kind="ExternalInput").ap()
    g, wg, wu, wd = _ffn_inputs(nc, "")
    gm = nc.dram_tensor("gm", [D], F32, kind="ExternalInput").ap()
    xo = nc.dram_tensor("xo", [D, NTOK], F32, kind="ExternalOutput").ap()
    xn = nc.dram_tensor("xn", [NTOK // 128, 128, KD, 128], BF16, kind="ExternalOutput").ap()
    with ExitStack() as st:
        C = Ctx(nc, st)
        phase_ffn(C, x, xo, g, wg, wu, wd, NTOK, TT_FFN, "A")
        phase_norm(C, xo, gm, xn, NTOK, 512, "N")
        C.S.barrier()
    return nc


def build_PM():
    nc = bass.Bass("TRN2", target_bir_lowering=False)
    dt = lambda n, sh, d=F32: nc.dram_tensor(n, sh, d, kind="ExternalInput").ap()
    xn = dt("xn", [NT_SEQ, 128, KD, 128], BF16)
    whg, wdf, wml = dt("whg", [D, 1536]), dt("wdf", [D, 768]), dt("wml", [D, 832])
    lbz, cm, og = dt("lbz", [4, 384]), dt("cm", [16]), dt("og", [128])
    dfv, dfb = dt("dfv", [8, 128]), dt("dfb", [2, 2, 128, 128])
    wuq, wukv, mlv = dt("wuq", [512, 576]), dt("wukv", [256, 768]), dt("mlv", [4, 512])
    cs = dt("cs", [128, 2 * NT_SEQ * 32])
    call, cmask = dt("c_all", [128, 512]), dt("c_mask", [128, 128])
    o = nc.dram_tensor("o", [1024, SEQ], BF16, kind="ExternalOutput").ap()
    with ExitStack() as st:
        C = Ctx(nc, st)
        phase_hg(C, xn, whg, lbz, cm, og, o, call, NT_SEQ, "H", 3)
        phase_df(C, xn, wdf, dfv, dfb, o, call, cmask, NT_SEQ, "F", 2, row0=384)
        phase_ml(C, xn, wml, wuq, wukv, mlv, cs, o, call, cmask, NT_SEQ, "M", 3, row0=640)
        C.S.barrier()
    return nc


def build_PB():
    nc = bass.Bass("TRN2", target_bir_lowering=False)
    x = nc.dram_tensor("x", [D, NTOK], F32, kind="ExternalInput").ap()
    o = nc.dram_tensor("o", [D, NTOK], BF16, kind="ExternalInput").ap()
    wo = nc.dram_tensor("wo", [D, D], F32, kind="ExternalInput").ap()
    g, wg, wu, wd = _ffn_inputs(nc, "")
    xmid = nc.dram_tensor("xmid", [D, NTOK], F32, kind="ExternalOutput").ap()
    xo = nc.dram_tensor("xo", [D, NTOK], F32, kind="ExternalOutput").ap()
    with ExitStack() as st:
        C = Ctx(nc, st)
        phase_wout(C, x, xmid, o, wo, NTOK, "O")
        phase_ffn(C, xmid, xo, g, wg, wu, wd, NTOK, TT_FFN, "B")
        C.S.barrier()
    return nc


def _consts():
    idx = np.arange(128)
    same = (idx[:, None] // 64) == (idx[None, :] // 64)
    u2 = (same & (idx[:, None] <= idx[None, :])).astype(np.float32)
    mid = (same & ((idx[:, None] % 64) <= 31)).astype(np.float32)
    cmat = u2 - mid
    sel = np.zeros((128, 128), np.float32)
    for c in range(2):
        inch = (idx // 64) == c
        sel[:, 3 * c + 0] = inch & ((idx % 64) <= 31)
        sel[:, 3 * c + 1] = inch & ((idx % 64) >= 32)
        sel[:, 3 * c + 2] = inch
    c_all = np.ascontiguousarray(np.concatenate([np.eye(128, dtype=np.float32), u2, cmat, sel], axis=1))
    ok = (idx[:, None] // 64) <= (idx[None, :] // 64)
    c_mask = np.where(ok, 0.0, NEG).astype(np.float32)
    pos = np.arange(SEQ, dtype=np.float32)
    freqs = (np.float32(10000.0) ** (-np.arange(0, 64, 2, dtype=np.float32) / np.float32(64))).astype(np.float32)
    ang = pos[:, None] * freqs[None, :]
    cos = np.cos(ang).astype(np.float32).reshape(NT_SEQ, 128, 32).transpose(1, 0, 2).reshape(128, NT_SEQ * 32)
    sin = np.sin(ang).astype(np.float32).reshape(NT_SEQ, 128, 32).transpose(1, 0, 2).reshape(128, NT_SEQ * 32)
    cs = np.ascontiguousarray(np.concatenate([cos, sin], axis=1))
    return c_all, c_mask, cs


def _t5_bucket_idx():
    import jax
    import jax.numpy as jnp
    idx = np.arange(128)
    out = []
    with jax.default_device(jax.devices("cpu")[0]):
        for r in (0, -1):
            rel = jnp.asarray(((idx[:, None] + 128 * r) - idx[None, :]).astype(np.int32))
            half, max_exact = 16, 8
            ret = (rel > 0).astype(jnp.int32) * half
            n = jnp.abs(rel)
            large = max_exact + (jnp.log(jnp.maximum(n, 1).astype(jnp.float32) / max_exact)
                                 / math.log(128 / max_exact) * (half - max_exact)).astype(jnp.int32)
            large = jnp.minimum(large, half - 1)
            out.append(np.asarray(ret + jnp.where(n < max_exact, n, large)))
    return np.stack(out, 0)


_IN_OFF = [0, 768, 1536, 2304, 3072, 3584, 4096, 4608, 5120, 5376, 5440]


PAIRS = [[0, 1], [2, 3], [4, 5], [6, 7]]


def build_fused(L=4):
    nc = bass.Bass("TRN2", target_bir_lowering=False)
    dt = lambda n, sh, d=F32: nc.dram_tensor(n, sh, d, kind="ExternalInput").ap()
    x = dt("x", [D, NTOK])
    ffn = {}
    for ab in ("a", "b"):
        ffn[ab] = (dt("ffn_%s_norm" % ab, [L, D]), dt("ffn_%s_w_gate" % ab, [L, D, DFF]), dt("ffn_%s_w_up" % ab, [L, D, DFF]),
                   dt("ffn_%s_w_down" % ab, [L, DFF, D]))
    gm = dt("mix_norm", [L, D])
    whg, wdf, wml = dt("whg", [L, D, 1536]), dt("wdf", [L, D, 768]), dt("wml", [L, D, 832])
    lbz, cm, og = dt("lbz", [4, 384]), dt("cm", [L, 16]), dt("og", [L, 128])
    dfv, dfb = dt("dfv", [L, 8, 128]), dt("dfb", [2, 2, 128, 128])
    wuq, wukv, mlv = dt("wuq", [L, 512, 576]), dt("wukv", [L, 256, 768]), dt("mlv", [L, 4, 512])
    cs = dt("cs", [128, 2 * NT_SEQ * 32])
    call, cmask = dt("c_all", [128, 512]), dt("c_mask", [128, 128])
    wo = dt("wo", [L, D, D])
    sel = dt("sel", [16])
    xo = nc.dram_tensor("xo", [D, NTOK], F32, kind="ExternalOutput").ap()
    internal = lambda n, sh, d: nc.dram_tensor(n, sh, d, kind="Internal").ap()
    local = lambda n, sh, d: nc.dram_tensor(n, sh, d, addr_space="Local", kind="Internal").ap()
    with ExitStack() as st:
        C = Ctx(nc, st)
        S = C.S
        xcur = x
        for l in range(L):
            C.sfx = "_%d" % l
            xa = internal("xa%d" % l, [D, NTOK], F32)
            xb = internal("xb%d" % l, [D, NTOK], F32)
            xc = xo if l == L - 1 else internal("xc%d" % l, [D, NTOK], F32)
            xns = internal("xns%d" % l, [NTOK // 128, 128, KD, 128], BF16)
            xnf = local("xnf%d" % l, [NT_SEQ, 128, KD, 128], BF16)
            osd = internal("osd%d" % l, [1024, SEQ], BF16)
            ofl = local("ofl%d" % l, [2048, SEQ], BF16)
            g, wg, wu, wd = ffn["a"]
            phase_ffn(C, xcur, xa, g[l], wg[l], wu[l], wd[l], NTOK, TT_FFN, "A")
            phase_norm(C, xa, gm[l], xns, NTOK, 512, "N")
            xns2 = xns.rearrange("n p k t -> (n p) (k t)")
            xnf2 = xnf.rearrange("n p k t -> (n p) (k t)")
            S.barrier()
            for j in range(8):
                S.cc(lambda e: e.collective_compute("AllGather", ALU.bypass, replica_groups=PAIRS,
                                                    ins=[xns2[j * 256:(j + 1) * 256, :]], outs=[xnf2[j * 512:(j + 1) * 512, :]]))
            S.barrier()
            xv = XnView(xnf2)
            phase_hg(C, xv, whg[l], lbz, cm[l], og[l], osd, call, NT_SEQ, "H", 3)
            phase_df(C, xv, wdf[l], dfv[l], dfb, osd, call, cmask, NT_SEQ, "F", 2, row0=384)
            phase_ml(C, xv, wml[l], wuq[l], wukv[l], mlv[l], cs, osd, call, cmask, NT_SEQ, "M", 3, row0=640)
            S.barrier()
            for j in range(8):
                S.cc(lambda e: e.collective_compute("AllGather", ALU.bypass, replica_groups=PAIRS,
                                                    ins=[osd[j * 128:(j + 1) * 128, :]], outs=[ofl[j * 256:(j + 1) * 256, :]]))
            S.barrier()
            phase_wout(C, xa, xb, ofl, wo[l], NTOK, "O", sel_d=sel)
            g, wg, wu, wd = ffn["b"]
            phase_ffn(C, xb, xc, g[l], wg[l], wu[l], wd[l], NTOK, TT_FFN, "B")
            xcur = xc
        S.barrier()
    return nc


class XnView:
    def __init__(self, g2):
        self.g2 = g2

    def __getitem__(self, key):
        t = key[0]
        rank, lt = t // 16, t % 16
        r0 = (lt // 2) * 512 + rank * 256 + (lt % 2) * 128
        return self.g2[r0:r0 + 128, :].rearrange("p (k t) -> p k t", t=128)


def _gathered_row_perm():
    perm = []
    for q in range(16):
        j, r = q // 2, q % 2
        if j < 3:
            b = 3 * r + j
        elif j < 5:
            b = 6 + 2 * r + (j - 3)
        else:
            b = 10 + 3 * r + (j - 5)
        perm += list(range(b * 128, (b + 1) * 128))
    return np.asarray(perm)


def kernel(**inputs):
    f = lambda k: np.ascontiguousarray(np.asarray(inputs[k], dtype=np.float32))
    x = f("x")
    L = 4
    c_all, c_mask, cs = _consts()
    bk = _t5_bucket_idx()
    rel_bias = f("rel_bias")
    w_in = f("w_in")
    ar = np.arange(128)
    shared = {k: f(k) for k in ("ffn_a_norm", "ffn_a_w_gate", "ffn_a_w_up", "ffn_a_w_down", "mix_norm",
                                "ffn_b_norm", "ffn_b_w_gate", "ffn_b_w_up", "ffn_b_w_down")}
    shared["wo"] = np.ascontiguousarray(f("w_out")[:, _gathered_row_perm(), :])
    shared["og"] = f("hgrn_out_norm")
    shared["cs"], shared["c_all"], shared["c_mask"] = cs, c_all, c_mask
    cm = np.zeros((L, 16), np.float32)
    dfv = np.zeros((L, 8, 128), np.float32)
    mlv = np.zeros((L, 4, 512), np.float32)
    for l in range(L):
        linit = 0.8 - 0.6 * math.exp(-0.3 * l)
        cm[l, 1:l + 1] = 1.0
        dfv[l, 0, :64] = f("diff_q_norm")[l]
        dfv[l, 1, :64] = f("diff_k_norm")[l]
        dfv[l, 2, :64] = f("diff_lambda_q1")[l]
        dfv[l, 3, :64] = f("diff_lambda_k1")[l]
        dfv[l, 4, :64] = f("diff_lambda_q2")[l]
        dfv[l, 5, :64] = f("diff_lambda_k2")[l]
        dfv[l, 6, :] = f("diff_subln")[l]
        dfv[l, 7, 0] = linit
        dfv[l, 7, 1] = 1.0 - linit
        mlv[l, 0, :] = f("mla_q_lora_norm")[l]
        mlv[l, 1, :256] = f("mla_kv_lora_norm")[l]
        mlv[l, 2, :192] = f("mla_q_norm")[l]
        mlv[l, 3, :192] = f("mla_k_norm")[l]
    shared["cm"], shared["mlv"] = cm, mlv
    per_g = []
    for g in range(2):
        hg_cols = np.concatenate([_IN_OFF[j] + (3 * g + h) * 128 + ar for h in range(3) for j in range(4)])
        df_cols = np.concatenate([_IN_OFF[4 + j] + (2 * g + h) * 128 + ar for h in range(2) for j in range(3)])
        dg = dfv.copy()
        dg[:, 7, 2:4] = rel_bias[15, 2 * g:2 * g + 2]
        sel = np.zeros(16, np.float32)
        sel[g] = 1.0
        per_g.append({
            "whg": np.ascontiguousarray(w_in[:, :, hg_cols]), "wdf": np.ascontiguousarray(w_in[:, :, df_cols]),
            "wml": np.ascontiguousarray(w_in[:, :, 4608:5440]),
            "lbz": np.ascontiguousarray(f("hgrn_lb_logits")[:, g * 384:(g + 1) * 384]),
            "dfv": dg, "dfb": np.ascontiguousarray(rel_bias[bk][..., 2 * g:2 * g + 2].transpose(3, 0, 1, 2)),
            "wuq": np.ascontiguousarray(f("mla_w_uq")[:, :, g * 576:(g + 1) * 576]),
            "wukv": np.ascontiguousarray(f("mla_w_ukv")[:, :, g * 768:(g + 1) * 768]),
            "sel": sel,
        })
    ims = []
    for c in range(8):
        m = dict(shared)
        m.update(per_g[c % 2])
        m["x"] = np.ascontiguousarray(x[c // 2, (c % 2) * NTOK:(c % 2 + 1) * NTOK, :].T)
        ims.append(m)
    nc = build_fused(L)
    res = run_bass_kernel_spmd(nc, ims, core_ids=list(range(8))).results
    out = np.empty((4, SEQ, D), np.float32)
    for c in range(8):
        out[c // 2, (c % 2) * NTOK:(c % 2 + 1) * NTOK, :] = np.asarray(res[c]["xo"]).T
    return out
```

```python
import math
from contextlib import ExitStack

import numpy as np
import ml_dtypes

import concourse.bass as bass
import concourse.mybir as mybir
from concourse.bass_utils import run_bass_kernel_spmd

F32 = mybir.dt.float32
BF16 = mybir.dt.bfloat16
AF = mybir.ActivationFunctionType
ALU = mybir.AluOpType
AX = mybir.AxisListType

D = 2048
DFF = 5504
NF = DFF // 128
KD = D // 128
EPS = 1e-6
NEG = -60.0


class Buf:
    __slots__ = ("name", "w", "r")

    def __init__(self, name=""):
        self.name = name
        self.w = None
        self.r = []


class T:
    __slots__ = ("t", "b")

    def __init__(self, t, name=""):
        self.t = t
        self.b = Buf(name)


class Sched:
    def __init__(self, nc, stack):
        self.nc = nc
        self.stack = stack
        self.engs = {"pe": nc.tensor, "act": nc.scalar, "dve": nc.vector, "pool": nc.gpsimd, "sp": nc.sync}
        self.sem = {}
        self.cnt = {}
        for e in self.engs:
            self.sem[e] = stack.enter_context(nc.semaphore("s_" + e))
            self.cnt[e] = 0
        self.waited = {}
        self.dsems = {}
        self.dcnt = {}
        self.nsem = 0

    def dsem(self, name):
        if name not in self.dsems:
            s = self.stack.enter_context(self.nc.semaphore("d_" + name))
            self.dsems[name] = s
            self.dcnt[name] = 0
        return name

    def _wait(self, e, deps):
        best = {}
        for (k, v) in deps:
            if k == e and e == "pe":
                continue
            if k not in best or best[k] < v:
                best[k] = v
        for k, v in best.items():
            if self.waited.get((e, k), 0) >= v:
                continue
            s = self.sem[k] if k in self.sem else self.dsems[k]
            self.engs[e].wait_ge(s, v)
            self.waited[(e, k)] = v

    def op(self, e, fn, rd=(), wr=(), dsem=None):
        self.nops = getattr(self, "nops", 0) + 1
        if self.nops > getattr(self, "max_ops", 1 << 60):
            return None
        deps = []
        rd = [b.b if isinstance(b, T) else b for b in rd]
        wr = [b.b if isinstance(b, T) else b for b in wr]
        wr = wr + [b for b in rd if b.name.startswith(("bank", "psb"))]
        rd = [b for b in rd if not b.name.startswith(("bank", "psb"))]
        for b in rd:
            b = b.b if isinstance(b, T) else b
            if b.w is not None:
                deps.append(b.w)
        for b in wr:
            b = b.b if isinstance(b, T) else b
            if b.w is not None:
                deps.append(b.w)
            deps.extend(b.r)
        self._wait(e, deps)
        ins = fn(self.engs[e])
        if dsem is not None:
            self.dsem(dsem)
            self.dcnt[dsem] += 16
            ins.then_inc(self.dsems[dsem], 16)
            tok = (dsem, self.dcnt[dsem])
        else:
            self.cnt[e] += 1
            ins.then_inc(self.sem[e], 1)
            tok = (e, self.cnt[e])
        for b in rd:
            b = b.b if isinstance(b, T) else b
            b.r.append(tok)
            if len(b.r) > 64:
                m = {}
                for (k, v) in b.r:
                    if k not in m or m[k] < v:
                        m[k] = v
                b.r = list(m.items())
        for b in wr:
            b = b.b if isinstance(b, T) else b
            b.w = tok
            b.r = []
        return tok

    def cc(self, fn):
        name = self.dsem("ccsem")
        ins = fn(self.engs["pool"])
        self.dcnt[name] += 1
        ins.then_inc(self.dsems[name], 1)

    def barrier(self, engines=None):
        toks = [(e, c) for e, c in self.cnt.items() if c > 0]
        toks += [(n, c) for n, c in self.dcnt.items() if c > 0]
        for e in (engines or self.engs):
            self._wait_all(e, toks)

    def _wait_all(self, e, toks):
        for (k, v) in toks:
            if k == e:
                continue
            if self.waited.get((e, k), 0) >= v:
                continue
            s = self.sem[k] if k in self.sem else self.dsems[k]
            self.engs[e].wait_ge(s, v)
            self.waited[(e, k)] = v


class Ring:
    def __init__(self, items):
        self.items = items
        self.i = 0

    def next(self):
        x = self.items[self.i % len(self.items)]
        self.i += 1
        return x


class Ctx:
    def __init__(self, nc, st):
        self.nc = nc
        self.sfx = ""
        self.S = Sched(nc, st)
        self.psf = [st.enter_context(nc.psum_tensor("psf%d" % i, [128, 512], F32)) for i in range(7)]
        self.psb = st.enter_context(nc.psum_tensor("psb", [128, 1024], BF16))
        self.bankT = [T(self.psf[i], "bank%d" % i) for i in range(7)]
        self.psbT = [T(None, "psb") for i in range(8)]
        for x in self.psbT:
            x.b = self.psbT[0].b

    def banks(self, ids):
        return [self.bankT[i] for i in ids]


def _groups(n, g):
    out = []
    i = 0
    while i < n:
        out.append((i, min(g, n - i)))
        i += g
    return out


def emit_rstd(S, out_ap, out_T, in_ap, in_T, scale, tmp_ap, tmp_T):
    S.op("act", lambda e: e.activation(out=tmp_ap, in_=in_ap, func=AF.Ln, scale=scale, bias=EPS), rd=[in_T], wr=[tmp_T])
    S.op("act", lambda e: e.activation(out=out_ap, in_=tmp_ap, func=AF.Exp, scale=-0.5), rd=[tmp_T], wr=[out_T])


def norm_tile(C, st_bufs, x_d, tok0, TT, banks):
    S = C.S
    NS = TT // 512
    xin, sq, hT, rstd, lnt, gcol, ones = (st_bufs[k] for k in ("xin", "sq", "hT", "rstd", "lnt", "gcol", "ones"))
    bk = [banks.next() for _ in range(NS)]
    for k in range(KD):
        xi = xin.next()
        si = sq.next()
        S.op("sp", lambda e: e.dma_start(out=xi.t[:, :], in_=x_d[k * 128:(k + 1) * 128, tok0:tok0 + TT]), wr=[xi], dsem=xi.b.name)
        S.op("act", lambda e: e.activation(out=si.t[:, :], in_=xi.t[:, :], func=AF.Square), rd=[xi], wr=[si])
        for s in range(NS):
            S.op("pe", lambda e: e.matmul(bk[s].t[:, :], ones.t[:, :], si.t[:, s * 512:(s + 1) * 512], start=(k == 0), stop=(k == KD - 1)),
                 rd=[ones, si], wr=[bk[s]])
    for s in range(NS):
        emit_rstd(S, rstd.t[:, s * 512:(s + 1) * 512], rstd, bk[s].t[:, :], bk[s], 1.0 / D, rstd.t[:, s * 512:(s + 1) * 512], rstd)
    for k in range(KD):
        xi = xin.next()
        S.op("sp", lambda e: e.dma_start(out=xi.t[:, :], in_=x_d[k * 128:(k + 1) * 128, tok0:tok0 + TT]), wr=[xi], dsem=xi.b.name)
        S.op("dve", lambda e: e.scalar_tensor_tensor(out=hT.t[:, k, :], in0=xi.t[:, :], scalar=gcol.t[:, k:k + 1], in1=rstd.t[:, :],
                                                     op0=ALU.mult, op1=ALU.mult), rd=[xi, gcol, rstd], wr=[hT])


def norm_bufs(C, st, g_d, TT, pfx):
    nc, S = C.nc, C.S
    sb = lambda n, sh, dt: T(st.enter_context(nc.sbuf_tensor(pfx + n + C.sfx, sh, dt)), pfx + n)
    B = {}
    B["xin"] = Ring([sb("xin%d" % i, [128, TT], F32) for i in range(2)])
    B["sq"] = Ring([sb("sq%d" % i, [128, TT], BF16) for i in range(2)])
    B["hT"] = sb("hT", [128, KD, TT], BF16)
    B["rstd"] = sb("rstd", [128, TT], F32)
    B["lnt"] = None
    B["gcol"] = sb("gcol", [128, KD], F32)
    B["ones"] = sb("ones", [128, 128], BF16)
    S.op("sp", lambda e: e.dma_start(out=B["gcol"].t[:, :], in_=g_d.rearrange("(k p) -> p k", p=128), allow_slow_non_contiguous=True),
         wr=[B["gcol"]], dsem=pfx + "gcol")
    S.op("dve", lambda e: e.memset(B["ones"].t[:, :], 1.0), wr=[B["ones"]])
    return B


def phase_ffn(C, x_d, xo_d, g_d, wg_d, wu_d, wd_d, NTOK, TT, pfx):
    nc, S = C.nc, C.S
    NS = TT // 512
    S.barrier()
    with ExitStack() as st:
        sb = lambda n, sh, dt: T(st.enter_context(nc.sbuf_tensor(pfx + n + C.sfx, sh, dt)), pfx + n)
        NB = norm_bufs(C, st, g_d, TT, pfx)
        hT = NB["hT"]
        GW = 256
        wg = Ring([sb("wg%d" % i, [128, KD, GW], BF16) for i in range(2)])
        wu = Ring([sb("wu%d" % i, [128, KD, GW], BF16) for i in range(2)])
        actT = [T(None, pfx + "act%d" % f) for f in range(NF)]
        actT_t = st.enter_context(nc.sbuf_tensor(pfx + "actT" + C.sfx, [128, NF, TT], BF16))
        sg = Ring([sb("sg%d" % i, [128, 512], BF16) for i in range(2)])
        DGC = 4 // NS
        wd = Ring([sb("wd%d" % i, [128, 4, DGC * 128], BF16) for i in range(2)])
        xres = Ring([sb("xres%d" % i, [128, 512], F32) for i in range(2)])
        yo = Ring([sb("yo%d" % i, [128, 512], F32) for i in range(2)])
        banks = Ring(C.banks([0, 1, 2, 3, 4, 5]))
        dbanks = Ring(C.banks([0, 1, 2, 3, 4, 5, 6]))
        for tt in range(NTOK // TT):
            tok0 = tt * TT
            norm_tile(C, NB, x_d, tok0, TT, banks)
            for (f0, nf) in _groups(NF, GW // 128):
                g_s = wg.next()
                u_s = wu.next()
                S.op("pool", lambda e: e.dma_start(out=g_s.t[:, :, 0:nf * 128],
                                                   in_=wg_d[:, f0 * 128:(f0 + nf) * 128].rearrange("(k p) f -> p k f", p=128)),
                     wr=[g_s], dsem=g_s.b.name)
                S.op("pool", lambda e: e.dma_start(out=u_s.t[:, :, 0:nf * 128],
                                                   in_=wu_d[:, f0 * 128:(f0 + nf) * 128].rearrange("(k p) f -> p k f", p=128)),
                     wr=[u_s], dsem=u_s.b.name)
                for fi in range(nf):
                    f = f0 + fi
                    for s in range(NS):
                        bg = banks.next()
                        bu = banks.next()
                        for k in range(KD):
                            S.op("pe", lambda e: e.matmul(bg.t[:, :], g_s.t[:, k, fi * 128:(fi + 1) * 128], hT.t[:, k, s * 512:(s + 1) * 512],
                                                          start=(k == 0), stop=(k == KD - 1)), rd=[g_s, hT], wr=[bg])
                        for k in range(KD):
                            S.op("pe", lambda e: e.matmul(bu.t[:, :], u_s.t[:, k, fi * 128:(fi + 1) * 128], hT.t[:, k, s * 512:(s + 1) * 512],
                                                          start=(k == 0), stop=(k == KD - 1)), rd=[u_s, hT], wr=[bu])
                        sgi = sg.next()
                        S.op("act", lambda e: e.activation(out=sgi.t[:, :], in_=bg.t[:, :], func=AF.Silu), rd=[bg], wr=[sgi])
                        S.op("dve", lambda e: e.tensor_tensor(out=actT_t[:, f, s * 512:(s + 1) * 512], in0=bu.t[:, :], in1=sgi.t[:, :], op=ALU.mult),
                             rd=[bu, sgi], wr=[actT[f]])
            for dg in range(KD // DGC):
                db = [[dbanks.next() for s in range(NS)] for dd in range(DGC)]
                for (f0, nf) in _groups(NF, 4):
                    w_s = wd.next()
                    S.op("pool", lambda e: e.dma_start(out=w_s.t[:, 0:nf, :],
                                                       in_=wd_d[f0 * 128:(f0 + nf) * 128, dg * DGC * 128:(dg + 1) * DGC * 128].rearrange("(j p) c -> p j c", p=128)),
                         wr=[w_s], dsem=w_s.b.name)
                    for fi in range(nf):
                        f = f0 + fi
                        for dd in range(DGC):
                            for s in range(NS):
                                S.op("pe", lambda e: e.matmul(db[dd][s].t[:, :], w_s.t[:, fi, dd * 128:(dd + 1) * 128],
                                                              actT_t[:, f, s * 512:(s + 1) * 512], start=(f == 0), stop=(f == NF - 1)),
                                     rd=[w_s, actT[f]], wr=[db[dd][s]])
                for dd in range(DGC):
                    d = dg * DGC + dd
                    for s in range(NS):
                        xr = xres.next()
                        y = yo.next()
                        c0 = tok0 + s * 512
                        S.op("sp", lambda e: e.dma_start(out=xr.t[:, :], in_=x_d[d * 128:(d + 1) * 128, c0:c0 + 512]), wr=[xr], dsem=xr.b.name)
                        S.op("dve", lambda e: e.scalar_tensor_tensor(out=y.t[:, :], in0=db[dd][s].t[:, :], scalar=0.5, in1=xr.t[:, :],
                                                                     op0=ALU.mult, op1=ALU.add), rd=[db[dd][s], xr], wr=[y])
                        S.op("sp", lambda e: e.dma_start(out=xo_d[d * 128:(d + 1) * 128, c0:c0 + 512], in_=y.t[:, :]), rd=[y], dsem=y.b.name)
        S.barrier()


def phase_norm(C, x_d, g_d, xn_d, NTOK, TT, pfx):
    nc, S = C.nc, C.S
    S.barrier()
    with ExitStack() as st:
        NB = norm_bufs(C, st, g_d, TT, pfx)
        banks = Ring(C.banks([0, 1, 2, 3]))
        for tt in range(NTOK // TT):
            norm_tile(C, NB, x_d, tt * TT, TT, banks)
            for j in range(TT // 128):
                S.op("sp", lambda e: e.dma_start(out=xn_d[tt * (TT // 128) + j, :, :, :], in_=NB["hT"].t[:, :, j * 128:(j + 1) * 128]),
                     rd=[NB["hT"]], dsem=pfx + "xnout")
        S.barrier()


def phase_wout(C, x_d, xo_d, o_d, wo_d, NTOK, pfx, sel_d=None):
    nc, S = C.nc, C.S
    S.barrier()
    with ExitStack() as st:
        sb = lambda n, sh, dt: T(st.enter_context(nc.sbuf_tensor(pfx + n + C.sfx, sh, dt)), pfx + n)
        wo = sb("wo", [128, KD, D], BF16)
        for k in range(KD):
            S.op("pool", lambda e: e.dma_start(out=wo.t[:, k, :], in_=wo_d[k * 128:(k + 1) * 128, :]), wr=[wo], dsem=pfx + "wo")
        ot = Ring([sb("ot%d" % i, [128, KD, 512], BF16) for i in range(2)])
        xres = Ring([sb("xres%d" % i, [128, 512], F32) for i in range(2)])
        yo = Ring([sb("yo%d" % i, [128, 512], F32) for i in range(2)])
        if sel_d is not None:
            selt = load_bcast(C, st, pfx + "sel", sel_d, 16)
            oa, ob = sb("oa", [128, KD, 512], BF16), sb("ob", [128, KD, 512], BF16)
            otmp = sb("otmp", [128, KD, 512], F32)
        banks = Ring(C.banks([0, 1, 2, 3]))
        for s in range(NTOK // 512):
            c0 = s * 512
            o_s = ot.next()
            if sel_d is None:
                S.op("sp", lambda e: e.dma_start(out=o_s.t[:, :, :], in_=o_d[:, c0:c0 + 512].rearrange("(k p) t -> p k t", p=128)),
                     wr=[o_s], dsem=o_s.b.name)
            else:
                S.op("sp", lambda e: e.dma_start(out=oa.t[:, :, :], in_=o_d[:, c0:c0 + 512].rearrange("(k p) t -> p k t", p=128)),
                     wr=[oa], dsem=oa.b.name)
                S.op("sp", lambda e: e.dma_start(out=ob.t[:, :, :], in_=o_d[:, NTOK + c0:NTOK + c0 + 512].rearrange("(k p) t -> p k t", p=128)),
                     wr=[ob], dsem=ob.b.name)
                S.op("dve", lambda e: e.tensor_scalar(out=otmp.t[:, :, :], in0=oa.t[:, :, :], scalar1=selt.t[:, 0:1], scalar2=None, op0=ALU.mult),
                     rd=[oa, selt], wr=[otmp])
                S.op("dve", lambda e: e.scalar_tensor_tensor(out=o_s.t[:, :, :], in0=ob.t[:, :, :], scalar=selt.t[:, 1:2], in1=otmp.t[:, :, :],
                                                             op0=ALU.mult, op1=ALU.add), rd=[ob, selt, otmp], wr=[o_s])
            for d in range(KD):
                bk = banks.next()
                for k in range(KD):
                    S.op("pe", lambda e: e.matmul(bk.t[:, :], wo.t[:, k, d * 128:(d + 1) * 128], o_s.t[:, k, :], start=(k == 0), stop=(k == KD - 1)),
                         rd=[wo, o_s], wr=[bk])
                xr = xres.next()
                y = yo.next()
                S.op("sp", lambda e: e.dma_start(out=xr.t[:, :], in_=x_d[d * 128:(d + 1) * 128, c0:c0 + 512]), wr=[xr], dsem=xr.b.name)
                S.op("dve", lambda e: e.tensor_tensor(out=y.t[:, :], in0=bk.t[:, :], in1=xr.t[:, :], op=ALU.add), rd=[bk, xr], wr=[y])
                S.op("sp", lambda e: e.dma_start(out=xo_d[d * 128:(d + 1) * 128, c0:c0 + 512], in_=y.t[:, :]), rd=[y], dsem=y.b.name)
        S.barrier()


def load_bcast(C, st, name, vec_ap, n):
    t = T(st.enter_context(C.nc.sbuf_tensor(name + C.sfx, [128, n], F32)), name)
    C.S.op("sp", lambda e: e.dma_start(out=t.t[:, :], in_=vec_ap.partition_broadcast(128)), wr=[t], dsem=name)
    return t


def load_const(C, st, name, ap, shape, dt):
    t = T(st.enter_context(C.nc.sbuf_tensor(name + C.sfx, shape, dt)), name)
    eng = "sp" if dt == F32 else "pool"
    C.S.op(eng, lambda e: e.dma_start(out=t.t[:, :], in_=ap), wr=[t], dsem=name)
    return t


def psb_region(C, i):
    return C.psb[:, i * 128:(i + 1) * 128], C.psbT[i]


def phase_hg(C, xn_d, w_d, lbz_d, cm_d, og_d, o_d, consts, NT, pfx, NH=3):
    nc, S = C.nc, C.S
    S.barrier()
    with ExitStack() as st:
        sb = lambda n, sh, dt: T(st.enter_context(nc.sbuf_tensor(pfx + n + C.sfx, sh, dt)), pfx + n)
        w = sb("w", [128, KD, NH * 512], BF16)
        for h in range(NH):
            S.op("pool", lambda e: e.dma_start(out=w.t[:, :, h * 512:(h + 1) * 512],
                                               in_=w_d[:, h * 512:(h + 1) * 512].rearrange("(k p) c -> p k c", p=128)), wr=[w], dsem=pfx + "w")
        cf = load_const(C, st, pfx + "cf", consts, [128, 512], F32)
        cb = sb("cb", [128, 512], BF16)
        S.op("dve", lambda e: e.tensor_copy(out=cb.t[:, :], in_=cf.t[:, :]), rd=[cf], wr=[cb])
        ident = T(cb.t[:, 0:128]); ident.b = cb.b
        u2 = T(cf.t[:, 128:256]); u2.b = cf.b
        cmat = T(cb.t[:, 256:384]); cmat.b = cb.b
        sel = T(cb.t[:, 384:390]); sel.b = cb.b
        ogain = load_bcast(C, st, pfx + "ogain", og_d, 128)
        NC_ = NH * 128
        lbz = sb("lbz", [128, 4, NC_], F32)
        for j in range(4):
            S.op("sp", lambda e: e.dma_start(out=lbz.t[:, j, :], in_=lbz_d[j, :].partition_broadcast(128)), wr=[lbz], dsem=pfx + "lbz")
        cm = load_bcast(C, st, pfx + "cm", cm_d, 16)
        lb = sb("lb", [128, NC_], F32)
        oml = sb("oml", [128, NC_], F32)
        den = sb("den", [128, NC_], F32)
        S.op("act", lambda e: e.activation(out=lbz.t[:, :, :], in_=lbz.t[:, :, :], func=AF.Exp), rd=[lbz], wr=[lbz])
        S.op("dve", lambda e: e.tensor_tensor(out=den.t[:, :], in0=lbz.t[:, 0, :], in1=lbz.t[:, 1, :], op=ALU.add), rd=[lbz], wr=[den])
        S.op("dve", lambda e: e.tensor_tensor(out=den.t[:, :], in0=den.t[:, :], in1=lbz.t[:, 2, :], op=ALU.add), rd=[lbz, den], wr=[den])
        S.op("dve", lambda e: e.tensor_tensor(out=den.t[:, :], in0=den.t[:, :], in1=lbz.t[:, 3, :], op=ALU.add), rd=[lbz, den], wr=[den])
        S.op("dve", lambda e: e.reciprocal(out=den.t[:, :], in_=den.t[:, :]), rd=[den], wr=[den])
        S.op("dve", lambda e: e.tensor_scalar(out=lb.t[:, :], in0=lbz.t[:, 0, :], scalar1=cm.t[:, 0:1], scalar2=None, op0=ALU.mult), rd=[lbz, cm], wr=[lb])
        for j in range(1, 4):
            S.op("dve", lambda e: e.scalar_tensor_tensor(out=lb.t[:, :], in0=lbz.t[:, j, :], scalar=cm.t[:, j:j + 1], in1=lb.t[:, :],
                                                         op0=ALU.mult, op1=ALU.add), rd=[lbz, cm, lb], wr=[lb])
        S.op("dve", lambda e: e.tensor_tensor(out=lb.t[:, :], in0=lb.t[:, :], in1=den.t[:, :], op=ALU.mult), rd=[lb, den], wr=[lb])
        S.op("dve", lambda e: e.tensor_scalar(out=oml.t[:, :], in0=lb.t[:, :], scalar1=-1.0, scalar2=1.0, op0=ALU.mult, op1=ALU.add), rd=[lb], wr=[oml])

        St = [sb("S%d" % h, [128, 128], F32) for h in range(NH)]
        for h in range(NH):
            S.op("dve", lambda e: e.memset(St[h].t[:, :], 0.0), wr=[St[h]])
        xt = Ring([sb("xt%d" % i, [128, KD, 128], BF16) for i in range(2)])
        R = lambda n, sh, dt, k=2 * NH: Ring([sb("%s%d" % (n, i), sh, dt) for i in range(k)])
        ef, eg, ff, lf, kk, E, Ei = (R(n, [128, 128], F32) for n in ("ef", "eg", "ff", "lf", "kk", "E", "Ei"))
        esc = R("esc", [128, 6], F32)
        qh, kh, vv, kT, sm, Sp0, Sp1, yb = (R(n, [128, 128], BF16) for n in ("qh", "kh", "vv", "kT", "sm", "Sp0", "Sp1", "yb"))
        A = R("A", [128, 128], F32, 4 * NH)
        lfh, lfl = R("lfh", [128, 128], BF16), R("lfl", [128, 128], BF16)
        ss, lnv, rs = (R(n, [128, 1], F32) for n in ("ss", "lnv", "rs"))
        junk = R("junk", [128, 128], F32)
        qA = [sb("qA%d" % i, [128, 128], BF16) for i in range(2 * NH)]
        qB = [sb("qB%d" % i, [128, 128], BF16) for i in range(2 * NH)]
        for i in range(2 * NH):
            S.op("dve", lambda e: e.memset(qA[i].t[:, :], 0.0), wr=[qA[i]])
            S.op("dve", lambda e: e.memset(qB[i].t[:, :], 0.0), wr=[qB[i]])
        qAr, qBr = Ring(qA), Ring(qB)
        kA = [sb("kA%d" % i, [128, 128], BF16) for i in range(2 * NH)]
        kB = [sb("kB%d" % i, [128, 128], BF16) for i in range(2 * NH)]
        for i in range(2 * NH):
            S.op("dve", lambda e: e.memset(kA[i].t[:, :], 0.0), wr=[kA[i]])
            S.op("dve", lambda e: e.memset(kB[i].t[:, :], 0.0), wr=[kB[i]])
        kAr, kBr = Ring(kA), Ring(kB)
        ost = R("ost", [128, NH, 128], BF16, 2)
        pjb = C.banks([0, 1, 2])
        wkh = C.banks([3, 4, 5])
        b6 = C.banks([6])[0]
        pbi = [0]

        def psb_next():
            i = pbi[0] % 8
            pbi[0] += 1
            return psb_region(C, i)

        for t in range(NT):
            x_s = xt.next()
            S.op("sp", lambda e: e.dma_start(out=x_s.t[:, :, :], in_=xn_d[t, :, :, :]), wr=[x_s], dsem=x_s.b.name)
            o_s = ost.next()
            def head(h):
                pj = pjb[h]
                for k in range(KD):
                    S.op("pe", lambda e: e.matmul(pj.t[:, :], x_s.t[:, k, :], w.t[:, k, h * 512:(h + 1) * 512], start=(k == 0), stop=(k == KD - 1)),
                         rd=[x_s, w], wr=[pj])
                yield
                q_ap, fl_ap, vi_ap, gt_ap = (pj.t[:, i * 128:(i + 1) * 128] for i in range(4))
                hs = slice(h * 128, (h + 1) * 128)
                ef_, eg_, ff_, lf_, kk_, E_, Ei_ = (r.next() for r in (ef, eg, ff, lf, kk, E, Ei))
                S.op("act", lambda e: e.activation(out=ef_.t[:, :], in_=fl_ap, func=AF.Exp, scale=-1.0), rd=[pj], wr=[ef_])
                S.op("act", lambda e: e.activation(out=eg_.t[:, :], in_=gt_ap, func=AF.Exp, scale=-1.0), rd=[pj], wr=[eg_])
                yield
                S.op("dve", lambda e: e.tensor_scalar(out=ef_.t[:, :], in0=ef_.t[:, :], scalar1=1.0, scalar2=None, op0=ALU.add), rd=[ef_], wr=[ef_])
                S.op("dve", lambda e: e.reciprocal(out=ef_.t[:, :], in_=ef_.t[:, :]), rd=[ef_], wr=[ef_])
                S.op("dve", lambda e: e.tensor_tensor(out=ff_.t[:, :], in0=ef_.t[:, :], in1=oml.t[:, hs], op=ALU.mult), rd=[ef_, oml], wr=[ff_])
                S.op("dve", lambda e: e.tensor_tensor(out=ff_.t[:, :], in0=ff_.t[:, :], in1=lb.t[:, hs], op=ALU.add), rd=[ff_, lb], wr=[ff_])
                S.op("act", lambda e: e.activation(out=lf_.t[:, :], in_=ff_.t[:, :], func=AF.Ln), rd=[ff_], wr=[lf_])
                yield
                S.op("dve", lambda e: e.tensor_scalar(out=kk_.t[:, :], in0=ff_.t[:, :], scalar1=-1.0, scalar2=1.0, op0=ALU.mult, op1=ALU.add), rd=[ff_], wr=[kk_])
                wk = wkh[h]
                lh_, ll_ = lfh.next(), lfl.next()
                S.op("dve", lambda e: e.tensor_copy(out=lh_.t[:, :], in_=lf_.t[:, :]), rd=[lf_], wr=[lh_])
                S.op("dve", lambda e: e.tensor_tensor(out=ll_.t[:, :], in0=lf_.t[:, :], in1=lh_.t[:, :], op=ALU.subtract), rd=[lf_, lh_], wr=[ll_])
                yield
                S.op("pe", lambda e: e.matmul(wk.t[:, 0:128], cmat.t[:, :], lh_.t[:, :], start=True, stop=False), rd=[cmat, lh_], wr=[wk])
                S.op("pe", lambda e: e.matmul(wk.t[:, 0:128], cmat.t[:, :], ll_.t[:, :], start=False, stop=True), rd=[cmat, ll_], wr=[wk])
                S.op("pe", lambda e: e.matmul(b6.t[:, h * 8:h * 8 + 6], lh_.t[:, :], sel.t[:, :], start=True, stop=False), rd=[sel, lh_], wr=[b6])
                S.op("pe", lambda e: e.matmul(b6.t[:, h * 8:h * 8 + 6], ll_.t[:, :], sel.t[:, :], start=False, stop=True), rd=[sel, ll_], wr=[b6])
                yield
                esc_ = esc.next()
                S.op("act", lambda e: e.activation(out=E_.t[:, :], in_=wk.t[:, 0:128], func=AF.Exp), rd=[wk], wr=[E_])
                S.op("act", lambda e: e.activation(out=Ei_.t[:, :], in_=wk.t[:, 0:128], func=AF.Exp, scale=-1.0), rd=[wk], wr=[Ei_])
                S.op("act", lambda e: e.activation(out=esc_.t[:, :], in_=b6.t[:, h * 8:h * 8 + 6], func=AF.Exp), rd=[b6], wr=[esc_])
                yield
                qh_, kh_, vv_, kT_, sm_, Sp0_, Sp1_, yb_ = (r.next() for r in (qh, kh, vv, kT, sm, Sp0, Sp1, yb))
                S.op("dve", lambda e: e.scalar_tensor_tensor(out=qh_.t[:, :], in0=q_ap, scalar=128.0 ** -0.5, in1=E_.t[:, :], op0=ALU.mult, op1=ALU.mult),
                     rd=[pj, E_], wr=[qh_])
                S.op("dve", lambda e: e.tensor_tensor(out=kh_.t[:, :], in0=kk_.t[:, :], in1=Ei_.t[:, :], op=ALU.mult), rd=[kk_, Ei_], wr=[kh_])
                S.op("dve", lambda e: e.tensor_copy(out=vv_.t[:, :], in_=vi_ap), rd=[pj], wr=[vv_])
                yield
                tq_ap, tq = psb_next()
                tk_ap, tk = psb_next()
                S.op("pe", lambda e: e.transpose(tq_ap, qh_.t[:, :], ident.t[:, :]), rd=[qh_, ident], wr=[tq])
                S.op("pe", lambda e: e.transpose(tk_ap, kh_.t[:, :], ident.t[:, :]), rd=[kh_, ident], wr=[tk])
                yield
                qA_, qB_ = qAr.next(), qBr.next()
                S.op("act", lambda e: e.copy(out=qA_.t[:, 0:64], in_=tq_ap[:, 0:64]), rd=[tq], wr=[qA_])
                S.op("dve", lambda e: e.tensor_copy(out=qB_.t[:, 64:128], in_=tq_ap[:, 64:128]), rd=[tq], wr=[qB_])
                S.op("act", lambda e: e.copy(out=kT_.t[:, :], in_=tk_ap), rd=[tk], wr=[kT_])
                yield
                wk2 = wkh[h]
                S.op("pe", lambda e: e.matmul(wk2.t[:, 0:128], kT_.t[:, :], qA_.t[:, :], start=True, stop=False), rd=[kT_, qA_], wr=[wk2])
                S.op("pe", lambda e: e.matmul(wk2.t[:, 0:128], kT_.t[:, :], qB_.t[:, :], start=False, stop=True), rd=[kT_, qB_], wr=[wk2])
                S.op("dve", lambda e: e.tensor_tensor(out=sm_.t[:, :], in0=wk2.t[:, 0:128], in1=u2.t[:, :], op=ALU.mult), rd=[wk2, u2], wr=[sm_])
                yield
                kA_, kB_ = kAr.next(), kBr.next()
                S.op("act", lambda e: e.copy(out=kA_.t[0:64, :], in_=kh_.t[0:64, :]), rd=[kh_], wr=[kA_])
                S.op("act", lambda e: e.copy(out=kB_.t[64:128, :], in_=kh_.t[64:128, :]), rd=[kh_], wr=[kB_])
                S.op("pe", lambda e: e.matmul(wk2.t[:, 128:256], kA_.t[:, :], vv_.t[:, :], start=True, stop=True), rd=[kA_, vv_], wr=[wk2])
                S.op("pe", lambda e: e.matmul(wk2.t[:, 256:384], kB_.t[:, :], vv_.t[:, :], start=True, stop=True), rd=[kB_, vv_], wr=[wk2])
                yield
                Sh = St[h]
                for c, Sp_ in ((0, Sp0_), (1, Sp1_)):
                    A_ = A.next()
                    S.op("dve", lambda e: e.tensor_scalar(out=Sp_.t[:, :], in0=Sh.t[:, :], scalar1=esc_.t[:, 3 * c:3 * c + 1], scalar2=None, op0=ALU.mult),
                         rd=[Sh, esc_], wr=[Sp_])
                    S.op("dve", lambda e: e.tensor_scalar(out=A_.t[:, :], in0=Sh.t[:, :], scalar1=esc_.t[:, 3 * c + 2:3 * c + 3], scalar2=None, op0=ALU.mult),
                         rd=[Sh, esc_], wr=[A_])
                    S.op("dve", lambda e: e.scalar_tensor_tensor(out=Sh.t[:, :], in0=wk2.t[:, 128 * (c + 1):128 * (c + 2)],
                                                                 scalar=esc_.t[:, 3 * c + 1:3 * c + 2], in1=A_.t[:, :], op0=ALU.mult, op1=ALU.add),
                         rd=[wk2, esc_, A_], wr=[Sh])
                yield
                S.op("pe", lambda e: e.matmul(wk2.t[:, 384:512], sm_.t[:, :], vv_.t[:, :], start=True, stop=False), rd=[sm_, vv_], wr=[wk2])
                S.op("pe", lambda e: e.matmul(wk2.t[:, 384:512], qA_.t[:, :], Sp0_.t[:, :], start=False, stop=False), rd=[qA_, Sp0_], wr=[wk2])
                S.op("pe", lambda e: e.matmul(wk2.t[:, 384:512], qB_.t[:, :], Sp1_.t[:, :], start=False, stop=True), rd=[qB_, Sp1_], wr=[wk2])
                yield
                o_ap = wk2.t[:, 384:512]
                ss_, lnv_, rs_, junk_ = ss.next(), lnv.next(), rs.next(), junk.next()
                S.op("dve", lambda e: e.memset(ss_.t[:, :], 0.0), wr=[ss_])
                S.op("act", lambda e: e.activation(out=junk_.t[:, :], in_=o_ap, func=AF.Square, accum_out=ss_.t[:, 0:1]), rd=[wk2, ss_], wr=[junk_, ss_])
                emit_rstd(S, rs_.t[:, :], rs_, ss_.t[:, :], ss_, 1.0 / 128, lnv_.t[:, :], lnv_)
                yield
                S.op("dve", lambda e: e.tensor_scalar(out=eg_.t[:, :], in0=eg_.t[:, :], scalar1=1.0, scalar2=None, op0=ALU.add), rd=[eg_], wr=[eg_])
                S.op("dve", lambda e: e.reciprocal(out=eg_.t[:, :], in_=eg_.t[:, :]), rd=[eg_], wr=[eg_])
                S.op("dve", lambda e: e.tensor_tensor(out=eg_.t[:, :], in0=eg_.t[:, :], in1=gt_ap, op=ALU.mult), rd=[eg_, pj], wr=[eg_])
                S.op("dve", lambda e: e.tensor_tensor(out=eg_.t[:, :], in0=eg_.t[:, :], in1=ogain.t[:, :], op=ALU.mult), rd=[eg_, ogain], wr=[eg_])
                S.op("dve", lambda e: e.scalar_tensor_tensor(out=yb_.t[:, :], in0=o_ap, scalar=rs_.t[:, 0:1], in1=eg_.t[:, :], op0=ALU.mult, op1=ALU.mult),
                     rd=[wk2, rs_, eg_], wr=[yb_])
                yield
                ty_ap, ty = psb_next()
                S.op("pe", lambda e: e.transpose(ty_ap, yb_.t[:, :], ident.t[:, :]), rd=[yb_, ident], wr=[ty])
                S.op("act", lambda e: e.copy(out=o_s.t[:, h, :], in_=ty_ap), rd=[ty], wr=[o_s])

            run_interleaved([head(h) for h in range(NH)])
            S.op("sp", lambda e: e.dma_start(out=o_d[0:NH * 128, t * 128:(t + 1) * 128].rearrange("(h p) t -> p h t", p=128), in_=o_s.t[:, :, :]),
                 rd=[o_s], dsem=o_s.b.name)
        S.barrier()


def attn_tile(C, t, qk_parts, v_fn, acc, acc_ap, scale, far_bias, near, negshift, P, tmpf, qkb):
    S = C.S
    np_ = len(qk_parts)
    groups = [list(range(g0, min(g0 + 4, t + 1))) for g0 in range(0, t + 1, 4)]

    def emit_qk(grp):
        bank = qkb.next()
        for j, kt in enumerate(grp):
            for pi, (K_fn, q_ap, q_T) in enumerate(qk_parts):
                k_ap, k_T = K_fn(kt)
                S.op("pe", lambda e: e.matmul(bank.t[:, j * 128:(j + 1) * 128], k_ap, q_ap, start=(pi == 0), stop=(pi == np_ - 1)),
                     rd=[k_T, q_T], wr=[bank])
        return bank

    def emit_exp(grp, bank):
        P_ = P.next()
        nfar = len([kt for kt in grp if (kt - t) not in near])
        if nfar:
            S.op("act", lambda e: e.activation(out=P_.t[:, 0:nfar * 128], in_=bank.t[:, 0:nfar * 128], func=AF.Exp, scale=scale, bias=far_bias.t[:, :]),
                 rd=[bank, far_bias], wr=[P_])
        for j, kt in enumerate(grp):
            if (kt - t) in near:
                b = near[kt - t]
                tm = tmpf.next()
                S.op("dve", lambda e: e.scalar_tensor_tensor(out=tm.t[:, :], in0=bank.t[:, j * 128:(j + 1) * 128], scalar=scale, in1=b.t[:, :],
                                                             op0=ALU.mult, op1=ALU.add), rd=[bank, b], wr=[tm])
                S.op("act", lambda e: e.activation(out=P_.t[:, j * 128:(j + 1) * 128], in_=tm.t[:, :], func=AF.Exp, bias=negshift.t[:, :]),
                     rd=[tm, negshift], wr=[P_])
        return P_

    def emit_pv(grp, P_):
        for j, kt in enumerate(grp):
            v_ap, v_T = v_fn(kt)
            S.op("pe", lambda e: e.matmul(acc_ap, P_.t[:, j * 128:(j + 1) * 128], v_ap, start=(kt == 0), stop=(kt == t)), rd=[P_, v_T], wr=[acc])

    look = len(qkb.items) >= 2
    bank = emit_qk(groups[0])
    yield
    for gi, grp in enumerate(groups):
        nxt = None
        if look and gi + 1 < len(groups):
            nxt = emit_qk(groups[gi + 1])
            yield
        P_ = emit_exp(grp, bank)
        yield
        emit_pv(grp, P_)
        yield
        if not look and gi + 1 < len(groups):
            nxt = emit_qk(groups[gi + 1])
            yield
        bank = nxt


def run_interleaved(gens):
    gens = list(gens)
    while gens:
        for g in list(gens):
            try:
                next(g)
            except StopIteration:
                gens.remove(g)


def sub_T(parent, ap):
    x = T(ap)
    x.b = parent.b
    return x


VW = 132


def phase_df(C, xn_d, w_d, vecs_d, bias_d, o_d, consts, mask_d, NT, pfx, NH=2, row0=384):
    nc, S = C.nc, C.S
    SHIFT = 8.0
    S.barrier()
    with ExitStack() as st:
        sb = lambda n, sh, dt: T(st.enter_context(nc.sbuf_tensor(pfx + n + C.sfx, sh, dt)), pfx + n)
        R = lambda n, sh, dt, k=2: Ring([sb("%s%d" % (n, i), sh, dt) for i in range(k)])
        w = sb("w", [128, KD, NH * 384], BF16)
        for h in range(NH):
            S.op("pool", lambda e: e.dma_start(out=w.t[:, :, h * 384:(h + 1) * 384],
                                               in_=w_d[:, h * 384:(h + 1) * 384].rearrange("(k p) c -> p k c", p=128)), wr=[w], dsem=pfx + "w")
        cf = load_const(C, st, pfx + "cf", consts, [128, 512], F32)
        cb = sb("cb", [128, 512], BF16)
        S.op("dve", lambda e: e.tensor_copy(out=cb.t[:, :], in_=cf.t[:, :]), rd=[cf], wr=[cb])
        ident = sub_T(cb, cb.t[:, 0:128])
        mask = load_const(C, st, pfx + "mask", mask_d, [128, 128], F32)
        vecs = sb("vecs", [128, 8, 128], F32)
        for j in range(8):
            S.op("sp", lambda e: e.dma_start(out=vecs.t[:, j, :], in_=vecs_d[j, :].partition_broadcast(128)), wr=[vecs], dsem=pfx + "vecs")
        qg, kg = vecs.t[:, 0, 0:64], vecs.t[:, 1, 0:64]
        junk64 = sb("junk64", [128, 64], F32)
        prod = sb("prod", [128, 64], F32)
        lsum = sb("lsum", [128, 2], F32)
        S.op("dve", lambda e: e.memset(lsum.t[:, :], 0.0), wr=[lsum])
        for i in range(2):
            S.op("dve", lambda e: e.tensor_tensor(out=prod.t[:, :], in0=vecs.t[:, 2 + 2 * i, 0:64], in1=vecs.t[:, 3 + 2 * i, 0:64], op=ALU.mult), rd=[vecs], wr=[prod])
            S.op("act", lambda e: e.activation(out=junk64.t[:, :], in_=prod.t[:, :], func=AF.Identity, accum_out=lsum.t[:, i:i + 1]), rd=[prod, lsum], wr=[junk64, lsum])
        S.op("act", lambda e: e.activation(out=lsum.t[:, :], in_=lsum.t[:, :], func=AF.Exp), rd=[lsum], wr=[lsum])
        lam = sb("lam", [128, 1], F32)
        S.op("dve", lambda e: e.tensor_tensor(out=lam.t[:, :], in0=lsum.t[:, 0:1], in1=lsum.t[:, 1:2], op=ALU.subtract), rd=[lsum], wr=[lam])
        S.op("dve", lambda e: e.tensor_tensor(out=lam.t[:, :], in0=lam.t[:, :], in1=vecs.t[:, 7, 0:1], op=ALU.add), rd=[lam, vecs], wr=[lam])
        sgain = sb("sgain", [128, 128], F32)
        S.op("dve", lambda e: e.tensor_scalar(out=sgain.t[:, :], in0=vecs.t[:, 6, :], scalar1=vecs.t[:, 7, 1:2], scalar2=None, op0=ALU.mult), rd=[vecs], wr=[sgain])
        negshift = sb("negshift", [128, 1], F32)
        S.op("dve", lambda e: e.memset(negshift.t[:, :], -SHIFT), wr=[negshift])
        farb = [sb("farb%d" % h, [128, 1], F32) for h in range(NH)]
        for h in range(NH):
            S.op("dve", lambda e: e.tensor_scalar(out=farb[h].t[:, :], in0=vecs.t[:, 7, 2 + h:3 + h], scalar1=-SHIFT, scalar2=None, op0=ALU.add), rd=[vecs], wr=[farb[h]])
        bt = [[sb("bt%d_%d" % (h, r), [128, 128], F32) for r in range(2)] for h in range(NH)]
        for h in range(NH):
            for r in range(2):
                S.op("sp", lambda e: e.dma_start(out=bt[h][r].t[:, :], in_=bias_d[h, r, :, :]), wr=[bt[h][r]], dsem=pfx + "bt%d_%d" % (h, r))
            S.op("dve", lambda e: e.tensor_tensor(out=bt[h][0].t[:, :], in0=bt[h][0].t[:, :], in1=mask.t[:, :], op=ALU.add), rd=[bt[h][0], mask], wr=[bt[h][0]])
        KT = [[sb("KT%d_%d" % (h, m), [128, NT * 128], BF16) for m in range(2)] for h in range(NH)]
        for h in range(NH):
            for m in range(2):
                S.op("dve", lambda e: e.memset(KT[h][m].t[:, :], 0.0), wr=[KT[h][m]])
        Va = sb("Va", [128, NT * NH * VW], BF16)
        S.op("dve", lambda e: e.memset(Va.t[:, :], 1.0), wr=[Va])
        vofs = lambda t_, h_: (t_ * NH + h_) * VW
        xt = R("xt", [128, KD, 128], BF16)
        ss4, ln4, rs4 = (R(n, [128, 4], F32, 3) for n in ("ss4", "ln4", "rs4"))
        junk = R("junk", [128, 128], F32)
        qk = R("qk", [128, 256], BF16, 3)
        qT = [R("qT%d" % h, [128, 128], BF16) for h in range(NH)]
        P = R("P", [128, 512], BF16, 3)
        tmpf = R("tmpf", [128, 128], F32, 3)
        rr = R("rr", [128, 4], F32, 3)
        t2, of = R("t2", [128, 128], F32), R("of", [128, 128], F32)
        ss1, ln1, rs1 = (R(n, [128, 1], F32) for n in ("ss1", "ln1", "rs1"))
        yb = R("yb", [128, 128], BF16)
        ost = R("ost", [128, NH, 128], BF16)
        pjb = C.banks([0, 1])
        qkbs = [Ring(C.banks([0, 1])), Ring(C.banks([2, 3]))]
        Ps = [R("Ps%d_" % m, [128, 512], BF16, 2) for m in range(2)]
        tmpfs = [R("tmpfs%d_" % m, [128, 128], F32, 2) for m in range(2)]
        accs = C.banks([5, 6])
        pbi = [0]

        def psb_next():
            i = pbi[0] % 8
            pbi[0] += 1
            return psb_region(C, i)

        for t in range(NT):
            x_s = xt.next()
            S.op("sp", lambda e: e.dma_start(out=x_s.t[:, :, :], in_=xn_d[t, :, :, :]), wr=[x_s], dsem=x_s.b.name)
            o_s = ost.next()
            qTs = []
            for h in range(NH):
                pj = pjb[h]
                for k in range(KD):
                    S.op("pe", lambda e: e.matmul(pj.t[:, 0:384], x_s.t[:, k, :], w.t[:, k, h * 384:(h + 1) * 384], start=(k == 0), stop=(k == KD - 1)),
                         rd=[x_s, w], wr=[pj])
                ss_, ln_, rs_ = ss4.next(), ln4.next(), rs4.next()
                S.op("dve", lambda e: e.memset(ss_.t[:, :], 0.0), wr=[ss_])
                for j in range(4):
                    jk = junk.next()
                    S.op("act", lambda e: e.activation(out=jk.t[:, 0:64], in_=pj.t[:, j * 64:(j + 1) * 64], func=AF.Square, accum_out=ss_.t[:, j:j + 1]),
                         rd=[pj, ss_], wr=[jk, ss_])
                emit_rstd(S, rs_.t[:, :], rs_, ss_.t[:, :], ss_, 1.0 / 64, ln_.t[:, :], ln_)
                qk_ = qk.next()
                for j in range(4):
                    g_ap = qg if j < 2 else kg
                    S.op("dve", lambda e: e.scalar_tensor_tensor(out=qk_.t[:, j * 64:(j + 1) * 64], in0=pj.t[:, j * 64:(j + 1) * 64], scalar=rs_.t[:, j:j + 1],
                                                                 in1=g_ap, op0=ALU.mult, op1=ALU.mult), rd=[pj, rs_, vecs], wr=[qk_])
                S.op("dve", lambda e: e.tensor_copy(out=Va.t[:, vofs(t, h):vofs(t, h) + 128], in_=pj.t[:, 256:384]), rd=[pj], wr=[Va])
                tq_ap, tq = psb_next()
                tk_ap, tk = psb_next()
                S.op("pe", lambda e: e.transpose(tq_ap, qk_.t[:, 0:128], ident.t), rd=[qk_, ident], wr=[tq])
                S.op("pe", lambda e: e.transpose(tk_ap, qk_.t[:, 128:256], ident.t), rd=[qk_, ident], wr=[tk])
                qT_ = qT[h].next()
                S.op("act", lambda e: e.copy(out=qT_.t[:, :], in_=tq_ap), rd=[tq], wr=[qT_])
                S.op("act", lambda e: e.copy(out=KT[h][0].t[0:64, t * 128:(t + 1) * 128], in_=tk_ap[0:64, :]), rd=[tk], wr=[KT[h][0]])
                S.op("act", lambda e: e.copy(out=KT[h][1].t[64:128, t * 128:(t + 1) * 128], in_=tk_ap[64:128, :]), rd=[tk], wr=[KT[h][1]])
                qTs.append(qT_)
            for h in range(NH):
                acc0, acc1 = accs
                run_interleaved([
                    attn_tile(C, t, [(lambda kt, h=h, m=m: (KT[h][m].t[:, kt * 128:(kt + 1) * 128], KT[h][m]), qTs[h].t[:, :], qTs[h])],
                              lambda kt, h=h: (Va.t[:, vofs(kt, h):vofs(kt, h) + 129], Va), accs[m], accs[m].t[:, 0:129], 0.125, farb[h],
                              {0: bt[h][0], -1: bt[h][1]}, negshift, Ps[m], tmpfs[m], qkbs[m])
                    for m in range(2)])
                r_ = rr.next()
                S.op("dve", lambda e: e.reciprocal(out=r_.t[:, 0:1], in_=acc0.t[:, 128:129]), rd=[acc0], wr=[r_])
                S.op("dve", lambda e: e.reciprocal(out=r_.t[:, 1:2], in_=acc1.t[:, 128:129]), rd=[acc1], wr=[r_])
                S.op("dve", lambda e: e.tensor_tensor(out=r_.t[:, 2:3], in0=r_.t[:, 1:2], in1=lam.t[:, :], op=ALU.mult), rd=[r_, lam], wr=[r_])
                t2_, of_ = t2.next(), of.next()
                S.op("dve", lambda e: e.tensor_scalar(out=t2_.t[:, :], in0=acc1.t[:, 0:128], scalar1=r_.t[:, 2:3], scalar2=None, op0=ALU.mult), rd=[acc1, r_], wr=[t2_])
                S.op("dve", lambda e: e.scalar_tensor_tensor(out=of_.t[:, :], in0=acc0.t[:, 0:128], scalar=r_.t[:, 0:1], in1=t2_.t[:, :],
                                                             op0=ALU.mult, op1=ALU.subtract), rd=[acc0, r_, t2_], wr=[of_])
                s1, l1, r1, jk = ss1.next(), ln1.next(), rs1.next(), junk.next()
                S.op("dve", lambda e: e.memset(s1.t[:, :], 0.0), wr=[s1])
                S.op("act", lambda e: e.activation(out=jk.t[:, :], in_=of_.t[:, :], func=AF.Square, accum_out=s1.t[:, 0:1]), rd=[of_, s1], wr=[jk, s1])
                emit_rstd(S, r1.t[:, :], r1, s1.t[:, :], s1, 1.0 / 128, l1.t[:, :], l1)
                yb_ = yb.next()
                S.op("dve", lambda e: e.scalar_tensor_tensor(out=yb_.t[:, :], in0=of_.t[:, :], scalar=r1.t[:, 0:1], in1=sgain.t[:, :],
                                                             op0=ALU.mult, op1=ALU.mult), rd=[of_, r1, sgain], wr=[yb_])
                ty_ap, ty = psb_next()
                S.op("pe", lambda e: e.transpose(ty_ap, yb_.t[:, :], ident.t), rd=[yb_, ident], wr=[ty])
                S.op("act", lambda e: e.copy(out=o_s.t[:, h, :], in_=ty_ap), rd=[ty], wr=[o_s])
            S.op("sp", lambda e: e.dma_start(out=o_d[row0:row0 + NH * 128, t * 128:(t + 1) * 128].rearrange("(h p) t -> p h t", p=128), in_=o_s.t[:, :, :]),
                 rd=[o_s], dsem=o_s.b.name)
        S.barrier()


def phase_ml(C, xn_d, w_d, wuq_d, wukv_d, vecs_d, cs_d, o_d, consts, mask_d, NT, pfx, NH=3, row0=640):
    nc, S = C.nc, C.S
    SHIFT = 14.0
    SCALE = 192.0 ** -0.5
    S.barrier()
    with ExitStack() as st:
        sb = lambda n, sh, dt: T(st.enter_context(nc.sbuf_tensor(pfx + n + C.sfx, sh, dt)), pfx + n)
        R = lambda n, sh, dt, k=2: Ring([sb("%s%d" % (n, i), sh, dt) for i in range(k)])
        w = sb("w", [128, KD, 832], BF16)
        for (c0, c1) in ((0, 512), (512, 832)):
            S.op("pool", lambda e: e.dma_start(out=w.t[:, :, c0:c1], in_=w_d[:, c0:c1].rearrange("(k p) c -> p k c", p=128)), wr=[w], dsem=pfx + "w")
        wuq = sb("wuq", [128, 4, NH * 192], BF16)
        S.op("pool", lambda e: e.dma_start(out=wuq.t[:, :, :], in_=wuq_d.rearrange("(k p) c -> p k c", p=128)), wr=[wuq], dsem=pfx + "wuq")
        wukv = sb("wukv", [128, 2, NH * 256], BF16)
        S.op("pool", lambda e: e.dma_start(out=wukv.t[:, :, :], in_=wukv_d.rearrange("(k p) c -> p k c", p=128)), wr=[wukv], dsem=pfx + "wukv")
        cf = load_const(C, st, pfx + "cf", consts, [128, 512], F32)
        cb = sb("cb", [128, 512], BF16)
        S.op("dve", lambda e: e.tensor_copy(out=cb.t[:, :], in_=cf.t[:, :]), rd=[cf], wr=[cb])
        ident = sub_T(cb, cb.t[:, 0:128])
        mask = load_const(C, st, pfx + "mask", mask_d, [128, 128], F32)
        vecs = sb("vecs", [128, 4, 512], F32)
        for j in range(4):
            S.op("sp", lambda e: e.dma_start(out=vecs.t[:, j, :], in_=vecs_d[j, :].partition_broadcast(128)), wr=[vecs], dsem=pfx + "vecs")
        cs = load_const(C, st, pfx + "cs", cs_d, [128, 2 * NT * 32], F32)
        negshift = sb("negshift", [128, 1], F32)
        S.op("dve", lambda e: e.memset(negshift.t[:, :], -SHIFT), wr=[negshift])
        KTa = [sb("KTa%d" % h, [128, NT * 128], BF16) for h in range(NH)]
        KTb = [sb("KTb%d" % h, [128, NT * 128], BF16) for h in range(NH)]
        Va = sb("Va", [128, NT * NH * VW], BF16)
        S.op("dve", lambda e: e.memset(Va.t[:, :], 1.0), wr=[Va])
        vofs = lambda t_, h_: (t_ * NH + h_) * VW
        xt = R("xt", [128, KD, 128], BF16)
        ss2, ln2, rs2 = (R(n, [128, 4], F32) for n in ("ss2", "ln2", "rs2"))
        junk = R("junk", [128, 512], F32)
        cqn = R("cqn", [128, 512], BF16)
        ckvn = R("ckvn", [128, 256], BF16)
        kr = R("kr", [128, 64], F32)
        cqT = R("cqT", [128, 4, 128], BF16)
        ckvT = R("ckvT", [128, 2, 128], BF16)
        ssh, lnh, rsh = (R(n, [128, 4], F32, 3) for n in ("ssh", "lnh", "rsh"))
        qn_b = R("qnb", [128, 128], BF16, 3)
        kn_b = R("knb", [128, 128], BF16, 3)
        qr_f = R("qrf", [128, 64], F32, 3)
        kr_f = R("krf", [128, 64], F32, 3)
        ra, rb = R("ra", [128, 32], F32, 3), R("rb", [128, 32], F32, 3)
        qrp = [sb("qrp%d" % i, [128, 128], BF16) for i in range(2)]
        krp = [sb("krp%d" % i, [128, 128], BF16) for i in range(2)]
        for i in range(2):
            S.op("dve", lambda e: e.memset(qrp[i].t[:, :], 0.0), wr=[qrp[i]])
            S.op("dve", lambda e: e.memset(krp[i].t[:, :], 0.0), wr=[krp[i]])
        qrpr, krpr = Ring(qrp), Ring(krp)
        qTa = [R("qTa%d" % h, [128, 128], BF16) for h in range(NH)]
        qTb = [R("qTb%d" % h, [128, 128], BF16) for h in range(NH)]
        P = R("P", [128, 512], BF16, 3)
        tmpf = R("tmpf", [128, 128], F32, 3)
        rr = R("rr", [128, 1], F32, 3)
        yb = R("yb", [128, 128], BF16)
        ost = R("ost", [128, NH, 128], BF16)
        b0, b1, b2, b3, b4 = C.banks([0, 1, 2, 3, 4])
        qreg = [(b2, 0), (b2, 192), (b3, 0)]
        kvreg = [(b4, 0), (b4, 256), (b3, 192)]
        qkbs = [Ring(C.banks([h])) for h in range(NH)]
        accs = C.banks([3, 4, 5])
        Ps = [R("Ps%d_" % h, [128, 512], BF16, 2) for h in range(NH)]
        tmpfs = [R("tmpfs%d_" % h, [128, 128], F32, 2) for h in range(NH)]
        accT = C.banks([6])[0]
        pbi = [0]

        def psb_next():
            i = pbi[0] % 8
            pbi[0] += 1
            return psb_region(C, i)

        def transpose_to(src_ap, src_T, dst_ap, dst_T, eng="act"):
            tp_ap, tp = psb_next()
            S.op("pe", lambda e: e.transpose(tp_ap, src_ap, ident.t), rd=[src_T, ident], wr=[tp])
            if eng == "act":
                S.op("act", lambda e: e.copy(out=dst_ap, in_=tp_ap), rd=[tp], wr=[dst_T])
            else:
                S.op("dve", lambda e: e.tensor_copy(out=dst_ap, in_=tp_ap), rd=[tp], wr=[dst_T])

        def rope(src_T, src, dst_T, dst, t):
            cos = cs.t[:, t * 32:(t + 1) * 32]
            sin = cs.t[:, NT * 32 + t * 32:NT * 32 + (t + 1) * 32]
            x1, x2 = src[:, 0:32], src[:, 32:64]
            a, b = ra.next(), rb.next()
            S.op("dve", lambda e: e.tensor_tensor(out=a.t[:, :], in0=x1, in1=cos, op=ALU.mult), rd=[src_T, cs], wr=[a])
            S.op("dve", lambda e: e.tensor_tensor(out=b.t[:, :], in0=x2, in1=sin, op=ALU.mult), rd=[src_T, cs], wr=[b])
            S.op("dve", lambda e: e.tensor_tensor(out=dst[:, 0:32], in0=a.t[:, :], in1=b.t[:, :], op=ALU.subtract), rd=[a, b], wr=[dst_T])
            a, b = ra.next(), rb.next()
            S.op("dve", lambda e: e.tensor_tensor(out=a.t[:, :], in0=x2, in1=cos, op=ALU.mult), rd=[src_T, cs], wr=[a])
            S.op("dve", lambda e: e.tensor_tensor(out=b.t[:, :], in0=x1, in1=sin, op=ALU.mult), rd=[src_T, cs], wr=[b])
            S.op("dve", lambda e: e.tensor_tensor(out=dst[:, 32:64], in0=a.t[:, :], in1=b.t[:, :], op=ALU.add), rd=[a, b], wr=[dst_T])

        for t in range(NT):
            x_s = xt.next()
            S.op("sp", lambda e: e.dma_start(out=x_s.t[:, :, :], in_=xn_d[t, :, :, :]), wr=[x_s], dsem=x_s.b.name)
            o_s = ost.next()
            for k in range(KD):
                S.op("pe", lambda e: e.matmul(b0.t[:, :], x_s.t[:, k, :], w.t[:, k, 0:512], start=(k == 0), stop=(k == KD - 1)), rd=[x_s, w], wr=[b0])
            for k in range(KD):
                S.op("pe", lambda e: e.matmul(b1.t[:, 0:320], x_s.t[:, k, :], w.t[:, k, 512:832], start=(k == 0), stop=(k == KD - 1)), rd=[x_s, w], wr=[b1])
            ss_, ln_, rs_ = ss2.next(), ln2.next(), rs2.next()
            S.op("dve", lambda e: e.memset(ss_.t[:, :], 0.0), wr=[ss_])
            jk = junk.next()
            S.op("act", lambda e: e.activation(out=jk.t[:, :], in_=b0.t[:, :], func=AF.Square, accum_out=ss_.t[:, 0:1]), rd=[b0, ss_], wr=[jk, ss_])
            jk = junk.next()
            S.op("act", lambda e: e.activation(out=jk.t[:, 0:256], in_=b1.t[:, 0:256], func=AF.Square, accum_out=ss_.t[:, 1:2]), rd=[b1, ss_], wr=[jk, ss_])
            kr_ = kr.next()
            S.op("act", lambda e: e.copy(out=kr_.t[:, :], in_=b1.t[:, 256:320]), rd=[b1], wr=[kr_])
            jk = junk.next()
            S.op("act", lambda e: e.activation(out=jk.t[:, 0:64], in_=kr_.t[:, :], func=AF.Square, accum_out=ss_.t[:, 2:3]), rd=[kr_, ss_], wr=[jk, ss_])
            S.op("dve", lambda e: e.tensor_scalar(out=ss_.t[:, 0:1], in0=ss_.t[:, 0:1], scalar1=0.5, scalar2=None, op0=ALU.mult), rd=[ss_], wr=[ss_])
            emit_rstd(S, rs_.t[:, 0:2], rs_, ss_.t[:, 0:2], ss_, 1.0 / 256, ln_.t[:, 0:2], ln_)
            cqn_, ckvn_ = cqn.next(), ckvn.next()
            S.op("dve", lambda e: e.scalar_tensor_tensor(out=cqn_.t[:, :], in0=b0.t[:, :], scalar=rs_.t[:, 0:1], in1=vecs.t[:, 0, :], op0=ALU.mult, op1=ALU.mult),
                 rd=[b0, rs_, vecs], wr=[cqn_])
            S.op("dve", lambda e: e.scalar_tensor_tensor(out=ckvn_.t[:, :], in0=b1.t[:, 0:256], scalar=rs_.t[:, 1:2], in1=vecs.t[:, 1, 0:256], op0=ALU.mult, op1=ALU.mult),
                 rd=[b1, rs_, vecs], wr=[ckvn_])
            cqT_, ckvT_ = cqT.next(), ckvT.next()
            for r in range(4):
                transpose_to(cqn_.t[:, r * 128:(r + 1) * 128], cqn_, cqT_.t[:, r, :], cqT_, "act" if r % 2 == 0 else "dve")
            for r in range(2):
                transpose_to(ckvn_.t[:, r * 128:(r + 1) * 128], ckvn_, ckvT_.t[:, r, :], ckvT_, "act" if r % 2 == 0 else "dve")
            for h in range(NH):
                qb, qo = qreg[h]
                for r in range(4):
                    S.op("pe", lambda e: e.matmul(qb.t[:, qo:qo + 192], cqT_.t[:, r, :], wuq.t[:, r, h * 192:(h + 1) * 192], start=(r == 0), stop=(r == 3)),
                         rd=[cqT_, wuq], wr=[qb])
                kb, ko = kvreg[h]
                for r in range(2):
                    S.op("pe", lambda e: e.matmul(kb.t[:, ko:ko + 256], ckvT_.t[:, r, :], wukv.t[:, r, h * 256:(h + 1) * 256], start=(r == 0), stop=(r == 1)),
                         rd=[ckvT_, wukv], wr=[kb])
            qTs = []
            for h in range(NH):
                qb, qo = qreg[h]
                kb, ko = kvreg[h]
                sh_, lh_, rh_ = ssh.next(), lnh.next(), rsh.next()
                S.op("dve", lambda e: e.memset(sh_.t[:, :], 0.0), wr=[sh_])
                jk = junk.next()
                S.op("act", lambda e: e.activation(out=jk.t[:, 0:192], in_=qb.t[:, qo:qo + 192], func=AF.Square, accum_out=sh_.t[:, 0:1]), rd=[qb, sh_], wr=[jk, sh_])
                jk = junk.next()
                S.op("act", lambda e: e.activation(out=jk.t[:, 0:128], in_=kb.t[:, ko:ko + 128], func=AF.Square, accum_out=sh_.t[:, 1:2]), rd=[kb, sh_], wr=[jk, sh_])
                S.op("dve", lambda e: e.tensor_tensor(out=sh_.t[:, 1:2], in0=sh_.t[:, 1:2], in1=ss_.t[:, 2:3], op=ALU.add), rd=[sh_, ss_], wr=[sh_])
                emit_rstd(S, rh_.t[:, 0:2], rh_, sh_.t[:, 0:2], sh_, 1.0 / 192, lh_.t[:, 0:2], lh_)
                qn_, kn_, qr_, kf_ = qn_b.next(), kn_b.next(), qr_f.next(), kr_f.next()
                S.op("dve", lambda e: e.scalar_tensor_tensor(out=qn_.t[:, :], in0=qb.t[:, qo:qo + 128], scalar=rh_.t[:, 0:1], in1=vecs.t[:, 2, 0:128], op0=ALU.mult, op1=ALU.mult),
                     rd=[qb, rh_, vecs], wr=[qn_])
                S.op("dve", lambda e: e.scalar_tensor_tensor(out=qr_.t[:, :], in0=qb.t[:, qo + 128:qo + 192], scalar=rh_.t[:, 0:1], in1=vecs.t[:, 2, 128:192], op0=ALU.mult, op1=ALU.mult),
                     rd=[qb, rh_, vecs], wr=[qr_])
                S.op("dve", lambda e: e.scalar_tensor_tensor(out=kn_.t[:, :], in0=kb.t[:, ko:ko + 128], scalar=rh_.t[:, 1:2], in1=vecs.t[:, 3, 0:128], op0=ALU.mult, op1=ALU.mult),
                     rd=[kb, rh_, vecs], wr=[kn_])
                S.op("dve", lambda e: e.scalar_tensor_tensor(out=kf_.t[:, :], in0=kr_.t[:, :], scalar=rh_.t[:, 1:2], in1=vecs.t[:, 3, 128:192], op0=ALU.mult, op1=ALU.mult),
                     rd=[kr_, rh_, vecs], wr=[kf_])
                S.op("act", lambda e: e.copy(out=Va.t[:, vofs(t, h):vofs(t, h) + 128], in_=kb.t[:, ko + 128:ko + 256]), rd=[kb], wr=[Va])
                qp_, kp_ = qrpr.next(), krpr.next()
                rope(qr_, qr_.t, qp_, qp_.t, t)
                rope(kf_, kf_.t, kp_, kp_.t, t)
                qa_, qb_ = qTa[h].next(), qTb[h].next()
                transpose_to(qn_.t[:, :], qn_, qa_.t[:, :], qa_, "act")
                transpose_to(qp_.t[:, :], qp_, qb_.t[:, :], qb_, "dve")
                transpose_to(kn_.t[:, :], kn_, KTa[h].t[:, t * 128:(t + 1) * 128], KTa[h], "act")
                transpose_to(kp_.t[:, :], kp_, KTb[h].t[:, t * 128:(t + 1) * 128], KTb[h], "dve")
                qTs.append((qa_, qb_))
            run_interleaved([
                attn_tile(C, t, [(lambda kt, h=h: (KTa[h].t[:, kt * 128:(kt + 1) * 128], KTa[h]), qTs[h][0].t[:, :], qTs[h][0]),
                                 (lambda kt, h=h: (KTb[h].t[:, kt * 128:(kt + 1) * 128], KTb[h]), qTs[h][1].t[:, :], qTs[h][1])],
                          lambda kt, h=h: (Va.t[:, vofs(kt, h):vofs(kt, h) + 129], Va), accs[h], accs[h].t[:, 0:129], SCALE, negshift,
                          {0: mask}, negshift, Ps[h], tmpfs[h], qkbs[h])
                for h in range(NH)])
            for h in range(NH):
                a0 = 0
                acc = accs[h]
                r_ = rr.next()
                S.op("dve", lambda e: e.reciprocal(out=r_.t[:, 0:1], in_=acc.t[:, a0 + 128:a0 + 129]), rd=[acc], wr=[r_])
                yb_ = yb.next()
                S.op("dve", lambda e: e.tensor_scalar(out=yb_.t[:, :], in0=acc.t[:, a0:a0 + 128], scalar1=r_.t[:, 0:1], scalar2=None, op0=ALU.mult), rd=[acc, r_], wr=[yb_])
                transpose_to(yb_.t[:, :], yb_, o_s.t[:, h, :], o_s, "act")
            S.op("sp", lambda e: e.dma_start(out=o_d[row0:row0 + NH * 128, t * 128:(t + 1) * 128].rearrange("(h p) t -> p h t", p=128), in_=o_s.t[:, :, :]),
                 rd=[o_s], dsem=o_s.b.name)
        S.barrier()


NTOK = 2048
SEQ = 4096
NT_SEQ = SEQ // 128
TT_FFN = 1024


def _ffn_inputs(nc, sfx):
    g = nc.dram_tensor("g" + sfx, [D], F32, kind="ExternalInput").ap()
    wg = nc.dram_tensor("wg" + sfx, [D, DFF], F32, kind="ExternalInput").ap()
    wu = nc.dram_tensor("wu" + sfx, [D, DFF], F32, kind="ExternalInput").ap()
    wd = nc.dram_tensor("wd" + sfx, [DFF, D], F32, kind="ExternalInput").ap()
    return g, wg, wu, wd


def build_PA():
    nc = bass.Bass("TRN2", target_bir_lowering=False)
    x = nc.dram_tensor("x", [D, NTOK], F32, kind="ExternalInput").ap()
    g, wg, wu, wd = _ffn_inputs(nc, "")
    gm = nc.dram_tensor("gm", [D], F32, kind="ExternalInput").ap()
    xo = nc.dram_tensor("xo", [D, NTOK], F32, kind="ExternalOutput").ap()
    xn = nc.dram_tensor("xn", [NTOK // 128, 128, KD, 128], BF16, kind="ExternalOutput").ap()
    with ExitStack() as st:
        C = Ctx(nc, st)
        phase_ffn(C, x, xo, g, wg, wu, wd, NTOK, TT_FFN, "A")
        phase_norm(C, xo, gm, xn, NTOK, 512, "N")
        C.S.barrier()
    return nc


def build_PM():
    nc = bass.Bass("TRN2", target_bir_lowering=False)
    dt = lambda n, sh, d=F32: nc.dram_tensor(n, sh, d, kind="ExternalInput").ap()
    xn = dt("xn", [NT_SEQ, 128, KD, 128], BF16)
    whg, wdf, wml = dt("whg", [D, 1536]), dt("wdf", [D, 768]), dt("wml", [D, 832])
    lbz, cm, og = dt("lbz", [4, 384]), dt("cm", [16]), dt("og", [128])
    dfv, dfb = dt("dfv", [8, 128]), dt("dfb", [2, 2, 128, 128])
    wuq, wukv, mlv = dt("wuq", [512, 576]), dt("wukv", [256, 768]), dt("mlv", [4, 512])
    cs = dt("cs", [128, 2 * NT_SEQ * 32])
    call, cmask = dt("c_all", [128, 512]), dt("c_mask", [128, 128])
    o = nc.dram_tensor("o", [1024, SEQ], BF16, kind="ExternalOutput").ap()
    with ExitStack() as st:
        C = Ctx(nc, st)
        phase_hg(C, xn, whg, lbz, cm, og, o, call, NT_SEQ, "H", 3)
        phase_df(C, xn, wdf, dfv, dfb, o, call, cmask, NT_SEQ, "F", 2, row0=384)
        phase_ml(C, xn, wml, wuq, wukv, mlv, cs, o, call, cmask, NT_SEQ, "M", 3, row0=640)
        C.S.barrier()
    return nc


def build_PB():
    nc = bass.Bass("TRN2", target_bir_lowering=False)
    x = nc.dram_tensor("x", [D, NTOK], F32, kind="ExternalInput").ap()
    o = nc.dram_tensor("o", [D, NTOK], BF16, kind="ExternalInput").ap()
    wo = nc.dram_tensor("wo", [D, D], F32, kind="ExternalInput").ap()
    g, wg, wu, wd = _ffn_inputs(nc, "")
    xmid = nc.dram_tensor("xmid", [D, NTOK], F32, kind="ExternalOutput").ap()
    xo = nc.dram_tensor("xo", [D, NTOK], F32, kind="ExternalOutput").ap()
    with ExitStack() as st:
        C = Ctx(nc, st)
        phase_wout(C, x, xmid, o, wo, NTOK, "O")
        phase_ffn(C, xmid, xo, g, wg, wu, wd, NTOK, TT_FFN, "B")
        C.S.barrier()
    return nc


def _consts():
    idx = np.arange(128)
    same = (idx[:, None] // 64) == (idx[None, :] // 64)
    u2 = (same & (idx[:, None] <= idx[None, :])).astype(np.float32)
    mid = (same & ((idx[:, None] % 64) <= 31)).astype(np.float32)
    cmat = u2 - mid
    sel = np.zeros((128, 128), np.float32)
    for c in range(2):
        inch = (idx // 64) == c
        sel[:, 3 * c + 0] = inch & ((idx % 64) <= 31)
        sel[:, 3 * c + 1] = inch & ((idx % 64) >= 32)
        sel[:, 3 * c + 2] = inch
    c_all = np.ascontiguousarray(np.concatenate([np.eye(128, dtype=np.float32), u2, cmat, sel], axis=1))
    ok = (idx[:, None] // 64) <= (idx[None, :] // 64)
    c_mask = np.where(ok, 0.0, NEG).astype(np.float32)
    pos = np.arange(SEQ, dtype=np.float32)
    freqs = (np.float32(10000.0) ** (-np.arange(0, 64, 2, dtype=np.float32) / np.float32(64))).astype(np.float32)
    ang = pos[:, None] * freqs[None, :]
    cos = np.cos(ang).astype(np.float32).reshape(NT_SEQ, 128, 32).transpose(1, 0, 2).reshape(128, NT_SEQ * 32)
    sin = np.sin(ang).astype(np.float32).reshape(NT_SEQ, 128, 32).transpose(1, 0, 2).reshape(128, NT_SEQ * 32)
    cs = np.ascontiguousarray(np.concatenate([cos, sin], axis=1))
    return c_all, c_mask, cs


def _t5_bucket_idx():
    import jax
    import jax.numpy as jnp
    idx = np.arange(128)
    out = []
    with jax.default_device(jax.devices("cpu")[0]):
        for r in (0, -1):
            rel = jnp.asarray(((idx[:, None] + 128 * r) - idx[None, :]).astype(np.int32))
            half, max_exact = 16, 8
            ret = (rel > 0).astype(jnp.int32) * half
            n = jnp.abs(rel)
            large = max_exact + (jnp.log(jnp.maximum(n, 1).astype(jnp.float32) / max_exact)
                                 / math.log(128 / max_exact) * (half - max_exact)).astype(jnp.int32)
            large = jnp.minimum(large, half - 1)
            out.append(np.asarray(ret + jnp.where(n < max_exact, n, large)))
    return np.stack(out, 0)


_IN_OFF = [0, 768, 1536, 2304, 3072, 3584, 4096, 4608, 5120, 5376, 5440]


PAIRS = [[0, 1], [2, 3], [4, 5], [6, 7]]


def build_fused(L=4):
    nc = bass.Bass("TRN2", target_bir_lowering=False)
    dt = lambda n, sh, d=F32: nc.dram_tensor(n, sh, d, kind="ExternalInput").ap()
    x = dt("x", [D, NTOK])
    ffn = {}
    for ab in ("a", "b"):
        ffn[ab] = (dt("ffn_%s_norm" % ab, [L, D]), dt("ffn_%s_w_gate" % ab, [L, D, DFF]), dt("ffn_%s_w_up" % ab, [L, D, DFF]),
                   dt("ffn_%s_w_down" % ab, [L, DFF, D]))
    gm = dt("mix_norm", [L, D])
    whg, wdf, wml = dt("whg", [L, D, 1536]), dt("wdf", [L, D, 768]), dt("wml", [L, D, 832])
    lbz, cm, og = dt("lbz", [4, 384]), dt("cm", [L, 16]), dt("og", [L, 128])
    dfv, dfb = dt("dfv", [L, 8, 128]), dt("dfb", [2, 2, 128, 128])
    wuq, wukv, mlv = dt("wuq", [L, 512, 576]), dt("wukv", [L, 256, 768]), dt("mlv", [L, 4, 512])
    cs = dt("cs", [128, 2 * NT_SEQ * 32])
    call, cmask = dt("c_all", [128, 512]), dt("c_mask", [128, 128])
    wo = dt("wo", [L, D, D])
    sel = dt("sel", [16])
    xo = nc.dram_tensor("xo", [D, NTOK], F32, kind="ExternalOutput").ap()
    internal = lambda n, sh, d: nc.dram_tensor(n, sh, d, kind="Internal").ap()
    local = lambda n, sh, d: nc.dram_tensor(n, sh, d, addr_space="Local", kind="Internal").ap()
    with ExitStack() as st:
        C = Ctx(nc, st)
        S = C.S
        xcur = x
        for l in range(L):
            C.sfx = "_%d" % l
            xa = internal("xa%d" % l, [D, NTOK], F32)
            xb = internal("xb%d" % l, [D, NTOK], F32)
            xc = xo if l == L - 1 else internal("xc%d" % l, [D, NTOK], F32)
            xns = internal("xns%d" % l, [NTOK // 128, 128, KD, 128], BF16)
            xnf = local("xnf%d" % l, [NT_SEQ, 128, KD, 128], BF16)
            osd = internal("osd%d" % l, [1024, SEQ], BF16)
            ofl = local("ofl%d" % l, [2048, SEQ], BF16)
            g, wg, wu, wd = ffn["a"]
            phase_ffn(C, xcur, xa, g[l], wg[l], wu[l], wd[l], NTOK, TT_FFN, "A")
            phase_norm(C, xa, gm[l], xns, NTOK, 512, "N")
            xns2 = xns.rearrange("n p k t -> (n p) (k t)")
            xnf2 = xnf.rearrange("n p k t -> (n p) (k t)")
            S.barrier()
            for j in range(8):
                S.cc(lambda e: e.collective_compute("AllGather", ALU.bypass, replica_groups=PAIRS,
                                                    ins=[xns2[j * 256:(j + 1) * 256, :]], outs=[xnf2[j * 512:(j + 1) * 512, :]]))
            S.barrier()
            xv = XnView(xnf2)
            phase_hg(C, xv, whg[l], lbz, cm[l], og[l], osd, call, NT_SEQ, "H", 3)
            phase_df(C, xv, wdf[l], dfv[l], dfb, osd, call, cmask, NT_SEQ, "F", 2, row0=384)
            phase_ml(C, xv, wml[l], wuq[l], wukv[l], mlv[l], cs, osd, call, cmask, NT_SEQ, "M", 3, row0=640)
            S.barrier()
            for j in range(8):
                S.cc(lambda e: e.collective_compute("AllGather", ALU.bypass, replica_groups=PAIRS,
                                                    ins=[osd[j * 128:(j + 1) * 128, :]], outs=[ofl[j * 256:(j + 1) * 256, :]]))
            S.barrier()
            phase_wout(C, xa, xb, ofl, wo[l], NTOK, "O", sel_d=sel)
            g, wg, wu, wd = ffn["b"]
            phase_ffn(C, xb, xc, g[l], wg[l], wu[l], wd[l], NTOK, TT_FFN, "B")
            xcur = xc
        S.barrier()
    return nc


class XnView:
    def __init__(self, g2):
        self.g2 = g2

    def __getitem__(self, key):
        t = key[0]
        rank, lt = t // 16, t % 16
        r0 = (lt // 2) * 512 + rank * 256 + (lt % 2) * 128
        return self.g2[r0:r0 + 128, :].rearrange("p (k t) -> p k t", t=128)


def _gathered_row_perm():
    perm = []
    for q in range(16):
        j, r = q // 2, q % 2
        if j < 3:
            b = 3 * r + j
        elif j < 5:
            b = 6 + 2 * r + (j - 3)
        else:
            b = 10 + 3 * r + (j - 5)
        perm += list(range(b * 128, (b + 1) * 128))
    return np.asarray(perm)


def kernel(**inputs):
    f = lambda k: np.ascontiguousarray(np.asarray(inputs[k], dtype=np.float32))
    x = f("x")
    L = 4
    c_all, c_mask, cs = _consts()
    bk = _t5_bucket_idx()
    rel_bias = f("rel_bias")
    w_in = f("w_in")
    ar = np.arange(128)
    shared = {k: f(k) for k in ("ffn_a_norm", "ffn_a_w_gate", "ffn_a_w_up", "ffn_a_w_down", "mix_norm",
                                "ffn_b_norm", "ffn_b_w_gate", "ffn_b_w_up", "ffn_b_w_down")}
    shared["wo"] = np.ascontiguousarray(f("w_out")[:, _gathered_row_perm(), :])
    shared["og"] = f("hgrn_out_norm")
    shared["cs"], shared["c_all"], shared["c_mask"] = cs, c_all, c_mask
    cm = np.zeros((L, 16), np.float32)
    dfv = np.zeros((L, 8, 128), np.float32)
    mlv = np.zeros((L, 4, 512), np.float32)
    for l in range(L):
        linit = 0.8 - 0.6 * math.exp(-0.3 * l)
        cm[l, 1:l + 1] = 1.0
        dfv[l, 0, :64] = f("diff_q_norm")[l]
        dfv[l, 1, :64] = f("diff_k_norm")[l]
        dfv[l, 2, :64] = f("diff_lambda_q1")[l]
        dfv[l, 3, :64] = f("diff_lambda_k1")[l]
        dfv[l, 4, :64] = f("diff_lambda_q2")[l]
        dfv[l, 5, :64] = f("diff_lambda_k2")[l]
        dfv[l, 6, :] = f("diff_subln")[l]
        dfv[l, 7, 0] = linit
        dfv[l, 7, 1] = 1.0 - linit
        mlv[l, 0, :] = f("mla_q_lora_norm")[l]
        mlv[l, 1, :256] = f("mla_kv_lora_norm")[l]
        mlv[l, 2, :192] = f("mla_q_norm")[l]
        mlv[l, 3, :192] = f("mla_k_norm")[l]
    shared["cm"], shared["mlv"] = cm, mlv
    per_g = []
    for g in range(2):
        hg_cols = np.concatenate([_IN_OFF[j] + (3 * g + h) * 128 + ar for h in range(3) for j in range(4)])
        df_cols = np.concatenate([_IN_OFF[4 + j] + (2 * g + h) * 128 + ar for h in range(2) for j in range(3)])
        dg = dfv.copy()
        dg[:, 7, 2:4] = rel_bias[15, 2 * g:2 * g + 2]
        sel = np.zeros(16, np.float32)
        sel[g] = 1.0
        per_g.append({
            "whg": np.ascontiguousarray(w_in[:, :, hg_cols]), "wdf": np.ascontiguousarray(w_in[:, :, df_cols]),
            "wml": np.ascontiguousarray(w_in[:, :, 4608:5440]),
            "lbz": np.ascontiguousarray(f("hgrn_lb_logits")[:, g * 384:(g + 1) * 384]),
            "dfv": dg, "dfb": np.ascontiguousarray(rel_bias[bk][..., 2 * g:2 * g + 2].transpose(3, 0, 1, 2)),
            "wuq": np.ascontiguousarray(f("mla_w_uq")[:, :, g * 576:(g + 1) * 576]),
            "wukv": np.ascontiguousarray(f("mla_w_ukv")[:, :, g * 768:(g + 1) * 768]),
            "sel": sel,
        })
    ims = []
    for c in range(8):
        m = dict(shared)
        m.update(per_g[c % 2])
        m["x"] = np.ascontiguousarray(x[c // 2, (c % 2) * NTOK:(c % 2 + 1) * NTOK, :].T)
        ims.append(m)
    nc = build_fused(L)
    res = run_bass_kernel_spmd(nc, ims, core_ids=list(range(8))).results
    out = np.empty((4, SEQ, D), np.float32)
    for c in range(8):
        out[c // 2, (c % 2) * NTOK:(c % 2 + 1) * NTOK, :] = np.asarray(res[c]["xo"]).T
    return out
```

```python
import math
from contextlib import ExitStack

import numpy as np
import ml_dtypes

import concourse.bass as bass
import concourse.mybir as mybir
from concourse.bass_utils import run_bass_kernel_spmd

F32 = mybir.dt.float32
BF16 = mybir.dt.bfloat16
AF = mybir.ActivationFunctionType
ALU = mybir.AluOpType
AX = mybir.AxisListType

D = 2048
DFF = 5504
NF = DFF // 128
KD = D // 128
EPS = 1e-6
NEG = -60.0


class Buf:
    __slots__ = ("name", "w", "r")

    def __init__(self, name=""):
        self.name = name
        self.w = None
        self.r = []


class T:
    __slots__ = ("t", "b")

    def __init__(self, t, name=""):
        self.t = t
        self.b = Buf(name)


class Sched:
    def __init__(self, nc, stack):
        self.nc = nc
        self.stack = stack
        self.engs = {"pe": nc.tensor, "act": nc.scalar, "dve": nc.vector, "pool": nc.gpsimd, "sp": nc.sync}
        self.sem = {}
        self.cnt = {}
        for e in self.engs:
            self.sem[e] = stack.enter_context(nc.semaphore("s_" + e))
            self.cnt[e] = 0
        self.waited = {}
        self.dsems = {}
        self.dcnt = {}
        self.nsem = 0

    def dsem(self, name):
        if name not in self.dsems:
            s = self.stack.enter_context(self.nc.semaphore("d_" + name))
            self.dsems[name] = s
            self.dcnt[name] = 0
        return name

    def _wait(self, e, deps):
        best = {}
        for (k, v) in deps:
            if k == e and e == "pe":
                continue
            if k not in best or best[k] < v:
                best[k] = v
        for k, v in best.items():
            if self.waited.get((e, k), 0) >= v:
                continue
            s = self.sem[k] if k in self.sem else self.dsems[k]
            self.engs[e].wait_ge(s, v)
            self.waited[(e, k)] = v

    def op(self, e, fn, rd=(), wr=(), dsem=None):
        self.nops = getattr(self, "nops", 0) + 1
        if self.nops > getattr(self, "max_ops", 1 << 60):
            return None
        deps = []
        rd = [b.b if isinstance(b, T) else b for b in rd]
        wr = [b.b if isinstance(b, T) else b for b in wr]
        wr = wr + [b for b in rd if b.name.startswith(("bank", "psb"))]
        rd = [b for b in rd if not b.name.startswith(("bank", "psb"))]
        for b in rd:
            b = b.b if isinstance(b, T) else b
            if b.w is not None:
                deps.append(b.w)
        for b in wr:
            b = b.b if isinstance(b, T) else b
            if b.w is not None:
                deps.append(b.w)
            deps.extend(b.r)
        self._wait(e, deps)
        ins = fn(self.engs[e])
        if dsem is not None:
            self.dsem(dsem)
            self.dcnt[dsem] += 16
            ins.then_inc(self.dsems[dsem], 16)
            tok = (dsem, self.dcnt[dsem])
        else:
            self.cnt[e] += 1
            ins.then_inc(self.sem[e], 1)
            tok = (e, self.cnt[e])
        for b in rd:
            b = b.b if isinstance(b, T) else b
            b.r.append(tok)
            if len(b.r) > 64:
                m = {}
                for (k, v) in b.r:
                    if k not in m or m[k] < v:
                        m[k] = v
                b.r = list(m.items())
        for b in wr:
            b = b.b if isinstance(b, T) else b
            b.w = tok
            b.r = []
        return tok

    def cc(self, fn):
        name = self.dsem("ccsem")
        ins = fn(self.engs["pool"])
        self.dcnt[name] += 1
        ins.then_inc(self.dsems[name], 1)

    def barrier(self, engines=None):
        toks = [(e, c) for e, c in self.cnt.items() if c > 0]
        toks += [(n, c) for n, c in self.dcnt.items() if c > 0]
        for e in (engines or self.engs):
            self._wait_all(e, toks)

    def _wait_all(self, e, toks):
        for (k, v) in toks:
            if k == e:
                continue
            if self.waited.get((e, k), 0) >= v:
                continue
            s = self.sem[k] if k in self.sem else self.dsems[k]
            self.engs[e].wait_ge(s, v)
            self.waited[(e, k)] = v


class Ring:
    def __init__(self, items):
        self.items = items
        self.i = 0

    def next(self):
        x = self.items[self.i % len(self.items)]
        self.i += 1
        return x


class Ctx:
    def __init__(self, nc, st):
        self.nc = nc
        self.sfx = ""
        self.S = Sched(nc, st)
        self.psf = [st.enter_context(nc.psum_tensor("psf%d" % i, [128, 512], F32)) for i in range(7)]
        self.psb = st.enter_context(nc.psum_tensor("psb", [128, 1024], BF16))
        self.bankT = [T(self.psf[i], "bank%d" % i) for i in range(7)]
        self.psbT = [T(None, "psb") for i in range(8)]
        for x in self.psbT:
            x.b = self.psbT[0].b

    def banks(self, ids):
        return [self.bankT[i] for i in ids]


def _groups(n, g):
    out = []
    i = 0
    while i < n:
        out.append((i, min(g, n - i)))
        i += g
    return out


def emit_rstd(S, out_ap, out_T, in_ap, in_T, scale, tmp_ap, tmp_T):
    S.op("act", lambda e: e.activation(out=tmp_ap, in_=in_ap, func=AF.Ln, scale=scale, bias=EPS), rd=[in_T], wr=[tmp_T])
    S.op("act", lambda e: e.activation(out=out_ap, in_=tmp_ap, func=AF.Exp, scale=-0.5), rd=[tmp_T], wr=[out_T])


def norm_tile(C, st_bufs, x_d, tok0, TT, banks):
    S = C.S
    NS = TT // 512
    xin, sq, hT, rstd, lnt, gcol, ones = (st_bufs[k] for k in ("xin", "sq", "hT", "rstd", "lnt", "gcol", "ones"))
    bk = [banks.next() for _ in range(NS)]
    for k in range(KD):
        xi = xin.next()
        si = sq.next()
        S.op("sp", lambda e: e.dma_start(out=xi.t[:, :], in_=x_d[k * 128:(k + 1) * 128, tok0:tok0 + TT]), wr=[xi], dsem=xi.b.name)
        S.op("act", lambda e: e.activation(out=si.t[:, :], in_=xi.t[:, :], func=AF.Square), rd=[xi], wr=[si])
        for s in range(NS):
            S.op("pe", lambda e: e.matmul(bk[s].t[:, :], ones.t[:, :], si.t[:, s * 512:(s + 1) * 512], start=(k == 0), stop=(k == KD - 1)),
                 rd=[ones, si], wr=[bk[s]])
    for s in range(NS):
        emit_rstd(S, rstd.t[:, s * 512:(s + 1) * 512], rstd, bk[s].t[:, :], bk[s], 1.0 / D, rstd.t[:, s * 512:(s + 1) * 512], rstd)
    for k in range(KD):
        xi = xin.next()
        S.op("sp", lambda e: e.dma_start(out=xi.t[:, :], in_=x_d[k * 128:(k + 1) * 128, tok0:tok0 + TT]), wr=[xi], dsem=xi.b.name)
        S.op("dve", lambda e: e.scalar_tensor_tensor(out=hT.t[:, k, :], in0=xi.t[:, :], scalar=gcol.t[:, k:k + 1], in1=rstd.t[:, :],
                                                     op0=ALU.mult, op1=ALU.mult), rd=[xi, gcol, rstd], wr=[hT])


def norm_bufs(C, st, g_d, TT, pfx):
    nc, S = C.nc, C.S
    sb = lambda n, sh, dt: T(st.enter_context(nc.sbuf_tensor(pfx + n + C.sfx, sh, dt)), pfx + n)
    B = {}
    B["xin"] = Ring([sb("xin%d" % i, [128, TT], F32) for i in range(2)])
    B["sq"] = Ring([sb("sq%d" % i, [128, TT], BF16) for i in range(2)])
    B["hT"] = sb("hT", [128, KD, TT], BF16)
    B["rstd"] = sb("rstd", [128, TT], F32)
    B["lnt"] = None
    B["gcol"] = sb("gcol", [128, KD], F32)
    B["ones"] = sb("ones", [128, 128], BF16)
    S.op("sp", lambda e: e.dma_start(out=B["gcol"].t[:, :], in_=g_d.rearrange("(k p) -> p k", p=128), allow_slow_non_contiguous=True),
         wr=[B["gcol"]], dsem=pfx + "gcol")
    S.op("dve", lambda e: e.memset(B["ones"].t[:, :], 1.0), wr=[B["ones"]])
    return B


def phase_ffn(C, x_d, xo_d, g_d, wg_d, wu_d, wd_d, NTOK, TT, pfx):
    nc, S = C.nc, C.S
    NS = TT // 512
    S.barrier()
    with ExitStack() as st:
        sb = lambda n, sh, dt: T(st.enter_context(nc.sbuf_tensor(pfx + n + C.sfx, sh, dt)), pfx + n)
        NB = norm_bufs(C, st, g_d, TT, pfx)
        hT = NB["hT"]
        GW = 256
        wg = Ring([sb("wg%d" % i, [128, KD, GW], BF16) for i in range(2)])
        wu = Ring([sb("wu%d" % i, [128, KD, GW], BF16) for i in range(2)])
        actT = [T(None, pfx + "act%d" % f) for f in range(NF)]
        actT_t = st.enter_context(nc.sbuf_tensor(pfx + "actT" + C.sfx, [128, NF, TT], BF16))
        sg = Ring([sb("sg%d" % i, [128, 512], BF16) for i in range(2)])
        DGC = 4 // NS
        wd = Ring([sb("wd%d" % i, [128, 4, DGC * 128], BF16) for i in range(8)])
        xres = Ring([sb("xres%d" % i, [128, 512], F32) for i in range(2)])
        yo = Ring([sb("yo%d" % i, [128, 512], F32) for i in range(2)])
        banks = Ring(C.banks([0, 1, 2, 3, 4, 5]))
        dbanks = Ring(C.banks([0, 1, 2, 3, 4, 5, 6]))
        for tt in range(NTOK // TT):
            tok0 = tt * TT
            norm_tile(C, NB, x_d, tok0, TT, banks)
            for (f0, nf) in _groups(NF, GW // 128):
                g_s = wg.next()
                u_s = wu.next()
                S.op("pool", lambda e: e.dma_start(out=g_s.t[:, :, 0:nf * 128],
                                                   in_=wg_d[:, f0 * 128:(f0 + nf) * 128].rearrange("(k p) f -> p k f", p=128)),
                     wr=[g_s], dsem=g_s.b.name)
                S.op("pool", lambda e: e.dma_start(out=u_s.t[:, :, 0:nf * 128],
                                                   in_=wu_d[:, f0 * 128:(f0 + nf) * 128].rearrange("(k p) f -> p k f", p=128)),
                     wr=[u_s], dsem=u_s.b.name)
                for fi in range(nf):
                    f = f0 + fi
                    for s in range(NS):
                        bg = banks.next()
                        bu = banks.next()
                        for k in range(KD):
                            S.op("pe", lambda e: e.matmul(bg.t[:, :], g_s.t[:, k, fi * 128:(fi + 1) * 128], hT.t[:, k, s * 512:(s + 1) * 512],
                                                          start=(k == 0), stop=(k == KD - 1)), rd=[g_s, hT], wr=[bg])
                        for k in range(KD):
                            S.op("pe", lambda e: e.matmul(bu.t[:, :], u_s.t[:, k, fi * 128:(fi + 1) * 128], hT.t[:, k, s * 512:(s + 1) * 512],
                                                          start=(k == 0), stop=(k == KD - 1)), rd=[u_s, hT], wr=[bu])
                        sgi = sg.next()
                        S.op("act", lambda e: e.activation(out=sgi.t[:, :], in_=bg.t[:, :], func=AF.Silu), rd=[bg], wr=[sgi])
                        S.op("dve", lambda e: e.tensor_tensor(out=actT_t[:, f, s * 512:(s + 1) * 512], in0=bu.t[:, :], in1=sgi.t[:, :], op=ALU.mult),
                             rd=[bu, sgi], wr=[actT[f]])
            for dg in range(KD // DGC):
                db = [[dbanks.next() for s in range(NS)] for dd in range(DGC)]
                for (f0, nf) in _groups(NF, 4):
                    w_s = wd.next()
                    S.op("pool", lambda e: e.dma_start(out=w_s.t[:, 0:nf, :],
                                                       in_=wd_d[f0 * 128:(f0 + nf) * 128, dg * DGC * 128:(dg + 1) * DGC * 128].rearrange("(j p) c -> p j c", p=128)),
                         wr=[w_s], dsem=w_s.b.name)
                    for fi in range(nf):
                        f = f0 + fi
                        for dd in range(DGC):
                            for s in range(NS):
                                S.op("pe", lambda e: e.matmul(db[dd][s].t[:, :], w_s.t[:, fi, dd * 128:(dd + 1) * 128],
                                                              actT_t[:, f, s * 512:(s + 1) * 512], start=(f == 0), stop=(f == NF - 1)),
                                     rd=[w_s, actT[f]], wr=[db[dd][s]])
                for dd in range(DGC):
                    d = dg * DGC + dd
                    for s in range(NS):
                        xr = xres.next()
                        y = yo.next()
                        c0 = tok0 + s * 512
                        S.op("sp", lambda e: e.dma_start(out=xr.t[:, :], in_=x_d[d * 128:(d + 1) * 128, c0:c0 + 512]), wr=[xr], dsem=xr.b.name)
                        S.op("dve", lambda e: e.scalar_tensor_tensor(out=y.t[:, :], in0=db[dd][s].t[:, :], scalar=0.5, in1=xr.t[:, :],
                                                                     op0=ALU.mult, op1=ALU.add), rd=[db[dd][s], xr], wr=[y])
                        S.op("sp", lambda e: e.dma_start(out=xo_d[d * 128:(d + 1) * 128, c0:c0 + 512], in_=y.t[:, :]), rd=[y], dsem=y.b.name)
        S.barrier()


def phase_norm(C, x_d, g_d, xn_d, NTOK, TT, pfx):
    nc, S = C.nc, C.S
    S.barrier()
    with ExitStack() as st:
        NB = norm_bufs(C, st, g_d, TT, pfx)
        banks = Ring(C.banks([0, 1, 2, 3]))
        for tt in range(NTOK // TT):
            norm_tile(C, NB, x_d, tt * TT, TT, banks)
            for j in range(TT // 128):
                S.op("sp", lambda e: e.dma_start(out=xn_d[tt * (TT // 128) + j, :, :, :], in_=NB["hT"].t[:, :, j * 128:(j + 1) * 128]),
                     rd=[NB["hT"]], dsem=pfx + "xnout")
        S.barrier()


def phase_wout(C, x_d, xo_d, o_d, wo_d, NTOK, pfx, sel_d=None):
    nc, S = C.nc, C.S
    S.barrier()
    with ExitStack() as st:
        sb = lambda n, sh, dt: T(st.enter_context(nc.sbuf_tensor(pfx + n + C.sfx, sh, dt)), pfx + n)
        wo = sb("wo", [128, KD, D], BF16)
        for k in range(KD):
            S.op("pool", lambda e: e.dma_start(out=wo.t[:, k, :], in_=wo_d[k * 128:(k + 1) * 128, :]), wr=[wo], dsem=pfx + "wo")
        ot = Ring([sb("ot%d" % i, [128, KD, 512], BF16) for i in range(2)])
        xres = Ring([sb("xres%d" % i, [128, 512], F32) for i in range(2)])
        yo = Ring([sb("yo%d" % i, [128, 512], F32) for i in range(2)])
        if sel_d is not None:
            selt = load_bcast(C, st, pfx + "sel", sel_d, 16)
            oa, ob = sb("oa", [128, KD, 512], BF16), sb("ob", [128, KD, 512], BF16)
            otmp = sb("otmp", [128, KD, 512], F32)
        banks = Ring(C.banks([0, 1, 2, 3]))
        for s in range(NTOK // 512):
            c0 = s * 512
            o_s = ot.next()
            if sel_d is None:
                S.op("sp", lambda e: e.dma_start(out=o_s.t[:, :, :], in_=o_d[:, c0:c0 + 512].rearrange("(k p) t -> p k t", p=128)),
                     wr=[o_s], dsem=o_s.b.name)
            else:
                S.op("sp", lambda e: e.dma_start(out=oa.t[:, :, :], in_=o_d[:, c0:c0 + 512].rearrange("(k p) t -> p k t", p=128)),
                     wr=[oa], dsem=oa.b.name)
                S.op("sp", lambda e: e.dma_start(out=ob.t[:, :, :], in_=o_d[:, NTOK + c0:NTOK + c0 + 512].rearrange("(k p) t -> p k t", p=128)),
                     wr=[ob], dsem=ob.b.name)
                S.op("dve", lambda e: e.tensor_scalar(out=otmp.t[:, :, :], in0=oa.t[:, :, :], scalar1=selt.t[:, 0:1], scalar2=None, op0=ALU.mult),
                     rd=[oa, selt], wr=[otmp])
                S.op("dve", lambda e: e.scalar_tensor_tensor(out=o_s.t[:, :, :], in0=ob.t[:, :, :], scalar=selt.t[:, 1:2], in1=otmp.t[:, :, :],
                                                             op0=ALU.mult, op1=ALU.add), rd=[ob, selt, otmp], wr=[o_s])
            for d in range(KD):
                bk = banks.next()
                for k in range(KD):
                    S.op("pe", lambda e: e.matmul(bk.t[:, :], wo.t[:, k, d * 128:(d + 1) * 128], o_s.t[:, k, :], start=(k == 0), stop=(k == KD - 1)),
                         rd=[wo, o_s], wr=[bk])
                xr = xres.next()
                y = yo.next()
                S.op("sp", lambda e: e.dma_start(out=xr.t[:, :], in_=x_d[d * 128:(d + 1) * 128, c0:c0 + 512]), wr=[xr], dsem=xr.b.name)
                S.op("dve", lambda e: e.tensor_tensor(out=y.t[:, :], in0=bk.t[:, :], in1=xr.t[:, :], op=ALU.add), rd=[bk, xr], wr=[y])
                S.op("sp", lambda e: e.dma_start(out=xo_d[d * 128:(d + 1) * 128, c0:c0 + 512], in_=y.t[:, :]), rd=[y], dsem=y.b.name)
        S.barrier()


def load_bcast(C, st, name, vec_ap, n):
    t = T(st.enter_context(C.nc.sbuf_tensor(name + C.sfx, [128, n], F32)), name)
    C.S.op("sp", lambda e: e.dma_start(out=t.t[:, :], in_=vec_ap.partition_broadcast(128)), wr=[t], dsem=name)
    return t


def load_const(C, st, name, ap, shape, dt):
    t = T(st.enter_context(C.nc.sbuf_tensor(name + C.sfx, shape, dt)), name)
    eng = "sp" if dt == F32 else "pool"
    C.S.op(eng, lambda e: e.dma_start(out=t.t[:, :], in_=ap), wr=[t], dsem=name)
    return t


def psb_region(C, i):
    return C.psb[:, i * 128:(i + 1) * 128], C.psbT[i]


def phase_hg(C, xn_d, w_d, lbz_d, cm_d, og_d, o_d, consts, NT, pfx, NH=3):
    nc, S = C.nc, C.S
    S.barrier()
    with ExitStack() as st:
        sb = lambda n, sh, dt: T(st.enter_context(nc.sbuf_tensor(pfx + n + C.sfx, sh, dt)), pfx + n)
        w = sb("w", [128, KD, NH * 512], BF16)
        for h in range(NH):
            S.op("pool", lambda e: e.dma_start(out=w.t[:, :, h * 512:(h + 1) * 512],
                                               in_=w_d[:, h * 512:(h + 1) * 512].rearrange("(k p) c -> p k c", p=128)), wr=[w], dsem=pfx + "w")
        cf = load_const(C, st, pfx + "cf", consts, [128, 512], F32)
        cb = sb("cb", [128, 512], BF16)
        S.op("dve", lambda e: e.tensor_copy(out=cb.t[:, :], in_=cf.t[:, :]), rd=[cf], wr=[cb])
        ident = T(cb.t[:, 0:128]); ident.b = cb.b
        u2 = T(cf.t[:, 128:256]); u2.b = cf.b
        cmat = T(cb.t[:, 256:384]); cmat.b = cb.b
        sel = T(cb.t[:, 384:390]); sel.b = cb.b
        ogain = load_bcast(C, st, pfx + "ogain", og_d, 128)
        NC_ = NH * 128
        lbz = sb("lbz", [128, 4, NC_], F32)
        for j in range(4):
            S.op("sp", lambda e: e.dma_start(out=lbz.t[:, j, :], in_=lbz_d[j, :].partition_broadcast(128)), wr=[lbz], dsem=pfx + "lbz")
        cm = load_bcast(C, st, pfx + "cm", cm_d, 16)
        lb = sb("lb", [128, NC_], F32)
        oml = sb("oml", [128, NC_], F32)
        den = sb("den", [128, NC_], F32)
        S.op("act", lambda e: e.activation(out=lbz.t[:, :, :], in_=lbz.t[:, :, :], func=AF.Exp), rd=[lbz], wr=[lbz])
        S.op("dve", lambda e: e.tensor_tensor(out=den.t[:, :], in0=lbz.t[:, 0, :], in1=lbz.t[:, 1, :], op=ALU.add), rd=[lbz], wr=[den])
        S.op("dve", lambda e: e.tensor_tensor(out=den.t[:, :], in0=den.t[:, :], in1=lbz.t[:, 2, :], op=ALU.add), rd=[lbz, den], wr=[den])
        S.op("dve", lambda e: e.tensor_tensor(out=den.t[:, :], in0=den.t[:, :], in1=lbz.t[:, 3, :], op=ALU.add), rd=[lbz, den], wr=[den])
        S.op("dve", lambda e: e.reciprocal(out=den.t[:, :], in_=den.t[:, :]), rd=[den], wr=[den])
        S.op("dve", lambda e: e.tensor_scalar(out=lb.t[:, :], in0=lbz.t[:, 0, :], scalar1=cm.t[:, 0:1], scalar2=None, op0=ALU.mult), rd=[lbz, cm], wr=[lb])
        for j in range(1, 4):
            S.op("dve", lambda e: e.scalar_tensor_tensor(out=lb.t[:, :], in0=lbz.t[:, j, :], scalar=cm.t[:, j:j + 1], in1=lb.t[:, :],
                                                         op0=ALU.mult, op1=ALU.add), rd=[lbz, cm, lb], wr=[lb])
        S.op("dve", lambda e: e.tensor_tensor(out=lb.t[:, :], in0=lb.t[:, :], in1=den.t[:, :], op=ALU.mult), rd=[lb, den], wr=[lb])
        S.op("dve", lambda e: e.tensor_scalar(out=oml.t[:, :], in0=lb.t[:, :], scalar1=-1.0, scalar2=1.0, op0=ALU.mult, op1=ALU.add), rd=[lb], wr=[oml])

        St = [sb("S%d" % h, [128, 128], F32) for h in range(NH)]
        for h in range(NH):
            S.op("dve", lambda e: e.memset(St[h].t[:, :], 0.0), wr=[St[h]])
        xt = Ring([sb("xt%d" % i, [128, KD, 128], BF16) for i in range(2)])
        R = lambda n, sh, dt, k=2 * NH: Ring([sb("%s%d" % (n, i), sh, dt) for i in range(k)])
        ef, eg, ff, lf, kk, E, Ei = (R(n, [128, 128], F32) for n in ("ef", "eg", "ff", "lf", "kk", "E", "Ei"))
        esc = R("esc", [128, 6], F32)
        qh, kh, vv, kT, sm, Sp0, Sp1, yb = (R(n, [128, 128], BF16) for n in ("qh", "kh", "vv", "kT", "sm", "Sp0", "Sp1", "yb"))
        A = R("A", [128, 128], F32, 4 * NH)
        lfh, lfl = R("lfh", [128, 128], BF16), R("lfl", [128, 128], BF16)
        ss, lnv, rs = (R(n, [128, 1], F32) for n in ("ss", "lnv", "rs"))
        junk = R("junk", [128, 128], F32)
        qA = [sb("qA%d" % i, [128, 128], BF16) for i in range(2 * NH)]
        qB = [sb("qB%d" % i, [128, 128], BF16) for i in range(2 * NH)]
        for i in range(2 * NH):
            S.op("dve", lambda e: e.memset(qA[i].t[:, :], 0.0), wr=[qA[i]])
            S.op("dve", lambda e: e.memset(qB[i].t[:, :], 0.0), wr=[qB[i]])
        qAr, qBr = Ring(qA), Ring(qB)
        kA = [sb("kA%d" % i, [128, 128], BF16) for i in range(2 * NH)]
        kB = [sb("kB%d" % i, [128, 128], BF16) for i in range(2 * NH)]
        for i in range(2 * NH):
            S.op("dve", lambda e: e.memset(kA[i].t[:, :], 0.0), wr=[kA[i]])
            S.op("dve", lambda e: e.memset(kB[i].t[:, :], 0.0), wr=[kB[i]])
        kAr, kBr = Ring(kA), Ring(kB)
        ost = R("ost", [128, NH, 128], BF16, 2)
        pjb = C.banks([0, 1, 2])
        wkh = C.banks([3, 4, 5])
        b6 = C.banks([6])[0]
        pbi = [0]

        def psb_next():
            i = pbi[0] % 8
            pbi[0] += 1
            return psb_region(C, i)

        for t in range(NT):
            x_s = xt.next()
            S.op("sp", lambda e: e.dma_start(out=x_s.t[:, :, :], in_=xn_d[t, :, :, :]), wr=[x_s], dsem=x_s.b.name)
            o_s = ost.next()
            def head(h):
                pj = pjb[h]
                for k in range(KD):
                    S.op("pe", lambda e: e.matmul(pj.t[:, :], x_s.t[:, k, :], w.t[:, k, h * 512:(h + 1) * 512], start=(k == 0), stop=(k == KD - 1)),
                         rd=[x_s, w], wr=[pj])
                yield
                q_ap, fl_ap, vi_ap, gt_ap = (pj.t[:, i * 128:(i + 1) * 128] for i in range(4))
                hs = slice(h * 128, (h + 1) * 128)
                ef_, eg_, ff_, lf_, kk_, E_, Ei_ = (r.next() for r in (ef, eg, ff, lf, kk, E, Ei))
                S.op("act", lambda e: e.activation(out=ef_.t[:, :], in_=fl_ap, func=AF.Exp, scale=-1.0), rd=[pj], wr=[ef_])
                S.op("act", lambda e: e.activation(out=eg_.t[:, :], in_=gt_ap, func=AF.Exp, scale=-1.0), rd=[pj], wr=[eg_])
                yield
                S.op("dve", lambda e: e.tensor_scalar(out=ef_.t[:, :], in0=ef_.t[:, :], scalar1=1.0, scalar2=None, op0=ALU.add), rd=[ef_], wr=[ef_])
                S.op("dve", lambda e: e.reciprocal(out=ef_.t[:, :], in_=ef_.t[:, :]), rd=[ef_], wr=[ef_])
                S.op("dve", lambda e: e.tensor_tensor(out=ff_.t[:, :], in0=ef_.t[:, :], in1=oml.t[:, hs], op=ALU.mult), rd=[ef_, oml], wr=[ff_])
                S.op("dve", lambda e: e.tensor_tensor(out=ff_.t[:, :], in0=ff_.t[:, :], in1=lb.t[:, hs], op=ALU.add), rd=[ff_, lb], wr=[ff_])
                S.op("act", lambda e: e.activation(out=lf_.t[:, :], in_=ff_.t[:, :], func=AF.Ln), rd=[ff_], wr=[lf_])
                yield
                S.op("dve", lambda e: e.tensor_scalar(out=kk_.t[:, :], in0=ff_.t[:, :], scalar1=-1.0, scalar2=1.0, op0=ALU.mult, op1=ALU.add), rd=[ff_], wr=[kk_])
                wk = wkh[h]
                lh_, ll_ = lfh.next(), lfl.next()
                S.op("dve", lambda e: e.tensor_copy(out=lh_.t[:, :], in_=lf_.t[:, :]), rd=[lf_], wr=[lh_])
                S.op("dve", lambda e: e.tensor_tensor(out=ll_.t[:, :], in0=lf_.t[:, :], in1=lh_.t[:, :], op=ALU.subtract), rd=[lf_, lh_], wr=[ll_])
                yield
                S.op("pe", lambda e: e.matmul(wk.t[:, 0:128], cmat.t[:, :], lh_.t[:, :], start=True, stop=False), rd=[cmat, lh_], wr=[wk])
                S.op("pe", lambda e: e.matmul(wk.t[:, 0:128], cmat.t[:, :], ll_.t[:, :], start=False, stop=True), rd=[cmat, ll_], wr=[wk])
                S.op("pe", lambda e: e.matmul(b6.t[:, h * 8:h * 8 + 6], lh_.t[:, :], sel.t[:, :], start=True, stop=False), rd=[sel, lh_], wr=[b6])
                S.op("pe", lambda e: e.matmul(b6.t[:, h * 8:h * 8 + 6], ll_.t[:, :], sel.t[:, :], start=False, stop=True), rd=[sel, ll_], wr=[b6])
                yield
                esc_ = esc.next()
                S.op("act", lambda e: e.activation(out=E_.t[:, :], in_=wk.t[:, 0:128], func=AF.Exp), rd=[wk], wr=[E_])
                S.op("act", lambda e: e.activation(out=Ei_.t[:, :], in_=wk.t[:, 0:128], func=AF.Exp, scale=-1.0), rd=[wk], wr=[Ei_])
                S.op("act", lambda e: e.activation(out=esc_.t[:, :], in_=b6.t[:, h * 8:h * 8 + 6], func=AF.Exp), rd=[b6], wr=[esc_])
                yield
                qh_, kh_, vv_, kT_, sm_, Sp0_, Sp1_, yb_ = (r.next() for r in (qh, kh, vv, kT, sm, Sp0, Sp1, yb))
                S.op("dve", lambda e: e.scalar_tensor_tensor(out=qh_.t[:, :], in0=q_ap, scalar=128.0 ** -0.5, in1=E_.t[:, :], op0=ALU.mult, op1=ALU.mult),
                     rd=[pj, E_], wr=[qh_])
                S.op("dve", lambda e: e.tensor_tensor(out=kh_.t[:, :], in0=kk_.t[:, :], in1=Ei_.t[:, :], op=ALU.mult), rd=[kk_, Ei_], wr=[kh_])
                S.op("dve", lambda e: e.tensor_copy(out=vv_.t[:, :], in_=vi_ap), rd=[pj], wr=[vv_])
                yield
                tq_ap, tq = psb_next()
                tk_ap, tk = psb_next()
                S.op("pe", lambda e: e.transpose(tq_ap, qh_.t[:, :], ident.t[:, :]), rd=[qh_, ident], wr=[tq])
                S.op("pe", lambda e: e.transpose(tk_ap, kh_.t[:, :], ident.t[:, :]), rd=[kh_, ident], wr=[tk])
                yield
                qA_, qB_ = qAr.next(), qBr.next()
                S.op("act", lambda e: e.copy(out=qA_.t[:, 0:64], in_=tq_ap[:, 0:64]), rd=[tq], wr=[qA_])
                S.op("dve", lambda e: e.tensor_copy(out=qB_.t[:, 64:128], in_=tq_ap[:, 64:128]), rd=[tq], wr=[qB_])
                S.op("act", lambda e: e.copy(out=kT_.t[:, :], in_=tk_ap), rd=[tk], wr=[kT_])
                yield
                wk2 = wkh[h]
                S.op("pe", lambda e: e.matmul(wk2.t[:, 0:128], kT_.t[:, :], qA_.t[:, :], start=True, stop=False), rd=[kT_, qA_], wr=[wk2])
                S.op("pe", lambda e: e.matmul(wk2.t[:, 0:128], kT_.t[:, :], qB_.t[:, :], start=False, stop=True), rd=[kT_, qB_], wr=[wk2])
                S.op("dve", lambda e: e.tensor_tensor(out=sm_.t[:, :], in0=wk2.t[:, 0:128], in1=u2.t[:, :], op=ALU.mult), rd=[wk2, u2], wr=[sm_])
                yield
                kA_, kB_ = kAr.next(), kBr.next()
                S.op("act", lambda e: e.copy(out=kA_.t[0:64, :], in_=kh_.t[0:64, :]), rd=[kh_], wr=[kA_])
                S.op("act", lambda e: e.copy(out=kB_.t[64:128, :], in_=kh_.t[64:128, :]), rd=[kh_], wr=[kB_])
                S.op("pe", lambda e: e.matmul(wk2.t[:, 128:256], kA_.t[:, :], vv_.t[:, :], start=True, stop=True), rd=[kA_, vv_], wr=[wk2])
                S.op("pe", lambda e: e.matmul(wk2.t[:, 256:384], kB_.t[:, :], vv_.t[:, :], start=True, stop=True), rd=[kB_, vv_], wr=[wk2])
                yield
                Sh = St[h]
                for c, Sp_ in ((0, Sp0_), (1, Sp1_)):
                    A_ = A.next()
                    S.op("dve", lambda e: e.tensor_scalar(out=Sp_.t[:, :], in0=Sh.t[:, :], scalar1=esc_.t[:, 3 * c:3 * c + 1], scalar2=None, op0=ALU.mult),
                         rd=[Sh, esc_], wr=[Sp_])
                    S.op("dve", lambda e: e.tensor_scalar(out=A_.t[:, :], in0=Sh.t[:, :], scalar1=esc_.t[:, 3 * c + 2:3 * c + 3], scalar2=None, op0=ALU.mult),
                         rd=[Sh, esc_], wr=[A_])
                    S.op("dve", lambda e: e.scalar_tensor_tensor(out=Sh.t[:, :], in0=wk2.t[:, 128 * (c + 1):128 * (c + 2)],
                                                                 scalar=esc_.t[:, 3 * c + 1:3 * c + 2], in1=A_.t[:, :], op0=ALU.mult, op1=ALU.add),
                         rd=[wk2, esc_, A_], wr=[Sh])
                yield
                S.op("pe", lambda e: e.matmul(wk2.t[:, 384:512], sm_.t[:, :], vv_.t[:, :], start=True, stop=False), rd=[sm_, vv_], wr=[wk2])
                S.op("pe", lambda e: e.matmul(wk2.t[:, 384:512], qA_.t[:, :], Sp0_.t[:, :], start=False, stop=False), rd=[qA_, Sp0_], wr=[wk2])
                S.op("pe", lambda e: e.matmul(wk2.t[:, 384:512], qB_.t[:, :], Sp1_.t[:, :], start=False, stop=True), rd=[qB_, Sp1_], wr=[wk2])
                yield
                o_ap = wk2.t[:, 384:512]
                ss_, lnv_, rs_, junk_ = ss.next(), lnv.next(), rs.next(), junk.next()
                S.op("dve", lambda e: e.memset(ss_.t[:, :], 0.0), wr=[ss_])
                S.op("act", lambda e: e.activation(out=junk_.t[:, :], in_=o_ap, func=AF.Square, accum_out=ss_.t[:, 0:1]), rd=[wk2, ss_], wr=[junk_, ss_])
                emit_rstd(S, rs_.t[:, :], rs_, ss_.t[:, :], ss_, 1.0 / 128, lnv_.t[:, :], lnv_)
                yield
                S.op("dve", lambda e: e.tensor_scalar(out=eg_.t[:, :], in0=eg_.t[:, :], scalar1=1.0, scalar2=None, op0=ALU.add), rd=[eg_], wr=[eg_])
                S.op("dve", lambda e: e.reciprocal(out=eg_.t[:, :], in_=eg_.t[:, :]), rd=[eg_], wr=[eg_])
                S.op("dve", lambda e: e.tensor_tensor(out=eg_.t[:, :], in0=eg_.t[:, :], in1=gt_ap, op=ALU.mult), rd=[eg_, pj], wr=[eg_])
                S.op("dve", lambda e: e.tensor_tensor(out=eg_.t[:, :], in0=eg_.t[:, :], in1=ogain.t[:, :], op=ALU.mult), rd=[eg_, ogain], wr=[eg_])
                S.op("dve", lambda e: e.scalar_tensor_tensor(out=yb_.t[:, :], in0=o_ap, scalar=rs_.t[:, 0:1], in1=eg_.t[:, :], op0=ALU.mult, op1=ALU.mult),
                     rd=[wk2, rs_, eg_], wr=[yb_])
                yield
                ty_ap, ty = psb_next()
                S.op("pe", lambda e: e.transpose(ty_ap, yb_.t[:, :], ident.t[:, :]), rd=[yb_, ident], wr=[ty])
                S.op("act", lambda e: e.copy(out=o_s.t[:, h, :], in_=ty_ap), rd=[ty], wr=[o_s])

            run_interleaved([head(h) for h in range(NH)])
            S.op("sp", lambda e: e.dma_start(out=o_d[0:NH * 128, t * 128:(t + 1) * 128].rearrange("(h p) t -> p h t", p=128), in_=o_s.t[:, :, :]),
                 rd=[o_s], dsem=o_s.b.name)
        S.barrier()


def attn_tile(C, t, qk_parts, v_fn, acc, acc_ap, scale, far_bias, near, negshift, P, tmpf, qkb):
    S = C.S
    np_ = len(qk_parts)
    groups = [list(range(g0, min(g0 + 4, t + 1))) for g0 in range(0, t + 1, 4)]

    def emit_qk(grp):
        bank = qkb.next()
        for j, kt in enumerate(grp):
            for pi, (K_fn, q_ap, q_T) in enumerate(qk_parts):
                k_ap, k_T = K_fn(kt)
                S.op("pe", lambda e: e.matmul(bank.t[:, j * 128:(j + 1) * 128], k_ap, q_ap, start=(pi == 0), stop=(pi == np_ - 1)),
                     rd=[k_T, q_T], wr=[bank])
        return bank

    def emit_exp(grp, bank):
        P_ = P.next()
        nfar = len([kt for kt in grp if (kt - t) not in near])
        if nfar:
            S.op("act", lambda e: e.activation(out=P_.t[:, 0:nfar * 128], in_=bank.t[:, 0:nfar * 128], func=AF.Exp, scale=scale, bias=far_bias.t[:, :]),
                 rd=[bank, far_bias], wr=[P_])
        for j, kt in enumerate(grp):
            if (kt - t) in near:
                b = near[kt - t]
                tm = tmpf.next()
                S.op("dve", lambda e: e.scalar_tensor_tensor(out=tm.t[:, :], in0=bank.t[:, j * 128:(j + 1) * 128], scalar=scale, in1=b.t[:, :],
                                                             op0=ALU.mult, op1=ALU.add), rd=[bank, b], wr=[tm])
                S.op("act", lambda e: e.activation(out=P_.t[:, j * 128:(j + 1) * 128], in_=tm.t[:, :], func=AF.Exp, bias=negshift.t[:, :]),
                     rd=[tm, negshift], wr=[P_])
        return P_

    def emit_pv(grp, P_):
        for j, kt in enumerate(grp):
            v_ap, v_T = v_fn(kt)
            S.op("pe", lambda e: e.matmul(acc_ap, P_.t[:, j * 128:(j + 1) * 128], v_ap, start=(kt == 0), stop=(kt == t)), rd=[P_, v_T], wr=[acc])

    look = len(qkb.items) >= 2
    bank = emit_qk(groups[0])
    yield
    for gi, grp in enumerate(groups):
        nxt = None
        if look and gi + 1 < len(groups):
            nxt = emit_qk(groups[gi + 1])
            yield
        P_ = emit_exp(grp, bank)
        yield
        emit_pv(grp, P_)
        yield
        if not look and gi + 1 < len(groups):
            nxt = emit_qk(groups[gi + 1])
            yield
        bank = nxt


def run_interleaved(gens):
    gens = list(gens)
    while gens:
        for g in list(gens):
            try:
                next(g)
            except StopIteration:
                gens.remove(g)


def sub_T(parent, ap):
    x = T(ap)
    x.b = parent.b
    return x


VW = 132


def phase_df(C, xn_d, w_d, vecs_d, bias_d, o_d, consts, mask_d, NT, pfx, NH=2, row0=384):
    nc, S = C.nc, C.S
    SHIFT = 8.0
    S.barrier()
    with ExitStack() as st:
        sb = lambda n, sh, dt: T(st.enter_context(nc.sbuf_tensor(pfx + n + C.sfx, sh, dt)), pfx + n)
        R = lambda n, sh, dt, k=2: Ring([sb("%s%d" % (n, i), sh, dt) for i in range(k)])
        w = sb("w", [128, KD, NH * 384], BF16)
        for h in range(NH):
            S.op("pool", lambda e: e.dma_start(out=w.t[:, :, h * 384:(h + 1) * 384],
                                               in_=w_d[:, h * 384:(h + 1) * 384].rearrange("(k p) c -> p k c", p=128)), wr=[w], dsem=pfx + "w")
        cf = load_const(C, st, pfx + "cf", consts, [128, 512], F32)
        cb = sb("cb", [128, 512], BF16)
        S.op("dve", lambda e: e.tensor_copy(out=cb.t[:, :], in_=cf.t[:, :]), rd=[cf], wr=[cb])
        ident = sub_T(cb, cb.t[:, 0:128])
        mask = load_const(C, st, pfx + "mask", mask_d, [128, 128], F32)
        vecs = sb("vecs", [128, 8, 128], F32)
        for j in range(8):
            S.op("sp", lambda e: e.dma_start(out=vecs.t[:, j, :], in_=vecs_d[j, :].partition_broadcast(128)), wr=[vecs], dsem=pfx + "vecs")
        qg, kg = vecs.t[:, 0, 0:64], vecs.t[:, 1, 0:64]
        junk64 = sb("junk64", [128, 64], F32)
        prod = sb("prod", [128, 64], F32)
        lsum = sb("lsum", [128, 2], F32)
        S.op("dve", lambda e: e.memset(lsum.t[:, :], 0.0), wr=[lsum])
        for i in range(2):
            S.op("dve", lambda e: e.tensor_tensor(out=prod.t[:, :], in0=vecs.t[:, 2 + 2 * i, 0:64], in1=vecs.t[:, 3 + 2 * i, 0:64], op=ALU.mult), rd=[vecs], wr=[prod])
            S.op("act", lambda e: e.activation(out=junk64.t[:, :], in_=prod.t[:, :], func=AF.Identity, accum_out=lsum.t[:, i:i + 1]), rd=[prod, lsum], wr=[junk64, lsum])
        S.op("act", lambda e: e.activation(out=lsum.t[:, :], in_=lsum.t[:, :], func=AF.Exp), rd=[lsum], wr=[lsum])
        lam = sb("lam", [128, 1], F32)
        S.op("dve", lambda e: e.tensor_tensor(out=lam.t[:, :], in0=lsum.t[:, 0:1], in1=lsum.t[:, 1:2], op=ALU.subtract), rd=[lsum], wr=[lam])
        S.op("dve", lambda e: e.tensor_tensor(out=lam.t[:, :], in0=lam.t[:, :], in1=vecs.t[:, 7, 0:1], op=ALU.add), rd=[lam, vecs], wr=[lam])
        sgain = sb("sgain", [128, 128], F32)
        S.op("dve", lambda e: e.tensor_scalar(out=sgain.t[:, :], in0=vecs.t[:, 6, :], scalar1=vecs.t[:, 7, 1:2], scalar2=None, op0=ALU.mult), rd=[vecs], wr=[sgain])
        negshift = sb("negshift", [128, 1], F32)
        S.op("dve", lambda e: e.memset(negshift.t[:, :], -SHIFT), wr=[negshift])
        farb = [sb("farb%d" % h, [128, 1], F32) for h in range(NH)]
        for h in range(NH):
            S.op("dve", lambda e: e.tensor_scalar(out=farb[h].t[:, :], in0=vecs.t[:, 7, 2 + h:3 + h], scalar1=-SHIFT, scalar2=None, op0=ALU.add), rd=[vecs], wr=[farb[h]])
        bt = [[sb("bt%d_%d" % (h, r), [128, 128], F32) for r in range(2)] for h in range(NH)]
        for h in range(NH):
            for r in range(2):
                S.op("sp", lambda e: e.dma_start(out=bt[h][r].t[:, :], in_=bias_d[h, r, :, :]), wr=[bt[h][r]], dsem=pfx + "bt%d_%d" % (h, r))
            S.op("dve", lambda e: e.tensor_tensor(out=bt[h][0].t[:, :], in0=bt[h][0].t[:, :], in1=mask.t[:, :], op=ALU.add), rd=[bt[h][0], mask], wr=[bt[h][0]])
        KT = [[sb("KT%d_%d" % (h, m), [128, NT * 128], BF16) for m in range(2)] for h in range(NH)]
        for h in range(NH):
            for m in range(2):
                S.op("dve", lambda e: e.memset(KT[h][m].t[:, :], 0.0), wr=[KT[h][m]])
        Va = sb("Va", [128, NT * NH * VW], BF16)
        S.op("dve", lambda e: e.memset(Va.t[:, :], 1.0), wr=[Va])
        vofs = lambda t_, h_: (t_ * NH + h_) * VW
        xt = R("xt", [128, KD, 128], BF16)
        ss4, ln4, rs4 = (R(n, [128, 4], F32, 3) for n in ("ss4", "ln4", "rs4"))
        junk = R("junk", [128, 128], F32)
        qk = R("qk", [128, 256], BF16, 3)
        qT = [R("qT%d" % h, [128, 128], BF16) for h in range(NH)]
        P = R("P", [128, 512], BF16, 3)
        tmpf = R("tmpf", [128, 128], F32, 3)
        rr = R("rr", [128, 4], F32, 3)
        t2, of = R("t2", [128, 128], F32), R("of", [128, 128], F32)
        ss1, ln1, rs1 = (R(n, [128, 1], F32) for n in ("ss1", "ln1", "rs1"))
        yb = R("yb", [128, 128], BF16)
        ost = R("ost", [128, NH, 128], BF16)
        pjb = C.banks([0, 1])
        qkbs = [Ring(C.banks([0, 1])), Ring(C.banks([2, 3]))]
        Ps = [R("Ps%d_" % m, [128, 512], BF16, 2) for m in range(2)]
        tmpfs = [R("tmpfs%d_" % m, [128, 128], F32, 2) for m in range(2)]
        accs = C.banks([5, 6])
        pbi = [0]

        def psb_next():
            i = pbi[0] % 8
            pbi[0] += 1
            return psb_region(C, i)

        for t in range(NT):
            x_s = xt.next()
            S.op("sp", lambda e: e.dma_start(out=x_s.t[:, :, :], in_=xn_d[t, :, :, :]), wr=[x_s], dsem=x_s.b.name)
            o_s = ost.next()
            qTs = []
            for h in range(NH):
                pj = pjb[h]
                for k in range(KD):
                    S.op("pe", lambda e: e.matmul(pj.t[:, 0:384], x_s.t[:, k, :], w.t[:, k, h * 384:(h + 1) * 384], start=(k == 0), stop=(k == KD - 1)),
                         rd=[x_s, w], wr=[pj])
                ss_, ln_, rs_ = ss4.next(), ln4.next(), rs4.next()
                S.op("dve", lambda e: e.memset(ss_.t[:, :], 0.0), wr=[ss_])
                for j in range(4):
                    jk = junk.next()
                    S.op("act", lambda e: e.activation(out=jk.t[:, 0:64], in_=pj.t[:, j * 64:(j + 1) * 64], func=AF.Square, accum_out=ss_.t[:, j:j + 1]),
                         rd=[pj, ss_], wr=[jk, ss_])
                emit_rstd(S, rs_.t[:, :], rs_, ss_.t[:, :], ss_, 1.0 / 64, ln_.t[:, :], ln_)
                qk_ = qk.next()
                for j in range(4):
                    g_ap = qg if j < 2 else kg
                    S.op("dve", lambda e: e.scalar_tensor_tensor(out=qk_.t[:, j * 64:(j + 1) * 64], in0=pj.t[:, j * 64:(j + 1) * 64], scalar=rs_.t[:, j:j + 1],
                                                                 in1=g_ap, op0=ALU.mult, op1=ALU.mult), rd=[pj, rs_, vecs], wr=[qk_])
                S.op("dve", lambda e: e.tensor_copy(out=Va.t[:, vofs(t, h):vofs(t, h) + 128], in_=pj.t[:, 256:384]), rd=[pj], wr=[Va])
                tq_ap, tq = psb_next()
                tk_ap, tk = psb_next()
                S.op("pe", lambda e: e.transpose(tq_ap, qk_.t[:, 0:128], ident.t), rd=[qk_, ident], wr=[tq])
                S.op("pe", lambda e: e.transpose(tk_ap, qk_.t[:, 128:256], ident.t), rd=[qk_, ident], wr=[tk])
                qT_ = qT[h].next()
                S.op("act", lambda e: e.copy(out=qT_.t[:, :], in_=tq_ap), rd=[tq], wr=[qT_])
                S.op("act", lambda e: e.copy(out=KT[h][0].t[0:64, t * 128:(t + 1) * 128], in_=tk_ap[0:64, :]), rd=[tk], wr=[KT[h][0]])
                S.op("act", lambda e: e.copy(out=KT[h][1].t[64:128, t * 128:(t + 1) * 128], in_=tk_ap[64:128, :]), rd=[tk], wr=[KT[h][1]])
                qTs.append(qT_)
            for h in range(NH):
                acc0, acc1 = accs
                run_interleaved([
                    attn_tile(C, t, [(lambda kt, h=h, m=m: (KT[h][m].t[:, kt * 128:(kt + 1) * 128], KT[h][m]), qTs[h].t[:, :], qTs[h])],
                              lambda kt, h=h: (Va.t[:, vofs(kt, h):vofs(kt, h) + 129], Va), accs[m], accs[m].t[:, 0:129], 0.125, farb[h],
                              {0: bt[h][0], -1: bt[h][1]}, negshift, Ps[m], tmpfs[m], qkbs[m])
                    for m in range(2)])
                r_ = rr.next()
                S.op("dve", lambda e: e.reciprocal(out=r_.t[:, 0:1], in_=acc0.t[:, 128:129]), rd=[acc0], wr=[r_])
                S.op("dve", lambda e: e.reciprocal(out=r_.t[:, 1:2], in_=acc1.t[:, 128:129]), rd=[acc1], wr=[r_])
                S.op("dve", lambda e: e.tensor_tensor(out=r_.t[:, 2:3], in0=r_.t[:, 1:2], in1=lam.t[:, :], op=ALU.mult), rd=[r_, lam], wr=[r_])
                t2_, of_ = t2.next(), of.next()
                S.op("dve", lambda e: e.tensor_scalar(out=t2_.t[:, :], in0=acc1.t[:, 0:128], scalar1=r_.t[:, 2:3], scalar2=None, op0=ALU.mult), rd=[acc1, r_], wr=[t2_])
                S.op("dve", lambda e: e.scalar_tensor_tensor(out=of_.t[:, :], in0=acc0.t[:, 0:128], scalar=r_.t[:, 0:1], in1=t2_.t[:, :],
                                                             op0=ALU.mult, op1=ALU.subtract), rd=[acc0, r_, t2_], wr=[of_])
                s1, l1, r1, jk = ss1.next(), ln1.next(), rs1.next(), junk.next()
                S.op("dve", lambda e: e.memset(s1.t[:, :], 0.0), wr=[s1])
                S.op("act", lambda e: e.activation(out=jk.t[:, :], in_=of_.t[:, :], func=AF.Square, accum_out=s1.t[:, 0:1]), rd=[of_, s1], wr=[jk, s1])
                emit_rstd(S, r1.t[:, :], r1, s1.t[:, :], s1, 1.0 / 128, l1.t[:, :], l1)
                yb_ = yb.next()
                S.op("dve", lambda e: e.scalar_tensor_tensor(out=yb_.t[:, :], in0=of_.t[:, :], scalar=r1.t[:, 0:1], in1=sgain.t[:, :],
                                                             op0=ALU.mult, op1=ALU.mult), rd=[of_, r1, sgain], wr=[yb_])
                ty_ap, ty = psb_next()
                S.op("pe", lambda e: e.transpose(ty_ap, yb_.t[:, :], ident.t), rd=[yb_, ident], wr=[ty])
                S.op("act", lambda e: e.copy(out=o_s.t[:, h, :], in_=ty_ap), rd=[ty], wr=[o_s])
            S.op("sp", lambda e: e.dma_start(out=o_d[row0:row0 + NH * 128, t * 128:(t + 1) * 128].rearrange("(h p) t -> p h t", p=128), in_=o_s.t[:, :, :]),
                 rd=[o_s], dsem=o_s.b.name)
        S.barrier()


def phase_ml(C, xn_d, w_d, wuq_d, wukv_d, vecs_d, cs_d, o_d, consts, mask_d, NT, pfx, NH=3, row0=640):
    nc, S = C.nc, C.S
    SHIFT = 14.0
    SCALE = 192.0 ** -0.5
    S.barrier()
    with ExitStack() as st:
        sb = lambda n, sh, dt: T(st.enter_context(nc.sbuf_tensor(pfx + n + C.sfx, sh, dt)), pfx + n)
        R = lambda n, sh, dt, k=2: Ring([sb("%s%d" % (n, i), sh, dt) for i in range(k)])
        w = sb("w", [128, KD, 832], BF16)
        for (c0, c1) in ((0, 512), (512, 832)):
            S.op("pool", lambda e: e.dma_start(out=w.t[:, :, c0:c1], in_=w_d[:, c0:c1].rearrange("(k p) c -> p k c", p=128)), wr=[w], dsem=pfx + "w")
        wuq = sb("wuq", [128, 4, NH * 192], BF16)
        S.op("pool", lambda e: e.dma_start(out=wuq.t[:, :, :], in_=wuq_d.rearrange("(k p) c -> p k c", p=128)), wr=[wuq], dsem=pfx + "wuq")
        wukv = sb("wukv", [128, 2, NH * 256], BF16)
        S.op("pool", lambda e: e.dma_start(out=wukv.t[:, :, :], in_=wukv_d.rearrange("(k p) c -> p k c", p=128)), wr=[wukv], dsem=pfx + "wukv")
        cf = load_const(C, st, pfx + "cf", consts, [128, 512], F32)
        cb = sb("cb", [128, 512], BF16)
        S.op("dve", lambda e: e.tensor_copy(out=cb.t[:, :], in_=cf.t[:, :]), rd=[cf], wr=[cb])
        ident = sub_T(cb, cb.t[:, 0:128])
        mask = load_const(C, st, pfx + "mask", mask_d, [128, 128], F32)
        vecs = sb("vecs", [128, 4, 512], F32)
        for j in range(4):
            S.op("sp", lambda e: e.dma_start(out=vecs.t[:, j, :], in_=vecs_d[j, :].partition_broadcast(128)), wr=[vecs], dsem=pfx + "vecs")
        cs = load_const(C, st, pfx + "cs", cs_d, [128, 2 * NT * 32], F32)
        negshift = sb("negshift", [128, 1], F32)
        S.op("dve", lambda e: e.memset(negshift.t[:, :], -SHIFT), wr=[negshift])
        KTa = [sb("KTa%d" % h, [128, NT * 128], BF16) for h in range(NH)]
        KTb = [sb("KTb%d" % h, [128, NT * 128], BF16) for h in range(NH)]
        Va = sb("Va", [128, NT * NH * VW], BF16)
        S.op("dve", lambda e: e.memset(Va.t[:, :], 1.0), wr=[Va])
        vofs = lambda t_, h_: (t_ * NH + h_) * VW
        xt = R("xt", [128, KD, 128], BF16)
        ss2, ln2, rs2 = (R(n, [128, 4], F32) for n in ("ss2", "ln2", "rs2"))
        junk = R("junk", [128, 512], F32)
        cqn = R("cqn", [128, 512], BF16)
        ckvn = R("ckvn", [128, 256], BF16)
        kr = R("kr", [128, 64], F32)
        cqT = R("cqT", [128, 4, 128], BF16)
        ckvT = R("ckvT", [128, 2, 128], BF16)
        ssh, lnh, rsh = (R(n, [128, 4], F32, 3) for n in ("ssh", "lnh", "rsh"))
        qn_b = R("qnb", [128, 128], BF16, 3)
        kn_b = R("knb", [128, 128], BF16, 3)
        qr_f = R("qrf", [128, 64], F32, 3)
        kr_f = R("krf", [128, 64], F32, 3)
        ra, rb = R("ra", [128, 32], F32, 3), R("rb", [128, 32], F32, 3)
        qrp = [sb("qrp%d" % i, [128, 128], BF16) for i in range(2)]
        krp = [sb("krp%d" % i, [128, 128], BF16) for i in range(2)]
        for i in range(2):
            S.op("dve", lambda e: e.memset(qrp[i].t[:, :], 0.0), wr=[qrp[i]])
            S.op("dve", lambda e: e.memset(krp[i].t[:, :], 0.0), wr=[krp[i]])
        qrpr, krpr = Ring(qrp), Ring(krp)
        qTa = [R("qTa%d" % h, [128, 128], BF16) for h in range(NH)]
        qTb = [R("qTb%d" % h, [128, 128], BF16) for h in range(NH)]
        P = R("P", [128, 512], BF16, 3)
        tmpf = R("tmpf", [128, 128], F32, 3)
        rr = R("rr", [128, 1], F32, 3)
        yb = R("yb", [128, 128], BF16)
        ost = R("ost", [128, NH, 128], BF16)
        b0, b1, b2, b3, b4 = C.banks([0, 1, 2, 3, 4])
        qreg = [(b2, 0), (b2, 192), (b3, 0)]
        kvreg = [(b4, 0), (b4, 256), (b3, 192)]
        qkbs = [Ring(C.banks([h])) for h in range(NH)]
        accs = C.banks([3, 4, 5])
        Ps = [R("Ps%d_" % h, [128, 512], BF16, 2) for h in range(NH)]
        tmpfs = [R("tmpfs%d_" % h, [128, 128], F32, 2) for h in range(NH)]
        accT = C.banks([6])[0]
        pbi = [0]

        def psb_next():
            i = pbi[0] % 8
            pbi[0] += 1
            return psb_region(C, i)

        def transpose_to(src_ap, src_T, dst_ap, dst_T, eng="act"):
            tp_ap, tp = psb_next()
            S.op("pe", lambda e: e.transpose(tp_ap, src_ap, ident.t), rd=[src_T, ident], wr=[tp])
            if eng == "act":
                S.op("act", lambda e: e.copy(out=dst_ap, in_=tp_ap), rd=[tp], wr=[dst_T])
            else:
                S.op("dve", lambda e: e.tensor_copy(out=dst_ap, in_=tp_ap), rd=[tp], wr=[dst_T])

        def rope(src_T, src, dst_T, dst, t):
            cos = cs.t[:, t * 32:(t + 1) * 32]
            sin = cs.t[:, NT * 32 + t * 32:NT * 32 + (t + 1) * 32]
            x1, x2 = src[:, 0:32], src[:, 32:64]
            a, b = ra.next(), rb.next()
            S.op("dve", lambda e: e.tensor_tensor(out=a.t[:, :], in0=x1, in1=cos, op=ALU.mult), rd=[src_T, cs], wr=[a])
            S.op("dve", lambda e: e.tensor_tensor(out=b.t[:, :], in0=x2, in1=sin, op=ALU.mult), rd=[src_T, cs], wr=[b])
            S.op("dve", lambda e: e.tensor_tensor(out=dst[:, 0:32], in0=a.t[:, :], in1=b.t[:, :], op=ALU.subtract), rd=[a, b], wr=[dst_T])
            a, b = ra.next(), rb.next()
            S.op("dve", lambda e: e.tensor_tensor(out=a.t[:, :], in0=x2, in1=cos, op=ALU.mult), rd=[src_T, cs], wr=[a])
            S.op("dve", lambda e: e.tensor_tensor(out=b.t[:, :], in0=x1, in1=sin, op=ALU.mult), rd=[src_T, cs], wr=[b])
            S.op("dve", lambda e: e.tensor_tensor(out=dst[:, 32:64], in0=a.t[:, :], in1=b.t[:, :], op=ALU.add), rd=[a, b], wr=[dst_T])

        for t in range(NT):
            x_s = xt.next()
            S.op("sp", lambda e: e.dma_start(out=x_s.t[:, :, :], in_=xn_d[t, :, :, :]), wr=[x_s], dsem=x_s.b.name)
            o_s = ost.next()
            for k in range(KD):
                S.op("pe", lambda e: e.matmul(b0.t[:, :], x_s.t[:, k, :], w.t[:, k, 0:512], start=(k == 0), stop=(k == KD - 1)), rd=[x_s, w], wr=[b0])
            for k in range(KD):
                S.op("pe", lambda e: e.matmul(b1.t[:, 0:320], x_s.t[:, k, :], w.t[:, k, 512:832], start=(k == 0), stop=(k == KD - 1)), rd=[x_s, w], wr=[b1])
            ss_, ln_, rs_ = ss2.next(), ln2.next(), rs2.next()
            S.op("dve", lambda e: e.memset(ss_.t[:, :], 0.0), wr=[ss_])
            jk = junk.next()
            S.op("act", lambda e: e.activation(out=jk.t[:, :], in_=b0.t[:, :], func=AF.Square, accum_out=ss_.t[:, 0:1]), rd=[b0, ss_], wr=[jk, ss_])
            jk = junk.next()
            S.op("act", lambda e: e.activation(out=jk.t[:, 0:256], in_=b1.t[:, 0:256], func=AF.Square, accum_out=ss_.t[:, 1:2]), rd=[b1, ss_], wr=[jk, ss_])
            kr_ = kr.next()
            S.op("act", lambda e: e.copy(out=kr_.t[:, :], in_=b1.t[:, 256:320]), rd=[b1], wr=[kr_])
            jk = junk.next()
            S.op("act", lambda e: e.activation(out=jk.t[:, 0:64], in_=kr_.t[:, :], func=AF.Square, accum_out=ss_.t[:, 2:3]), rd=[kr_, ss_], wr=[jk, ss_])
            S.op("dve", lambda e: e.tensor_scalar(out=ss_.t[:, 0:1], in0=ss_.t[:, 0:1], scalar1=0.5, scalar2=None, op0=ALU.mult), rd=[ss_], wr=[ss_])
            emit_rstd(S, rs_.t[:, 0:2], rs_, ss_.t[:, 0:2], ss_, 1.0 / 256, ln_.t[:, 0:2], ln_)
            cqn_, ckvn_ = cqn.next(), ckvn.next()
            S.op("dve", lambda e: e.scalar_tensor_tensor(out=cqn_.t[:, :], in0=b0.t[:, :], scalar=rs_.t[:, 0:1], in1=vecs.t[:, 0, :], op0=ALU.mult, op1=ALU.mult),
                 rd=[b0, rs_, vecs], wr=[cqn_])
            S.op("dve", lambda e: e.scalar_tensor_tensor(out=ckvn_.t[:, :], in0=b1.t[:, 0:256], scalar=rs_.t[:, 1:2], in1=vecs.t[:, 1, 0:256], op0=ALU.mult, op1=ALU.mult),
                 rd=[b1, rs_, vecs], wr=[ckvn_])
            cqT_, ckvT_ = cqT.next(), ckvT.next()
            for r in range(4):
                transpose_to(cqn_.t[:, r * 128:(r + 1) * 128], cqn_, cqT_.t[:, r, :], cqT_, "act" if r % 2 == 0 else "dve")
            for r in range(2):
                transpose_to(ckvn_.t[:, r * 128:(r + 1) * 128], ckvn_, ckvT_.t[:, r, :], ckvT_, "act" if r % 2 == 0 else "dve")
            for h in range(NH):
                qb, qo = qreg[h]
                for r in range(4):
                    S.op("pe", lambda e: e.matmul(qb.t[:, qo:qo + 192], cqT_.t[:, r, :], wuq.t[:, r, h * 192:(h + 1) * 192], start=(r == 0), stop=(r == 3)),
                         rd=[cqT_, wuq], wr=[qb])
                kb, ko = kvreg[h]
                for r in range(2):
                    S.op("pe", lambda e: e.matmul(kb.t[:, ko:ko + 256], ckvT_.t[:, r, :], wukv.t[:, r, h * 256:(h + 1) * 256], start=(r == 0), stop=(r == 1)),
                         rd=[ckvT_, wukv], wr=[kb])
            qTs = []
            for h in range(NH):
                qb, qo = qreg[h]
                kb, ko = kvreg[h]
                sh_, lh_, rh_ = ssh.next(), lnh.next(), rsh.next()
                S.op("dve", lambda e: e.memset(sh_.t[:, :], 0.0), wr=[sh_])
                jk = junk.next()
                S.op("act", lambda e: e.activation(out=jk.t[:, 0:192], in_=qb.t[:, qo:qo + 192], func=AF.Square, accum_out=sh_.t[:, 0:1]), rd=[qb, sh_], wr=[jk, sh_])
                jk = junk.next()
                S.op("act", lambda e: e.activation(out=jk.t[:, 0:128], in_=kb.t[:, ko:ko + 128], func=AF.Square, accum_out=sh_.t[:, 1:2]), rd=[kb, sh_], wr=[jk, sh_])
                S.op("dve", lambda e: e.tensor_tensor(out=sh_.t[:, 1:2], in0=sh_.t[:, 1:2], in1=ss_.t[:, 2:3], op=ALU.add), rd=[sh_, ss_], wr=[sh_])
                emit_rstd(S, rh_.t[:, 0:2], rh_, sh_.t[:, 0:2], sh_, 1.0 / 192, lh_.t[:, 0:2], lh_)
                qn_, kn_, qr_, kf_ = qn_b.next(), kn_b.next(), qr_f.next(), kr_f.next()
                S.op("dve", lambda e: e.scalar_tensor_tensor(out=qn_.t[:, :], in0=qb.t[:, qo:qo + 128], scalar=rh_.t[:, 0:1], in1=vecs.t[:, 2, 0:128], op0=ALU.mult, op1=ALU.mult),
                     rd=[qb, rh_, vecs], wr=[qn_])
                S.op("dve", lambda e: e.scalar_tensor_tensor(out=qr_.t[:, :], in0=qb.t[:, qo + 128:qo + 192], scalar=rh_.t[:, 0:1], in1=vecs.t[:, 2, 128:192], op0=ALU.mult, op1=ALU.mult),
                     rd=[qb, rh_, vecs], wr=[qr_])
                S.op("dve", lambda e: e.scalar_tensor_tensor(out=kn_.t[:, :], in0=kb.t[:, ko:ko + 128], scalar=rh_.t[:, 1:2], in1=vecs.t[:, 3, 0:128], op0=ALU.mult, op1=ALU.mult),
                     rd=[kb, rh_, vecs], wr=[kn_])
                S.op("dve", lambda e: e.scalar_tensor_tensor(out=kf_.t[:, :], in0=kr_.t[:, :], scalar=rh_.t[:, 1:2], in1=vecs.t[:, 3, 128:192], op0=ALU.mult, op1=ALU.mult),
                     rd=[kr_, rh_, vecs], wr=[kf_])
                S.op("act", lambda e: e.copy(out=Va.t[:, vofs(t, h):vofs(t, h) + 128], in_=kb.t[:, ko + 128:ko + 256]), rd=[kb], wr=[Va])
                qp_, kp_ = qrpr.next(), krpr.next()
                rope(qr_, qr_.t, qp_, qp_.t, t)
                rope(kf_, kf_.t, kp_, kp_.t, t)
                qa_, qb_ = qTa[h].next(), qTb[h].next()
                transpose_to(qn_.t[:, :], qn_, qa_.t[:, :], qa_, "act")
                transpose_to(qp_.t[:, :], qp_, qb_.t[:, :], qb_, "dve")
                transpose_to(kn_.t[:, :], kn_, KTa[h].t[:, t * 128:(t + 1) * 128], KTa[h], "act")
                transpose_to(kp_.t[:, :], kp_, KTb[h].t[:, t * 128:(t + 1) * 128], KTb[h], "dve")
                qTs.append((qa_, qb_))
            run_interleaved([
                attn_tile(C, t, [(lambda kt, h=h: (KTa[h].t[:, kt * 128:(kt + 1) * 128], KTa[h]), qTs[h][0].t[:, :], qTs[h][0]),
                                 (lambda kt, h=h: (KTb[h].t[:, kt * 128:(kt + 1) * 128], KTb[h]), qTs[h][1].t[:, :], qTs[h][1])],
                          lambda kt, h=h: (Va.t[:, vofs(kt, h):vofs(kt, h) + 129], Va), accs[h], accs[h].t[:, 0:129], SCALE, negshift,
                          {0: mask}, negshift, Ps[h], tmpfs[h], qkbs[h])
                for h in range(NH)])
            for h in range(NH):
                a0 = 0
                acc = accs[h]
                r_ = rr.next()
                S.op("dve", lambda e: e.reciprocal(out=r_.t[:, 0:1], in_=acc.t[:, a0 + 128:a0 + 129]), rd=[acc], wr=[r_])
                yb_ = yb.next()
                S.op("dve", lambda e: e.tensor_scalar(out=yb_.t[:, :], in0=acc.t[:, a0:a0 + 128], scalar1=r_.t[:, 0:1], scalar2=None, op0=ALU.mult), rd=[acc, r_], wr=[yb_])
                transpose_to(yb_.t[:, :], yb_, o_s.t[:, h, :], o_s, "act")
            S.op("sp", lambda e: e.dma_start(out=o_d[row0:row0 + NH * 128, t * 128:(t + 1) * 128].rearrange("(h p) t -> p h t", p=128), in_=o_s.t[:, :, :]),
                 rd=[o_s], dsem=o_s.b.name)
        S.barrier()


NTOK = 2048
SEQ = 4096
NT_SEQ = SEQ // 128
TT_FFN = 1024


def _ffn_inputs(nc, sfx):
    g = nc.dram_tensor("g" + sfx, [D], F32, kind="ExternalInput").ap()
    wg = nc.dram_tensor("wg" + sfx, [D, DFF], F32, kind="ExternalInput").ap()
    wu = nc.dram_tensor("wu" + sfx, [D, DFF], F32, kind="ExternalInput").ap()
    wd = nc.dram_tensor("wd" + sfx, [DFF, D], F32, kind="ExternalInput").ap()
    return g, wg, wu, wd


def build_PA():
    nc = bass.Bass("TRN2", target_bir_lowering=False)
    x = nc.dram_tensor("x", [D, NTOK], F32, kind="ExternalInput").ap()
    g, wg, wu, wd = _ffn_inputs(nc, "")
    gm = nc.dram_tensor("gm", [D], F32, kind="ExternalInput").ap()
    xo = nc.dram_tensor("xo", [D, NTOK], F32, kind="ExternalOutput").ap()
    xn = nc.dram_tensor("xn", [NTOK // 128, 128, KD, 128], BF16, kind="ExternalOutput").ap()
    with ExitStack() as st:
        C = Ctx(nc, st)
        phase_ffn(C, x, xo, g, wg, wu, wd, NTOK, TT_FFN, "A")
        phase_norm(C, xo, gm, xn, NTOK, 512, "N")
        C.S.barrier()
    return nc


def build_PM():
    nc = bass.Bass("TRN2", target_bir_lowering=False)
    dt = lambda n, sh, d=F32: nc.dram_tensor(n, sh, d, kind="ExternalInput").ap()
    xn = dt("xn", [NT_SEQ, 128, KD, 128], BF16)
    whg, wdf, wml = dt("whg", [D, 1536]), dt("wdf", [D, 768]), dt("wml", [D, 832])
    lbz, cm, og = dt("lbz", [4, 384]), dt("cm", [16]), dt("og", [128])
    dfv, dfb = dt("dfv", [8, 128]), dt("dfb", [2, 2, 128, 128])
    wuq, wukv, mlv = dt("wuq", [512, 576]), dt("wukv", [256, 768]), dt("mlv", [4, 512])
    cs = dt("cs", [128, 2 * NT_SEQ * 32])
    call, cmask = dt("c_all", [128, 512]), dt("c_mask", [128, 128])
    o = nc.dram_tensor("o", [1024, SEQ], BF16, kind="ExternalOutput").ap()
    with ExitStack() as st:
        C = Ctx(nc, st)
        phase_hg(C, xn, whg, lbz, cm, og, o, call, NT_SEQ, "H", 3)
        phase_df(C, xn, wdf, dfv, dfb, o, call, cmask, NT_SEQ, "F", 2, row0=384)
        phase_ml(C, xn, wml, wuq, wukv, mlv, cs, o, call, cmask, NT_SEQ, "M", 3, row0=640)
        C.S.barrier()
    return nc


def build_PB():
    nc = bass.Bass("TRN2", target_bir_lowering=False)
    x = nc.dram_tensor("x", [D, NTOK], F32, kind="ExternalInput").ap()
    o = nc.dram_tensor("o", [D, NTOK], BF16, kind="ExternalInput").ap()
    wo = nc.dram_tensor("wo", [D, D], F32, kind="ExternalInput").ap()
    g, wg, wu, wd = _ffn_inputs(nc, "")
    xmid = nc.dram_tensor("xmid", [D, NTOK], F32, kind="ExternalOutput").ap()
    xo = nc.dram_tensor("xo", [D, NTOK], F32, kind="ExternalOutput").ap()
    with ExitStack() as st:
        C = Ctx(nc, st)
        phase_wout(C, x, xmid, o, wo, NTOK, "O")
        phase_ffn(C, xmid, xo, g, wg, wu, wd, NTOK, TT_FFN, "B")
        C.S.barrier()
    return nc


def _consts():
    idx = np.arange(128)
    same = (idx[:, None] // 64) == (idx[None, :] // 64)
    u2 = (same & (idx[:, None] <= idx[None, :])).astype(np.float32)
    mid = (same & ((idx[:, None] % 64) <= 31)).astype(np.float32)
    cmat = u2 - mid
    sel = np.zeros((128, 128), np.float32)
    for c in range(2):
        inch = (idx // 64) == c
        sel[:, 3 * c + 0] = inch & ((idx % 64) <= 31)
        sel[:, 3 * c + 1] = inch & ((idx % 64) >= 32)
        sel[:, 3 * c + 2] = inch
    c_all = np.ascontiguousarray(np.concatenate([np.eye(128, dtype=np.float32), u2, cmat, sel], axis=1))
    ok = (idx[:, None] // 64) <= (idx[None, :] // 64)
    c_mask = np.where(ok, 0.0, NEG).astype(np.float32)
    pos = np.arange(SEQ, dtype=np.float32)
    freqs = (np.float32(10000.0) ** (-np.arange(0, 64, 2, dtype=np.float32) / np.float32(64))).astype(np.float32)
    ang = pos[:, None] * freqs[None, :]
    cos = np.cos(ang).astype(np.float32).reshape(NT_SEQ, 128, 32).transpose(1, 0, 2).reshape(128, NT_SEQ * 32)
    sin = np.sin(ang).astype(np.float32).reshape(NT_SEQ, 128, 32).transpose(1, 0, 2).reshape(128, NT_SEQ * 32)
    cs = np.ascontiguousarray(np.concatenate([cos, sin], axis=1))
    return c_all, c_mask, cs


def _t5_bucket_idx():
    import jax
    import jax.numpy as jnp
    idx = np.arange(128)
    out = []
    with jax.default_device(jax.devices("cpu")[0]):
        for r in (0, -1):
            rel = jnp.asarray(((idx[:, None] + 128 * r) - idx[None, :]).astype(np.int32))
            half, max_exact = 16, 8
            ret = (rel > 0).astype(jnp.int32) * half
            n = jnp.abs(rel)
            large = max_exact + (jnp.log(jnp.maximum(n, 1).astype(jnp.float32) / max_exact)
                                 / math.log(128 / max_exact) * (half - max_exact)).astype(jnp.int32)
            large = jnp.minimum(large, half - 1)
            out.append(np.asarray(ret + jnp.where(n < max_exact, n, large)))
    return np.stack(out, 0)


_IN_OFF = [0, 768, 1536, 2304, 3072, 3584, 4096, 4608, 5120, 5376, 5440]


PAIRS = [[0, 1], [2, 3], [4, 5], [6, 7]]


def build_fused(L=4):
    nc = bass.Bass("TRN2", target_bir_lowering=False)
    dt = lambda n, sh, d=F32: nc.dram_tensor(n, sh, d, kind="ExternalInput").ap()
    x = dt("x", [D, NTOK])
    ffn = {}
    for ab in ("a", "b"):
        ffn[ab] = (dt("ffn_%s_norm" % ab, [L, D]), dt("ffn_%s_w_gate" % ab, [L, D, DFF]), dt("ffn_%s_w_up" % ab, [L, D, DFF]),
                   dt("ffn_%s_w_down" % ab, [L, DFF, D]))
    gm = dt("mix_norm", [L, D])
    whg, wdf, wml = dt("whg", [L, D, 1536]), dt("wdf", [L, D, 768]), dt("wml", [L, D, 832])
    lbz, cm, og = dt("lbz", [4, 384]), dt("cm", [L, 16]), dt("og", [L, 128])
    dfv, dfb = dt("dfv", [L, 8, 128]), dt("dfb", [2, 2, 128, 128])
    wuq, wukv, mlv = dt("wuq", [L, 512, 576]), dt("wukv", [L, 256, 768]), dt("mlv", [L, 4, 512])
    cs = dt("cs", [128, 2 * NT_SEQ * 32])
    call, cmask = dt("c_all", [128, 512]), dt("c_mask", [128, 128])
    wo = dt("wo", [L, D, D])
    sel = dt("sel", [16])
    xo = nc.dram_tensor("xo", [D, NTOK], F32, kind="ExternalOutput").ap()
    internal = lambda n, sh, d: nc.dram_tensor(n, sh, d, kind="Internal").ap()
    local = lambda n, sh, d: nc.dram_tensor(n, sh, d, addr_space="Local", kind="Internal").ap()
    with ExitStack() as st:
        C = Ctx(nc, st)
        S = C.S
        xcur = x
        for l in range(L):
            C.sfx = "_%d" % l
            xa = internal("xa%d" % l, [D, NTOK], F32)
            xb = internal("xb%d" % l, [D, NTOK], F32)
            xc = xo if l == L - 1 else internal("xc%d" % l, [D, NTOK], F32)
            xns = internal("xns%d" % l, [NTOK // 128, 128, KD, 128], BF16)
            xnf = local("xnf%d" % l, [NT_SEQ, 128, KD, 128], BF16)
            osd = internal("osd%d" % l, [1024, SEQ], BF16)
            ofl = local("ofl%d" % l, [2048, SEQ], BF16)
            g, wg, wu, wd = ffn["a"]
            phase_ffn(C, xcur, xa, g[l], wg[l], wu[l], wd[l], NTOK, TT_FFN, "A")
            phase_norm(C, xa, gm[l], xns, NTOK, 512, "N")
            xns2 = xns.rearrange("n p k t -> (n p) (k t)")
            xnf2 = xnf.rearrange("n p k t -> (n p) (k t)")
            S.barrier()
            for j in range(8):
                S.cc(lambda e: e.collective_compute("AllGather", ALU.bypass, replica_groups=PAIRS,
                                                    ins=[xns2[j * 256:(j + 1) * 256, :]], outs=[xnf2[j * 512:(j + 1) * 512, :]]))
            S.barrier()
            xv = XnView(xnf2)
            phase_hg(C, xv, whg[l], lbz, cm[l], og[l], osd, call, NT_SEQ, "H", 3)
            phase_df(C, xv, wdf[l], dfv[l], dfb, osd, call, cmask, NT_SEQ, "F", 2, row0=384)
            phase_ml(C, xv, wml[l], wuq[l], wukv[l], mlv[l], cs, osd, call, cmask, NT_SEQ, "M", 3, row0=640)
            S.barrier()
            for j in range(8):
                S.cc(lambda e: e.collective_compute("AllGather", ALU.bypass, replica_groups=PAIRS,
                                                    ins=[osd[j * 128:(j + 1) * 128, :]], outs=[ofl[j * 256:(j + 1) * 256, :]]))
            S.barrier()
            phase_wout(C, xa, xb, ofl, wo[l], NTOK, "O", sel_d=sel)
            g, wg, wu, wd = ffn["b"]
            phase_ffn(C, xb, xc, g[l], wg[l], wu[l], wd[l], NTOK, TT_FFN, "B")
            xcur = xc
        S.barrier()
    return nc


class XnView:
    def __init__(self, g2):
        self.g2 = g2

    def __getitem__(self, key):
        t = key[0]
        rank, lt = t // 16, t % 16
        r0 = (lt // 2) * 512 + rank * 256 + (lt % 2) * 128
        return self.g2[r0:r0 + 128, :].rearrange("p (k t) -> p k t", t=128)


def _gathered_row_perm():
    perm = []
    for q in range(16):
        j, r = q // 2, q % 2
        if j < 3:
            b = 3 * r + j
        elif j < 5:
            b = 6 + 2 * r + (j - 3)
        else:
            b = 10 + 3 * r + (j - 5)
        perm += list(range(b * 128, (b + 1) * 128))
    return np.asarray(perm)


def kernel(**inputs):
    f = lambda k: np.ascontiguousarray(np.asarray(inputs[k], dtype=np.float32))
    x = f("x")
    L = 4
    c_all, c_mask, cs = _consts()
    bk = _t5_bucket_idx()
    rel_bias = f("rel_bias")
    w_in = f("w_in")
    ar = np.arange(128)
    shared = {k: f(k) for k in ("ffn_a_norm", "ffn_a_w_gate", "ffn_a_w_up", "ffn_a_w_down", "mix_norm",
                                "ffn_b_norm", "ffn_b_w_gate", "ffn_b_w_up", "ffn_b_w_down")}
    shared["wo"] = np.ascontiguousarray(f("w_out")[:, _gathered_row_perm(), :])
    shared["og"] = f("hgrn_out_norm")
    shared["cs"], shared["c_all"], shared["c_mask"] = cs, c_all, c_mask
    cm = np.zeros((L, 16), np.float32)
    dfv = np.zeros((L, 8, 128), np.float32)
    mlv = np.zeros((L, 4, 512), np.float32)
    for l in range(L):
        linit = 0.8 - 0.6 * math.exp(-0.3 * l)
        cm[l, 1:l + 1] = 1.0
        dfv[l, 0, :64] = f("diff_q_norm")[l]
        dfv[l, 1, :64] = f("diff_k_norm")[l]
        dfv[l, 2, :64] = f("diff_lambda_q1")[l]
        dfv[l, 3, :64] = f("diff_lambda_k1")[l]
        dfv[l, 4, :64] = f("diff_lambda_q2")[l]
        dfv[l, 5, :64] = f("diff_lambda_k2")[l]
        dfv[l, 6, :] = f("diff_subln")[l]
        dfv[l, 7, 0] = linit
        dfv[l, 7, 1] = 1.0 - linit
        mlv[l, 0, :] = f("mla_q_lora_norm")[l]
        mlv[l, 1, :256] = f("mla_kv_lora_norm")[l]
        mlv[l, 2, :192] = f("mla_q_norm")[l]
        mlv[l, 3, :192] = f("mla_k_norm")[l]
    shared["cm"], shared["mlv"] = cm, mlv
    per_g = []
    for g in range(2):
        hg_cols = np.concatenate([_IN_OFF[j] + (3 * g + h) * 128 + ar for h in range(3) for j in range(4)])
        df_cols = np.concatenate([_IN_OFF[4 + j] + (2 * g + h) * 128 + ar for h in range(2) for j in range(3)])
        dg = dfv.copy()
        dg[:, 7, 2:4] = rel_bias[15, 2 * g:2 * g + 2]
        sel = np.zeros(16, np.float32)
        sel[g] = 1.0
        per_g.append({
            "whg": np.ascontiguousarray(w_in[:, :, hg_cols]), "wdf": np.ascontiguousarray(w_in[:, :, df_cols]),
            "wml": np.ascontiguousarray(w_in[:, :, 4608:5440]),
            "lbz": np.ascontiguousarray(f("hgrn_lb_logits")[:, g * 384:(g + 1) * 384]),
            "dfv": dg, "dfb": np.ascontiguousarray(rel_bias[bk][..., 2 * g:2 * g + 2].transpose(3, 0, 1, 2)),
            "wuq": np.ascontiguousarray(f("mla_w_uq")[:, :, g * 576:(g + 1) * 576]),
            "wukv": np.ascontiguousarray(f("mla_w_ukv")[:, :, g * 768:(g + 1) * 768]),
            "sel": sel,
        })
    ims = []
    for c in range(8):
        m = dict(shared)
        m.update(per_g[c % 2])
        m["x"] = np.ascontiguousarray(x[c // 2, (c % 2) * NTOK:(c % 2 + 1) * NTOK, :].T)
        ims.append(m)
    nc = build_fused(L)
    res = run_bass_kernel_spmd(nc, ims, core_ids=list(range(8))).results
    out = np.empty((4, SEQ, D), np.float32)
    for c in range(8):
        out[c // 2, (c % 2) * NTOK:(c % 2 + 1) * NTOK, :] = np.asarray(res[c]["xo"]).T
    return out
```
